# Optimizing a Trainium2 kernel written in Bass

```python
import jax, jax.numpy as jnp
from jax import lax
import numpy as np

D_MODEL = 1024
BATCH = 8
SEQ = 4096
DEPTH = 2

EPS = 1e-6
ROPE_THETA = 10000.0
Q_BLOCK = 128

LRU_WIDTH = 1024
LRU_BLOCKS = 8
LRU_BLOCK_W = LRU_WIDTH // LRU_BLOCKS
CONV_WIDTH = 4
LRU_C = 8.0

MLA_HEADS = 8
MLA_NOPE = 64
MLA_ROPE = 32
MLA_QK = MLA_NOPE + MLA_ROPE
MLA_V = 64
Q_LORA = 256
KV_LORA = 128
MLA_WIDTH = MLA_HEADS * MLA_V

DIL_GROUPS = ((128, 1), (512, 4), (2048, 16))
DIL_HEADS = 8
DIL_HD = 64
DIL_QKV = len(DIL_GROUPS) * DIL_HEADS * DIL_HD
DIL_WIDTH = DIL_HEADS * DIL_HD

N_BRANCH = 3
SPLITS = (LRU_WIDTH, LRU_WIDTH, Q_LORA, KV_LORA, MLA_ROPE, MLA_WIDTH,
          DIL_QKV, DIL_QKV, DIL_QKV, DIL_WIDTH, N_BRANCH * D_MODEL)
IN_WIDTH = (2 * LRU_WIDTH + Q_LORA + KV_LORA + MLA_ROPE + MLA_WIDTH
            + 3 * DIL_QKV + DIL_WIDTH + N_BRANCH * D_MODEL)

kernel_name = 'hybrid_rglru_mla_dilated_swa'


def rms_norm(x, g):
    xf = x.astype(jnp.float32)
    y = xf * lax.rsqrt(jnp.mean(xf * xf, axis=-1, keepdims=True) + EPS)
    return (y * g.astype(jnp.float32)).astype(x.dtype)


def rotary(x, pos):
    d = x.shape[-1]
    inv = ROPE_THETA ** (-jnp.arange(0, d, 2, dtype=jnp.float32) / d)
    ang = pos.astype(jnp.float32)[..., None] * inv
    cos = jnp.cos(ang)[:, :, None, :]
    sin = jnp.sin(ang)[:, :, None, :]
    xf = x.astype(jnp.float32)
    x1, x2 = xf[..., : d // 2], xf[..., d // 2:]
    return jnp.concatenate([x1 * cos - x2 * sin, x2 * cos + x1 * sin], axis=-1).astype(x.dtype)


def rg_lru_branch(xb, conv_w, conv_b, w_gx, b_gx, w_ga, b_ga, lam):
    B, S, _ = xb.shape
    xp = jnp.pad(xb, ((0, 0), (CONV_WIDTH - 1, 0), (0, 0)))
    xc = conv_b
    for k in range(CONV_WIDTH):
        xc = xc + xp[:, k:k + S] * conv_w[k]
    xblk = xc.reshape(B, S, LRU_BLOCKS, LRU_BLOCK_W)
    gx = jax.nn.sigmoid(jnp.einsum('bsnc,ncd->bsnd', xblk, w_gx) + b_gx).reshape(B, S, LRU_WIDTH)
    ga = jax.nn.sigmoid(jnp.einsum('bsnc,ncd->bsnd', xblk, w_ga) + b_ga).reshape(B, S, LRU_WIDTH)
    log_a = -LRU_C * ga.astype(jnp.float32) * jax.nn.softplus(-lam.astype(jnp.float32))
    a = jnp.exp(log_a)
    mult = jnp.sqrt(-jnp.expm1(2.0 * log_a))
    b = mult * (gx * xc).astype(jnp.float32)

    def combine(left, right):
        a1, b1 = left
        a2, b2 = right
        return a1 * a2, a2 * b1 + b2

    _, h = lax.associative_scan(combine, (a, b), axis=1)
    return h.astype(xb.dtype)


def causal_attention(q, k, v, scale):
    B, S, H, dk = q.shape
    nb = S // Q_BLOCK
    qb = q.reshape(B, nb, Q_BLOCK, H, dk).transpose(1, 0, 3, 2, 4)
    kt = k.transpose(0, 2, 1, 3)
    vt = v.transpose(0, 2, 1, 3)
    kpos = jnp.arange(S)

    def one_block(args):
        qi, i = args
        s = jnp.einsum('bhqd,bhkd->bhqk', qi, kt).astype(jnp.float32) * scale
        qpos = i * Q_BLOCK + jnp.arange(Q_BLOCK)
        s = jnp.where(kpos[None, :] <= qpos[:, None], s, -jnp.inf)
        p = jax.nn.softmax(s, axis=-1)
        return jnp.einsum('bhqk,bhkd->bhqd', p.astype(vt.dtype), vt)

    o = lax.map(one_block, (qb, jnp.arange(nb)))
    return o.transpose(1, 0, 3, 2, 4).reshape(B, S, H, -1)


def mla_branch(cq, ckv, kr, pos, g_cq, g_ckv, w_uq, w_ukv, g_qn, g_kn):
    B, S, _ = cq.shape
    q = (rms_norm(cq, g_cq) @ w_uq).reshape(B, S, MLA_HEADS, MLA_QK)
    kv = (rms_norm(ckv, g_ckv) @ w_ukv).reshape(B, S, MLA_HEADS, MLA_NOPE + MLA_V)
    k_nope, v = kv[..., :MLA_NOPE], kv[..., MLA_NOPE:]
    k_rope = jnp.broadcast_to(kr[:, :, None, :], (B, S, MLA_HEADS, MLA_ROPE))
    k = jnp.concatenate([k_nope, k_rope], axis=-1)
    q = rms_norm(q, g_qn)
    k = rms_norm(k, g_kn)
    q = jnp.concatenate([q[..., :MLA_NOPE], rotary(q[..., MLA_NOPE:], pos)], axis=-1)
    k = jnp.concatenate([k[..., :MLA_NOPE], rotary(k[..., MLA_NOPE:], pos)], axis=-1)
    o = causal_attention(q, k, v, MLA_QK ** -0.5)
    return o.reshape(B, S, MLA_WIDTH)


def dilated_group(q, k, v, window, dilation):
    B, S, H, d = q.shape
    nk = window // dilation
    span = dilation * nk
    s_pad = -(-S // span) * span
    M = s_pad // dilation
    nb = M // nk

    def to_strided(t):
        t = jnp.pad(t, ((0, 0), (0, s_pad - S), (0, 0), (0, 0))).reshape(B, M, dilation, H, d)
        return t.transpose(0, 2, 3, 1, 4).reshape(B, dilation, H, nb, nk, d)

    qb, kb, vb = to_strided(q), to_strided(k), to_strided(v)

    def prev(t):
        return jnp.pad(t, ((0, 0), (0, 0), (0, 0), (1, 0), (0, 0), (0, 0)))[:, :, :, :-1]

    kw = jnp.concatenate([prev(kb), kb], axis=4)
    vw = jnp.concatenate([prev(vb), vb], axis=4)
    s = jnp.einsum('brhnqd,brhnkd->brhnqk', qb, kw).astype(jnp.float32) * (DIL_HD ** -0.5)
    qi = jnp.arange(nk)[:, None]
    ki = jnp.arange(2 * nk)[None, :]
    band = (ki >= qi) & (ki <= qi + nk)
    not_first = jnp.arange(nb)[:, None, None] > 0
    mask = band[None] & (not_first | (ki >= nk)[None])
    s = jnp.where(mask, s, -jnp.inf)
    m = jnp.max(s, axis=-1, keepdims=True)
    e = jnp.exp(s - m)
    den = jnp.sum(e, axis=-1, keepdims=True)
    o = jnp.einsum('brhnqk,brhnkd->brhnqd', (e / den).astype(vw.dtype), vw)
    lse = (m + jnp.log(den))[..., 0]
    o = o.reshape(B, dilation, H, M, d).transpose(0, 3, 1, 2, 4).reshape(B, s_pad, H, d)[:, :S]
    lse = lse.reshape(B, dilation, H, M).transpose(0, 3, 1, 2).reshape(B, s_pad, H)[:, :S]
    return o, lse


def dilated_branch(q, k, v, pos, g_qn, g_kn):
    B, S, _ = q.shape
    nh = len(DIL_GROUPS) * DIL_HEADS
    q = rotary(rms_norm(q.reshape(B, S, nh, DIL_HD), g_qn), pos)
    k = rotary(rms_norm(k.reshape(B, S, nh, DIL_HD), g_kn), pos)
    v = v.reshape(B, S, nh, DIL_HD)
    outs, lses = [], []
    for gi, (window, dilation) in enumerate(DIL_GROUPS):
        sl = slice(gi * DIL_HEADS, (gi + 1) * DIL_HEADS)
        o, l = dilated_group(q[:, :, sl], k[:, :, sl], v[:, :, sl], window, dilation)
        outs.append(o)
        lses.append(l)
    wts = jax.nn.softmax(jnp.stack(lses, axis=0), axis=0)
    o = jnp.sum(wts[..., None].astype(v.dtype) * jnp.stack(outs, axis=0), axis=0)
    return o.reshape(B, S, DIL_WIDTH)


def hybrid_layer(x, pos, norm_g, w_in, conv_w, conv_b, w_gx, b_gx, w_ga, b_ga, lam, w_lru_o,
                 g_cq, g_ckv, w_uq, w_ukv, g_mqn, g_mkn, w_mla_o, g_dqn, g_dkn, w_dil_o,
                 b_merge, w_out):
    B, S, _ = x.shape
    h = rms_norm(x, norm_g)
    z = h @ w_in
    idx = np.cumsum(SPLITS)[:-1].tolist()
    (lru_x, lru_g, cq, ckv, kr, mla_g, dq, dk, dv, dil_g, merge) = jnp.split(z, idx, axis=-1)
    y_lru = rg_lru_branch(lru_x, conv_w, conv_b, w_gx, b_gx, w_ga, b_ga, lam) * jax.nn.silu(lru_g)
    y_mla = mla_branch(cq, ckv, kr, pos, g_cq, g_ckv, w_uq, w_ukv, g_mqn, g_mkn) * jax.nn.silu(mla_g)
    y_dil = dilated_branch(dq, dk, dv, pos, g_dqn, g_dkn) * jax.nn.silu(dil_g)
    gates = jax.nn.sigmoid(merge + b_merge).reshape(B, S, N_BRANCH, D_MODEL)
    merged = (gates[:, :, 0] * (y_lru @ w_lru_o)
              + gates[:, :, 1] * (y_mla @ w_mla_o)
              + gates[:, :, 2] * (y_dil @ w_dil_o))
    return x + merged @ w_out


def setup_inputs(seed: int = 0) -> dict:
    key = jax.random.key(seed)
    ks = jax.random.split(key, 24)

    def nrm(k, shape, scale):
        return jax.random.normal(k, shape, jnp.float32) * scale

    def gain(k, shape):
        return 1.0 + 0.05 * jax.random.normal(k, shape, jnp.float32)

    x = nrm(ks[0], (BATCH, SEQ, D_MODEL), 1.0)
    offsets = jax.random.randint(ks[1], (BATCH, 1), 0, 1024, dtype=jnp.int32)
    positions = offsets + jnp.arange(SEQ, dtype=jnp.int32)[None, :]
    a0 = jax.random.uniform(ks[8], (DEPTH, LRU_WIDTH), jnp.float32, 0.9, 0.999)
    return {
        'x': x,
        'positions': positions,
        'norm_g': gain(ks[2], (DEPTH, D_MODEL)),
        'w_in': nrm(ks[3], (DEPTH, D_MODEL, IN_WIDTH), D_MODEL ** -0.5),
        'conv_w': nrm(ks[4], (DEPTH, CONV_WIDTH, LRU_WIDTH), CONV_WIDTH ** -0.5),
        'conv_b': nrm(ks[5], (DEPTH, LRU_WIDTH), 0.02),
        'w_gate_x': nrm(ks[6], (DEPTH, LRU_BLOCKS, LRU_BLOCK_W, LRU_BLOCK_W), LRU_BLOCK_W ** -0.5),
        'b_gate_x': nrm(ks[7], (DEPTH, LRU_BLOCKS, LRU_BLOCK_W), 0.1),
        'w_gate_a': nrm(ks[9], (DEPTH, LRU_BLOCKS, LRU_BLOCK_W, LRU_BLOCK_W), LRU_BLOCK_W ** -0.5),
        'b_gate_a': nrm(ks[10], (DEPTH, LRU_BLOCKS, LRU_BLOCK_W), 0.1),
        'lru_lambda': jnp.log(a0) - jnp.log1p(-a0),
        'w_lru_o': nrm(ks[11], (DEPTH, LRU_WIDTH, D_MODEL), LRU_WIDTH ** -0.5),
        'cq_norm_g': gain(ks[12], (DEPTH, Q_LORA)),
        'ckv_norm_g': gain(ks[13], (DEPTH, KV_LORA)),
        'w_uq': nrm(ks[14], (DEPTH, Q_LORA, MLA_HEADS * MLA_QK), Q_LORA ** -0.5),
        'w_ukv': nrm(ks[15], (DEPTH, KV_LORA, MLA_HEADS * (MLA_NOPE + MLA_V)), KV_LORA ** -0.5),
        'mla_q_norm_g': gain(ks[16], (DEPTH, MLA_QK)),
        'mla_k_norm_g': gain(ks[17], (DEPTH, MLA_QK)),
        'w_mla_o': nrm(ks[18], (DEPTH, MLA_WIDTH, D_MODEL), MLA_WIDTH ** -0.5),
        'dil_q_norm_g': gain(ks[19], (DEPTH, DIL_HD)),
        'dil_k_norm_g': gain(ks[20], (DEPTH, DIL_HD)),
        'w_dil_o': nrm(ks[21], (DEPTH, DIL_WIDTH, D_MODEL), DIL_WIDTH ** -0.5),
        'b_merge': nrm(ks[22], (DEPTH, N_BRANCH * D_MODEL), 0.1),
        'w_out': nrm(ks[23], (DEPTH, D_MODEL, D_MODEL), D_MODEL ** -0.5),
    }


def reference(x, positions, norm_g, w_in, conv_w, conv_b, w_gate_x, b_gate_x, w_gate_a, b_gate_a,
              lru_lambda, w_lru_o, cq_norm_g, ckv_norm_g, w_uq, w_ukv, mla_q_norm_g, mla_k_norm_g,
              w_mla_o, dil_q_norm_g, dil_k_norm_g, w_dil_o, b_merge, w_out):
    for l in range(DEPTH):
        x = hybrid_layer(x, positions, norm_g[l], w_in[l], conv_w[l], conv_b[l],
                         w_gate_x[l], b_gate_x[l], w_gate_a[l], b_gate_a[l], lru_lambda[l],
                         w_lru_o[l], cq_norm_g[l], ckv_norm_g[l], w_uq[l], w_ukv[l],
                         mla_q_norm_g[l], mla_k_norm_g[l], w_mla_o[l], dil_q_norm_g[l],
                         dil_k_norm_g[l], w_dil_o[l], b_merge[l], w_out[l])
    return x
```

```python
import math
import numpy as np
import ml_dtypes
import concourse.bass as bass
import concourse.mybir as mybir
from concourse.bass_utils import run_bass_kernel_spmd

F32 = mybir.dt.float32
BF16 = mybir.dt.bfloat16
I32 = mybir.dt.int32
AF = mybir.ActivationFunctionType
ALU = mybir.AluOpType

S = 4096
D = 1024
DEPTH = 2
INW = 11168
EPS = 1e-6
C_LX, C_LG, C_CQ, C_CKV, C_KR, C_MG, C_DQ, C_DK, C_DV, C_DG, C_MRG = (
    0, 1024, 2048, 2304, 2432, 2464, 2976, 4512, 6048, 7584, 8096)
DIL = (1, 4, 16)
NDMA_SLOTS = 12

PV_NORMG = 0
PV_CONVW = 8
PV_CONVB = 40
PV_BGX = 48
PV_BGA = 56
PV_LAM = 64
PV_CQG = 72
PV_CKVG = 74
PV_MQG = 75
PV_MKG = 76
PV_DQKG = 77
PV_BMRG = 78
NPV = 102


class Tok:
    __slots__ = ("w", "r", "excl")

    def __init__(self, excl=False):
        self.w = None
        self.r = []
        self.excl = excl


class Prog:
    ENG = ("pe", "act", "dve", "pool", "sp")

    def __init__(self, nc):
        self.nc = nc
        self.h = {"pe": nc.tensor, "act": nc.scalar, "dve": nc.vector, "pool": nc.gpsimd, "sp": nc.sync}
        self.ops = []
        self.seq = {e: 0 for e in self.ENG}
        self.wm = {e: {} for e in self.ENG}
        self.dma_rr = {"sp": 0, "pool": 0, "act": 0}
        self.dma_cnt = {}
        self.out_events = []

    def _deps(self, reads, writes, eng=None):
        deps = []
        for t in reads:
            if t.w is not None:
                deps.append(t.w)
            if t.excl:
                deps.extend(r for r in t.r if r[0] != eng)
        for t in writes:
            if t.w is not None:
                deps.append(t.w)
            deps.extend(t.r)
        return deps

    def _mk_waits(self, eng, deps, skip_same_pe=True):
        best = {}
        for ev in deps:
            key, val = ev
            if key == eng and eng == "pe":
                continue
            if val > best.get(key, 0):
                best[key] = val
        waits = []
        wm = self.wm[eng]
        for key, val in best.items():
            if wm.get(key, 0) >= val:
                continue
            wm[key] = val
            waits.append((key, val))
        return waits

    def op(self, eng, fn, reads=(), writes=()):
        deps = self._deps(reads, writes, eng)
        waits = self._mk_waits(eng, deps)
        self.seq[eng] += 1
        ev = (eng, self.seq[eng])
        self.ops.append(["c", eng, fn, waits, ev, False])
        for t in reads:
            if t.excl:
                t.r = [ev]
            else:
                t.r.append(ev)
        for t in writes:
            t.w = ev
            t.r = []
        return ev

    def dma(self, q, fn, reads=(), writes=(), is_output=False):
        deps = self._deps(reads, writes)
        slot = (q, self.dma_rr[q] % NDMA_SLOTS)
        self.dma_rr[q] += 1
        cnt = self.dma_cnt.get(slot, 0)
        if cnt > 0:
            deps.append((slot, cnt))
        waits = self._mk_waits(q, deps)
        self.dma_cnt[slot] = cnt + 1
        ev = (slot, cnt + 1)
        self.ops.append(["d", q, fn, waits, ev, True])
        for t in reads:
            t.r.append(ev)
        for t in writes:
            t.w = ev
            t.r = []
        if is_output:
            self.out_events.append(ev)
        return ev

    def barrier(self):
        deps = [(e, self.seq[e]) for e in ("pe", "act", "dve", "pool") if self.seq[e] > 0]
        deps += [(slot, cnt) for slot, cnt in self.dma_cnt.items()]
        waits = self._mk_waits("sp", deps)
        self.seq["sp"] += 1
        ev = ("sp", self.seq["sp"])
        self.ops.append(["c", "sp", lambda: self.nc.sync.nop(), waits, ev, False])
        for e in ("pe", "act", "dve", "pool"):
            w = self._mk_waits(e, [ev])
            self.ops.append(["w", e, None, w, None, False])

    def finish(self):
        deps = list(self.out_events)
        waits = self._mk_waits("sp", deps)
        self.ops.append(["w", "sp", None, waits, None, False])

    def emit(self, sems, dma_sems):
        targets = {e: set() for e in self.ENG}
        for o in self.ops:
            for key, val in o[3]:
                if isinstance(key, str):
                    targets[key].add(val)
        valmap = {}
        for e in self.ENG:
            c = 0
            m = {}
            for s in sorted(targets[e]):
                c += 1
                m[s] = c
            valmap[e] = m
        for kind, eng, fn, waits, ev, _ in self.ops:
            hnd = self.h[eng]
            for key, val in waits:
                if isinstance(key, str):
                    hnd.wait_ge(sems[key], valmap[key][val])
                else:
                    hnd.wait_ge(dma_sems[key], 16 * val)
            if kind == "w":
                continue
            ins = fn()
            if kind == "d":
                ins.then_inc(dma_sems[ev[0]], 16)
            else:
                if ev[1] in targets[eng]:
                    ins.then_inc(sems[eng], 1)


class Arena:
    def __init__(self, ap, nbytes):
        self.ap = ap
        self.nbytes = nbytes
        self.off = 0

    def alloc(self, shape, dt):
        esz = 4 if dt in (F32, I32) else 2
        n = 1
        for s_ in shape:
            n *= s_
        nb = (n * esz + 63) // 64 * 64
        assert self.off + nb <= self.nbytes, ("arena overflow", self.off, nb, self.nbytes)
        a = self.ap[:, self.off // 2:(self.off + n * esz) // 2]
        self.off += nb
        if dt != BF16:
            a = a.bitcast(dt)
        if len(shape) == 2:
            a = a.rearrange("p (a b) -> p a b", a=shape[0])
        elif len(shape) == 3:
            a = a.rearrange("p (a b c) -> p a b c", a=shape[0], b=shape[1])
        return a

    def mark(self):
        return self.off

    def release(self, m):
        self.off = m


def toks(n):
    return [Tok() for _ in range(n)]


def build_program(n_layers=DEPTH, stages=None, debug=False, extra_barriers=0, stop=None):
    nc = bass.Bass("TRN2", target_bir_lowering=False)
    P = Prog(nc)
    dram_in = {}

    def din(name, shape, dt=F32):
        dram_in[name] = nc.dram_tensor(name, list(shape), dt, kind="ExternalInput").ap()
        return dram_in[name]

    x_in = din("x", [S, D])
    pos_in = din("pos", [1, S], I32)
    w_in = din("w_in", [DEPTH, D, INW])
    w_gx = din("w_gate_x", [DEPTH, 8, 128, 128])
    w_ga = din("w_gate_a", [DEPTH, 8, 128, 128])
    w_lru_o = din("w_lru_o", [DEPTH, 1024, 1024])
    w_uq = din("w_uq", [DEPTH, 256, 768])
    w_ukv = din("w_ukv", [DEPTH, 128, 1024])
    w_mla_o = din("w_mla_o", [DEPTH, 512, 1024])
    w_dil_o = din("w_dil_o", [DEPTH, 512, 1024])
    w_out = din("w_out", [DEPTH, 1024, 1024])
    pvec_in = din("pvec", [DEPTH, 128, NPV])
    cmat_in = din("cmat", [8, 128, 128])
    cmask_in = din("cmask", [128, 4 * 512 + 256])
    cinv_in = din("cinv", [128, 2])
    out_d = nc.dram_tensor("out", [S, D], F32, kind="ExternalOutput").ap()
    sk = "ExternalOutput" if debug else "Internal"
    xres = nc.dram_tensor("xres", [S, D], F32, kind=sk).ap()
    ylru = nc.dram_tensor("ylru", [1024, S], BF16, kind=sk).ap()
    ymla = nc.dram_tensor("ymla", [512, S], BF16, kind=sk).ap()
    ydil = nc.dram_tensor("ydil", [512, S], BF16, kind=sk).ap()
    gts = nc.dram_tensor("gts", [3072, S], BF16, kind=sk).ap()
    dbg = {}
    if debug:
        dbg["hT"] = nc.dram_tensor("hT_dbg", [128, 8 * S], BF16, kind="ExternalOutput").ap()

    ARENA_BYTES = 200 * 1024
    import contextlib
    with contextlib.ExitStack() as es:
        arena_t = es.enter_context(nc.sbuf_tensor("arena", [128, ARENA_BYTES // 2], BF16))
        banks = [es.enter_context(nc.psum_tensor(f"ps{i}", [128, 512], F32)) for i in range(8)]
        sems = {e: es.enter_context(nc.semaphore(f"s_{e}")) for e in Prog.ENG}
        dma_sems = {}
        for q in ("sp", "pool"):
            for i in range(NDMA_SLOTS):
                dma_sems[(q, i)] = es.enter_context(nc.semaphore(f"d_{q}{i}"))
        es.enter_context(nc.Block())
        es.enter_context(nc.allow_low_precision("bf16 PE transposes / bf16 matmul operands with fp32 accumulation"))
        AR = Arena(arena_t, ARENA_BYTES)
        bank_tok = [Tok(excl=True) for _ in range(8)]
        _emit_all(nc, P, AR, banks, bank_tok, locals(), n_layers, dbg)
        P.finish()
        P.emit(sems, dma_sems)
    return nc


def _emit_all(nc, P, AR, banks, bank_tok, env, n_layers, dbg):
    x_in = env["x_in"]; pos_in = env["pos_in"]; w_in = env["w_in"]
    out_d = env["out_d"]; xres = env["xres"]
    pvec_in = env["pvec_in"]; cmat_in = env["cmat_in"]; cmask_in = env["cmask_in"]; cinv_in = env["cinv_in"]
    V, A_, T_, G_ = nc.vector, nc.scalar, nc.tensor, nc.gpsimd

    cm = AR.alloc([8, 128], BF16)
    cm_t = Tok()
    P.dma("pool", lambda: G_.dma_start(out=cm, in_=cmat_in.rearrange("m p c -> p m c")), writes=[cm_t])
    IDENT, ONES256, ONES128, ONESB64, ONES96, RD_, RM_, EPAD = [cm[:, i, :] for i in range(8)]
    cmask = AR.alloc([4 * 512 + 256], BF16)
    cmask_t = Tok()
    P.dma("pool", lambda: G_.dma_start(out=cmask, in_=cmask_in), writes=[cmask_t])
    cinv = AR.alloc([2], F32)
    cinv_t = Tok()
    P.dma("sp", lambda: nc.sync.dma_start(out=cinv, in_=cinv_in), writes=[cinv_t])
    pv = AR.alloc([DEPTH, NPV], F32)
    pv_t = Tok()
    P.dma("sp", lambda: nc.sync.dma_start(out=pv, in_=pvec_in.rearrange("l p c -> p l c")), writes=[pv_t])
    hbgx = AR.alloc([DEPTH, 8], F32)
    hbga = AR.alloc([DEPTH, 8], F32)
    hc = AR.alloc([DEPTH, 8], F32)
    nhc = AR.alloc([DEPTH, 8], F32)
    hbm = AR.alloc([DEPTH, 24], F32)
    tmpv = AR.alloc([DEPTH, 8], F32)
    der_t = Tok()
    P.op("dve", lambda: V.tensor_scalar_mul(hbgx, pv[:, :, PV_BGX:PV_BGX + 8], -1.0), reads=[pv_t], writes=[der_t])
    P.op("dve", lambda: V.tensor_scalar_mul(hbga, pv[:, :, PV_BGA:PV_BGA + 8], -1.0), reads=[pv_t], writes=[der_t])
    P.op("dve", lambda: V.tensor_scalar_mul(hbm, pv[:, :, PV_BMRG:PV_BMRG + 24], -1.0), reads=[pv_t], writes=[der_t])
    P.op("act", lambda: A_.activation(out=tmpv, in_=pv[:, :, PV_LAM:PV_LAM + 8], func=AF.Exp, scale=-1.0),
         reads=[pv_t], writes=[der_t])
    P.op("act", lambda: A_.activation(out=tmpv, in_=tmpv, func=AF.Ln, bias=1.0, scale=1.0),
         reads=[der_t], writes=[der_t])
    P.op("dve", lambda: V.tensor_scalar_mul(hc, tmpv, -8.0), reads=[der_t], writes=[der_t])

    perm_mark = AR.mark()
    ctx = dict(nc=nc, P=P, AR=AR, banks=banks, bank_tok=bank_tok, env=env, dbg=dbg,
               cm=cm, cm_t=cm_t, cmask=cmask, cmask_t=cmask_t, cinv=cinv, cinv_t=cinv_t,
               pv=pv, pv_t=pv_t, der_t=der_t, hbgx=hbgx, hbga=hbga, hc=hc, nhc=nhc, hbm=hbm)
    for l in range(n_layers):
        src = x_in if l == 0 else xres
        dst = out_d if l == n_layers - 1 else xres
        AR.release(perm_mark)
        P.barrier()
        hT = AR.alloc([8, S], BF16)
        hT_t = toks(8)
        layer_mark = AR.mark()
        phase_norm(ctx, l, src, hT, hT_t)
        if dbg and l == 0:
            P.dma("sp", lambda: nc.sync.dma_start(out=dbg["hT"].rearrange("p (c t) -> p c t", c=8), in_=hT), reads=hT_t)
        stages = env.get("stages") or ("gates", "lru", "mla", "dil", "out")
        if "gates" in stages:
            P.barrier(); AR.release(layer_mark)
            phase_gates(ctx, l, hT, hT_t)
        if "lru" in stages:
            P.barrier(); AR.release(layer_mark)
            phase_lru(ctx, l, hT, hT_t)
        if "mla" in stages:
            P.barrier(); AR.release(layer_mark)
            phase_mla(ctx, l, hT, hT_t)
        if "dil" in stages:
            P.barrier(); AR.release(layer_mark)
            phase_dil(ctx, l, hT, hT_t)
        if "out" in stages:
            P.barrier(); AR.release(perm_mark)
            phase_out(ctx, l, src, dst)
    for _ in range(ctx["env"].get("extra_barriers", 0)):
        P.barrier()
    P.barrier()


def _mm(nc, out, lhsT, rhs, start, stop):
    return lambda: nc.tensor.matmul(out, lhsT, rhs, start=start, stop=stop)


def proj_group(ctx, bank_i, wtile, w_t, c0, m, hT, hT_t, tt, rows=None):
    nc, P = ctx["nc"], ctx["P"]
    bank = ctx["banks"][bank_i]
    bt = ctx["bank_tok"][bank_i]
    for k in range(8):
        P.op("pe", _mm(nc, bank[0:m, :], wtile[:, k, c0:c0 + m], hT[:, k, tt * 512:(tt + 1) * 512], k == 0, k == 7),
             reads=[w_t, hT_t[tt]], writes=[bt])


def phase_norm(ctx, l, src, hT, hT_t):
    nc, P, AR = ctx["nc"], ctx["P"], ctx["AR"]
    V, A_, T_ = nc.vector, nc.scalar, nc.tensor
    banks, bank_tok = ctx["banks"], ctx["bank_tok"]
    pv, pv_t = ctx["pv"], ctx["pv_t"]
    IDENT = ctx["cm"][:, 0, :]
    xt = [AR.alloc([4, D], F32) for _ in range(2)]
    xt_t = toks(2)
    xn = [AR.alloc([D], BF16) for _ in range(2)]
    xn_t = toks(2)
    ss = [AR.alloc([4], F32) for _ in range(2)]
    ss_t = toks(2)
    junk = AR.alloc([D], BF16)
    gB = pv[:, l, PV_NORMG:PV_NORMG + 8].unsqueeze(2).to_broadcast([128, 8, 128])
    for g4 in range(8):
        b = g4 % 2
        P.dma("sp", (lambda b=b, g4=g4: nc.sync.dma_start(
            out=xt[b], in_=src[g4 * 512:(g4 + 1) * 512, :].rearrange("(j p) d -> p j d", p=128))),
            writes=[xt_t[b]])
        P.op("dve", (lambda b=b: V.memset(ss[b], 0.0)), writes=[ss_t[b]])
        for j in range(4):
            P.op("act", (lambda b=b, j=j: A_.activation(out=junk, in_=xt[b][:, j, :], func=AF.Square,
                                                         accum_out=ss[b][:, j:j + 1])),
                 reads=[xt_t[b]], writes=[ss_t[b]])
        P.op("act", (lambda b=b: A_.activation(out=ss[b], in_=ss[b], func=AF.Ln, bias=EPS, scale=1.0 / D)),
             reads=[ss_t[b]], writes=[ss_t[b]])
        P.op("act", (lambda b=b: A_.activation(out=ss[b], in_=ss[b], func=AF.Exp, scale=-0.5)),
             reads=[ss_t[b]], writes=[ss_t[b]])
        for j in range(4):
            kb = (g4 * 4 + j) % 2
            P.op("dve", (lambda b=b, j=j, kb=kb: V.tensor_scalar(xn[kb], xt[b][:, j, :], ss[b][:, j:j + 1], None, ALU.mult)),
                 reads=[xt_t[b], ss_t[b]], writes=[xn_t[kb]])
            bbf = banks[kb][:, :].bitcast(BF16)
            for c in range(8):
                P.op("pe", (lambda kb=kb, c=c, bbf=bbf: T_.transpose(bbf[:, c * 128:(c + 1) * 128],
                                                                     xn[kb][:, c * 128:(c + 1) * 128], IDENT)),
                     reads=[xn_t[kb], ctx["cm_t"]], writes=[bank_tok[kb]])
            t0 = g4 * 512 + j * 128
            P.op("dve", (lambda bbf=bbf, t0=t0: V.tensor_tensor(
                out=hT[:, :, t0:t0 + 128], in0=bbf.rearrange("p (c t) -> p c t", c=8), in1=gB, op=ALU.mult)),
                reads=[bank_tok[kb], pv_t], writes=[hT_t[g4]])


def phase_lru(ctx, l, hT, hT_t):
    nc, P, AR = ctx["nc"], ctx["P"], ctx["AR"]
    V, A_, T_, G_ = nc.vector, nc.scalar, nc.tensor, nc.gpsimd
    env = ctx["env"]
    banks, bank_tok = ctx["banks"], ctx["bank_tok"]
    pv, pv_t, der_t = ctx["pv"], ctx["pv_t"], ctx["der_t"]
    w_in, ylru = env["w_in"], env["ylru"]
    wl = [AR.alloc([8, 256], BF16) for _ in range(2)]
    wl_t = toks(2)
    wgx = AR.alloc([8, 128], BF16)
    wga = AR.alloc([8, 128], BF16)
    wg_t = Tok()
    P.dma("pool", lambda: G_.dma_start(out=wgx, in_=env["w_gx"][l].rearrange("n c d -> c n d")), writes=[wg_t])
    P.dma("pool", lambda: G_.dma_start(out=wga, in_=env["w_ga"][l].rearrange("n c d -> c n d")), writes=[wg_t])
    X = AR.alloc([3 + S], F32)
    X_t = toks(8)
    Xh_t = Tok()
    names = ("G", "XC", "TGX", "TGA", "A", "T", "A2", "H")
    buf = [{nm: AR.alloc([512], F32) for nm in names} for _ in range(2)]
    for p_ in range(2):
        buf[p_]["XCb"] = AR.alloc([512], BF16)
        buf[p_]["Yb"] = AR.alloc([512], BF16)
    bt = [{nm: Tok() for nm in buf[0]} for _ in range(2)]
    P.op("dve", lambda: V.memset(X[:, 0:3], 0.0), writes=[Xh_t])
    w_r = w_in[l].rearrange("(c p) n -> p c n", p=128)
    it = 0
    for n in range(8):
        wb = n % 2
        P.dma("pool", (lambda wb=wb, n=n: G_.dma_start(out=wl[wb][:, :, 0:128], in_=w_r[:, :, C_LX + n * 128:C_LX + (n + 1) * 128])),
              writes=[wl_t[wb]])
        P.dma("pool", (lambda wb=wb, n=n: G_.dma_start(out=wl[wb][:, :, 128:256], in_=w_r[:, :, C_LG + n * 128:C_LG + (n + 1) * 128])),
              writes=[wl_t[wb]])
        cw = lambda k, n=n: pv[:, l, PV_CONVW + n * 4 + k:PV_CONVW + n * 4 + k + 1]
        cb = pv[:, l, PV_CONVB + n:PV_CONVB + n + 1]
        hbx = ctx["hbgx"][:, l, n:n + 1]
        hba = ctx["hbga"][:, l, n:n + 1]
        hcn = ctx["hc"][:, l, n:n + 1]
        nhcn = ctx["nhc"][:, l, n:n + 1]
        for tt in range(8):
            par = it % 2
            it += 1
            B_, Bt = buf[par], bt[par]
            c0 = 3 + tt * 512
            bx, bg, bgx, bga = par, 2 + par, 4 + par, 6 + par
            proj_group(ctx, bx, wl[wb], wl_t[wb], 0, 128, hT, hT_t, tt)
            P.op("act", (lambda bx=bx, c0=c0: A_.copy(X[:, c0:c0 + 512], banks[bx][:, :])),
                 reads=[bank_tok[bx]], writes=[X_t[tt]])
            proj_group(ctx, bg, wl[wb], wl_t[wb], 128, 128, hT, hT_t, tt)
            P.op("act", (lambda bg=bg, B_=B_: A_.copy(B_["G"], banks[bg][:, :])),
                 reads=[bank_tok[bg]], writes=[Bt["G"]])
            xprev = Xh_t if tt == 0 else X_t[tt - 1]
            P.op("dve", (lambda B_=B_, c0=c0, cw=cw, cb=cb: V.tensor_scalar(B_["XC"], X[:, c0:c0 + 512], cw(3), cb, ALU.mult, ALU.add)),
                 reads=[X_t[tt], pv_t], writes=[Bt["XC"]])
            for k in (2, 1, 0):
                P.op("dve", (lambda B_=B_, c0=c0, cw=cw, k=k: V.scalar_tensor_tensor(
                    B_["XC"], X[:, c0 - 3 + k:c0 - 3 + k + 512], cw(k), B_["XC"], ALU.mult, ALU.add)),
                    reads=[X_t[tt], xprev, Bt["XC"]], writes=[Bt["XC"]])
            P.op("act", (lambda B_=B_: A_.copy(B_["XCb"], B_["XC"])), reads=[Bt["XC"]], writes=[Bt["XCb"]])
            P.op("pe", _mm(nc, banks[bgx][:, :], wgx[:, n, :], B_["XCb"], True, True),
                 reads=[wg_t, Bt["XCb"]], writes=[bank_tok[bgx]])
            P.op("pe", _mm(nc, banks[bga][:, :], wga[:, n, :], B_["XCb"], True, True),
                 reads=[wg_t, Bt["XCb"]], writes=[bank_tok[bga]])
            for nm, bk, hb_ in (("TGX", bgx, hbx), ("TGA", bga, hba)):
                P.op("act", (lambda B_=B_, nm=nm, bk=bk, hb_=hb_: A_.activation(out=B_[nm], in_=banks[bk][:, :], func=AF.Exp, bias=hb_, scale=-1.0)),
                     reads=[bank_tok[bk], der_t], writes=[Bt[nm]])
                P.op("dve", (lambda B_=B_, nm=nm: V.tensor_scalar_add(B_[nm], B_[nm], 1.0)), reads=[Bt[nm]], writes=[Bt[nm]])
                P.op("dve", (lambda B_=B_, nm=nm: V.reciprocal(B_[nm], B_[nm])), reads=[Bt[nm]], writes=[Bt[nm]])
            P.op("act", (lambda B_=B_, hcn=hcn: A_.activation(out=B_["A"], in_=B_["TGA"], func=AF.Exp, scale=hcn)),
                 reads=[Bt["TGA"], der_t], writes=[Bt["A"]])
            P.op("act", (lambda B_=B_: A_.activation(out=B_["A2"], in_=B_["A"], func=AF.Square)),
                 reads=[Bt["A"]], writes=[Bt["A2"]])
            P.op("dve", (lambda B_=B_: V.tensor_scalar(B_["T"], B_["A2"], -1.0, 1.0, ALU.mult, ALU.add)),
                 reads=[Bt["A2"]], writes=[Bt["T"]])
            P.op("act", (lambda B_=B_: A_.activation(out=B_["T"], in_=B_["T"], func=AF.Ln)),
                 reads=[Bt["T"]], writes=[Bt["T"]])
            P.op("act", (lambda B_=B_: A_.activation(out=B_["T"], in_=B_["T"], func=AF.Exp, scale=0.5)),
                 reads=[Bt["T"]], writes=[Bt["T"]])
            P.op("dve", (lambda B_=B_: V.tensor_tensor(out=B_["TGX"], in0=B_["TGX"], in1=B_["XC"], op=ALU.mult)),
                 reads=[Bt["TGX"], Bt["XC"]], writes=[Bt["TGX"]])
            P.op("dve", (lambda B_=B_: V.tensor_tensor(out=B_["T"], in0=B_["T"], in1=B_["TGX"], op=ALU.mult)),
                 reads=[Bt["T"], Bt["TGX"]], writes=[Bt["T"]])
            if tt == 0:
                P.op("dve", (lambda B_=B_: V.tensor_tensor_scan(B_["H"], B_["A"], B_["T"], 0.0, ALU.mult, ALU.add)),
                     reads=[Bt["A"], Bt["T"]], writes=[Bt["H"]])
            else:
                Bp, Bpt = buf[1 - par], bt[1 - par]
                P.op("dve", (lambda B_=B_, Bp=Bp: V.tensor_tensor_scan(B_["H"], B_["A"], B_["T"], Bp["H"][:, 511:512], ALU.mult, ALU.add)),
                     reads=[Bt["A"], Bt["T"], Bpt["H"]], writes=[Bt["H"]])
            P.op("act", (lambda B_=B_: A_.activation(out=B_["A2"], in_=B_["G"], func=AF.Exp, scale=-1.0)),
                 reads=[Bt["G"], Bt["A2"]], writes=[Bt["A2"]])
            P.op("dve", (lambda B_=B_: V.tensor_scalar_add(B_["A2"], B_["A2"], 1.0)), reads=[Bt["A2"]], writes=[Bt["A2"]])
            P.op("dve", (lambda B_=B_: V.reciprocal(B_["A2"], B_["A2"])), reads=[Bt["A2"]], writes=[Bt["A2"]])
            P.op("dve", (lambda B_=B_: V.tensor_tensor(out=B_["A2"], in0=B_["A2"], in1=B_["G"], op=ALU.mult)),
                 reads=[Bt["A2"], Bt["G"]], writes=[Bt["A2"]])
            P.op("dve", (lambda B_=B_: V.tensor_tensor(out=B_["Yb"], in0=B_["H"], in1=B_["A2"], op=ALU.mult)),
                 reads=[Bt["H"], Bt["A2"]], writes=[Bt["Yb"]])
            P.dma("sp", (lambda B_=B_, n=n, tt=tt: nc.sync.dma_start(out=ylru[n * 128:(n + 1) * 128, tt * 512:(tt + 1) * 512], in_=B_["Yb"])),
                  reads=[Bt["Yb"]])


def phase_gates(ctx, l, hT, hT_t):
    nc, P, AR = ctx["nc"], ctx["P"], ctx["AR"]
    A_, G_ = nc.scalar, nc.gpsimd
    env = ctx["env"]
    banks, bank_tok = ctx["banks"], ctx["bank_tok"]
    w_r = env["w_in"][l].rearrange("(c p) n -> p c n", p=128)
    gts = env["gts"]
    wt = [AR.alloc([8, 128], BF16) for _ in range(2)]
    wt_t = toks(2)
    st = [AR.alloc([S], BF16) for _ in range(2)]
    st_t = toks(2)
    ebuf = [AR.alloc([512], F32) for _ in range(2)]
    ebuf_t = toks(2)
    for ch in range(24):
        b = ch % 2
        P.dma("pool", (lambda b=b, ch=ch: G_.dma_start(out=wt[b], in_=w_r[:, :, C_MRG + ch * 128:C_MRG + (ch + 1) * 128])),
              writes=[wt_t[b]])
        hb = ctx["hbm"][:, l, ch:ch + 1]
        for tt in range(8):
            bi = (ch * 8 + tt) % 4
            proj_group(ctx, bi, wt[b], wt_t[b], 0, 128, hT, hT_t, tt)
            eb = (ch * 8 + tt) % 2
            P.op("act", (lambda bi=bi, eb=eb, hb=hb: A_.activation(out=ebuf[eb], in_=banks[bi][:, :], func=AF.Exp, bias=hb, scale=-1.0)),
                 reads=[bank_tok[bi], ctx["der_t"]], writes=[ebuf_t[eb]])
            P.op("dve", (lambda eb=eb: nc.vector.tensor_scalar_add(ebuf[eb], ebuf[eb], 1.0)), reads=[ebuf_t[eb]], writes=[ebuf_t[eb]])
            P.op("dve", (lambda b=b, eb=eb, tt=tt: nc.vector.reciprocal(st[b][:, tt * 512:(tt + 1) * 512], ebuf[eb])),
                 reads=[ebuf_t[eb]], writes=[st_t[b]])
        P.dma("sp", (lambda b=b, ch=ch: nc.sync.dma_start(out=gts[ch * 128:(ch + 1) * 128, :], in_=st[b])),
              reads=[st_t[b]])


def _host_consts():
    cmat = np.zeros((8, 128, 128), np.float32)
    cmat[0] = np.eye(128, dtype=np.float32)
    cmat[1] = 1.0 / 256
    cmat[2] = 1.0 / 128
    cmat[3, 0:64, 0:64] = 1.0 / 64
    cmat[3, 64:128, 64:128] = 1.0 / 64
    cmat[4, 0:96, 0:96] = 1.0
    for blk in (0, 64):
        for i in range(32):
            cmat[5, blk + i + 32, blk + i] = -1.0
            cmat[5, blk + i, blk + i + 32] = 1.0
    for i in range(16):
        cmat[6, 64 + i + 16, 64 + i] = -1.0
        cmat[6, 64 + i, 64 + i + 16] = 1.0
    for k in range(32):
        cmat[7, k, 64 + k] = 1.0
    cmask = np.zeros((128, 4 * 512 + 256), np.float32)
    kk = np.arange(128)[:, None]
    qq = np.arange(512)[None, :]
    for j in range(4):
        cmask[:, j * 512:(j + 1) * 512] = (qq >= 128 * j + kk)
    q1 = np.arange(128)[None, :]
    cmask[:, 2048:2048 + 128] = (kk >= q1)
    cmask[:, 2048 + 128:2048 + 256] = (kk <= q1)
    cinv = np.zeros((128, 2), np.float32)
    theta = np.float32(10000.0)
    inv64 = theta ** (-np.arange(0, 64, 2, dtype=np.float32) / np.float32(64))
    inv32 = theta ** (-np.arange(0, 32, 2, dtype=np.float32) / np.float32(32))
    for p in range(128):
        cinv[p, 0] = inv64[p % 32]
        if 64 <= p < 96:
            cinv[p, 1] = inv32[(p - 64) % 16]
    return cmat, cmask, cinv


def _host_pvec(inp):
    pv = np.zeros((DEPTH, 128, NPV), np.float32)
    for l in range(DEPTH):
        pv[l, :, PV_NORMG:PV_NORMG + 8] = inp["norm_g"][l].reshape(8, 128).T
        pv[l, :, PV_CONVW:PV_CONVW + 32] = inp["conv_w"][l].reshape(4, 8, 128).transpose(2, 1, 0).reshape(128, 32)
        pv[l, :, PV_CONVB:PV_CONVB + 8] = inp["conv_b"][l].reshape(8, 128).T
        pv[l, :, PV_BGX:PV_BGX + 8] = inp["b_gate_x"][l].T
        pv[l, :, PV_BGA:PV_BGA + 8] = inp["b_gate_a"][l].T
        pv[l, :, PV_LAM:PV_LAM + 8] = inp["lru_lambda"][l].reshape(8, 128).T
        pv[l, :, PV_CQG:PV_CQG + 2] = inp["cq_norm_g"][l].reshape(2, 128).T
        pv[l, :, PV_CKVG] = inp["ckv_norm_g"][l]
        pv[l, 0:96, PV_MQG] = inp["mla_q_norm_g"][l]
        pv[l, 0:96, PV_MKG] = inp["mla_k_norm_g"][l]
        pv[l, 0:64, PV_DQKG] = inp["dil_q_norm_g"][l]
        pv[l, 64:128, PV_DQKG] = inp["dil_k_norm_g"][l]
        pv[l, :, PV_BMRG:PV_BMRG + 24] = inp["b_merge"][l].reshape(24, 128).T
    return pv


def make_in_maps(inp, n_cores=8):
    cmat, cmask, cinv = _host_consts()
    pvec = _host_pvec(inp)
    shared = {
        "w_in": np.ascontiguousarray(inp["w_in"]), "w_gate_x": np.ascontiguousarray(inp["w_gate_x"]),
        "w_gate_a": np.ascontiguousarray(inp["w_gate_a"]), "w_lru_o": np.ascontiguousarray(inp["w_lru_o"]),
        "w_uq": np.ascontiguousarray(inp["w_uq"]), "w_ukv": np.ascontiguousarray(inp["w_ukv"]),
        "w_mla_o": np.ascontiguousarray(inp["w_mla_o"]), "w_dil_o": np.ascontiguousarray(inp["w_dil_o"]),
        "w_out": np.ascontiguousarray(inp["w_out"]), "pvec": pvec, "cmat": cmat, "cmask": cmask, "cinv": cinv,
    }
    maps = []
    for b in range(n_cores):
        m = dict(shared)
        m["x"] = np.ascontiguousarray(inp["x"][b])
        m["pos"] = np.ascontiguousarray(inp["positions"][b].reshape(1, S).astype(np.int32))
        maps.append(m)
    return maps


def kernel(**inputs):
    inp = {k: np.asarray(v) for k, v in inputs.items()}
    nc = build_program()
    maps = make_in_maps(inp, 8)
    res = run_bass_kernel_spmd(nc, maps, core_ids=list(range(8)))
    out = np.stack([np.asarray(res.results[b]["out"]).reshape(S, D) for b in range(8)], axis=0)
    return out.astype(np.float32)


def make_tables(ctx, which, CT, ST, tab_t):
    nc, P, AR = ctx["nc"], ctx["P"], ctx["AR"]
    V, A_ = nc.vector, nc.scalar
    pos_in = ctx["env"]["pos_in"]
    inv = ctx["cinv"][:, which:which + 1]
    m = AR.mark()
    posi = [AR.alloc([1024], I32) for _ in range(2)]
    y0 = [AR.alloc([1024], F32) for _ in range(2)]
    yy = [AR.alloc([1024], F32) for _ in range(2)]
    ki = [AR.alloc([1024], I32) for _ in range(2)]
    kf = [AR.alloc([1024], F32) for _ in range(2)]
    tk = [{n_: Tok() for n_ in ("posi", "y0", "yy", "ki", "kf")} for _ in range(2)]
    TWO_PI = 2.0 * math.pi * (1.0 - 1e-6)
    for c in range(4):
        b = c % 2
        t = tk[b]
        P.dma("sp", (lambda b=b, c=c: nc.sync.dma_start(out=posi[b], in_=pos_in[:, c * 1024:(c + 1) * 1024].broadcast_to([128, 1024]))),
              writes=[t["posi"]])
        P.op("dve", (lambda b=b: V.tensor_copy(y0[b], posi[b])), reads=[t["posi"]], writes=[t["y0"]])
        P.op("dve", (lambda b=b: V.tensor_scalar(y0[b], y0[b], inv, None, ALU.mult)), reads=[t["y0"], ctx["cinv_t"]], writes=[t["y0"]])
        P.op("dve", (lambda b=b: V.tensor_scalar(y0[b], y0[b], 1.0 / (2.0 * math.pi), None, ALU.mult)), reads=[t["y0"]], writes=[t["y0"]])
        for ph, OUT in ((0.0, ST), (0.25, CT)):
            P.op("dve", (lambda b=b, ph=ph: V.tensor_scalar(yy[b], y0[b], ph, None, ALU.add)), reads=[t["y0"]], writes=[t["yy"]])
            P.op("dve", (lambda b=b: V.tensor_copy(ki[b], yy[b])), reads=[t["yy"]], writes=[t["ki"]])
            P.op("dve", (lambda b=b: V.tensor_copy(kf[b], ki[b])), reads=[t["ki"]], writes=[t["kf"]])
            P.op("dve", (lambda b=b: V.tensor_tensor(out=yy[b], in0=yy[b], in1=kf[b], op=ALU.subtract)), reads=[t["yy"], t["kf"]], writes=[t["yy"]])
            P.op("dve", (lambda b=b: V.tensor_scalar(kf[b], yy[b], 0.5, None, ALU.is_gt)), reads=[t["yy"]], writes=[t["kf"]])
            P.op("dve", (lambda b=b: V.tensor_tensor(out=yy[b], in0=yy[b], in1=kf[b], op=ALU.subtract)), reads=[t["yy"], t["kf"]], writes=[t["yy"]])
            P.op("act", (lambda b=b, OUT=OUT, c=c: A_.activation(out=OUT[:, c * 1024:(c + 1) * 1024], in_=yy[b], func=AF.Sin, scale=TWO_PI)),
                 reads=[t["yy"]], writes=[tab_t])
    P.barrier()
    AR.release(m)


def norm_rope_chain(ctx, src_bank, np_, ones_m, eps_v, gcol, CT, ST, rot_m, tt, out_ap, out_tok, tmp, tmp_t, bank_ss, bank_rot):
    nc, P = ctx["nc"], ctx["P"]
    V, A_ = nc.vector, nc.scalar
    banks, bank_tok = ctx["banks"], ctx["bank_tok"]
    IDENT = ctx["cm"][:, 0, :]
    src = banks[src_bank][0:np_, :]
    st_ = bank_tok[src_bank]
    cols = slice(tt * 512, (tt + 1) * 512)
    P.op("act", lambda: A_.activation(out=tmp["sq"][0:np_, :], in_=src, func=AF.Square), reads=[st_], writes=[tmp_t["sq"]])
    P.op("pe", _mm(nc, banks[bank_ss][0:np_, :], ones_m[0:np_, 0:np_], tmp["sq"][0:np_, :], True, True),
         reads=[tmp_t["sq"], ctx["cm_t"]], writes=[bank_tok[bank_ss]])
    P.op("act", lambda: A_.activation(out=tmp["rs"][0:np_, :], in_=banks[bank_ss][0:np_, :], func=AF.Ln, bias=eps_v),
         reads=[bank_tok[bank_ss]], writes=[tmp_t["rs"]])
    P.op("act", lambda: A_.activation(out=tmp["rs"][0:np_, :], in_=tmp["rs"][0:np_, :], func=AF.Exp, scale=-0.5),
         reads=[tmp_t["rs"]], writes=[tmp_t["rs"]])
    P.op("dve", lambda: V.scalar_tensor_tensor(tmp["u1"][0:np_, :], src, gcol[0:np_, :], CT[0:np_, cols], ALU.mult, ALU.mult),
         reads=[st_, ctx["pv_t"], tmp_t["tab"]], writes=[tmp_t["u1"]])
    P.op("dve", lambda: V.scalar_tensor_tensor(tmp["u2"][0:np_, :], src, gcol[0:np_, :], ST[0:np_, cols], ALU.mult, ALU.mult),
         reads=[st_, ctx["pv_t"], tmp_t["tab"]], writes=[tmp_t["u2"]])
    P.op("pe", _mm(nc, banks[bank_rot][0:np_, :], IDENT[0:np_, 0:np_], tmp["u1"][0:np_, :], True, False),
         reads=[tmp_t["u1"], ctx["cm_t"]], writes=[bank_tok[bank_rot]])
    P.op("pe", _mm(nc, banks[bank_rot][0:np_, :], rot_m[0:np_, 0:np_], tmp["u2"][0:np_, :], False, True),
         reads=[tmp_t["u2"], ctx["cm_t"]], writes=[bank_tok[bank_rot]])
    P.op("dve", lambda: V.tensor_tensor(out=out_ap, in0=banks[bank_rot][0:np_, :], in1=tmp["rs"][0:np_, :], op=ALU.mult),
         reads=[bank_tok[bank_rot], tmp_t["rs"]], writes=[out_tok])


def finalize_head(ctx, o_ap, o_tok, den_ap, den_tok, gate_w, gate_wt, gcol0, hT, hT_t, tt, ydst, tmp, tmp_t):
    nc, P = ctx["nc"], ctx["P"]
    V, A_ = nc.vector, nc.scalar
    banks, bank_tok = ctx["banks"], ctx["bank_tok"]
    proj_group(ctx, 7, gate_w, gate_wt, gcol0, 64, hT, hT_t, tt)
    g_ps = banks[7][0:64, :]
    P.op("act", lambda: A_.activation(out=tmp["e1"][0:64, :], in_=g_ps, func=AF.Exp, scale=-1.0), reads=[bank_tok[7]], writes=[tmp_t["e1"]])
    P.op("dve", lambda: V.tensor_copy(tmp["den0"][0:64, :], den_ap), reads=[den_tok], writes=[tmp_t["den0"]])
    P.op("dve", lambda: V.scalar_tensor_tensor(tmp["e1"][0:64, :], tmp["e1"][0:64, :], 1.0, tmp["den0"][0:64, :], ALU.add, ALU.mult),
         reads=[tmp_t["e1"], tmp_t["den0"]], writes=[tmp_t["e1"]])
    P.op("dve", lambda: V.reciprocal(tmp["e1"][0:64, :], tmp["e1"][0:64, :]), reads=[tmp_t["e1"]], writes=[tmp_t["e1"]])
    P.op("dve", lambda: V.tensor_tensor(out=tmp["e1"][0:64, :], in0=tmp["e1"][0:64, :], in1=g_ps, op=ALU.mult),
         reads=[tmp_t["e1"], bank_tok[7]], writes=[tmp_t["e1"]])
    P.op("dve", lambda: V.tensor_tensor(out=tmp["yb"][0:64, :], in0=o_ap, in1=tmp["e1"][0:64, :], op=ALU.mult),
         reads=[o_tok, tmp_t["e1"]], writes=[tmp_t["yb"]])
    P.dma("sp", lambda: nc.sync.dma_start(out=ydst, in_=tmp["yb"][0:64, :]), reads=[tmp_t["yb"]])


def _chain_tmp(AR):
    tmp = {"sq": AR.alloc([512], BF16), "rs": AR.alloc([512], F32), "u1": AR.alloc([512], BF16), "u2": AR.alloc([512], BF16),
           "e1": AR.alloc([512], F32), "den0": AR.alloc([512], F32), "yb": AR.alloc([512], BF16), "den": AR.alloc([512], F32)}
    tmp_t = {k: Tok() for k in list(tmp) + ["tab"]}
    return tmp, tmp_t


def phase_mla(ctx, l, hT, hT_t):
    nc, P, AR = ctx["nc"], ctx["P"], ctx["AR"]
    V, A_, G_ = nc.vector, nc.scalar, nc.gpsimd
    env = ctx["env"]
    banks, bank_tok = ctx["banks"], ctx["bank_tok"]
    pv, pv_t, cm, cm_t = ctx["pv"], ctx["pv_t"], ctx["cm"], ctx["cm_t"]
    ONES256, ONES128, ONES96, RM_, EPAD = cm[:, 1, :], cm[:, 2, :], cm[:, 4, :], cm[:, 6, :], cm[:, 7, :]
    cmask, cmask_t = ctx["cmask"], ctx["cmask_t"]
    ymla = env["ymla"]
    w_r = env["w_in"][l].rearrange("(c p) n -> p c n", p=128)
    CT = AR.alloc([S], BF16)
    ST = AR.alloc([S], BF16)
    tmp, tmp_t = _chain_tmp(AR)
    make_tables(ctx, 1, CT, ST, tmp_t["tab"])
    if env.get("stop") == "tables":
        return
    wm = AR.alloc([8, 416], BF16)
    wuq = AR.alloc([2, 768], BF16)
    wukv = AR.alloc([1024], BF16)
    wgm = AR.alloc([8, 512], BF16)
    wkp = AR.alloc([8, 96], BF16)
    w_t = Tok()
    wkp_t = Tok()
    P.dma("pool", lambda: G_.dma_start(out=wm, in_=w_r[:, :, C_CQ:C_CQ + 416]), writes=[w_t])
    P.dma("pool", lambda: G_.dma_start(out=wuq, in_=env["w_uq"][l].rearrange("(c p) n -> p c n", p=128)), writes=[w_t])
    P.dma("pool", lambda: G_.dma_start(out=wukv, in_=env["w_ukv"][l]), writes=[w_t])
    P.dma("pool", lambda: G_.dma_start(out=wgm, in_=w_r[:, :, C_MG:C_MG + 512]), writes=[w_t])
    P.op("dve", lambda: V.memset(wkp, 0.0), writes=[wkp_t])
    P.op("dve", lambda: V.tensor_copy(wkp[:, :, 0:64], wukv.rearrange("p (h c) -> p h c", h=8)[:, :, 0:64]), reads=[w_t], writes=[wkp_t])
    CQN = AR.alloc([2, S], BF16)
    CKVN = AR.alloc([S], BF16)
    KR = AR.alloc([S], BF16)
    lat_t = toks(8)
    sq3 = [AR.alloc([3, 512], BF16) for _ in range(2)]
    sq3_t = toks(2)
    rs2 = [AR.alloc([2, 512], F32) for _ in range(2)]
    rs2_t = toks(2)
    for tt in range(8):
        b = tt % 2
        cols = slice(tt * 512, (tt + 1) * 512)
        proj_group(ctx, 0, wm, w_t, 0, 128, hT, hT_t, tt)
        proj_group(ctx, 1, wm, w_t, 128, 128, hT, hT_t, tt)
        proj_group(ctx, 2, wm, w_t, 256, 128, hT, hT_t, tt)
        proj_group(ctx, 3, wm, w_t, 384, 32, hT, hT_t, tt)
        for i in range(3):
            P.op("act", (lambda b=b, i=i: A_.activation(out=sq3[b][:, i, :], in_=banks[i][:, :], func=AF.Square)),
                 reads=[bank_tok[i]], writes=[sq3_t[b]])
        P.op("pe", _mm(nc, banks[4][:, :], ONES256, sq3[b][:, 0, :], True, False), reads=[sq3_t[b], cm_t], writes=[bank_tok[4]])
        P.op("pe", _mm(nc, banks[4][:, :], ONES256, sq3[b][:, 1, :], False, True), reads=[sq3_t[b], cm_t], writes=[bank_tok[4]])
        P.op("pe", _mm(nc, banks[5][:, :], ONES128, sq3[b][:, 2, :], True, True), reads=[sq3_t[b], cm_t], writes=[bank_tok[5]])
        for i, bk in ((0, 4), (1, 5)):
            P.op("act", (lambda b=b, i=i, bk=bk: A_.activation(out=rs2[b][:, i, :], in_=banks[bk][:, :], func=AF.Ln, bias=EPS)),
                 reads=[bank_tok[bk]], writes=[rs2_t[b]])
            P.op("act", (lambda b=b, i=i: A_.activation(out=rs2[b][:, i, :], in_=rs2[b][:, i, :], func=AF.Exp, scale=-0.5)),
                 reads=[rs2_t[b]], writes=[rs2_t[b]])
        for c in range(2):
            P.op("dve", (lambda b=b, c=c, cols=cols: V.scalar_tensor_tensor(CQN[:, c, cols], banks[c][:, :], pv[:, l, PV_CQG + c:PV_CQG + c + 1],
                                                                            rs2[b][:, 0, :], ALU.mult, ALU.mult)),
                 reads=[bank_tok[c], rs2_t[b], pv_t], writes=[lat_t[tt]])
        P.op("dve", (lambda b=b, cols=cols: V.scalar_tensor_tensor(CKVN[:, cols], banks[2][:, :], pv[:, l, PV_CKVG:PV_CKVG + 1],
                                                                   rs2[b][:, 1, :], ALU.mult, ALU.mult)),
             reads=[bank_tok[2], rs2_t[b], pv_t], writes=[lat_t[tt]])
        P.op("act", (lambda cols=cols: A_.copy(KR[0:32, cols], banks[3][0:32, :])), reads=[bank_tok[3]], writes=[lat_t[tt]])
    if env.get("stop") == "latent":
        return
    QT = AR.alloc([S], BF16)
    KT = AR.alloc([S], BF16)
    QT_t = toks(8)
    KT_t = toks(8)
    VH = AR.alloc([32, 128], BF16)
    VH_t = Tok()
    P.op("dve", lambda: V.memset(VH[:, :, 64:128], 1.0), writes=[VH_t])
    Pb = [AR.alloc([512], BF16) for _ in range(3)]
    Pb_t = toks(3)
    gq = pv[:, l, PV_MQG:PV_MQG + 1]
    gk = pv[:, l, PV_MKG:PV_MKG + 1]
    SC = math.sqrt(96.0)
    cnt = 0
    for h in range(8):
        for tt in range(8):
            cols = slice(tt * 512, (tt + 1) * 512)
            ba = 0
            for c in range(2):
                P.op("pe", _mm(nc, banks[ba][0:96, :], wuq[:, c, h * 96:(h + 1) * 96], CQN[:, c, cols], c == 0, c == 1),
                     reads=[w_t, lat_t[tt]], writes=[bank_tok[ba]])
            if env.get("stop") == "q_proj":
                return
            norm_rope_chain(ctx, ba, 96, ONES96, 96 * EPS, gq, CT, ST, RM_, tt, QT[0:96, cols], QT_t[tt], tmp, tmp_t, 2, 3)
            if env.get("stop") == "q_chain":
                return
            ba = 1
            P.op("pe", _mm(nc, banks[ba][0:96, :], wkp[:, h, :], CKVN[:, cols], True, False), reads=[wkp_t, lat_t[tt]], writes=[bank_tok[ba]])
            P.op("pe", _mm(nc, banks[ba][0:96, :], EPAD[0:32, 0:96], KR[0:32, cols], False, True), reads=[cm_t, lat_t[tt]], writes=[bank_tok[ba]])
            norm_rope_chain(ctx, ba, 96, ONES96, 96 * EPS, gk, CT, ST, RM_, tt, KT[0:96, cols], KT_t[tt], tmp, tmp_t, 2, 3)
            if env.get("stop") == "k_chain":
                return
            for j in range(4):
                P.op("pe", _mm(nc, banks[7][:, j * 64:(j + 1) * 64], CKVN[:, tt * 512 + j * 128:tt * 512 + (j + 1) * 128],
                               wukv[:, h * 128 + 64:h * 128 + 128], True, True),
                     reads=[w_t, lat_t[tt]], writes=[bank_tok[7]])
            P.op("act", (lambda tt=tt: A_.copy(VH[:, tt * 4:(tt + 1) * 4, 0:64], banks[7][:, 0:256].rearrange("p (j c) -> p j c", j=4))),
                 reads=[bank_tok[7]], writes=[VH_t])
        if env.get("stop") == "prep":
            return
        for qt in range(8):
            if env.get("stop") == "attn1" and qt == 1:
                return
            qcols = slice(qt * 512, (qt + 1) * 512)
            nkb = 4 * qt + 4
            for kb in range(nkb):
                sb = 4 + cnt % 2
                pi = cnt % 3
                cnt += 1
                P.op("pe", _mm(nc, banks[sb][:, :], KT[0:96, kb * 128:(kb + 1) * 128], QT[0:96, qcols], True, True),
                     reads=[KT_t[kb // 4], QT_t[qt]], writes=[bank_tok[sb]])
                P.op("act", (lambda sb=sb, pi=pi: A_.activation(out=Pb[pi], in_=banks[sb][:, :], func=AF.Exp, scale=SC)),
                     reads=[bank_tok[sb]], writes=[Pb_t[pi]])
                j = kb - 4 * qt
                if j >= 0:
                    P.op("dve", (lambda pi=pi, j=j: V.tensor_tensor(out=Pb[pi], in0=Pb[pi], in1=cmask[:, j * 512:(j + 1) * 512], op=ALU.mult)),
                         reads=[Pb_t[pi], cmask_t], writes=[Pb_t[pi]])
                P.op("pe", _mm(nc, banks[6][:, :], VH[:, kb, :], Pb[pi], kb == 0, kb == nkb - 1),
                     reads=[VH_t, Pb_t[pi]], writes=[bank_tok[6]])
            P.op("act", lambda: A_.copy(tmp["den"][64:128, :], banks[6][64:128, :]), reads=[bank_tok[6]], writes=[tmp_t["den"]])
            finalize_head(ctx, banks[6][0:64, :], bank_tok[6], tmp["den"][64:128, :], tmp_t["den"], wgm, w_t, h * 64, hT, hT_t, qt,
                          ymla[h * 64:(h + 1) * 64, qcols], tmp, tmp_t)


def phase_dil(ctx, l, hT, hT_t):
    nc, P, AR = ctx["nc"], ctx["P"], ctx["AR"]
    V, A_, G_ = nc.vector, nc.scalar, nc.gpsimd
    env = ctx["env"]
    banks, bank_tok = ctx["banks"], ctx["bank_tok"]
    pv, pv_t, cm, cm_t = ctx["pv"], ctx["pv_t"], ctx["cm"], ctx["cm_t"]
    ONESB64, RD_ = cm[:, 3, :], cm[:, 5, :]
    cmask, cmask_t = ctx["cmask"], ctx["cmask_t"]
    DMASK = cmask[:, 2048:2048 + 256]
    ydil = env["ydil"]
    w_r = env["w_in"][l].rearrange("(c p) n -> p c n", p=128)
    CT = AR.alloc([S], BF16)
    ST = AR.alloc([S], BF16)
    tmp, tmp_t = _chain_tmp(AR)
    make_tables(ctx, 0, CT, ST, tmp_t["tab"])
    wqk = [AR.alloc([8, 128], BF16) for _ in range(2)]
    wv = [AR.alloc([8, 64], BF16) for _ in range(2)]
    wgh_t = toks(2)
    wdg = AR.alloc([8, 512], BF16)
    wdg_t = Tok()
    P.dma("pool", lambda: G_.dma_start(out=wdg, in_=w_r[:, :, C_DG:C_DG + 512]), writes=[wdg_t])
    QK = AR.alloc([S], BF16)
    K0 = AR.alloc([S], BF16)
    QK_t = toks(8)
    K0_t = toks(8)
    VD = AR.alloc([32, 128], BF16)
    VD_t = Tok()
    P.op("dve", lambda: V.memset(VD[:, :, 64:128], 1.0), writes=[VD_t])
    OACC = AR.alloc([S], F32)
    OACC_t = Tok()
    Pd = [AR.alloc([256], BF16) for _ in range(3)]
    Pd_t = toks(3)
    gqk = pv[:, l, PV_DQKG:PV_DQKG + 1]
    it = 0
    cnt = 0
    for h in range(8):
        for g in range(3):
            d = DIL[g]
            nb = S // (128 * d)
            wb = it % 2
            it += 1
            hd = g * 8 + h
            P.dma("pool", (lambda wb=wb, hd=hd: G_.dma_start(out=wqk[wb][:, :, 0:64], in_=w_r[:, :, C_DQ + hd * 64:C_DQ + (hd + 1) * 64])), writes=[wgh_t[wb]])
            P.dma("pool", (lambda wb=wb, hd=hd: G_.dma_start(out=wqk[wb][:, :, 64:128], in_=w_r[:, :, C_DK + hd * 64:C_DK + (hd + 1) * 64])), writes=[wgh_t[wb]])
            P.dma("pool", (lambda wb=wb, hd=hd: G_.dma_start(out=wv[wb], in_=w_r[:, :, C_DV + hd * 64:C_DV + (hd + 1) * 64])), writes=[wgh_t[wb]])
            for tt in range(8):
                cols = slice(tt * 512, (tt + 1) * 512)
                ba = tt % 2
                proj_group(ctx, ba, wqk[wb], wgh_t[wb], 0, 128, hT, hT_t, tt)
                norm_rope_chain(ctx, ba, 128, ONESB64, EPS, gqk, CT, ST, RD_, tt, QK[:, cols], QK_t[tt], tmp, tmp_t, 2, 3)
                P.op("dve", (lambda cols=cols: V.tensor_copy(K0[0:64, cols], QK[64:128, cols])), reads=[QK_t[tt]], writes=[K0_t[tt]])

            def ucols(r, n, d=d):
                st0 = r + d * 128 * n
                return slice(st0, st0 + d * 127 + 1, d)
            units = [(r, n) for r in range(d) for n in range(nb)]
            for u0 in range(0, 32, 4):
                for jj in range(4):
                    r, n = units[u0 + jj]
                    for k in range(8):
                        P.op("pe", _mm(nc, banks[7][:, jj * 64:(jj + 1) * 64], hT[:, k, ucols(r, n)], wv[wb][:, k, :], k == 0, k == 7),
                             reads=[wgh_t[wb]] + hT_t, writes=[bank_tok[7]])
                P.op("act", (lambda u0=u0: A_.copy(VD[:, u0:u0 + 4, 0:64], banks[7][:, 0:256].rearrange("p (j c) -> p j c", j=4))),
                     reads=[bank_tok[7]], writes=[VD_t])
            for u, (r, n) in enumerate(units):
                sb = 4 + cnt % 2
                pi = cnt % 3
                cnt += 1
                lo = 0 if n > 0 else 128
                qc = ucols(r, n)
                if n > 0:
                    P.op("pe", _mm(nc, banks[sb][:, 0:128], K0[0:64, ucols(r, n - 1)], QK[0:64, qc], True, True),
                         reads=K0_t + QK_t, writes=[bank_tok[sb]])
                P.op("pe", _mm(nc, banks[sb][:, 128:256], K0[0:64, qc], QK[0:64, qc], True, True),
                     reads=K0_t + QK_t, writes=[bank_tok[sb]])
                P.op("act", (lambda sb=sb, pi=pi, lo=lo: A_.activation(out=Pd[pi][:, lo:256], in_=banks[sb][:, lo:256], func=AF.Exp, scale=0.125)),
                     reads=[bank_tok[sb]], writes=[Pd_t[pi]])
                P.op("dve", (lambda pi=pi, lo=lo: V.tensor_tensor(out=Pd[pi][:, lo:256], in0=Pd[pi][:, lo:256], in1=DMASK[:, lo:256], op=ALU.mult)),
                     reads=[Pd_t[pi], cmask_t], writes=[Pd_t[pi]])
                ob = 6
                if n > 0:
                    P.op("pe", _mm(nc, banks[ob][:, 0:128], VD[:, u - 1, :], Pd[pi][:, 0:128], True, False), reads=[VD_t, Pd_t[pi]], writes=[bank_tok[ob]])
                P.op("pe", _mm(nc, banks[ob][:, 0:128], VD[:, u, :], Pd[pi][:, 128:256], n == 0, True), reads=[VD_t, Pd_t[pi]], writes=[bank_tok[ob]])
                if g == 0:
                    P.op("dve", (lambda qc=qc, ob=ob: V.tensor_copy(OACC[:, qc], banks[ob][:, 0:128])), reads=[bank_tok[ob]], writes=[OACC_t])
                else:
                    P.op("dve", (lambda qc=qc, ob=ob: V.tensor_tensor(out=OACC[:, qc], in0=banks[ob][:, 0:128], in1=OACC[:, qc], op=ALU.add)),
                         reads=[bank_tok[ob], OACC_t], writes=[OACC_t])
        for tt in range(8):
            cols = slice(tt * 512, (tt + 1) * 512)
            finalize_head(ctx, OACC[0:64, cols], OACC_t, OACC[64:128, cols], OACC_t, wdg, wdg_t, h * 64, hT, hT_t, tt,
                          ydil[h * 64:(h + 1) * 64, cols], tmp, tmp_t)


def phase_out(ctx, l, src, dst):
    nc, P, AR = ctx["nc"], ctx["P"], ctx["AR"]
    V, A_, G_ = nc.vector, nc.scalar, nc.gpsimd
    env = ctx["env"]
    banks, bank_tok = ctx["banks"], ctx["bank_tok"]
    TE = 256
    wlo = AR.alloc([8, 1024], BF16)
    wmo = AR.alloc([4, 1024], BF16)
    wdo = AR.alloc([4, 1024], BF16)
    wou = AR.alloc([8, 1024], BF16)
    w_t = Tok()
    for wt_, nm in ((wlo, "w_lru_o"), (wmo, "w_mla_o"), (wdo, "w_dil_o"), (wou, "w_out")):
        P.dma("pool", (lambda wt_=wt_, nm=nm: G_.dma_start(out=wt_, in_=env[nm][l].rearrange("(c p) n -> p c n", p=128))), writes=[w_t])
    YL = [AR.alloc([8, TE], BF16) for _ in range(2)]
    YM = [AR.alloc([4, TE], BF16) for _ in range(2)]
    YD = [AR.alloc([4, TE], BF16) for _ in range(2)]
    GT = [AR.alloc([24, TE], BF16) for _ in range(2)]
    XT = [AR.alloc([2, D], F32) for _ in range(2)]
    OUT = [AR.alloc([2, D], F32) for _ in range(2)]
    M = [AR.alloc([8, TE], BF16) for _ in range(2)]
    t1 = [AR.alloc([TE], F32) for _ in range(2)]
    t2 = [AR.alloc([TE], F32) for _ in range(2)]
    in_t = toks(2)
    out_t = toks(2)
    M_t = toks(2)
    t1_t = toks(2)
    t2_t = toks(2)
    ylru_r = env["ylru"].rearrange("(c p) t -> p c t", p=128)
    ymla_r = env["ymla"].rearrange("(c p) t -> p c t", p=128)
    ydil_r = env["ydil"].rearrange("(c p) t -> p c t", p=128)
    gts_r = env["gts"].rearrange("(c p) t -> p c t", p=128)
    is_final = dst is env["out_d"]
    k = 0
    for e in range(S // TE):
        b = e % 2
        tc_ = slice(e * TE, (e + 1) * TE)
        P.dma("sp", (lambda b=b, tc_=tc_: nc.sync.dma_start(out=YL[b], in_=ylru_r[:, :, tc_])), writes=[in_t[b]])
        P.dma("sp", (lambda b=b, tc_=tc_: nc.sync.dma_start(out=YM[b], in_=ymla_r[:, :, tc_])), writes=[in_t[b]])
        P.dma("sp", (lambda b=b, tc_=tc_: nc.sync.dma_start(out=YD[b], in_=ydil_r[:, :, tc_])), writes=[in_t[b]])
        P.dma("sp", (lambda b=b, tc_=tc_: nc.sync.dma_start(out=GT[b], in_=gts_r[:, :, tc_])), writes=[in_t[b]])
        P.dma("sp", (lambda b=b, e=e: nc.sync.dma_start(out=XT[b], in_=src[e * TE:(e + 1) * TE, :].rearrange("(j p) d -> p j d", p=128))), writes=[in_t[b]])
        for j in range(8):
            js = slice(j * 128, (j + 1) * 128)
            kk = k % 2
            k += 1
            bl, bm, bd = 0 + kk, 2 + kk, 4 + kk
            for c in range(8):
                P.op("pe", _mm(nc, banks[bl][:, 0:TE], wlo[:, c, js], YL[b][:, c, :], c == 0, c == 7), reads=[w_t, in_t[b]], writes=[bank_tok[bl]])
            for c in range(4):
                P.op("pe", _mm(nc, banks[bm][:, 0:TE], wmo[:, c, js], YM[b][:, c, :], c == 0, c == 3), reads=[w_t, in_t[b]], writes=[bank_tok[bm]])
            for c in range(4):
                P.op("pe", _mm(nc, banks[bd][:, 0:TE], wdo[:, c, js], YD[b][:, c, :], c == 0, c == 3), reads=[w_t, in_t[b]], writes=[bank_tok[bd]])
            P.op("dve", (lambda b=b, j=j, kk=kk, bl=bl: V.tensor_tensor(out=t1[kk], in0=banks[bl][:, 0:TE], in1=GT[b][:, j, :], op=ALU.mult)),
                 reads=[bank_tok[bl], in_t[b]], writes=[t1_t[kk]])
            P.op("dve", (lambda b=b, j=j, kk=kk, bm=bm: V.tensor_tensor(out=t2[kk], in0=banks[bm][:, 0:TE], in1=GT[b][:, 8 + j, :], op=ALU.mult)),
                 reads=[bank_tok[bm], in_t[b]], writes=[t2_t[kk]])
            P.op("dve", (lambda kk=kk: V.tensor_tensor(out=t1[kk], in0=t1[kk], in1=t2[kk], op=ALU.add)),
                 reads=[t1_t[kk], t2_t[kk]], writes=[t1_t[kk]])
            P.op("dve", (lambda b=b, j=j, kk=kk, bd=bd: V.tensor_tensor(out=t2[kk], in0=banks[bd][:, 0:TE], in1=GT[b][:, 16 + j, :], op=ALU.mult)),
                 reads=[bank_tok[bd], in_t[b]], writes=[t2_t[kk]])
            P.op("dve", (lambda b=b, j=j, kk=kk: V.tensor_tensor(out=M[b][:, j, :], in0=t1[kk], in1=t2[kk], op=ALU.add)),
                 reads=[t1_t[kk], t2_t[kk]], writes=[M_t[b]])
        for s_ in range(2):
            for hf in range(2):
                bo = 6 + (s_ * 2 + hf) % 2
                for c in range(8):
                    P.op("pe", _mm(nc, banks[bo][:, :], M[b][:, c, s_ * 128:(s_ + 1) * 128], wou[:, c, hf * 512:(hf + 1) * 512], c == 0, c == 7),
                         reads=[w_t, M_t[b]], writes=[bank_tok[bo]])
                P.op("dve", (lambda b=b, s_=s_, hf=hf, bo=bo: V.tensor_tensor(out=OUT[b][:, s_, hf * 512:(hf + 1) * 512], in0=banks[bo][:, :],
                                                                            in1=XT[b][:, s_, hf * 512:(hf + 1) * 512], op=ALU.add)),
                     reads=[bank_tok[bo], in_t[b]], writes=[out_t[b]])
        P.dma("sp", (lambda b=b, e=e: nc.sync.dma_start(out=dst[e * TE:(e + 1) * TE, :].rearrange("(j p) d -> p j d", p=128), in_=OUT[b])),
              reads=[out_t[b]], is_output=is_final)
```

```python
import math
import numpy as np
import ml_dtypes
import concourse.bass as bass
import concourse.mybir as mybir
from concourse.bass_utils import run_bass_kernel_spmd

F32 = mybir.dt.float32
BF16 = mybir.dt.bfloat16
I32 = mybir.dt.int32
AF = mybir.ActivationFunctionType
ALU = mybir.AluOpType

S = 4096
D = 1024
DEPTH = 2
INW = 11168
EPS = 1e-6
C_LX, C_LG, C_CQ, C_CKV, C_KR, C_MG, C_DQ, C_DK, C_DV, C_DG, C_MRG = (
    0, 1024, 2048, 2304, 2432, 2464, 2976, 4512, 6048, 7584, 8096)
DIL = (1, 4, 16)
NDMA_SLOTS = 12

PV_NORMG = 0
PV_CONVW = 8
PV_CONVB = 40
PV_BGX = 48
PV_BGA = 56
PV_LAM = 64
PV_CQG = 72
PV_CKVG = 74
PV_MQG = 75
PV_MKG = 76
PV_DQKG = 77
PV_BMRG = 78
NPV = 102


class Tok:
    __slots__ = ("w", "r", "excl")

    def __init__(self, excl=False):
        self.w = None
        self.r = []
        self.excl = excl


class Prog:
    ENG = ("pe", "act", "dve", "pool", "sp")

    def __init__(self, nc):
        self.nc = nc
        self.h = {"pe": nc.tensor, "act": nc.scalar, "dve": nc.vector, "pool": nc.gpsimd, "sp": nc.sync}
        self.ops = []
        self.seq = {e: 0 for e in self.ENG}
        self.wm = {e: {} for e in self.ENG}
        self.dma_rr = {"sp": 0, "pool": 0, "act": 0}
        self.dma_cnt = {}
        self.out_events = []

    def _deps(self, reads, writes, eng=None):
        deps = []
        for t in reads:
            if t.w is not None:
                deps.append(t.w)
            if t.excl:
                deps.extend(r for r in t.r if r[0] != eng)
        for t in writes:
            if t.w is not None:
                deps.append(t.w)
            deps.extend(t.r)
        return deps

    def _mk_waits(self, eng, deps, skip_same_pe=True):
        best = {}
        for ev in deps:
            key, val = ev
            if key == eng and eng == "pe":
                continue
            if val > best.get(key, 0):
                best[key] = val
        waits = []
        wm = self.wm[eng]
        for key, val in best.items():
            if wm.get(key, 0) >= val:
                continue
            wm[key] = val
            waits.append((key, val))
        return waits

    def op(self, eng, fn, reads=(), writes=()):
        deps = self._deps(reads, writes, eng)
        waits = self._mk_waits(eng, deps)
        self.seq[eng] += 1
        ev = (eng, self.seq[eng])
        self.ops.append(["c", eng, fn, waits, ev, False])
        for t in reads:
            if t.excl:
                t.r = [ev]
            else:
                t.r.append(ev)
        for t in writes:
            t.w = ev
            t.r = []
        return ev

    def dma(self, q, fn, reads=(), writes=(), is_output=False):
        deps = self._deps(reads, writes)
        slot = (q, self.dma_rr[q] % NDMA_SLOTS)
        self.dma_rr[q] += 1
        cnt = self.dma_cnt.get(slot, 0)
        if cnt > 0:
            deps.append((slot, cnt))
        waits = self._mk_waits(q, deps)
        self.dma_cnt[slot] = cnt + 1
        ev = (slot, cnt + 1)
        self.ops.append(["d", q, fn, waits, ev, True])
        for t in reads:
            t.r.append(ev)
        for t in writes:
            t.w = ev
            t.r = []
        if is_output:
            self.out_events.append(ev)
        return ev

    def barrier(self):
        deps = [(e, self.seq[e]) for e in ("pe", "act", "dve", "pool") if self.seq[e] > 0]
        deps += [(slot, cnt) for slot, cnt in self.dma_cnt.items()]
        waits = self._mk_waits("sp", deps)
        self.seq["sp"] += 1
        ev = ("sp", self.seq["sp"])
        self.ops.append(["c", "sp", lambda: self.nc.sync.nop(), waits, ev, False])
        for e in ("pe", "act", "dve", "pool"):
            w = self._mk_waits(e, [ev])
            self.ops.append(["w", e, None, w, None, False])

    def finish(self):
        deps = list(self.out_events)
        waits = self._mk_waits("sp", deps)
        self.ops.append(["w", "sp", None, waits, None, False])

    def emit(self, sems, dma_sems):
        targets = {e: set() for e in self.ENG}
        for o in self.ops:
            for key, val in o[3]:
                if isinstance(key, str):
                    targets[key].add(val)
        valmap = {}
        for e in self.ENG:
            c = 0
            m = {}
            for s in sorted(targets[e]):
                c += 1
                m[s] = c
            valmap[e] = m
        for kind, eng, fn, waits, ev, _ in self.ops:
            hnd = self.h[eng]
            for key, val in waits:
                if isinstance(key, str):
                    hnd.wait_ge(sems[key], valmap[key][val])
                else:
                    hnd.wait_ge(dma_sems[key], 16 * val)
            if kind == "w":
                continue
            ins = fn()
            if kind == "d":
                ins.then_inc(dma_sems[ev[0]], 16)
            else:
                if ev[1] in targets[eng]:
                    ins.then_inc(sems[eng], 1)


class Arena:
    def __init__(self, ap, nbytes):
        self.ap = ap
        self.nbytes = nbytes
        self.off = 0

    def alloc(self, shape, dt):
        esz = 4 if dt in (F32, I32) else 2
        n = 1
        for s_ in shape:
            n *= s_
        nb = (n * esz + 63) // 64 * 64
        assert self.off + nb <= self.nbytes, ("arena overflow", self.off, nb, self.nbytes)
        a = self.ap[:, self.off // 2:(self.off + n * esz) // 2]
        self.off += nb
        if dt != BF16:
            a = a.bitcast(dt)
        if len(shape) == 2:
            a = a.rearrange("p (a b) -> p a b", a=shape[0])
        elif len(shape) == 3:
            a = a.rearrange("p (a b c) -> p a b c", a=shape[0], b=shape[1])
        return a

    def mark(self):
        return self.off

    def release(self, m):
        self.off = m


def toks(n):
    return [Tok() for _ in range(n)]


def build_program(n_layers=DEPTH, stages=None, debug=False, extra_barriers=0, stop=None):
    nc = bass.Bass("TRN2", target_bir_lowering=False)
    P = Prog(nc)
    dram_in = {}

    def din(name, shape, dt=F32):
        dram_in[name] = nc.dram_tensor(name, list(shape), dt, kind="ExternalInput").ap()
        return dram_in[name]

    x_in = din("x", [S, D])
    pos_in = din("pos", [1, S], I32)
    w_in = din("w_in", [DEPTH, D, INW])
    w_gx = din("w_gate_x", [DEPTH, 8, 128, 128])
    w_ga = din("w_gate_a", [DEPTH, 8, 128, 128])
    w_lru_o = din("w_lru_o", [DEPTH, 1024, 1024])
    w_uq = din("w_uq", [DEPTH, 256, 768])
    w_ukv = din("w_ukv", [DEPTH, 128, 1024])
    w_mla_o = din("w_mla_o", [DEPTH, 512, 1024])
    w_dil_o = din("w_dil_o", [DEPTH, 512, 1024])
    w_out = din("w_out", [DEPTH, 1024, 1024])
    pvec_in = din("pvec", [DEPTH, 128, NPV])
    cmat_in = din("cmat", [8, 128, 128])
    cmask_in = din("cmask", [128, 4 * 512 + 256])
    cinv_in = din("cinv", [128, 2])
    out_d = nc.dram_tensor("out", [S, D], F32, kind="ExternalOutput").ap()
    sk = "ExternalOutput" if debug else "Internal"
    xres = nc.dram_tensor("xres", [S, D], F32, kind=sk).ap()
    ylru = nc.dram_tensor("ylru", [1024, S], BF16, kind=sk).ap()
    ymla = nc.dram_tensor("ymla", [512, S], BF16, kind=sk).ap()
    ydil = nc.dram_tensor("ydil", [512, S], BF16, kind=sk).ap()
    gts = nc.dram_tensor("gts", [3072, S], BF16, kind=sk).ap()
    dbg = {}
    if debug:
        dbg["hT"] = nc.dram_tensor("hT_dbg", [128, 8 * S], BF16, kind="ExternalOutput").ap()

    ARENA_BYTES = 204 * 1024
    import contextlib
    with contextlib.ExitStack() as es:
        arena_t = es.enter_context(nc.sbuf_tensor("arena", [128, ARENA_BYTES // 2], BF16))
        banks = [es.enter_context(nc.psum_tensor(f"ps{i}", [128, 512], F32)) for i in range(8)]
        sems = {e: es.enter_context(nc.semaphore(f"s_{e}")) for e in Prog.ENG}
        dma_sems = {}
        for q in ("sp", "pool"):
            for i in range(NDMA_SLOTS):
                dma_sems[(q, i)] = es.enter_context(nc.semaphore(f"d_{q}{i}"))
        es.enter_context(nc.Block())
        es.enter_context(nc.allow_low_precision("bf16 PE transposes / bf16 matmul operands with fp32 accumulation"))
        AR = Arena(arena_t, ARENA_BYTES)
        bank_tok = [Tok(excl=True) for _ in range(8)]
        _emit_all(nc, P, AR, banks, bank_tok, locals(), n_layers, dbg)
        P.finish()
        P.emit(sems, dma_sems)
    return nc


def _emit_all(nc, P, AR, banks, bank_tok, env, n_layers, dbg):
    x_in = env["x_in"]; pos_in = env["pos_in"]; w_in = env["w_in"]
    out_d = env["out_d"]; xres = env["xres"]
    pvec_in = env["pvec_in"]; cmat_in = env["cmat_in"]; cmask_in = env["cmask_in"]; cinv_in = env["cinv_in"]
    V, A_, T_, G_ = nc.vector, nc.scalar, nc.tensor, nc.gpsimd

    cm = AR.alloc([8, 128], BF16)
    cm_t = Tok()
    P.dma("pool", lambda: G_.dma_start(out=cm, in_=cmat_in.rearrange("m p c -> p m c")), writes=[cm_t])
    IDENT, ONES256, ONES128, ONESB64, ONES96, RD_, RM_, EPAD = [cm[:, i, :] for i in range(8)]
    cmask = AR.alloc([4 * 512 + 256], BF16)
    cmask_t = Tok()
    P.dma("pool", lambda: G_.dma_start(out=cmask, in_=cmask_in), writes=[cmask_t])
    cinv = AR.alloc([2], F32)
    cinv_t = Tok()
    P.dma("sp", lambda: nc.sync.dma_start(out=cinv, in_=cinv_in), writes=[cinv_t])
    pv = AR.alloc([DEPTH, NPV], F32)
    pv_t = Tok()
    P.dma("sp", lambda: nc.sync.dma_start(out=pv, in_=pvec_in.rearrange("l p c -> p l c")), writes=[pv_t])
    hbgx = AR.alloc([DEPTH, 8], F32)
    hbga = AR.alloc([DEPTH, 8], F32)
    hc = AR.alloc([DEPTH, 8], F32)
    nhc = AR.alloc([DEPTH, 8], F32)
    hbm = AR.alloc([DEPTH, 24], F32)
    tmpv = AR.alloc([DEPTH, 8], F32)
    der_t = Tok()
    P.op("dve", lambda: V.tensor_scalar_mul(hbgx, pv[:, :, PV_BGX:PV_BGX + 8], -1.0), reads=[pv_t], writes=[der_t])
    P.op("dve", lambda: V.tensor_scalar_mul(hbga, pv[:, :, PV_BGA:PV_BGA + 8], -1.0), reads=[pv_t], writes=[der_t])
    P.op("dve", lambda: V.tensor_scalar_mul(hbm, pv[:, :, PV_BMRG:PV_BMRG + 24], -1.0), reads=[pv_t], writes=[der_t])
    P.op("act", lambda: A_.activation(out=tmpv, in_=pv[:, :, PV_LAM:PV_LAM + 8], func=AF.Exp, scale=-1.0),
         reads=[pv_t], writes=[der_t])
    P.op("act", lambda: A_.activation(out=tmpv, in_=tmpv, func=AF.Ln, bias=1.0, scale=1.0),
         reads=[der_t], writes=[der_t])
    P.op("dve", lambda: V.tensor_scalar_mul(hc, tmpv, -8.0), reads=[der_t], writes=[der_t])

    perm_mark = AR.mark()
    ctx = dict(nc=nc, P=P, AR=AR, banks=banks, bank_tok=bank_tok, env=env, dbg=dbg,
               cm=cm, cm_t=cm_t, cmask=cmask, cmask_t=cmask_t, cinv=cinv, cinv_t=cinv_t,
               pv=pv, pv_t=pv_t, der_t=der_t, hbgx=hbgx, hbga=hbga, hc=hc, nhc=nhc, hbm=hbm)
    for l in range(n_layers):
        src = x_in if l == 0 else xres
        dst = out_d if l == n_layers - 1 else xres
        AR.release(perm_mark)
        P.barrier()
        hT = AR.alloc([8, S], BF16)
        hT_t = toks(8)
        layer_mark = AR.mark()
        phase_norm(ctx, l, src, hT, hT_t)
        if dbg and l == 0:
            P.dma("sp", lambda: nc.sync.dma_start(out=dbg["hT"].rearrange("p (c t) -> p c t", c=8), in_=hT), reads=hT_t)
        stages = env.get("stages") or ("gates", "lru", "mla", "dil", "out")
        if "gates" in stages:
            P.barrier(); AR.release(layer_mark)
            phase_gates(ctx, l, hT, hT_t)
        if "lru" in stages:
            P.barrier(); AR.release(layer_mark)
            phase_lru(ctx, l, hT, hT_t)
        if "mla" in stages:
            P.barrier(); AR.release(layer_mark)
            phase_mla(ctx, l, hT, hT_t)
        if "dil" in stages:
            P.barrier(); AR.release(layer_mark)
            phase_dil(ctx, l, hT, hT_t)
        if "out" in stages:
            P.barrier(); AR.release(perm_mark)
            phase_out(ctx, l, src, dst)
    for _ in range(ctx["env"].get("extra_barriers", 0)):
        P.barrier()
    P.barrier()


def _mm(nc, out, lhsT, rhs, start, stop):
    return lambda: nc.tensor.matmul(out, lhsT, rhs, start=start, stop=stop)


def proj_group(ctx, bank_i, wtile, w_t, c0, m, hT, hT_t, tt, rows=None):
    nc, P = ctx["nc"], ctx["P"]
    bank = ctx["banks"][bank_i]
    bt = ctx["bank_tok"][bank_i]
    for k in range(8):
        P.op("pe", _mm(nc, bank[0:m, :], wtile[:, k, c0:c0 + m], hT[:, k, tt * 512:(tt + 1) * 512], k == 0, k == 7),
             reads=[w_t, hT_t[tt]], writes=[bt])


def phase_norm(ctx, l, src, hT, hT_t):
    nc, P, AR = ctx["nc"], ctx["P"], ctx["AR"]
    V, A_, T_ = nc.vector, nc.scalar, nc.tensor
    banks, bank_tok = ctx["banks"], ctx["bank_tok"]
    pv, pv_t = ctx["pv"], ctx["pv_t"]
    IDENT = ctx["cm"][:, 0, :]
    xt = [AR.alloc([4, D], F32) for _ in range(2)]
    xt_t = toks(2)
    xn = [AR.alloc([D], BF16) for _ in range(2)]
    xn_t = toks(2)
    ss = [AR.alloc([4], F32) for _ in range(2)]
    ss_t = toks(2)
    junk = AR.alloc([D], BF16)
    gB = pv[:, l, PV_NORMG:PV_NORMG + 8].unsqueeze(2).to_broadcast([128, 8, 128])
    for g4 in range(8):
        b = g4 % 2
        P.dma("sp", (lambda b=b, g4=g4: nc.sync.dma_start(
            out=xt[b], in_=src[g4 * 512:(g4 + 1) * 512, :].rearrange("(j p) d -> p j d", p=128))),
            writes=[xt_t[b]])
        P.op("dve", (lambda b=b: V.memset(ss[b], 0.0)), writes=[ss_t[b]])
        for j in range(4):
            P.op("act", (lambda b=b, j=j: A_.activation(out=junk, in_=xt[b][:, j, :], func=AF.Square,
                                                         accum_out=ss[b][:, j:j + 1])),
                 reads=[xt_t[b]], writes=[ss_t[b]])
        P.op("act", (lambda b=b: A_.activation(out=ss[b], in_=ss[b], func=AF.Ln, bias=EPS, scale=1.0 / D)),
             reads=[ss_t[b]], writes=[ss_t[b]])
        P.op("act", (lambda b=b: A_.activation(out=ss[b], in_=ss[b], func=AF.Exp, scale=-0.5)),
             reads=[ss_t[b]], writes=[ss_t[b]])
        for j in range(4):
            kb = (g4 * 4 + j) % 2
            P.op("dve", (lambda b=b, j=j, kb=kb: V.tensor_scalar(xn[kb], xt[b][:, j, :], ss[b][:, j:j + 1], None, ALU.mult)),
                 reads=[xt_t[b], ss_t[b]], writes=[xn_t[kb]])
            bbf = banks[kb][:, :].bitcast(BF16)
            for c in range(8):
                P.op("pe", (lambda kb=kb, c=c, bbf=bbf: T_.transpose(bbf[:, c * 128:(c + 1) * 128],
                                                                     xn[kb][:, c * 128:(c + 1) * 128], IDENT)),
                     reads=[xn_t[kb], ctx["cm_t"]], writes=[bank_tok[kb]])
            t0 = g4 * 512 + j * 128
            P.op("dve", (lambda bbf=bbf, t0=t0: V.tensor_tensor(
                out=hT[:, :, t0:t0 + 128], in0=bbf.rearrange("p (c t) -> p c t", c=8), in1=gB, op=ALU.mult)),
                reads=[bank_tok[kb], pv_t], writes=[hT_t[g4]])


def phase_lru(ctx, l, hT, hT_t):
    nc, P, AR = ctx["nc"], ctx["P"], ctx["AR"]
    V, A_, T_, G_ = nc.vector, nc.scalar, nc.tensor, nc.gpsimd
    env = ctx["env"]
    banks, bank_tok = ctx["banks"], ctx["bank_tok"]
    pv, pv_t, der_t = ctx["pv"], ctx["pv_t"], ctx["der_t"]
    w_in, ylru = env["w_in"], env["ylru"]
    wl = [AR.alloc([8, 256], BF16) for _ in range(2)]
    wl_t = toks(2)
    wgx = AR.alloc([8, 128], BF16)
    wga = AR.alloc([8, 128], BF16)
    wg_t = Tok()
    P.dma("pool", lambda: G_.dma_start(out=wgx, in_=env["w_gx"][l].rearrange("n c d -> c n d")), writes=[wg_t])
    P.dma("pool", lambda: G_.dma_start(out=wga, in_=env["w_ga"][l].rearrange("n c d -> c n d")), writes=[wg_t])
    X = AR.alloc([3 + S], F32)
    X_t = toks(8)
    Xh_t = Tok()
    names = ("G", "XC", "TGX", "TGA", "A", "T", "A2", "H")
    NSET = 4
    buf = [{nm: AR.alloc([512], F32) for nm in names} for _ in range(NSET)]
    for p_ in range(NSET):
        buf[p_]["XCb"] = AR.alloc([512], BF16)
        buf[p_]["Yb"] = AR.alloc([512], BF16)
    bt = [{nm: Tok() for nm in buf[0]} for _ in range(NSET)]
    P.op("dve", lambda: V.memset(X[:, 0:3], 0.0), writes=[Xh_t])
    w_r = w_in[l].rearrange("(c p) n -> p c n", p=128)
    it = 0
    for n in range(8):
        wb = n % 2
        P.dma("pool", (lambda wb=wb, n=n: G_.dma_start(out=wl[wb][:, :, 0:128], in_=w_r[:, :, C_LX + n * 128:C_LX + (n + 1) * 128])),
              writes=[wl_t[wb]])
        P.dma("pool", (lambda wb=wb, n=n: G_.dma_start(out=wl[wb][:, :, 128:256], in_=w_r[:, :, C_LG + n * 128:C_LG + (n + 1) * 128])),
              writes=[wl_t[wb]])
        cw = lambda k, n=n: pv[:, l, PV_CONVW + n * 4 + k:PV_CONVW + n * 4 + k + 1]
        cb = pv[:, l, PV_CONVB + n:PV_CONVB + n + 1]
        hbx = ctx["hbgx"][:, l, n:n + 1]
        hba = ctx["hbga"][:, l, n:n + 1]
        hcn = ctx["hc"][:, l, n:n + 1]
        nhcn = ctx["nhc"][:, l, n:n + 1]
        for tt in range(8):
            par = it % NSET
            pb = it % 2
            it += 1
            B_, Bt = buf[par], bt[par]
            c0 = 3 + tt * 512
            bx, bg, bgx, bga = pb, 2 + pb, 4 + pb, 6 + pb
            proj_group(ctx, bx, wl[wb], wl_t[wb], 0, 128, hT, hT_t, tt)
            P.op("act", (lambda bx=bx, c0=c0: A_.copy(X[:, c0:c0 + 512], banks[bx][:, :])),
                 reads=[bank_tok[bx]], writes=[X_t[tt]])
            proj_group(ctx, bg, wl[wb], wl_t[wb], 128, 128, hT, hT_t, tt)
            P.op("act", (lambda bg=bg, B_=B_: A_.copy(B_["G"], banks[bg][:, :])),
                 reads=[bank_tok[bg]], writes=[Bt["G"]])
            xprev = Xh_t if tt == 0 else X_t[tt - 1]
            P.op("dve", (lambda B_=B_, c0=c0, cw=cw, cb=cb: V.tensor_scalar(B_["XC"], X[:, c0:c0 + 512], cw(3), cb, ALU.mult, ALU.add)),
                 reads=[X_t[tt], pv_t], writes=[Bt["XC"]])
            for k in (2, 1, 0):
                P.op("dve", (lambda B_=B_, c0=c0, cw=cw, k=k: V.scalar_tensor_tensor(
                    B_["XC"], X[:, c0 - 3 + k:c0 - 3 + k + 512], cw(k), B_["XC"], ALU.mult, ALU.add)),
                    reads=[X_t[tt], xprev, Bt["XC"]], writes=[Bt["XC"]])
            P.op("act", (lambda B_=B_: A_.copy(B_["XCb"], B_["XC"])), reads=[Bt["XC"]], writes=[Bt["XCb"]])
            P.op("pe", _mm(nc, banks[bgx][:, :], wgx[:, n, :], B_["XCb"], True, True),
                 reads=[wg_t, Bt["XCb"]], writes=[bank_tok[bgx]])
            P.op("pe", _mm(nc, banks[bga][:, :], wga[:, n, :], B_["XCb"], True, True),
                 reads=[wg_t, Bt["XCb"]], writes=[bank_tok[bga]])
            for nm, bk, hb_ in (("TGX", bgx, hbx), ("TGA", bga, hba)):
                P.op("act", (lambda B_=B_, nm=nm, bk=bk, hb_=hb_: A_.activation(out=B_[nm], in_=banks[bk][:, :], func=AF.Exp, bias=hb_, scale=-1.0)),
                     reads=[bank_tok[bk], der_t], writes=[Bt[nm]])
                P.op("dve", (lambda B_=B_, nm=nm: V.tensor_scalar_add(B_[nm], B_[nm], 1.0)), reads=[Bt[nm]], writes=[Bt[nm]])
                P.op("dve", (lambda B_=B_, nm=nm: V.reciprocal(B_[nm], B_[nm])), reads=[Bt[nm]], writes=[Bt[nm]])
            P.op("act", (lambda B_=B_, hcn=hcn: A_.activation(out=B_["A"], in_=B_["TGA"], func=AF.Exp, scale=hcn)),
                 reads=[Bt["TGA"], der_t], writes=[Bt["A"]])
            P.op("act", (lambda B_=B_: A_.activation(out=B_["A2"], in_=B_["A"], func=AF.Square)),
                 reads=[Bt["A"]], writes=[Bt["A2"]])
            P.op("dve", (lambda B_=B_: V.tensor_scalar(B_["T"], B_["A2"], -1.0, 1.0, ALU.mult, ALU.add)),
                 reads=[Bt["A2"]], writes=[Bt["T"]])
            P.op("act", (lambda B_=B_: A_.activation(out=B_["T"], in_=B_["T"], func=AF.Ln)),
                 reads=[Bt["T"]], writes=[Bt["T"]])
            P.op("act", (lambda B_=B_: A_.activation(out=B_["T"], in_=B_["T"], func=AF.Exp, scale=0.5)),
                 reads=[Bt["T"]], writes=[Bt["T"]])
            P.op("dve", (lambda B_=B_: V.tensor_tensor(out=B_["TGX"], in0=B_["TGX"], in1=B_["XC"], op=ALU.mult)),
                 reads=[Bt["TGX"], Bt["XC"]], writes=[Bt["TGX"]])
            P.op("dve", (lambda B_=B_: V.tensor_tensor(out=B_["T"], in0=B_["T"], in1=B_["TGX"], op=ALU.mult)),
                 reads=[Bt["T"], Bt["TGX"]], writes=[Bt["T"]])
            if tt == 0:
                P.op("dve", (lambda B_=B_: V.tensor_tensor_scan(B_["H"], B_["A"], B_["T"], 0.0, ALU.mult, ALU.add)),
                     reads=[Bt["A"], Bt["T"]], writes=[Bt["H"]])
            else:
                Bp, Bpt = buf[(par - 1) % NSET], bt[(par - 1) % NSET]
                P.op("dve", (lambda B_=B_, Bp=Bp: V.tensor_tensor_scan(B_["H"], B_["A"], B_["T"], Bp["H"][:, 511:512], ALU.mult, ALU.add)),
                     reads=[Bt["A"], Bt["T"], Bpt["H"]], writes=[Bt["H"]])
            P.op("act", (lambda B_=B_: A_.activation(out=B_["A2"], in_=B_["G"], func=AF.Exp, scale=-1.0)),
                 reads=[Bt["G"], Bt["A2"]], writes=[Bt["A2"]])
            P.op("dve", (lambda B_=B_: V.tensor_scalar_add(B_["A2"], B_["A2"], 1.0)), reads=[Bt["A2"]], writes=[Bt["A2"]])
            P.op("dve", (lambda B_=B_: V.reciprocal(B_["A2"], B_["A2"])), reads=[Bt["A2"]], writes=[Bt["A2"]])
            P.op("dve", (lambda B_=B_: V.tensor_tensor(out=B_["A2"], in0=B_["A2"], in1=B_["G"], op=ALU.mult)),
                 reads=[Bt["A2"], Bt["G"]], writes=[Bt["A2"]])
            P.op("dve", (lambda B_=B_: V.tensor_tensor(out=B_["Yb"], in0=B_["H"], in1=B_["A2"], op=ALU.mult)),
                 reads=[Bt["H"], Bt["A2"]], writes=[Bt["Yb"]])
            P.dma("sp", (lambda B_=B_, n=n, tt=tt: nc.sync.dma_start(out=ylru[n * 128:(n + 1) * 128, tt * 512:(tt + 1) * 512], in_=B_["Yb"])),
                  reads=[Bt["Yb"]])


def phase_gates(ctx, l, hT, hT_t):
    nc, P, AR = ctx["nc"], ctx["P"], ctx["AR"]
    A_, G_ = nc.scalar, nc.gpsimd
    env = ctx["env"]
    banks, bank_tok = ctx["banks"], ctx["bank_tok"]
    w_r = env["w_in"][l].rearrange("(c p) n -> p c n", p=128)
    gts = env["gts"]
    wt = [AR.alloc([8, 128], BF16) for _ in range(2)]
    wt_t = toks(2)
    st = [AR.alloc([S], BF16) for _ in range(2)]
    st_t = toks(2)
    ebuf = [AR.alloc([512], F32) for _ in range(2)]
    ebuf_t = toks(2)
    for ch in range(24):
        b = ch % 2
        P.dma("pool", (lambda b=b, ch=ch: G_.dma_start(out=wt[b], in_=w_r[:, :, C_MRG + ch * 128:C_MRG + (ch + 1) * 128])),
              writes=[wt_t[b]])
        hb = ctx["hbm"][:, l, ch:ch + 1]
        for tt in range(8):
            bi = (ch * 8 + tt) % 4
            proj_group(ctx, bi, wt[b], wt_t[b], 0, 128, hT, hT_t, tt)
            eb = (ch * 8 + tt) % 2
            P.op("act", (lambda bi=bi, eb=eb, hb=hb: A_.activation(out=ebuf[eb], in_=banks[bi][:, :], func=AF.Exp, bias=hb, scale=-1.0)),
                 reads=[bank_tok[bi], ctx["der_t"]], writes=[ebuf_t[eb]])
            P.op("dve", (lambda eb=eb: nc.vector.tensor_scalar_add(ebuf[eb], ebuf[eb], 1.0)), reads=[ebuf_t[eb]], writes=[ebuf_t[eb]])
            P.op("dve", (lambda b=b, eb=eb, tt=tt: nc.vector.reciprocal(st[b][:, tt * 512:(tt + 1) * 512], ebuf[eb])),
                 reads=[ebuf_t[eb]], writes=[st_t[b]])
        P.dma("sp", (lambda b=b, ch=ch: nc.sync.dma_start(out=gts[ch * 128:(ch + 1) * 128, :], in_=st[b])),
              reads=[st_t[b]])


def _host_consts():
    cmat = np.zeros((8, 128, 128), np.float32)
    cmat[0] = np.eye(128, dtype=np.float32)
    cmat[1] = 1.0 / 256
    cmat[2] = 1.0 / 128
    cmat[3, 0:64, 0:64] = 1.0 / 64
    cmat[3, 64:128, 64:128] = 1.0 / 64
    cmat[4, 0:96, 0:96] = 1.0
    for blk in (0, 64):
        for i in range(32):
            cmat[5, blk + i + 32, blk + i] = -1.0
            cmat[5, blk + i, blk + i + 32] = 1.0
    for i in range(16):
        cmat[6, 64 + i + 16, 64 + i] = -1.0
        cmat[6, 64 + i, 64 + i + 16] = 1.0
    for k in range(32):
        cmat[7, k, 64 + k] = 1.0
    cmask = np.zeros((128, 4 * 512 + 256), np.float32)
    kk = np.arange(128)[:, None]
    qq = np.arange(512)[None, :]
    for j in range(4):
        cmask[:, j * 512:(j + 1) * 512] = (qq >= 128 * j + kk)
    q1 = np.arange(128)[None, :]
    cmask[:, 2048:2048 + 128] = (kk >= q1)
    cmask[:, 2048 + 128:2048 + 256] = (kk <= q1)
    cinv = np.zeros((128, 2), np.float32)
    theta = np.float32(10000.0)
    inv64 = theta ** (-np.arange(0, 64, 2, dtype=np.float32) / np.float32(64))
    inv32 = theta ** (-np.arange(0, 32, 2, dtype=np.float32) / np.float32(32))
    for p in range(128):
        cinv[p, 0] = inv64[p % 32]
        if 64 <= p < 96:
            cinv[p, 1] = inv32[(p - 64) % 16]
    return cmat, cmask, cinv


def _host_pvec(inp):
    pv = np.zeros((DEPTH, 128, NPV), np.float32)
    for l in range(DEPTH):
        pv[l, :, PV_NORMG:PV_NORMG + 8] = inp["norm_g"][l].reshape(8, 128).T
        pv[l, :, PV_CONVW:PV_CONVW + 32] = inp["conv_w"][l].reshape(4, 8, 128).transpose(2, 1, 0).reshape(128, 32)
        pv[l, :, PV_CONVB:PV_CONVB + 8] = inp["conv_b"][l].reshape(8, 128).T
        pv[l, :, PV_BGX:PV_BGX + 8] = inp["b_gate_x"][l].T
        pv[l, :, PV_BGA:PV_BGA + 8] = inp["b_gate_a"][l].T
        pv[l, :, PV_LAM:PV_LAM + 8] = inp["lru_lambda"][l].reshape(8, 128).T
        pv[l, :, PV_CQG:PV_CQG + 2] = inp["cq_norm_g"][l].reshape(2, 128).T
        pv[l, :, PV_CKVG] = inp["ckv_norm_g"][l]
        pv[l, 0:96, PV_MQG] = inp["mla_q_norm_g"][l]
        pv[l, 0:96, PV_MKG] = inp["mla_k_norm_g"][l]
        pv[l, 0:64, PV_DQKG] = inp["dil_q_norm_g"][l]
        pv[l, 64:128, PV_DQKG] = inp["dil_k_norm_g"][l]
        pv[l, :, PV_BMRG:PV_BMRG + 24] = inp["b_merge"][l].reshape(24, 128).T
    return pv


def make_in_maps(inp, n_cores=8):
    cmat, cmask, cinv = _host_consts()
    pvec = _host_pvec(inp)
    shared = {
        "w_in": np.ascontiguousarray(inp["w_in"]), "w_gate_x": np.ascontiguousarray(inp["w_gate_x"]),
        "w_gate_a": np.ascontiguousarray(inp["w_gate_a"]), "w_lru_o": np.ascontiguousarray(inp["w_lru_o"]),
        "w_uq": np.ascontiguousarray(inp["w_uq"]), "w_ukv": np.ascontiguousarray(inp["w_ukv"]),
        "w_mla_o": np.ascontiguousarray(inp["w_mla_o"]), "w_dil_o": np.ascontiguousarray(inp["w_dil_o"]),
        "w_out": np.ascontiguousarray(inp["w_out"]), "pvec": pvec, "cmat": cmat, "cmask": cmask, "cinv": cinv,
    }
    maps = []
    for b in range(n_cores):
        m = dict(shared)
        m["x"] = np.ascontiguousarray(inp["x"][b])
        m["pos"] = np.ascontiguousarray(inp["positions"][b].reshape(1, S).astype(np.int32))
        maps.append(m)
    return maps


def kernel(**inputs):
    inp = {k: np.asarray(v) for k, v in inputs.items()}
    nc = build_program()
    maps = make_in_maps(inp, 8)
    res = run_bass_kernel_spmd(nc, maps, core_ids=list(range(8)))
    out = np.stack([np.asarray(res.results[b]["out"]).reshape(S, D) for b in range(8)], axis=0)
    return out.astype(np.float32)


def make_tables(ctx, which, CT, ST, tab_t):
    nc, P, AR = ctx["nc"], ctx["P"], ctx["AR"]
    V, A_ = nc.vector, nc.scalar
    pos_in = ctx["env"]["pos_in"]
    inv = ctx["cinv"][:, which:which + 1]
    m = AR.mark()
    posi = [AR.alloc([1024], I32) for _ in range(2)]
    y0 = [AR.alloc([1024], F32) for _ in range(2)]
    yy = [AR.alloc([1024], F32) for _ in range(2)]
    ki = [AR.alloc([1024], I32) for _ in range(2)]
    kf = [AR.alloc([1024], F32) for _ in range(2)]
    tk = [{n_: Tok() for n_ in ("posi", "y0", "yy", "ki", "kf")} for _ in range(2)]
    TWO_PI = 2.0 * math.pi * (1.0 - 1e-6)
    for c in range(4):
        b = c % 2
        t = tk[b]
        P.dma("sp", (lambda b=b, c=c: nc.sync.dma_start(out=posi[b], in_=pos_in[:, c * 1024:(c + 1) * 1024].broadcast_to([128, 1024]))),
              writes=[t["posi"]])
        P.op("dve", (lambda b=b: V.tensor_copy(y0[b], posi[b])), reads=[t["posi"]], writes=[t["y0"]])
        P.op("dve", (lambda b=b: V.tensor_scalar(y0[b], y0[b], inv, None, ALU.mult)), reads=[t["y0"], ctx["cinv_t"]], writes=[t["y0"]])
        P.op("dve", (lambda b=b: V.tensor_scalar(y0[b], y0[b], 1.0 / (2.0 * math.pi), None, ALU.mult)), reads=[t["y0"]], writes=[t["y0"]])
        for ph, OUT in ((0.0, ST), (0.25, CT)):
            P.op("dve", (lambda b=b, ph=ph: V.tensor_scalar(yy[b], y0[b], ph, None, ALU.add)), reads=[t["y0"]], writes=[t["yy"]])
            P.op("dve", (lambda b=b: V.tensor_copy(ki[b], yy[b])), reads=[t["yy"]], writes=[t["ki"]])
            P.op("dve", (lambda b=b: V.tensor_copy(kf[b], ki[b])), reads=[t["ki"]], writes=[t["kf"]])
            P.op("dve", (lambda b=b: V.tensor_tensor(out=yy[b], in0=yy[b], in1=kf[b], op=ALU.subtract)), reads=[t["yy"], t["kf"]], writes=[t["yy"]])
            P.op("dve", (lambda b=b: V.tensor_scalar(kf[b], yy[b], 0.5, None, ALU.is_gt)), reads=[t["yy"]], writes=[t["kf"]])
            P.op("dve", (lambda b=b: V.tensor_tensor(out=yy[b], in0=yy[b], in1=kf[b], op=ALU.subtract)), reads=[t["yy"], t["kf"]], writes=[t["yy"]])
            P.op("act", (lambda b=b, OUT=OUT, c=c: A_.activation(out=OUT[:, c * 1024:(c + 1) * 1024], in_=yy[b], func=AF.Sin, scale=TWO_PI)),
                 reads=[t["yy"]], writes=[tab_t])
    P.barrier()
    AR.release(m)


def norm_rope_chain(ctx, src_bank, np_, ones_m, eps_v, gcol, CT, ST, rot_m, tt, out_ap, out_tok, tmp, tmp_t, bank_ss, bank_rot):
    nc, P = ctx["nc"], ctx["P"]
    V, A_ = nc.vector, nc.scalar
    banks, bank_tok = ctx["banks"], ctx["bank_tok"]
    IDENT = ctx["cm"][:, 0, :]
    src = banks[src_bank][0:np_, :]
    st_ = bank_tok[src_bank]
    cols = slice(tt * 512, (tt + 1) * 512)
    P.op("act", lambda: A_.activation(out=tmp["sq"][0:np_, :], in_=src, func=AF.Square), reads=[st_], writes=[tmp_t["sq"]])
    P.op("pe", _mm(nc, banks[bank_ss][0:np_, :], ones_m[0:np_, 0:np_], tmp["sq"][0:np_, :], True, True),
         reads=[tmp_t["sq"], ctx["cm_t"]], writes=[bank_tok[bank_ss]])
    P.op("act", lambda: A_.activation(out=tmp["rs"][0:np_, :], in_=banks[bank_ss][0:np_, :], func=AF.Ln, bias=eps_v),
         reads=[bank_tok[bank_ss]], writes=[tmp_t["rs"]])
    P.op("act", lambda: A_.activation(out=tmp["rs"][0:np_, :], in_=tmp["rs"][0:np_, :], func=AF.Exp, scale=-0.5),
         reads=[tmp_t["rs"]], writes=[tmp_t["rs"]])
    P.op("dve", lambda: V.scalar_tensor_tensor(tmp["u1"][0:np_, :], src, gcol[0:np_, :], CT[0:np_, cols], ALU.mult, ALU.mult),
         reads=[st_, ctx["pv_t"], tmp_t["tab"]], writes=[tmp_t["u1"]])
    P.op("dve", lambda: V.scalar_tensor_tensor(tmp["u2"][0:np_, :], src, gcol[0:np_, :], ST[0:np_, cols], ALU.mult, ALU.mult),
         reads=[st_, ctx["pv_t"], tmp_t["tab"]], writes=[tmp_t["u2"]])
    P.op("pe", _mm(nc, banks[bank_rot][0:np_, :], IDENT[0:np_, 0:np_], tmp["u1"][0:np_, :], True, False),
         reads=[tmp_t["u1"], ctx["cm_t"]], writes=[bank_tok[bank_rot]])
    P.op("pe", _mm(nc, banks[bank_rot][0:np_, :], rot_m[0:np_, 0:np_], tmp["u2"][0:np_, :], False, True),
         reads=[tmp_t["u2"], ctx["cm_t"]], writes=[bank_tok[bank_rot]])
    P.op("dve", lambda: V.tensor_tensor(out=out_ap, in0=banks[bank_rot][0:np_, :], in1=tmp["rs"][0:np_, :], op=ALU.mult),
         reads=[bank_tok[bank_rot], tmp_t["rs"]], writes=[out_tok])


def finalize_head(ctx, o_ap, o_tok, den_ap, den_tok, gate_w, gate_wt, gcol0, hT, hT_t, tt, ydst, tmp, tmp_t):
    nc, P = ctx["nc"], ctx["P"]
    V, A_ = nc.vector, nc.scalar
    banks, bank_tok = ctx["banks"], ctx["bank_tok"]
    proj_group(ctx, 7, gate_w, gate_wt, gcol0, 64, hT, hT_t, tt)
    g_ps = banks[7][0:64, :]
    P.op("act", lambda: A_.activation(out=tmp["e1"][0:64, :], in_=g_ps, func=AF.Exp, scale=-1.0), reads=[bank_tok[7]], writes=[tmp_t["e1"]])
    P.op("dve", lambda: V.tensor_copy(tmp["den0"][0:64, :], den_ap), reads=[den_tok], writes=[tmp_t["den0"]])
    P.op("dve", lambda: V.scalar_tensor_tensor(tmp["e1"][0:64, :], tmp["e1"][0:64, :], 1.0, tmp["den0"][0:64, :], ALU.add, ALU.mult),
         reads=[tmp_t["e1"], tmp_t["den0"]], writes=[tmp_t["e1"]])
    P.op("dve", lambda: V.reciprocal(tmp["e1"][0:64, :], tmp["e1"][0:64, :]), reads=[tmp_t["e1"]], writes=[tmp_t["e1"]])
    P.op("dve", lambda: V.tensor_tensor(out=tmp["e1"][0:64, :], in0=tmp["e1"][0:64, :], in1=g_ps, op=ALU.mult),
         reads=[tmp_t["e1"], bank_tok[7]], writes=[tmp_t["e1"]])
    P.op("dve", lambda: V.tensor_tensor(out=tmp["yb"][0:64, :], in0=o_ap, in1=tmp["e1"][0:64, :], op=ALU.mult),
         reads=[o_tok, tmp_t["e1"]], writes=[tmp_t["yb"]])
    P.dma("sp", lambda: nc.sync.dma_start(out=ydst, in_=tmp["yb"][0:64, :]), reads=[tmp_t["yb"]])


def _chain_tmp(AR):
    tmp = {"sq": AR.alloc([512], BF16), "rs": AR.alloc([512], F32), "u1": AR.alloc([512], BF16), "u2": AR.alloc([512], BF16),
           "e1": AR.alloc([512], F32), "den0": AR.alloc([512], F32), "yb": AR.alloc([512], BF16), "den": AR.alloc([512], F32)}
    tmp_t = {k: Tok() for k in list(tmp) + ["tab"]}
    return tmp, tmp_t


def _chain_tmp2(AR, tmp_t):
    tmp = {"sq": AR.alloc([512], BF16), "rs": AR.alloc([512], F32), "u1": AR.alloc([512], BF16), "u2": AR.alloc([512], BF16)}
    t2 = {k: Tok() for k in tmp}
    t2["tab"] = tmp_t["tab"]
    return tmp, t2


def phase_mla(ctx, l, hT, hT_t):
    nc, P, AR = ctx["nc"], ctx["P"], ctx["AR"]
    V, A_, G_ = nc.vector, nc.scalar, nc.gpsimd
    env = ctx["env"]
    banks, bank_tok = ctx["banks"], ctx["bank_tok"]
    pv, pv_t, cm, cm_t = ctx["pv"], ctx["pv_t"], ctx["cm"], ctx["cm_t"]
    ONES256, ONES128, ONES96, RM_, EPAD = cm[:, 1, :], cm[:, 2, :], cm[:, 4, :], cm[:, 6, :], cm[:, 7, :]
    cmask, cmask_t = ctx["cmask"], ctx["cmask_t"]
    ymla = env["ymla"]
    w_r = env["w_in"][l].rearrange("(c p) n -> p c n", p=128)
    CT = AR.alloc([S], BF16)
    ST = AR.alloc([S], BF16)
    tmp, tmp_t = _chain_tmp(AR)
    make_tables(ctx, 1, CT, ST, tmp_t["tab"])
    if env.get("stop") == "tables":
        return
    wm = AR.alloc([8, 416], BF16)
    wuq = AR.alloc([2, 768], BF16)
    wukv = AR.alloc([1024], BF16)
    wgm = AR.alloc([8, 512], BF16)
    wkp = AR.alloc([8, 96], BF16)
    w_t = Tok()
    wkp_t = Tok()
    P.dma("pool", lambda: G_.dma_start(out=wm, in_=w_r[:, :, C_CQ:C_CQ + 416]), writes=[w_t])
    P.dma("pool", lambda: G_.dma_start(out=wuq, in_=env["w_uq"][l].rearrange("(c p) n -> p c n", p=128)), writes=[w_t])
    P.dma("pool", lambda: G_.dma_start(out=wukv, in_=env["w_ukv"][l]), writes=[w_t])
    P.dma("pool", lambda: G_.dma_start(out=wgm, in_=w_r[:, :, C_MG:C_MG + 512]), writes=[w_t])
    P.op("dve", lambda: V.memset(wkp, 0.0), writes=[wkp_t])
    P.op("dve", lambda: V.tensor_copy(wkp[:, :, 0:64], wukv.rearrange("p (h c) -> p h c", h=8)[:, :, 0:64]), reads=[w_t], writes=[wkp_t])
    CQN = AR.alloc([2, S], BF16)
    CKVN = AR.alloc([S], BF16)
    KR = AR.alloc([S], BF16)
    lat_t = toks(8)
    m1_mark = AR.mark()
    sq3 = [AR.alloc([3, 512], BF16) for _ in range(2)]
    sq3_t = toks(2)
    rs2 = [AR.alloc([2, 512], F32) for _ in range(2)]
    rs2_t = toks(2)
    for tt in range(8):
        b = tt % 2
        cols = slice(tt * 512, (tt + 1) * 512)
        proj_group(ctx, 0, wm, w_t, 0, 128, hT, hT_t, tt)
        proj_group(ctx, 1, wm, w_t, 128, 128, hT, hT_t, tt)
        proj_group(ctx, 2, wm, w_t, 256, 128, hT, hT_t, tt)
        proj_group(ctx, 3, wm, w_t, 384, 32, hT, hT_t, tt)
        for i in range(3):
            P.op("act", (lambda b=b, i=i: A_.activation(out=sq3[b][:, i, :], in_=banks[i][:, :], func=AF.Square)),
                 reads=[bank_tok[i]], writes=[sq3_t[b]])
        P.op("pe", _mm(nc, banks[4][:, :], ONES256, sq3[b][:, 0, :], True, False), reads=[sq3_t[b], cm_t], writes=[bank_tok[4]])
        P.op("pe", _mm(nc, banks[4][:, :], ONES256, sq3[b][:, 1, :], False, True), reads=[sq3_t[b], cm_t], writes=[bank_tok[4]])
        P.op("pe", _mm(nc, banks[5][:, :], ONES128, sq3[b][:, 2, :], True, True), reads=[sq3_t[b], cm_t], writes=[bank_tok[5]])
        for i, bk in ((0, 4), (1, 5)):
            P.op("act", (lambda b=b, i=i, bk=bk: A_.activation(out=rs2[b][:, i, :], in_=banks[bk][:, :], func=AF.Ln, bias=EPS)),
                 reads=[bank_tok[bk]], writes=[rs2_t[b]])
            P.op("act", (lambda b=b, i=i: A_.activation(out=rs2[b][:, i, :], in_=rs2[b][:, i, :], func=AF.Exp, scale=-0.5)),
                 reads=[rs2_t[b]], writes=[rs2_t[b]])
        for c in range(2):
            P.op("dve", (lambda b=b, c=c, cols=cols: V.scalar_tensor_tensor(CQN[:, c, cols], banks[c][:, :], pv[:, l, PV_CQG + c:PV_CQG + c + 1],
                                                                            rs2[b][:, 0, :], ALU.mult, ALU.mult)),
                 reads=[bank_tok[c], rs2_t[b], pv_t], writes=[lat_t[tt]])
        P.op("dve", (lambda b=b, cols=cols: V.scalar_tensor_tensor(CKVN[:, cols], banks[2][:, :], pv[:, l, PV_CKVG:PV_CKVG + 1],
                                                                   rs2[b][:, 1, :], ALU.mult, ALU.mult)),
             reads=[bank_tok[2], rs2_t[b], pv_t], writes=[lat_t[tt]])
        P.op("act", (lambda cols=cols: A_.copy(KR[0:32, cols], banks[3][0:32, :])), reads=[bank_tok[3]], writes=[lat_t[tt]])
    if env.get("stop") == "latent":
        return
    P.barrier()
    AR.release(m1_mark)
    tmpk, tmpk_t = _chain_tmp2(AR, tmp_t)
    QT = AR.alloc([S], BF16)
    KT = AR.alloc([S], BF16)
    QT_t = toks(8)
    KT_t = toks(8)
    VH = AR.alloc([32, 128], BF16)
    VH_t = Tok()
    P.op("dve", lambda: V.memset(VH[:, :, 64:128], 1.0), writes=[VH_t])
    Pb = [AR.alloc([512], BF16) for _ in range(3)]
    Pb_t = toks(3)
    gq = pv[:, l, PV_MQG:PV_MQG + 1]
    gk = pv[:, l, PV_MKG:PV_MKG + 1]
    SC = math.sqrt(96.0)
    cnt = 0
    for h in range(8):
        for tt in range(8):
            cols = slice(tt * 512, (tt + 1) * 512)
            ba = 0
            for c in range(2):
                P.op("pe", _mm(nc, banks[ba][0:96, :], wuq[:, c, h * 96:(h + 1) * 96], CQN[:, c, cols], c == 0, c == 1),
                     reads=[w_t, lat_t[tt]], writes=[bank_tok[ba]])
            if env.get("stop") == "q_proj":
                return
            norm_rope_chain(ctx, ba, 96, ONES96, 96 * EPS, gq, CT, ST, RM_, tt, QT[0:96, cols], QT_t[tt], tmp, tmp_t, 2, 3)
            if env.get("stop") == "q_chain":
                return
            ba = 1
            P.op("pe", _mm(nc, banks[ba][0:96, :], wkp[:, h, :], CKVN[:, cols], True, False), reads=[wkp_t, lat_t[tt]], writes=[bank_tok[ba]])
            P.op("pe", _mm(nc, banks[ba][0:96, :], EPAD[0:32, 0:96], KR[0:32, cols], False, True), reads=[cm_t, lat_t[tt]], writes=[bank_tok[ba]])
            norm_rope_chain(ctx, ba, 96, ONES96, 96 * EPS, gk, CT, ST, RM_, tt, KT[0:96, cols], KT_t[tt], tmpk, tmpk_t, 4, 5)
            if env.get("stop") == "k_chain":
                return
            for j in range(4):
                P.op("pe", _mm(nc, banks[7][:, j * 64:(j + 1) * 64], CKVN[:, tt * 512 + j * 128:tt * 512 + (j + 1) * 128],
                               wukv[:, h * 128 + 64:h * 128 + 128], True, True),
                     reads=[w_t, lat_t[tt]], writes=[bank_tok[7]])
            P.op("act", (lambda tt=tt: A_.copy(VH[:, tt * 4:(tt + 1) * 4, 0:64], banks[7][:, 0:256].rearrange("p (j c) -> p j c", j=4))),
                 reads=[bank_tok[7]], writes=[VH_t])
        if env.get("stop") == "prep":
            return
        for qt in range(8):
            if env.get("stop") == "attn1" and qt == 1:
                return
            qcols = slice(qt * 512, (qt + 1) * 512)
            nkb = 4 * qt + 4
            for kb in range(nkb):
                sb = 4 + cnt % 2
                pi = cnt % 3
                cnt += 1
                P.op("pe", _mm(nc, banks[sb][:, :], KT[0:96, kb * 128:(kb + 1) * 128], QT[0:96, qcols], True, True),
                     reads=[KT_t[kb // 4], QT_t[qt]], writes=[bank_tok[sb]])
                P.op("act", (lambda sb=sb, pi=pi: A_.activation(out=Pb[pi], in_=banks[sb][:, :], func=AF.Exp, scale=SC)),
                     reads=[bank_tok[sb]], writes=[Pb_t[pi]])
                j = kb - 4 * qt
                if j >= 0:
                    P.op("dve", (lambda pi=pi, j=j: V.tensor_tensor(out=Pb[pi], in0=Pb[pi], in1=cmask[:, j * 512:(j + 1) * 512], op=ALU.mult)),
                         reads=[Pb_t[pi], cmask_t], writes=[Pb_t[pi]])
                P.op("pe", _mm(nc, banks[6][:, :], VH[:, kb, :], Pb[pi], kb == 0, kb == nkb - 1),
                     reads=[VH_t, Pb_t[pi]], writes=[bank_tok[6]])
            P.op("act", lambda: A_.copy(tmp["den"][64:128, :], banks[6][64:128, :]), reads=[bank_tok[6]], writes=[tmp_t["den"]])
            finalize_head(ctx, banks[6][0:64, :], bank_tok[6], tmp["den"][64:128, :], tmp_t["den"], wgm, w_t, h * 64, hT, hT_t, qt,
                          ymla[h * 64:(h + 1) * 64, qcols], tmp, tmp_t)


def phase_dil(ctx, l, hT, hT_t):
    nc, P, AR = ctx["nc"], ctx["P"], ctx["AR"]
    V, A_, G_ = nc.vector, nc.scalar, nc.gpsimd
    env = ctx["env"]
    banks, bank_tok = ctx["banks"], ctx["bank_tok"]
    pv, pv_t, cm, cm_t = ctx["pv"], ctx["pv_t"], ctx["cm"], ctx["cm_t"]
    ONESB64, RD_ = cm[:, 3, :], cm[:, 5, :]
    cmask, cmask_t = ctx["cmask"], ctx["cmask_t"]
    DMASK = cmask[:, 2048:2048 + 256]
    ydil = env["ydil"]
    w_r = env["w_in"][l].rearrange("(c p) n -> p c n", p=128)
    CT = AR.alloc([S], BF16)
    ST = AR.alloc([S], BF16)
    tmp, tmp_t = _chain_tmp(AR)
    make_tables(ctx, 0, CT, ST, tmp_t["tab"])
    tmpb, tmpb_t = _chain_tmp2(AR, tmp_t)
    wqk = [AR.alloc([8, 128], BF16) for _ in range(2)]
    wv = [AR.alloc([8, 64], BF16) for _ in range(2)]
    wgh_t = toks(2)
    wdg = AR.alloc([8, 512], BF16)
    wdg_t = Tok()
    P.dma("pool", lambda: G_.dma_start(out=wdg, in_=w_r[:, :, C_DG:C_DG + 512]), writes=[wdg_t])
    QK = AR.alloc([S], BF16)
    K0 = AR.alloc([S], BF16)
    QK_t = toks(8)
    K0_t = toks(8)
    VD = AR.alloc([32, 128], BF16)
    VD_t = Tok()
    P.op("dve", lambda: V.memset(VD[:, :, 64:128], 1.0), writes=[VD_t])
    OACC = AR.alloc([S], F32)
    OACC_t = Tok()
    Pd = [AR.alloc([256], BF16) for _ in range(3)]
    Pd_t = toks(3)
    gqk = pv[:, l, PV_DQKG:PV_DQKG + 1]
    it = 0
    cnt = 0
    for h in range(8):
        for g in range(3):
            d = DIL[g]
            nb = S // (128 * d)
            wb = it % 2
            it += 1
            hd = g * 8 + h
            P.dma("pool", (lambda wb=wb, hd=hd: G_.dma_start(out=wqk[wb][:, :, 0:64], in_=w_r[:, :, C_DQ + hd * 64:C_DQ + (hd + 1) * 64])), writes=[wgh_t[wb]])
            P.dma("pool", (lambda wb=wb, hd=hd: G_.dma_start(out=wqk[wb][:, :, 64:128], in_=w_r[:, :, C_DK + hd * 64:C_DK + (hd + 1) * 64])), writes=[wgh_t[wb]])
            P.dma("pool", (lambda wb=wb, hd=hd: G_.dma_start(out=wv[wb], in_=w_r[:, :, C_DV + hd * 64:C_DV + (hd + 1) * 64])), writes=[wgh_t[wb]])
            for tt in range(8):
                cols = slice(tt * 512, (tt + 1) * 512)
                ba = tt % 2
                proj_group(ctx, ba, wqk[wb], wgh_t[wb], 0, 128, hT, hT_t, tt)
                if tt % 2 == 0:
                    norm_rope_chain(ctx, ba, 128, ONESB64, EPS, gqk, CT, ST, RD_, tt, QK[:, cols], QK_t[tt], tmp, tmp_t, 2, 3)
                else:
                    norm_rope_chain(ctx, ba, 128, ONESB64, EPS, gqk, CT, ST, RD_, tt, QK[:, cols], QK_t[tt], tmpb, tmpb_t, 4, 5)
                P.op("dve", (lambda cols=cols: V.tensor_copy(K0[0:64, cols], QK[64:128, cols])), reads=[QK_t[tt]], writes=[K0_t[tt]])

            def ucols(r, n, d=d):
                st0 = r + d * 128 * n
                return slice(st0, st0 + d * 127 + 1, d)
            units = [(r, n) for r in range(d) for n in range(nb)]
            for u0 in range(0, 32, 4):
                for jj in range(4):
                    r, n = units[u0 + jj]
                    for k in range(8):
                        P.op("pe", _mm(nc, banks[7][:, jj * 64:(jj + 1) * 64], hT[:, k, ucols(r, n)], wv[wb][:, k, :], k == 0, k == 7),
                             reads=[wgh_t[wb]] + hT_t, writes=[bank_tok[7]])
                P.op("act", (lambda u0=u0: A_.copy(VD[:, u0:u0 + 4, 0:64], banks[7][:, 0:256].rearrange("p (j c) -> p j c", j=4))),
                     reads=[bank_tok[7]], writes=[VD_t])
            for u, (r, n) in enumerate(units):
                sb = 4 + cnt % 2
                pi = cnt % 3
                cnt += 1
                lo = 0 if n > 0 else 128
                qc = ucols(r, n)
                if n > 0:
                    P.op("pe", _mm(nc, banks[sb][:, 0:128], K0[0:64, ucols(r, n - 1)], QK[0:64, qc], True, True),
                         reads=K0_t + QK_t, writes=[bank_tok[sb]])
                P.op("pe", _mm(nc, banks[sb][:, 128:256], K0[0:64, qc], QK[0:64, qc], True, True),
                     reads=K0_t + QK_t, writes=[bank_tok[sb]])
                P.op("act", (lambda sb=sb, pi=pi, lo=lo: A_.activation(out=Pd[pi][:, lo:256], in_=banks[sb][:, lo:256], func=AF.Exp, scale=0.125)),
                     reads=[bank_tok[sb]], writes=[Pd_t[pi]])
                P.op("dve", (lambda pi=pi, lo=lo: V.tensor_tensor(out=Pd[pi][:, lo:256], in0=Pd[pi][:, lo:256], in1=DMASK[:, lo:256], op=ALU.mult)),
                     reads=[Pd_t[pi], cmask_t], writes=[Pd_t[pi]])
                ob = 6 + cnt % 2
                if n > 0:
                    P.op("pe", _mm(nc, banks[ob][:, 0:128], VD[:, u - 1, :], Pd[pi][:, 0:128], True, False), reads=[VD_t, Pd_t[pi]], writes=[bank_tok[ob]])
                P.op("pe", _mm(nc, banks[ob][:, 0:128], VD[:, u, :], Pd[pi][:, 128:256], n == 0, True), reads=[VD_t, Pd_t[pi]], writes=[bank_tok[ob]])
                if g == 0:
                    P.op("dve", (lambda qc=qc, ob=ob: V.tensor_copy(OACC[:, qc], banks[ob][:, 0:128])), reads=[bank_tok[ob]], writes=[OACC_t])
                else:
                    P.op("dve", (lambda qc=qc, ob=ob: V.tensor_tensor(out=OACC[:, qc], in0=banks[ob][:, 0:128], in1=OACC[:, qc], op=ALU.add)),
                         reads=[bank_tok[ob], OACC_t], writes=[OACC_t])
        for tt in range(8):
            cols = slice(tt * 512, (tt + 1) * 512)
            finalize_head(ctx, OACC[0:64, cols], OACC_t, OACC[64:128, cols], OACC_t, wdg, wdg_t, h * 64, hT, hT_t, tt,
                          ydil[h * 64:(h + 1) * 64, cols], tmp, tmp_t)


def phase_out(ctx, l, src, dst):
    nc, P, AR = ctx["nc"], ctx["P"], ctx["AR"]
    V, A_, G_ = nc.vector, nc.scalar, nc.gpsimd
    env = ctx["env"]
    banks, bank_tok = ctx["banks"], ctx["bank_tok"]
    TE = 256
    wlo = AR.alloc([8, 1024], BF16)
    wmo = AR.alloc([4, 1024], BF16)
    wdo = AR.alloc([4, 1024], BF16)
    wou = AR.alloc([8, 1024], BF16)
    w_t = Tok()
    for wt_, nm in ((wlo, "w_lru_o"), (wmo, "w_mla_o"), (wdo, "w_dil_o"), (wou, "w_out")):
        P.dma("pool", (lambda wt_=wt_, nm=nm: G_.dma_start(out=wt_, in_=env[nm][l].rearrange("(c p) n -> p c n", p=128))), writes=[w_t])
    YL = [AR.alloc([8, TE], BF16) for _ in range(2)]
    YM = [AR.alloc([4, TE], BF16) for _ in range(2)]
    YD = [AR.alloc([4, TE], BF16) for _ in range(2)]
    GT = [AR.alloc([24, TE], BF16) for _ in range(2)]
    XT = [AR.alloc([2, D], F32) for _ in range(2)]
    OUT = [AR.alloc([2, D], F32) for _ in range(2)]
    M = [AR.alloc([8, TE], BF16) for _ in range(2)]
    t1 = [AR.alloc([TE], F32) for _ in range(2)]
    t2 = [AR.alloc([TE], F32) for _ in range(2)]
    in_t = toks(2)
    out_t = toks(2)
    M_t = toks(2)
    t1_t = toks(2)
    t2_t = toks(2)
    ylru_r = env["ylru"].rearrange("(c p) t -> p c t", p=128)
    ymla_r = env["ymla"].rearrange("(c p) t -> p c t", p=128)
    ydil_r = env["ydil"].rearrange("(c p) t -> p c t", p=128)
    gts_r = env["gts"].rearrange("(c p) t -> p c t", p=128)
    is_final = dst is env["out_d"]
    k = 0
    for e in range(S // TE):
        b = e % 2
        tc_ = slice(e * TE, (e + 1) * TE)
        P.dma("sp", (lambda b=b, tc_=tc_: nc.sync.dma_start(out=YL[b], in_=ylru_r[:, :, tc_])), writes=[in_t[b]])
        P.dma("sp", (lambda b=b, tc_=tc_: nc.sync.dma_start(out=YM[b], in_=ymla_r[:, :, tc_])), writes=[in_t[b]])
        P.dma("sp", (lambda b=b, tc_=tc_: nc.sync.dma_start(out=YD[b], in_=ydil_r[:, :, tc_])), writes=[in_t[b]])
        P.dma("sp", (lambda b=b, tc_=tc_: nc.sync.dma_start(out=GT[b], in_=gts_r[:, :, tc_])), writes=[in_t[b]])
        P.dma("sp", (lambda b=b, e=e: nc.sync.dma_start(out=XT[b], in_=src[e * TE:(e + 1) * TE, :].rearrange("(j p) d -> p j d", p=128))), writes=[in_t[b]])
        for j in range(8):
            js = slice(j * 128, (j + 1) * 128)
            kk = k % 2
            k += 1
            bl, bm, bd = 0 + kk, 2 + kk, 4 + kk
            for c in range(8):
                P.op("pe", _mm(nc, banks[bl][:, 0:TE], wlo[:, c, js], YL[b][:, c, :], c == 0, c == 7), reads=[w_t, in_t[b]], writes=[bank_tok[bl]])
            for c in range(4):
                P.op("pe", _mm(nc, banks[bm][:, 0:TE], wmo[:, c, js], YM[b][:, c, :], c == 0, c == 3), reads=[w_t, in_t[b]], writes=[bank_tok[bm]])
            for c in range(4):
                P.op("pe", _mm(nc, banks[bd][:, 0:TE], wdo[:, c, js], YD[b][:, c, :], c == 0, c == 3), reads=[w_t, in_t[b]], writes=[bank_tok[bd]])
            P.op("dve", (lambda b=b, j=j, kk=kk, bl=bl: V.tensor_tensor(out=t1[kk], in0=banks[bl][:, 0:TE], in1=GT[b][:, j, :], op=ALU.mult)),
                 reads=[bank_tok[bl], in_t[b]], writes=[t1_t[kk]])
            P.op("dve", (lambda b=b, j=j, kk=kk, bm=bm: V.tensor_tensor(out=t2[kk], in0=banks[bm][:, 0:TE], in1=GT[b][:, 8 + j, :], op=ALU.mult)),
                 reads=[bank_tok[bm], in_t[b]], writes=[t2_t[kk]])
            P.op("dve", (lambda kk=kk: V.tensor_tensor(out=t1[kk], in0=t1[kk], in1=t2[kk], op=ALU.add)),
                 reads=[t1_t[kk], t2_t[kk]], writes=[t1_t[kk]])
            P.op("dve", (lambda b=b, j=j, kk=kk, bd=bd: V.tensor_tensor(out=t2[kk], in0=banks[bd][:, 0:TE], in1=GT[b][:, 16 + j, :], op=ALU.mult)),
                 reads=[bank_tok[bd], in_t[b]], writes=[t2_t[kk]])
            P.op("dve", (lambda b=b, j=j, kk=kk: V.tensor_tensor(out=M[b][:, j, :], in0=t1[kk], in1=t2[kk], op=ALU.add)),
                 reads=[t1_t[kk], t2_t[kk]], writes=[M_t[b]])
        for s_ in range(2):
            for hf in range(2):
                bo = 6 + (s_ * 2 + hf) % 2
                for c in range(8):
                    P.op("pe", _mm(nc, banks[bo][:, :], M[b][:, c, s_ * 128:(s_ + 1) * 128], wou[:, c, hf * 512:(hf + 1) * 512], c == 0, c == 7),
                         reads=[w_t, M_t[b]], writes=[bank_tok[bo]])
                P.op("dve", (lambda b=b, s_=s_, hf=hf, bo=bo: V.tensor_tensor(out=OUT[b][:, s_, hf * 512:(hf + 1) * 512], in0=banks[bo][:, :],
                                                                            in1=XT[b][:, s_, hf * 512:(hf + 1) * 512], op=ALU.add)),
                     reads=[bank_tok[bo], in_t[b]], writes=[out_t[b]])
        P.dma("sp", (lambda b=b, e=e: nc.sync.dma_start(out=dst[e * TE:(e + 1) * TE, :].rearrange("(j p) d -> p j d", p=128), in_=OUT[b])),
              reads=[out_t[b]], is_output=is_final)
```

```python
import math
import numpy as np
import ml_dtypes
import concourse.bass as bass
import concourse.mybir as mybir
from concourse.bass_utils import run_bass_kernel_spmd

F32 = mybir.dt.float32
BF16 = mybir.dt.bfloat16
I32 = mybir.dt.int32
AF = mybir.ActivationFunctionType
ALU = mybir.AluOpType

S = 4096
D = 1024
DEPTH = 2
INW = 11168
EPS = 1e-6
C_LX, C_LG, C_CQ, C_CKV, C_KR, C_MG, C_DQ, C_DK, C_DV, C_DG, C_MRG = (
    0, 1024, 2048, 2304, 2432, 2464, 2976, 4512, 6048, 7584, 8096)
DIL = (1, 4, 16)
NDMA_SLOTS = 12

PV_NORMG = 0
PV_CONVW = 8
PV_CONVB = 40
PV_BGX = 48
PV_BGA = 56
PV_LAM = 64
PV_CQG = 72
PV_CKVG = 74
PV_MQG = 75
PV_MKG = 76
PV_DQKG = 77
PV_BMRG = 78
NPV = 102


class Tok:
    __slots__ = ("w", "r", "excl")

    def __init__(self, excl=False):
        self.w = None
        self.r = []
        self.excl = excl


class Prog:
    ENG = ("pe", "act", "dve", "pool", "sp")

    def __init__(self, nc):
        self.nc = nc
        self.h = {"pe": nc.tensor, "act": nc.scalar, "dve": nc.vector, "pool": nc.gpsimd, "sp": nc.sync}
        self.ops = []
        self.seq = {e: 0 for e in self.ENG}
        self.wm = {e: {} for e in self.ENG}
        self.dma_rr = {"sp": 0, "pool": 0, "act": 0}
        self.dma_cnt = {}
        self.out_events = []

    def _deps(self, reads, writes, eng=None):
        deps = []
        for t in reads:
            if t.w is not None:
                deps.append(t.w)
            if t.excl:
                deps.extend(r for r in t.r if r[0] != eng)
        for t in writes:
            if t.w is not None:
                deps.append(t.w)
            deps.extend(t.r)
        return deps

    def _mk_waits(self, eng, deps, skip_same_pe=True):
        best = {}
        for ev in deps:
            key, val = ev
            if key == eng and eng == "pe":
                continue
            if val > best.get(key, 0):
                best[key] = val
        waits = []
        wm = self.wm[eng]
        for key, val in best.items():
            if wm.get(key, 0) >= val:
                continue
            wm[key] = val
            waits.append((key, val))
        return waits

    def op(self, eng, fn, reads=(), writes=()):
        deps = self._deps(reads, writes, eng)
        waits = self._mk_waits(eng, deps)
        self.seq[eng] += 1
        ev = (eng, self.seq[eng])
        self.ops.append(["c", eng, fn, waits, ev, False])
        for t in reads:
            if t.excl:
                t.r = [ev]
            else:
                t.r.append(ev)
        for t in writes:
            t.w = ev
            t.r = []
        return ev

    def dma(self, q, fn, reads=(), writes=(), is_output=False):
        deps = self._deps(reads, writes)
        slot = (q, self.dma_rr[q] % NDMA_SLOTS)
        self.dma_rr[q] += 1
        cnt = self.dma_cnt.get(slot, 0)
        if cnt > 0:
            deps.append((slot, cnt))
        waits = self._mk_waits(q, deps)
        self.dma_cnt[slot] = cnt + 1
        ev = (slot, cnt + 1)
        self.ops.append(["d", q, fn, waits, ev, True])
        for t in reads:
            t.r.append(ev)
        for t in writes:
            t.w = ev
            t.r = []
        if is_output:
            self.out_events.append(ev)
        return ev

    def barrier(self):
        deps = [(e, self.seq[e]) for e in ("pe", "act", "dve", "pool") if self.seq[e] > 0]
        deps += [(slot, cnt) for slot, cnt in self.dma_cnt.items()]
        waits = self._mk_waits("sp", deps)
        self.seq["sp"] += 1
        ev = ("sp", self.seq["sp"])
        self.ops.append(["c", "sp", lambda: self.nc.sync.nop(), waits, ev, False])
        for e in ("pe", "act", "dve", "pool"):
            w = self._mk_waits(e, [ev])
            self.ops.append(["w", e, None, w, None, False])

    def finish(self):
        deps = list(self.out_events)
        waits = self._mk_waits("sp", deps)
        self.ops.append(["w", "sp", None, waits, None, False])

    def emit(self, sems, dma_sems):
        targets = {e: set() for e in self.ENG}
        for o in self.ops:
            for key, val in o[3]:
                if isinstance(key, str):
                    targets[key].add(val)
        valmap = {}
        for e in self.ENG:
            c = 0
            m = {}
            for s in sorted(targets[e]):
                c += 1
                m[s] = c
            valmap[e] = m
        for kind, eng, fn, waits, ev, _ in self.ops:
            hnd = self.h[eng]
            for key, val in waits:
                if isinstance(key, str):
                    hnd.wait_ge(sems[key], valmap[key][val])
                else:
                    hnd.wait_ge(dma_sems[key], 16 * val)
            if kind == "w":
                continue
            ins = fn()
            if kind == "d":
                ins.then_inc(dma_sems[ev[0]], 16)
            else:
                if ev[1] in targets[eng]:
                    ins.then_inc(sems[eng], 1)


class Arena:
    def __init__(self, ap, nbytes):
        self.ap = ap
        self.nbytes = nbytes
        self.off = 0

    def alloc(self, shape, dt):
        esz = 4 if dt in (F32, I32) else 2
        n = 1
        for s_ in shape:
            n *= s_
        nb = (n * esz + 63) // 64 * 64
        assert self.off + nb <= self.nbytes, ("arena overflow", self.off, nb, self.nbytes)
        a = self.ap[:, self.off // 2:(self.off + n * esz) // 2]
        self.off += nb
        if dt != BF16:
            a = a.bitcast(dt)
        if len(shape) == 2:
            a = a.rearrange("p (a b) -> p a b", a=shape[0])
        elif len(shape) == 3:
            a = a.rearrange("p (a b c) -> p a b c", a=shape[0], b=shape[1])
        return a

    def mark(self):
        return self.off

    def release(self, m):
        self.off = m


def toks(n):
    return [Tok() for _ in range(n)]


def build_program(n_layers=DEPTH, stages=None, debug=False, extra_barriers=0, stop=None):
    nc = bass.Bass("TRN2", target_bir_lowering=False)
    P = Prog(nc)
    dram_in = {}

    def din(name, shape, dt=F32):
        dram_in[name] = nc.dram_tensor(name, list(shape), dt, kind="ExternalInput").ap()
        return dram_in[name]

    x_in = din("x", [S, D])
    pos_in = din("pos", [1, S], I32)
    w_in = din("w_in", [DEPTH, D, INW])
    w_gx = din("w_gate_x", [DEPTH, 8, 128, 128])
    w_ga = din("w_gate_a", [DEPTH, 8, 128, 128])
    w_lru_o = din("w_lru_o", [DEPTH, 1024, 1024])
    w_uq = din("w_uq", [DEPTH, 256, 768])
    w_ukv = din("w_ukv", [DEPTH, 128, 1024])
    w_mla_o = din("w_mla_o", [DEPTH, 512, 1024])
    w_dil_o = din("w_dil_o", [DEPTH, 512, 1024])
    w_out = din("w_out", [DEPTH, 1024, 1024])
    pvec_in = din("pvec", [DEPTH, 128, NPV])
    cmat_in = din("cmat", [8, 128, 128])
    cmask_in = din("cmask", [128, 4 * 512 + 256])
    cinv_in = din("cinv", [128, 2])
    out_d = nc.dram_tensor("out", [S, D], F32, kind="ExternalOutput").ap()
    sk = "ExternalOutput" if debug else "Internal"
    xres = nc.dram_tensor("xres", [S, D], F32, kind=sk).ap()
    ylru = nc.dram_tensor("ylru", [1024, S], BF16, kind=sk).ap()
    ymla = nc.dram_tensor("ymla", [512, S], BF16, kind=sk).ap()
    ydil = nc.dram_tensor("ydil", [512, S], BF16, kind=sk).ap()
    gts = nc.dram_tensor("gts", [3072, S], BF16, kind=sk).ap()
    dbg = {}
    if debug:
        dbg["hT"] = nc.dram_tensor("hT_dbg", [128, 8 * S], BF16, kind="ExternalOutput").ap()

    ARENA_BYTES = 204 * 1024
    import contextlib
    with contextlib.ExitStack() as es:
        arena_t = es.enter_context(nc.sbuf_tensor("arena", [128, ARENA_BYTES // 2], BF16))
        banks = [es.enter_context(nc.psum_tensor(f"ps{i}", [128, 512], F32)) for i in range(8)]
        sems = {e: es.enter_context(nc.semaphore(f"s_{e}")) for e in Prog.ENG}
        dma_sems = {}
        for q in ("sp", "pool"):
            for i in range(NDMA_SLOTS):
                dma_sems[(q, i)] = es.enter_context(nc.semaphore(f"d_{q}{i}"))
        es.enter_context(nc.Block())
        es.enter_context(nc.allow_low_precision("bf16 PE transposes / bf16 matmul operands with fp32 accumulation"))
        AR = Arena(arena_t, ARENA_BYTES)
        bank_tok = [Tok(excl=True) for _ in range(8)]
        _emit_all(nc, P, AR, banks, bank_tok, locals(), n_layers, dbg)
        P.finish()
        P.emit(sems, dma_sems)
    return nc


def _emit_all(nc, P, AR, banks, bank_tok, env, n_layers, dbg):
    x_in = env["x_in"]; pos_in = env["pos_in"]; w_in = env["w_in"]
    out_d = env["out_d"]; xres = env["xres"]
    pvec_in = env["pvec_in"]; cmat_in = env["cmat_in"]; cmask_in = env["cmask_in"]; cinv_in = env["cinv_in"]
    V, A_, T_, G_ = nc.vector, nc.scalar, nc.tensor, nc.gpsimd

    cm = AR.alloc([8, 128], BF16)
    cm_t = Tok()
    P.dma("pool", lambda: G_.dma_start(out=cm, in_=cmat_in.rearrange("m p c -> p m c")), writes=[cm_t])
    IDENT, ONES256, ONES128, ONESB64, ONES96, RD_, RM_, EPAD = [cm[:, i, :] for i in range(8)]
    cmask = AR.alloc([4 * 512 + 256], BF16)
    cmask_t = Tok()
    P.dma("pool", lambda: G_.dma_start(out=cmask, in_=cmask_in), writes=[cmask_t])
    cinv = AR.alloc([2], F32)
    cinv_t = Tok()
    P.dma("sp", lambda: nc.sync.dma_start(out=cinv, in_=cinv_in), writes=[cinv_t])
    pv = AR.alloc([DEPTH, NPV], F32)
    pv_t = Tok()
    P.dma("sp", lambda: nc.sync.dma_start(out=pv, in_=pvec_in.rearrange("l p c -> p l c")), writes=[pv_t])
    hbgx = AR.alloc([DEPTH, 8], F32)
    hbga = AR.alloc([DEPTH, 8], F32)
    hc = AR.alloc([DEPTH, 8], F32)
    nhc = AR.alloc([DEPTH, 8], F32)
    hbm = AR.alloc([DEPTH, 24], F32)
    tmpv = AR.alloc([DEPTH, 8], F32)
    der_t = Tok()
    P.op("dve", lambda: V.tensor_scalar_mul(hbgx, pv[:, :, PV_BGX:PV_BGX + 8], -1.0), reads=[pv_t], writes=[der_t])
    P.op("dve", lambda: V.tensor_scalar_mul(hbga, pv[:, :, PV_BGA:PV_BGA + 8], -1.0), reads=[pv_t], writes=[der_t])
    P.op("dve", lambda: V.tensor_scalar_mul(hbm, pv[:, :, PV_BMRG:PV_BMRG + 24], -1.0), reads=[pv_t], writes=[der_t])
    P.op("act", lambda: A_.activation(out=tmpv, in_=pv[:, :, PV_LAM:PV_LAM + 8], func=AF.Exp, scale=-1.0),
         reads=[pv_t], writes=[der_t])
    P.op("act", lambda: A_.activation(out=tmpv, in_=tmpv, func=AF.Ln, bias=1.0, scale=1.0),
         reads=[der_t], writes=[der_t])
    P.op("dve", lambda: V.tensor_scalar_mul(hc, tmpv, -8.0), reads=[der_t], writes=[der_t])

    perm_mark = AR.mark()
    ctx = dict(nc=nc, P=P, AR=AR, banks=banks, bank_tok=bank_tok, env=env, dbg=dbg,
               cm=cm, cm_t=cm_t, cmask=cmask, cmask_t=cmask_t, cinv=cinv, cinv_t=cinv_t,
               pv=pv, pv_t=pv_t, der_t=der_t, hbgx=hbgx, hbga=hbga, hc=hc, nhc=nhc, hbm=hbm)
    for l in range(n_layers):
        src = x_in if l == 0 else xres
        dst = out_d if l == n_layers - 1 else xres
        AR.release(perm_mark)
        P.barrier()
        hT = AR.alloc([8, S], BF16)
        hT_t = toks(8)
        layer_mark = AR.mark()
        phase_norm(ctx, l, src, hT, hT_t)
        if dbg and l == 0:
            P.dma("sp", lambda: nc.sync.dma_start(out=dbg["hT"].rearrange("p (c t) -> p c t", c=8), in_=hT), reads=hT_t)
        stages = env.get("stages") or ("gates", "lru", "mla", "dil", "out")
        if "gates" in stages:
            P.barrier(); AR.release(layer_mark)
            phase_gates(ctx, l, hT, hT_t)
        if "lru" in stages:
            P.barrier(); AR.release(layer_mark)
            phase_lru(ctx, l, hT, hT_t)
        if "mla" in stages:
            P.barrier(); AR.release(layer_mark)
            phase_mla(ctx, l, hT, hT_t)
        if "dil" in stages:
            P.barrier(); AR.release(layer_mark)
            phase_dil(ctx, l, hT, hT_t)
        if "out" in stages:
            P.barrier(); AR.release(perm_mark)
            phase_out(ctx, l, src, dst)
    for _ in range(ctx["env"].get("extra_barriers", 0)):
        P.barrier()
    P.barrier()


def _mm(nc, out, lhsT, rhs, start, stop):
    return lambda: nc.tensor.matmul(out, lhsT, rhs, start=start, stop=stop)


def proj_group(ctx, bank_i, wtile, w_t, c0, m, hT, hT_t, tt, rows=None):
    nc, P = ctx["nc"], ctx["P"]
    bank = ctx["banks"][bank_i]
    bt = ctx["bank_tok"][bank_i]
    for k in range(8):
        P.op("pe", _mm(nc, bank[0:m, :], wtile[:, k, c0:c0 + m], hT[:, k, tt * 512:(tt + 1) * 512], k == 0, k == 7),
             reads=[w_t, hT_t[tt]], writes=[bt])


def phase_norm(ctx, l, src, hT, hT_t):
    nc, P, AR = ctx["nc"], ctx["P"], ctx["AR"]
    V, A_, T_ = nc.vector, nc.scalar, nc.tensor
    banks, bank_tok = ctx["banks"], ctx["bank_tok"]
    pv, pv_t = ctx["pv"], ctx["pv_t"]
    IDENT = ctx["cm"][:, 0, :]
    xt = [AR.alloc([4, D], F32) for _ in range(2)]
    xt_t = toks(2)
    xn = [AR.alloc([D], BF16) for _ in range(2)]
    xn_t = toks(2)
    ss = [AR.alloc([4], F32) for _ in range(2)]
    ss_t = toks(2)
    junk = AR.alloc([D], BF16)
    gB = pv[:, l, PV_NORMG:PV_NORMG + 8].unsqueeze(2).to_broadcast([128, 8, 128])
    for g4 in range(8):
        b = g4 % 2
        P.dma("sp", (lambda b=b, g4=g4: nc.sync.dma_start(
            out=xt[b], in_=src[g4 * 512:(g4 + 1) * 512, :].rearrange("(j p) d -> p j d", p=128))),
            writes=[xt_t[b]])
        P.op("dve", (lambda b=b: V.memset(ss[b], 0.0)), writes=[ss_t[b]])
        for j in range(4):
            P.op("act", (lambda b=b, j=j: A_.activation(out=junk, in_=xt[b][:, j, :], func=AF.Square,
                                                         accum_out=ss[b][:, j:j + 1])),
                 reads=[xt_t[b]], writes=[ss_t[b]])
        P.op("act", (lambda b=b: A_.activation(out=ss[b], in_=ss[b], func=AF.Ln, bias=EPS, scale=1.0 / D)),
             reads=[ss_t[b]], writes=[ss_t[b]])
        P.op("act", (lambda b=b: A_.activation(out=ss[b], in_=ss[b], func=AF.Exp, scale=-0.5)),
             reads=[ss_t[b]], writes=[ss_t[b]])
        for j in range(4):
            kb = (g4 * 4 + j) % 2
            P.op("dve", (lambda b=b, j=j, kb=kb: V.tensor_scalar(xn[kb], xt[b][:, j, :], ss[b][:, j:j + 1], None, ALU.mult)),
                 reads=[xt_t[b], ss_t[b]], writes=[xn_t[kb]])
            bbf = banks[kb][:, :].bitcast(BF16)
            for c in range(8):
                P.op("pe", (lambda kb=kb, c=c, bbf=bbf: T_.transpose(bbf[:, c * 128:(c + 1) * 128],
                                                                     xn[kb][:, c * 128:(c + 1) * 128], IDENT)),
                     reads=[xn_t[kb], ctx["cm_t"]], writes=[bank_tok[kb]])
            t0 = g4 * 512 + j * 128
            P.op("dve", (lambda bbf=bbf, t0=t0: V.tensor_tensor(
                out=hT[:, :, t0:t0 + 128], in0=bbf.rearrange("p (c t) -> p c t", c=8), in1=gB, op=ALU.mult)),
                reads=[bank_tok[kb], pv_t], writes=[hT_t[g4]])


def phase_lru(ctx, l, hT, hT_t):
    nc, P, AR = ctx["nc"], ctx["P"], ctx["AR"]
    V, A_, T_, G_ = nc.vector, nc.scalar, nc.tensor, nc.gpsimd
    env = ctx["env"]
    banks, bank_tok = ctx["banks"], ctx["bank_tok"]
    pv, pv_t, der_t = ctx["pv"], ctx["pv_t"], ctx["der_t"]
    w_in, ylru = env["w_in"], env["ylru"]
    wl = [AR.alloc([8, 256], BF16) for _ in range(2)]
    wl_t = toks(2)
    wgx = AR.alloc([8, 128], BF16)
    wga = AR.alloc([8, 128], BF16)
    wg_t = Tok()
    P.dma("pool", lambda: G_.dma_start(out=wgx, in_=env["w_gx"][l].rearrange("n c d -> c n d")), writes=[wg_t])
    P.dma("pool", lambda: G_.dma_start(out=wga, in_=env["w_ga"][l].rearrange("n c d -> c n d")), writes=[wg_t])
    X = AR.alloc([3 + S], F32)
    X_t = toks(8)
    Xh_t = Tok()
    names = ("G", "XC", "TGX", "TGA", "A", "T", "A2", "H")
    NSET = 4
    buf = [{nm: AR.alloc([512], F32) for nm in names} for _ in range(NSET)]
    for p_ in range(NSET):
        buf[p_]["XCb"] = AR.alloc([512], BF16)
        buf[p_]["Yb"] = AR.alloc([512], BF16)
    bt = [{nm: Tok() for nm in buf[0]} for _ in range(NSET)]
    P.op("dve", lambda: V.memset(X[:, 0:3], 0.0), writes=[Xh_t])
    w_r = w_in[l].rearrange("(c p) n -> p c n", p=128)
    it = 0
    for n in range(8):
        wb = n % 2
        P.dma("pool", (lambda wb=wb, n=n: G_.dma_start(out=wl[wb][:, :, 0:128], in_=w_r[:, :, C_LX + n * 128:C_LX + (n + 1) * 128])),
              writes=[wl_t[wb]])
        P.dma("pool", (lambda wb=wb, n=n: G_.dma_start(out=wl[wb][:, :, 128:256], in_=w_r[:, :, C_LG + n * 128:C_LG + (n + 1) * 128])),
              writes=[wl_t[wb]])
        cw = lambda k, n=n: pv[:, l, PV_CONVW + n * 4 + k:PV_CONVW + n * 4 + k + 1]
        cb = pv[:, l, PV_CONVB + n:PV_CONVB + n + 1]
        hbx = ctx["hbgx"][:, l, n:n + 1]
        hba = ctx["hbga"][:, l, n:n + 1]
        hcn = ctx["hc"][:, l, n:n + 1]
        nhcn = ctx["nhc"][:, l, n:n + 1]
        for tt in range(8):
            par = it % NSET
            pb = it % 2
            it += 1
            B_, Bt = buf[par], bt[par]
            c0 = 3 + tt * 512
            bx, bg, bgx, bga = pb, 2 + pb, 4 + pb, 6 + pb
            proj_group(ctx, bx, wl[wb], wl_t[wb], 0, 128, hT, hT_t, tt)
            P.op("act", (lambda bx=bx, c0=c0: A_.copy(X[:, c0:c0 + 512], banks[bx][:, :])),
                 reads=[bank_tok[bx]], writes=[X_t[tt]])
            proj_group(ctx, bg, wl[wb], wl_t[wb], 128, 128, hT, hT_t, tt)
            P.op("act", (lambda bg=bg, B_=B_: A_.copy(B_["G"], banks[bg][:, :])),
                 reads=[bank_tok[bg]], writes=[Bt["G"]])
            xprev = Xh_t if tt == 0 else X_t[tt - 1]
            P.op("dve", (lambda B_=B_, c0=c0, cw=cw, cb=cb: V.tensor_scalar(B_["XC"], X[:, c0:c0 + 512], cw(3), cb, ALU.mult, ALU.add)),
                 reads=[X_t[tt], pv_t], writes=[Bt["XC"]])
            for k in (2, 1, 0):
                P.op("dve", (lambda B_=B_, c0=c0, cw=cw, k=k: V.scalar_tensor_tensor(
                    B_["XC"], X[:, c0 - 3 + k:c0 - 3 + k + 512], cw(k), B_["XC"], ALU.mult, ALU.add)),
                    reads=[X_t[tt], xprev, Bt["XC"]], writes=[Bt["XC"]])
            P.op("act", (lambda B_=B_: A_.copy(B_["XCb"], B_["XC"])), reads=[Bt["XC"]], writes=[Bt["XCb"]])
            P.op("pe", _mm(nc, banks[bgx][:, :], wgx[:, n, :], B_["XCb"], True, True),
                 reads=[wg_t, Bt["XCb"]], writes=[bank_tok[bgx]])
            P.op("pe", _mm(nc, banks[bga][:, :], wga[:, n, :], B_["XCb"], True, True),
                 reads=[wg_t, Bt["XCb"]], writes=[bank_tok[bga]])
            for nm, bk, hb_ in (("TGX", bgx, hbx), ("TGA", bga, hba)):
                P.op("act", (lambda B_=B_, nm=nm, bk=bk, hb_=hb_: A_.activation(out=B_[nm], in_=banks[bk][:, :], func=AF.Exp, bias=hb_, scale=-1.0)),
                     reads=[bank_tok[bk], der_t], writes=[Bt[nm]])
                P.op("act", (lambda B_=B_, nm=nm: A_.activation(out=B_[nm], in_=B_[nm], func=AF.Ln, bias=1.0)), reads=[Bt[nm]], writes=[Bt[nm]])
                P.op("act", (lambda B_=B_, nm=nm: A_.activation(out=B_[nm], in_=B_[nm], func=AF.Exp, scale=-1.0)), reads=[Bt[nm]], writes=[Bt[nm]])
            P.op("act", (lambda B_=B_, hcn=hcn: A_.activation(out=B_["A"], in_=B_["TGA"], func=AF.Exp, scale=hcn)),
                 reads=[Bt["TGA"], der_t], writes=[Bt["A"]])
            P.op("act", (lambda B_=B_: A_.activation(out=B_["A2"], in_=B_["A"], func=AF.Square)),
                 reads=[Bt["A"]], writes=[Bt["A2"]])
            P.op("dve", (lambda B_=B_: V.tensor_scalar(B_["T"], B_["A2"], -1.0, 1.0, ALU.mult, ALU.add)),
                 reads=[Bt["A2"]], writes=[Bt["T"]])
            P.op("act", (lambda B_=B_: A_.activation(out=B_["T"], in_=B_["T"], func=AF.Ln)),
                 reads=[Bt["T"]], writes=[Bt["T"]])
            P.op("act", (lambda B_=B_: A_.activation(out=B_["T"], in_=B_["T"], func=AF.Exp, scale=0.5)),
                 reads=[Bt["T"]], writes=[Bt["T"]])
            P.op("dve", (lambda B_=B_: V.tensor_tensor(out=B_["TGX"], in0=B_["TGX"], in1=B_["XC"], op=ALU.mult)),
                 reads=[Bt["TGX"], Bt["XC"]], writes=[Bt["TGX"]])
            P.op("dve", (lambda B_=B_: V.tensor_tensor(out=B_["T"], in0=B_["T"], in1=B_["TGX"], op=ALU.mult)),
                 reads=[Bt["T"], Bt["TGX"]], writes=[Bt["T"]])
            if tt == 0:
                P.op("dve", (lambda B_=B_: V.tensor_tensor_scan(B_["H"], B_["A"], B_["T"], 0.0, ALU.mult, ALU.add)),
                     reads=[Bt["A"], Bt["T"]], writes=[Bt["H"]])
            else:
                Bp, Bpt = buf[(par - 1) % NSET], bt[(par - 1) % NSET]
                P.op("dve", (lambda B_=B_, Bp=Bp: V.tensor_tensor_scan(B_["H"], B_["A"], B_["T"], Bp["H"][:, 511:512], ALU.mult, ALU.add)),
                     reads=[Bt["A"], Bt["T"], Bpt["H"]], writes=[Bt["H"]])
            P.op("act", (lambda B_=B_: A_.activation(out=B_["A2"], in_=B_["G"], func=AF.Exp, scale=-1.0)),
                 reads=[Bt["G"], Bt["A2"]], writes=[Bt["A2"]])
            P.op("act", (lambda B_=B_: A_.activation(out=B_["A2"], in_=B_["A2"], func=AF.Ln, bias=1.0)), reads=[Bt["A2"]], writes=[Bt["A2"]])
            P.op("act", (lambda B_=B_: A_.activation(out=B_["A2"], in_=B_["A2"], func=AF.Exp, scale=-1.0)), reads=[Bt["A2"]], writes=[Bt["A2"]])
            P.op("dve", (lambda B_=B_: V.tensor_tensor(out=B_["A2"], in0=B_["A2"], in1=B_["G"], op=ALU.mult)),
                 reads=[Bt["A2"], Bt["G"]], writes=[Bt["A2"]])
            P.op("dve", (lambda B_=B_: V.tensor_tensor(out=B_["Yb"], in0=B_["H"], in1=B_["A2"], op=ALU.mult)),
                 reads=[Bt["H"], Bt["A2"]], writes=[Bt["Yb"]])
            P.dma("sp", (lambda B_=B_, n=n, tt=tt: nc.sync.dma_start(out=ylru[n * 128:(n + 1) * 128, tt * 512:(tt + 1) * 512], in_=B_["Yb"])),
                  reads=[Bt["Yb"]])


def phase_gates(ctx, l, hT, hT_t):
    nc, P, AR = ctx["nc"], ctx["P"], ctx["AR"]
    A_, G_ = nc.scalar, nc.gpsimd
    env = ctx["env"]
    banks, bank_tok = ctx["banks"], ctx["bank_tok"]
    w_r = env["w_in"][l].rearrange("(c p) n -> p c n", p=128)
    gts = env["gts"]
    wt = [AR.alloc([8, 128], BF16) for _ in range(2)]
    wt_t = toks(2)
    st = [AR.alloc([S], BF16) for _ in range(2)]
    st_t = toks(2)
    ebuf = [AR.alloc([512], F32) for _ in range(2)]
    ebuf_t = toks(2)
    for ch in range(24):
        b = ch % 2
        P.dma("pool", (lambda b=b, ch=ch: G_.dma_start(out=wt[b], in_=w_r[:, :, C_MRG + ch * 128:C_MRG + (ch + 1) * 128])),
              writes=[wt_t[b]])
        hb = ctx["hbm"][:, l, ch:ch + 1]
        for tt in range(8):
            bi = (ch * 8 + tt) % 4
            proj_group(ctx, bi, wt[b], wt_t[b], 0, 128, hT, hT_t, tt)
            eb = (ch * 8 + tt) % 2
            P.op("act", (lambda bi=bi, eb=eb, hb=hb: A_.activation(out=ebuf[eb], in_=banks[bi][:, :], func=AF.Exp, bias=hb, scale=-1.0)),
                 reads=[bank_tok[bi], ctx["der_t"]], writes=[ebuf_t[eb]])
            P.op("act", (lambda eb=eb: A_.activation(out=ebuf[eb], in_=ebuf[eb], func=AF.Ln, bias=1.0)), reads=[ebuf_t[eb]], writes=[ebuf_t[eb]])
            P.op("act", (lambda b=b, eb=eb, tt=tt: A_.activation(out=st[b][:, tt * 512:(tt + 1) * 512], in_=ebuf[eb], func=AF.Exp, scale=-1.0)),
                 reads=[ebuf_t[eb]], writes=[st_t[b]])
        P.dma("sp", (lambda b=b, ch=ch: nc.sync.dma_start(out=gts[ch * 128:(ch + 1) * 128, :], in_=st[b])),
              reads=[st_t[b]])


def _host_consts():
    cmat = np.zeros((8, 128, 128), np.float32)
    cmat[0] = np.eye(128, dtype=np.float32)
    cmat[1] = 1.0 / 256
    cmat[2] = 1.0 / 128
    cmat[3, 0:64, 0:64] = 1.0 / 64
    cmat[3, 64:128, 64:128] = 1.0 / 64
    cmat[4, 0:96, 0:96] = 1.0
    for blk in (0, 64):
        for i in range(32):
            cmat[5, blk + i + 32, blk + i] = -1.0
            cmat[5, blk + i, blk + i + 32] = 1.0
    for i in range(16):
        cmat[6, 64 + i + 16, 64 + i] = -1.0
        cmat[6, 64 + i, 64 + i + 16] = 1.0
    for k in range(32):
        cmat[7, k, 64 + k] = 1.0
    cmask = np.zeros((128, 4 * 512 + 256), np.float32)
    kk = np.arange(128)[:, None]
    qq = np.arange(512)[None, :]
    for j in range(4):
        cmask[:, j * 512:(j + 1) * 512] = (qq >= 128 * j + kk)
    q1 = np.arange(128)[None, :]
    cmask[:, 2048:2048 + 128] = (kk >= q1)
    cmask[:, 2048 + 128:2048 + 256] = (kk <= q1)
    cinv = np.zeros((128, 2), np.float32)
    theta = np.float32(10000.0)
    inv64 = theta ** (-np.arange(0, 64, 2, dtype=np.float32) / np.float32(64))
    inv32 = theta ** (-np.arange(0, 32, 2, dtype=np.float32) / np.float32(32))
    for p in range(128):
        cinv[p, 0] = inv64[p % 32]
        if 64 <= p < 96:
            cinv[p, 1] = inv32[(p - 64) % 16]
    return cmat, cmask, cinv


def _host_pvec(inp):
    pv = np.zeros((DEPTH, 128, NPV), np.float32)
    for l in range(DEPTH):
        pv[l, :, PV_NORMG:PV_NORMG + 8] = inp["norm_g"][l].reshape(8, 128).T
        pv[l, :, PV_CONVW:PV_CONVW + 32] = inp["conv_w"][l].reshape(4, 8, 128).transpose(2, 1, 0).reshape(128, 32)
        pv[l, :, PV_CONVB:PV_CONVB + 8] = inp["conv_b"][l].reshape(8, 128).T
        pv[l, :, PV_BGX:PV_BGX + 8] = inp["b_gate_x"][l].T
        pv[l, :, PV_BGA:PV_BGA + 8] = inp["b_gate_a"][l].T
        pv[l, :, PV_LAM:PV_LAM + 8] = inp["lru_lambda"][l].reshape(8, 128).T
        pv[l, :, PV_CQG:PV_CQG + 2] = inp["cq_norm_g"][l].reshape(2, 128).T
        pv[l, :, PV_CKVG] = inp["ckv_norm_g"][l]
        pv[l, 0:96, PV_MQG] = inp["mla_q_norm_g"][l]
        pv[l, 0:96, PV_MKG] = inp["mla_k_norm_g"][l]
        pv[l, 0:64, PV_DQKG] = inp["dil_q_norm_g"][l]
        pv[l, 64:128, PV_DQKG] = inp["dil_k_norm_g"][l]
        pv[l, :, PV_BMRG:PV_BMRG + 24] = inp["b_merge"][l].reshape(24, 128).T
    return pv


def make_in_maps(inp, n_cores=8):
    cmat, cmask, cinv = _host_consts()
    pvec = _host_pvec(inp)
    shared = {
        "w_in": np.ascontiguousarray(inp["w_in"]), "w_gate_x": np.ascontiguousarray(inp["w_gate_x"]),
        "w_gate_a": np.ascontiguousarray(inp["w_gate_a"]), "w_lru_o": np.ascontiguousarray(inp["w_lru_o"]),
        "w_uq": np.ascontiguousarray(inp["w_uq"]), "w_ukv": np.ascontiguousarray(inp["w_ukv"]),
        "w_mla_o": np.ascontiguousarray(inp["w_mla_o"]), "w_dil_o": np.ascontiguousarray(inp["w_dil_o"]),
        "w_out": np.ascontiguousarray(inp["w_out"]), "pvec": pvec, "cmat": cmat, "cmask": cmask, "cinv": cinv,
    }
    maps = []
    for b in range(n_cores):
        m = dict(shared)
        m["x"] = np.ascontiguousarray(inp["x"][b])
        m["pos"] = np.ascontiguousarray(inp["positions"][b].reshape(1, S).astype(np.int32))
        maps.append(m)
    return maps


def kernel(**inputs):
    inp = {k: np.asarray(v) for k, v in inputs.items()}
    nc = build_program()
    maps = make_in_maps(inp, 8)
    res = run_bass_kernel_spmd(nc, maps, core_ids=list(range(8)))
    out = np.stack([np.asarray(res.results[b]["out"]).reshape(S, D) for b in range(8)], axis=0)
    return out.astype(np.float32)


def make_tables(ctx, which, CT, ST, tab_t):
    nc, P, AR = ctx["nc"], ctx["P"], ctx["AR"]
    V, A_ = nc.vector, nc.scalar
    pos_in = ctx["env"]["pos_in"]
    inv = ctx["cinv"][:, which:which + 1]
    m = AR.mark()
    posi = [AR.alloc([1024], I32) for _ in range(2)]
    y0 = [AR.alloc([1024], F32) for _ in range(2)]
    yy = [AR.alloc([1024], F32) for _ in range(2)]
    ki = [AR.alloc([1024], I32) for _ in range(2)]
    kf = [AR.alloc([1024], F32) for _ in range(2)]
    tk = [{n_: Tok() for n_ in ("posi", "y0", "yy", "ki", "kf")} for _ in range(2)]
    TWO_PI = 2.0 * math.pi * (1.0 - 1e-6)
    for c in range(4):
        b = c % 2
        t = tk[b]
        P.dma("sp", (lambda b=b, c=c: nc.sync.dma_start(out=posi[b], in_=pos_in[:, c * 1024:(c + 1) * 1024].broadcast_to([128, 1024]))),
              writes=[t["posi"]])
        P.op("dve", (lambda b=b: V.tensor_copy(y0[b], posi[b])), reads=[t["posi"]], writes=[t["y0"]])
        P.op("dve", (lambda b=b: V.tensor_scalar(y0[b], y0[b], inv, None, ALU.mult)), reads=[t["y0"], ctx["cinv_t"]], writes=[t["y0"]])
        P.op("dve", (lambda b=b: V.tensor_scalar(y0[b], y0[b], 1.0 / (2.0 * math.pi), None, ALU.mult)), reads=[t["y0"]], writes=[t["y0"]])
        for ph, OUT in ((0.0, ST), (0.25, CT)):
            P.op("dve", (lambda b=b, ph=ph: V.tensor_scalar(yy[b], y0[b], ph, None, ALU.add)), reads=[t["y0"]], writes=[t["yy"]])
            P.op("dve", (lambda b=b: V.tensor_copy(ki[b], yy[b])), reads=[t["yy"]], writes=[t["ki"]])
            P.op("dve", (lambda b=b: V.tensor_copy(kf[b], ki[b])), reads=[t["ki"]], writes=[t["kf"]])
            P.op("dve", (lambda b=b: V.tensor_tensor(out=yy[b], in0=yy[b], in1=kf[b], op=ALU.subtract)), reads=[t["yy"], t["kf"]], writes=[t["yy"]])
            P.op("dve", (lambda b=b: V.tensor_scalar(kf[b], yy[b], 0.5, None, ALU.is_gt)), reads=[t["yy"]], writes=[t["kf"]])
            P.op("dve", (lambda b=b: V.tensor_tensor(out=yy[b], in0=yy[b], in1=kf[b], op=ALU.subtract)), reads=[t["yy"], t["kf"]], writes=[t["yy"]])
            P.op("act", (lambda b=b, OUT=OUT, c=c: A_.activation(out=OUT[:, c * 1024:(c + 1) * 1024], in_=yy[b], func=AF.Sin, scale=TWO_PI)),
                 reads=[t["yy"]], writes=[tab_t])
    P.barrier()
    AR.release(m)


def norm_rope_chain(ctx, src_bank, np_, ones_m, eps_v, gcol, CT, ST, rot_m, tt, out_ap, out_tok, tmp, tmp_t, bank_ss, bank_rot):
    nc, P = ctx["nc"], ctx["P"]
    V, A_ = nc.vector, nc.scalar
    banks, bank_tok = ctx["banks"], ctx["bank_tok"]
    IDENT = ctx["cm"][:, 0, :]
    src = banks[src_bank][0:np_, :]
    st_ = bank_tok[src_bank]
    cols = slice(tt * 512, (tt + 1) * 512)
    P.op("act", lambda: A_.activation(out=tmp["sq"][0:np_, :], in_=src, func=AF.Square), reads=[st_], writes=[tmp_t["sq"]])
    P.op("pe", _mm(nc, banks[bank_ss][0:np_, :], ones_m[0:np_, 0:np_], tmp["sq"][0:np_, :], True, True),
         reads=[tmp_t["sq"], ctx["cm_t"]], writes=[bank_tok[bank_ss]])
    P.op("act", lambda: A_.activation(out=tmp["rs"][0:np_, :], in_=banks[bank_ss][0:np_, :], func=AF.Ln, bias=eps_v),
         reads=[bank_tok[bank_ss]], writes=[tmp_t["rs"]])
    P.op("act", lambda: A_.activation(out=tmp["rs"][0:np_, :], in_=tmp["rs"][0:np_, :], func=AF.Exp, scale=-0.5),
         reads=[tmp_t["rs"]], writes=[tmp_t["rs"]])
    P.op("dve", lambda: V.scalar_tensor_tensor(tmp["u1"][0:np_, :], src, gcol[0:np_, :], CT[0:np_, cols], ALU.mult, ALU.mult),
         reads=[st_, ctx["pv_t"], tmp_t["tab"]], writes=[tmp_t["u1"]])
    P.op("dve", lambda: V.scalar_tensor_tensor(tmp["u2"][0:np_, :], src, gcol[0:np_, :], ST[0:np_, cols], ALU.mult, ALU.mult),
         reads=[st_, ctx["pv_t"], tmp_t["tab"]], writes=[tmp_t["u2"]])
    P.op("pe", _mm(nc, banks[bank_rot][0:np_, :], IDENT[0:np_, 0:np_], tmp["u1"][0:np_, :], True, False),
         reads=[tmp_t["u1"], ctx["cm_t"]], writes=[bank_tok[bank_rot]])
    P.op("pe", _mm(nc, banks[bank_rot][0:np_, :], rot_m[0:np_, 0:np_], tmp["u2"][0:np_, :], False, True),
         reads=[tmp_t["u2"], ctx["cm_t"]], writes=[bank_tok[bank_rot]])
    P.op("dve", lambda: V.tensor_tensor(out=out_ap, in0=banks[bank_rot][0:np_, :], in1=tmp["rs"][0:np_, :], op=ALU.mult),
         reads=[bank_tok[bank_rot], tmp_t["rs"]], writes=[out_tok])


def chain_group(ctx, chains, CT, ST):
    nc, P = ctx["nc"], ctx["P"]
    V, A_ = nc.vector, nc.scalar
    banks, bank_tok = ctx["banks"], ctx["bank_tok"]
    IDENT = ctx["cm"][:, 0, :]
    for c in chains:
        n_ = c["np"]
        P.op("act", (lambda c=c, n_=n_: A_.activation(out=c["tmp"]["sq"][0:n_, :], in_=banks[c["src"]][0:n_, :], func=AF.Square)),
             reads=[bank_tok[c["src"]]], writes=[c["tmp_t"]["sq"]])
    for c in chains:
        n_ = c["np"]
        P.op("pe", _mm(nc, banks[c["xb"]][0:n_, :], c["ones"][0:n_, 0:n_], c["tmp"]["sq"][0:n_, :], True, True),
             reads=[c["tmp_t"]["sq"], ctx["cm_t"]], writes=[bank_tok[c["xb"]]])
    for c in chains:
        n_ = c["np"]
        P.op("act", (lambda c=c, n_=n_: A_.activation(out=c["tmp"]["rs"][0:n_, :], in_=banks[c["xb"]][0:n_, :], func=AF.Ln, bias=c["eps"])),
             reads=[bank_tok[c["xb"]]], writes=[c["tmp_t"]["rs"]])
    for c in chains:
        n_ = c["np"]
        P.op("act", (lambda c=c, n_=n_: A_.activation(out=c["tmp"]["rs"][0:n_, :], in_=c["tmp"]["rs"][0:n_, :], func=AF.Exp, scale=-0.5)),
             reads=[c["tmp_t"]["rs"]], writes=[c["tmp_t"]["rs"]])
    for c in chains:
        n_ = c["np"]
        cols = slice(c["tt"] * 512, (c["tt"] + 1) * 512)
        P.op("dve", (lambda c=c, n_=n_, cols=cols: V.scalar_tensor_tensor(c["tmp"]["u1"][0:n_, :], banks[c["src"]][0:n_, :], c["g"][0:n_, :],
                                                                          CT[0:n_, cols], ALU.mult, ALU.mult)),
             reads=[bank_tok[c["src"]], ctx["pv_t"], c["tmp_t"]["tab"]], writes=[c["tmp_t"]["u1"]])
        P.op("dve", (lambda c=c, n_=n_, cols=cols: V.scalar_tensor_tensor(c["tmp"]["u2"][0:n_, :], banks[c["src"]][0:n_, :], c["g"][0:n_, :],
                                                                          ST[0:n_, cols], ALU.mult, ALU.mult)),
             reads=[bank_tok[c["src"]], ctx["pv_t"], c["tmp_t"]["tab"]], writes=[c["tmp_t"]["u2"]])
    for c in chains:
        n_ = c["np"]
        P.op("pe", _mm(nc, banks[c["xb"]][0:n_, :], IDENT[0:n_, 0:n_], c["tmp"]["u1"][0:n_, :], True, False),
             reads=[c["tmp_t"]["u1"], ctx["cm_t"]], writes=[bank_tok[c["xb"]]])
        P.op("pe", _mm(nc, banks[c["xb"]][0:n_, :], c["rot"][0:n_, 0:n_], c["tmp"]["u2"][0:n_, :], False, True),
             reads=[c["tmp_t"]["u2"], ctx["cm_t"]], writes=[bank_tok[c["xb"]]])
    for c in chains:
        n_ = c["np"]
        P.op("dve", (lambda c=c, n_=n_: V.tensor_tensor(out=c["out"], in0=banks[c["xb"]][0:n_, :], in1=c["tmp"]["rs"][0:n_, :], op=ALU.mult)),
             reads=[bank_tok[c["xb"]], c["tmp_t"]["rs"]], writes=[c["out_tok"]])


def finalize_head(ctx, o_ap, o_tok, den_ap, den_tok, gate_w, gate_wt, gcol0, hT, hT_t, tt, ydst, tmp, tmp_t):
    nc, P = ctx["nc"], ctx["P"]
    V, A_ = nc.vector, nc.scalar
    banks, bank_tok = ctx["banks"], ctx["bank_tok"]
    proj_group(ctx, 7, gate_w, gate_wt, gcol0, 64, hT, hT_t, tt)
    g_ps = banks[7][0:64, :]
    P.op("act", lambda: A_.activation(out=tmp["e1"][0:64, :], in_=g_ps, func=AF.Exp, scale=-1.0), reads=[bank_tok[7]], writes=[tmp_t["e1"]])
    P.op("dve", lambda: V.tensor_copy(tmp["den0"][0:64, :], den_ap), reads=[den_tok], writes=[tmp_t["den0"]])
    P.op("dve", lambda: V.scalar_tensor_tensor(tmp["e1"][0:64, :], tmp["e1"][0:64, :], 1.0, tmp["den0"][0:64, :], ALU.add, ALU.mult),
         reads=[tmp_t["e1"], tmp_t["den0"]], writes=[tmp_t["e1"]])
    P.op("act", lambda: A_.activation(out=tmp["e1"][0:64, :], in_=tmp["e1"][0:64, :], func=AF.Ln), reads=[tmp_t["e1"]], writes=[tmp_t["e1"]])
    P.op("act", lambda: A_.activation(out=tmp["e1"][0:64, :], in_=tmp["e1"][0:64, :], func=AF.Exp, scale=-1.0), reads=[tmp_t["e1"]], writes=[tmp_t["e1"]])
    P.op("dve", lambda: V.tensor_tensor(out=tmp["e1"][0:64, :], in0=tmp["e1"][0:64, :], in1=g_ps, op=ALU.mult),
         reads=[tmp_t["e1"], bank_tok[7]], writes=[tmp_t["e1"]])
    P.op("dve", lambda: V.tensor_tensor(out=tmp["yb"][0:64, :], in0=o_ap, in1=tmp["e1"][0:64, :], op=ALU.mult),
         reads=[o_tok, tmp_t["e1"]], writes=[tmp_t["yb"]])
    P.dma("sp", lambda: nc.sync.dma_start(out=ydst, in_=tmp["yb"][0:64, :]), reads=[tmp_t["yb"]])


def _chain_tmp(AR):
    tmp = {"sq": AR.alloc([512], BF16), "rs": AR.alloc([512], F32), "u1": AR.alloc([512], BF16), "u2": AR.alloc([512], BF16),
           "e1": AR.alloc([512], F32), "den0": AR.alloc([512], F32), "yb": AR.alloc([512], BF16), "den": AR.alloc([512], F32)}
    tmp_t = {k: Tok() for k in list(tmp) + ["tab"]}
    return tmp, tmp_t


def _chain_tmp2(AR, tmp_t):
    tmp = {"sq": AR.alloc([512], BF16), "rs": AR.alloc([512], F32), "u1": AR.alloc([512], BF16), "u2": AR.alloc([512], BF16)}
    t2 = {k: Tok() for k in tmp}
    t2["tab"] = tmp_t["tab"]
    return tmp, t2


def phase_mla(ctx, l, hT, hT_t):
    nc, P, AR = ctx["nc"], ctx["P"], ctx["AR"]
    V, A_, G_ = nc.vector, nc.scalar, nc.gpsimd
    env = ctx["env"]
    banks, bank_tok = ctx["banks"], ctx["bank_tok"]
    pv, pv_t, cm, cm_t = ctx["pv"], ctx["pv_t"], ctx["cm"], ctx["cm_t"]
    ONES256, ONES128, ONES96, RM_, EPAD = cm[:, 1, :], cm[:, 2, :], cm[:, 4, :], cm[:, 6, :], cm[:, 7, :]
    cmask, cmask_t = ctx["cmask"], ctx["cmask_t"]
    ymla = env["ymla"]
    w_r = env["w_in"][l].rearrange("(c p) n -> p c n", p=128)
    CT = AR.alloc([S], BF16)
    ST = AR.alloc([S], BF16)
    tmp, tmp_t = _chain_tmp(AR)
    make_tables(ctx, 1, CT, ST, tmp_t["tab"])
    if env.get("stop") == "tables":
        return
    wm = AR.alloc([8, 416], BF16)
    wuq = AR.alloc([2, 768], BF16)
    wukv = AR.alloc([1024], BF16)
    wgm = AR.alloc([8, 512], BF16)
    wkp = AR.alloc([8, 96], BF16)
    w_t = Tok()
    wkp_t = Tok()
    P.dma("pool", lambda: G_.dma_start(out=wm, in_=w_r[:, :, C_CQ:C_CQ + 416]), writes=[w_t])
    P.dma("pool", lambda: G_.dma_start(out=wuq, in_=env["w_uq"][l].rearrange("(c p) n -> p c n", p=128)), writes=[w_t])
    P.dma("pool", lambda: G_.dma_start(out=wukv, in_=env["w_ukv"][l]), writes=[w_t])
    P.dma("pool", lambda: G_.dma_start(out=wgm, in_=w_r[:, :, C_MG:C_MG + 512]), writes=[w_t])
    P.op("dve", lambda: V.memset(wkp, 0.0), writes=[wkp_t])
    P.op("dve", lambda: V.tensor_copy(wkp[:, :, 0:64], wukv.rearrange("p (h c) -> p h c", h=8)[:, :, 0:64]), reads=[w_t], writes=[wkp_t])
    CQN = AR.alloc([2, S], BF16)
    CKVN = AR.alloc([S], BF16)
    KR = AR.alloc([S], BF16)
    lat_t = toks(8)
    m1_mark = AR.mark()
    sq3 = [AR.alloc([3, 512], BF16) for _ in range(2)]
    sq3_t = toks(2)
    rs2 = [AR.alloc([2, 512], F32) for _ in range(2)]
    rs2_t = toks(2)
    for tt in range(8):
        b = tt % 2
        cols = slice(tt * 512, (tt + 1) * 512)
        proj_group(ctx, 0, wm, w_t, 0, 128, hT, hT_t, tt)
        proj_group(ctx, 1, wm, w_t, 128, 128, hT, hT_t, tt)
        proj_group(ctx, 2, wm, w_t, 256, 128, hT, hT_t, tt)
        proj_group(ctx, 3, wm, w_t, 384, 32, hT, hT_t, tt)
        for i in range(3):
            P.op("act", (lambda b=b, i=i: A_.activation(out=sq3[b][:, i, :], in_=banks[i][:, :], func=AF.Square)),
                 reads=[bank_tok[i]], writes=[sq3_t[b]])
        P.op("pe", _mm(nc, banks[4][:, :], ONES256, sq3[b][:, 0, :], True, False), reads=[sq3_t[b], cm_t], writes=[bank_tok[4]])
        P.op("pe", _mm(nc, banks[4][:, :], ONES256, sq3[b][:, 1, :], False, True), reads=[sq3_t[b], cm_t], writes=[bank_tok[4]])
        P.op("pe", _mm(nc, banks[5][:, :], ONES128, sq3[b][:, 2, :], True, True), reads=[sq3_t[b], cm_t], writes=[bank_tok[5]])
        for i, bk in ((0, 4), (1, 5)):
            P.op("act", (lambda b=b, i=i, bk=bk: A_.activation(out=rs2[b][:, i, :], in_=banks[bk][:, :], func=AF.Ln, bias=EPS)),
                 reads=[bank_tok[bk]], writes=[rs2_t[b]])
            P.op("act", (lambda b=b, i=i: A_.activation(out=rs2[b][:, i, :], in_=rs2[b][:, i, :], func=AF.Exp, scale=-0.5)),
                 reads=[rs2_t[b]], writes=[rs2_t[b]])
        for c in range(2):
            P.op("dve", (lambda b=b, c=c, cols=cols: V.scalar_tensor_tensor(CQN[:, c, cols], banks[c][:, :], pv[:, l, PV_CQG + c:PV_CQG + c + 1],
                                                                            rs2[b][:, 0, :], ALU.mult, ALU.mult)),
                 reads=[bank_tok[c], rs2_t[b], pv_t], writes=[lat_t[tt]])
        P.op("dve", (lambda b=b, cols=cols: V.scalar_tensor_tensor(CKVN[:, cols], banks[2][:, :], pv[:, l, PV_CKVG:PV_CKVG + 1],
                                                                   rs2[b][:, 1, :], ALU.mult, ALU.mult)),
             reads=[bank_tok[2], rs2_t[b], pv_t], writes=[lat_t[tt]])
        P.op("act", (lambda cols=cols: A_.copy(KR[0:32, cols], banks[3][0:32, :])), reads=[bank_tok[3]], writes=[lat_t[tt]])
    if env.get("stop") == "latent":
        return
    P.barrier()
    AR.release(m1_mark)
    tmpk, tmpk_t = _chain_tmp2(AR, tmp_t)
    QT = AR.alloc([S], BF16)
    KT = AR.alloc([S], BF16)
    QT_t = toks(8)
    KT_t = toks(8)
    VH = AR.alloc([32, 128], BF16)
    VH_t = Tok()
    P.op("dve", lambda: V.memset(VH[:, :, 64:128], 1.0), writes=[VH_t])
    Pb = [AR.alloc([512], BF16) for _ in range(4)]
    Pb_t = toks(4)
    gq = pv[:, l, PV_MQG:PV_MQG + 1]
    gk = pv[:, l, PV_MKG:PV_MKG + 1]
    SC = math.sqrt(96.0)
    cnt = 0
    csets = [(tmp, tmp_t), (tmpk, tmpk_t)] + [_chain_tmp2(AR, tmp_t) for _ in range(2)]
    for h in range(8):
        for t2 in range(0, 8, 2):
            chains = []
            for jj, tt in enumerate((t2, t2 + 1)):
                cols = slice(tt * 512, (tt + 1) * 512)
                for j in range(4):
                    P.op("pe", _mm(nc, banks[7][:, j * 64:(j + 1) * 64], CKVN[:, tt * 512 + j * 128:tt * 512 + (j + 1) * 128],
                                   wukv[:, h * 128 + 64:h * 128 + 128], True, True),
                         reads=[w_t, lat_t[tt]], writes=[bank_tok[7]])
                P.op("act", (lambda tt=tt: A_.copy(VH[:, tt * 4:(tt + 1) * 4, 0:64], banks[7][:, 0:256].rearrange("p (j c) -> p j c", j=4))),
                     reads=[bank_tok[7]], writes=[VH_t])
            for jj, tt in enumerate((t2, t2 + 1)):
                cols = slice(tt * 512, (tt + 1) * 512)
                bq, bk = 2 * jj, 2 * jj + 1
                for c in range(2):
                    P.op("pe", _mm(nc, banks[bq][0:96, :], wuq[:, c, h * 96:(h + 1) * 96], CQN[:, c, cols], c == 0, c == 1),
                         reads=[w_t, lat_t[tt]], writes=[bank_tok[bq]])
                P.op("pe", _mm(nc, banks[bk][0:96, :], wkp[:, h, :], CKVN[:, cols], True, False), reads=[wkp_t, lat_t[tt]], writes=[bank_tok[bk]])
                P.op("pe", _mm(nc, banks[bk][0:96, :], EPAD[0:32, 0:96], KR[0:32, cols], False, True), reads=[cm_t, lat_t[tt]], writes=[bank_tok[bk]])
                chains.append(dict(src=bq, np=96, ones=ONES96, eps=96 * EPS, g=gq, rot=RM_, tt=tt, out=QT[0:96, cols], out_tok=QT_t[tt],
                                   tmp=csets[2 * jj][0], tmp_t=csets[2 * jj][1], xb=4 + 2 * jj))
                chains.append(dict(src=bk, np=96, ones=ONES96, eps=96 * EPS, g=gk, rot=RM_, tt=tt, out=KT[0:96, cols], out_tok=KT_t[tt],
                                   tmp=csets[2 * jj + 1][0], tmp_t=csets[2 * jj + 1][1], xb=5 + 2 * jj))
            chain_group(ctx, chains, CT, ST)
        if env.get("stop") == "prep":
            return
        for qt in range(8):
            if env.get("stop") == "attn1" and qt == 1:
                return
            qcols = slice(qt * 512, (qt + 1) * 512)
            nkb = 4 * qt + 4
            SBK = (0, 1, 2, 3, 4, 5)
            base = cnt
            cnt += nkb
            for i in range(nkb + 3):
                if i < nkb:
                    kb = i
                    sb = SBK[(base + kb) % 6]
                    P.op("pe", _mm(nc, banks[sb][:, :], KT[0:96, kb * 128:(kb + 1) * 128], QT[0:96, qcols], True, True),
                         reads=[KT_t[kb // 4], QT_t[qt]], writes=[bank_tok[sb]])
                if 0 <= i - 2 < nkb:
                    kb = i - 2
                    sb = SBK[(base + kb) % 6]
                    pi = (base + kb) % 4
                    P.op("act", (lambda sb=sb, pi=pi: A_.activation(out=Pb[pi], in_=banks[sb][:, :], func=AF.Exp, scale=SC)),
                         reads=[bank_tok[sb]], writes=[Pb_t[pi]])
                    j = kb - 4 * qt
                    if j >= 0:
                        P.op("dve", (lambda pi=pi, j=j: V.tensor_tensor(out=Pb[pi], in0=Pb[pi], in1=cmask[:, j * 512:(j + 1) * 512], op=ALU.mult)),
                             reads=[Pb_t[pi], cmask_t], writes=[Pb_t[pi]])
                if 0 <= i - 3 < nkb:
                    kb = i - 3
                    pi = (base + kb) % 4
                    P.op("pe", _mm(nc, banks[6][:, :], VH[:, kb, :], Pb[pi], kb == 0, kb == nkb - 1),
                         reads=[VH_t, Pb_t[pi]], writes=[bank_tok[6]])
            P.op("act", lambda: A_.copy(tmp["den"][64:128, :], banks[6][64:128, :]), reads=[bank_tok[6]], writes=[tmp_t["den"]])
            finalize_head(ctx, banks[6][0:64, :], bank_tok[6], tmp["den"][64:128, :], tmp_t["den"], wgm, w_t, h * 64, hT, hT_t, qt,
                          ymla[h * 64:(h + 1) * 64, qcols], tmp, tmp_t)


def phase_dil(ctx, l, hT, hT_t):
    nc, P, AR = ctx["nc"], ctx["P"], ctx["AR"]
    V, A_, G_ = nc.vector, nc.scalar, nc.gpsimd
    env = ctx["env"]
    banks, bank_tok = ctx["banks"], ctx["bank_tok"]
    pv, pv_t, cm, cm_t = ctx["pv"], ctx["pv_t"], ctx["cm"], ctx["cm_t"]
    ONESB64, RD_ = cm[:, 3, :], cm[:, 5, :]
    cmask, cmask_t = ctx["cmask"], ctx["cmask_t"]
    DMASK = cmask[:, 2048:2048 + 256]
    ydil = env["ydil"]
    w_r = env["w_in"][l].rearrange("(c p) n -> p c n", p=128)
    CT = AR.alloc([S], BF16)
    ST = AR.alloc([S], BF16)
    tmp, tmp_t = _chain_tmp(AR)
    make_tables(ctx, 0, CT, ST, tmp_t["tab"])
    csets = [(tmp, tmp_t)] + [_chain_tmp2(AR, tmp_t) for _ in range(3)]
    wqk = [AR.alloc([8, 128], BF16) for _ in range(2)]
    wv = [AR.alloc([8, 64], BF16) for _ in range(2)]
    wgh_t = toks(2)
    wdg = AR.alloc([8, 512], BF16)
    wdg_t = Tok()
    P.dma("pool", lambda: G_.dma_start(out=wdg, in_=w_r[:, :, C_DG:C_DG + 512]), writes=[wdg_t])
    QK = AR.alloc([S], BF16)
    K0 = AR.alloc([S], BF16)
    QK_t = toks(8)
    K0_t = toks(8)
    VD = AR.alloc([32, 128], BF16)
    VD_t = Tok()
    P.op("dve", lambda: V.memset(VD[:, :, 64:128], 1.0), writes=[VD_t])
    OACC = AR.alloc([S], F32)
    OACC_t = Tok()
    Pd = [AR.alloc([256], BF16) for _ in range(4)]
    Pd_t = toks(4)
    gqk = pv[:, l, PV_DQKG:PV_DQKG + 1]
    it = 0
    cnt = 0
    for h in range(8):
        for g in range(3):
            d = DIL[g]
            nb = S // (128 * d)
            wb = it % 2
            it += 1
            hd = g * 8 + h
            P.dma("pool", (lambda wb=wb, hd=hd: G_.dma_start(out=wqk[wb][:, :, 0:64], in_=w_r[:, :, C_DQ + hd * 64:C_DQ + (hd + 1) * 64])), writes=[wgh_t[wb]])
            P.dma("pool", (lambda wb=wb, hd=hd: G_.dma_start(out=wqk[wb][:, :, 64:128], in_=w_r[:, :, C_DK + hd * 64:C_DK + (hd + 1) * 64])), writes=[wgh_t[wb]])
            P.dma("pool", (lambda wb=wb, hd=hd: G_.dma_start(out=wv[wb], in_=w_r[:, :, C_DV + hd * 64:C_DV + (hd + 1) * 64])), writes=[wgh_t[wb]])
            for t4 in range(0, 8, 4):
                chains = []
                for jj in range(4):
                    tt = t4 + jj
                    cols = slice(tt * 512, (tt + 1) * 512)
                    proj_group(ctx, jj, wqk[wb], wgh_t[wb], 0, 128, hT, hT_t, tt)
                    chains.append(dict(src=jj, np=128, ones=ONESB64, eps=EPS, g=gqk, rot=RD_, tt=tt, out=QK[:, cols], out_tok=QK_t[tt],
                                       tmp=csets[jj][0], tmp_t=csets[jj][1], xb=4 + jj))
                chain_group(ctx, chains, CT, ST)
                for jj in range(4):
                    tt = t4 + jj
                    cols = slice(tt * 512, (tt + 1) * 512)
                    P.op("dve", (lambda cols=cols: V.tensor_copy(K0[0:64, cols], QK[64:128, cols])), reads=[QK_t[tt]], writes=[K0_t[tt]])

            def ucols(r, n, d=d):
                st0 = r + d * 128 * n
                return slice(st0, st0 + d * 127 + 1, d)
            units = [(r, n) for r in range(d) for n in range(nb)]
            for u0 in range(0, 32, 4):
                for jj in range(4):
                    r, n = units[u0 + jj]
                    for k in range(8):
                        P.op("pe", _mm(nc, banks[7][:, jj * 64:(jj + 1) * 64], hT[:, k, ucols(r, n)], wv[wb][:, k, :], k == 0, k == 7),
                             reads=[wgh_t[wb]] + hT_t, writes=[bank_tok[7]])
                P.op("act", (lambda u0=u0: A_.copy(VD[:, u0:u0 + 4, 0:64], banks[7][:, 0:256].rearrange("p (j c) -> p j c", j=4))),
                     reads=[bank_tok[7]], writes=[VD_t])
            SBK = (0, 1, 2, 3, 4, 5)
            base = cnt
            cnt += 32
            for i in range(32 + 3):
                if i < 32:
                    u = i
                    r, n = units[u]
                    sb = SBK[(base + u) % 6]
                    qc = ucols(r, n)
                    if n > 0:
                        P.op("pe", _mm(nc, banks[sb][:, 0:128], K0[0:64, ucols(r, n - 1)], QK[0:64, qc], True, True),
                             reads=K0_t + QK_t, writes=[bank_tok[sb]])
                    P.op("pe", _mm(nc, banks[sb][:, 128:256], K0[0:64, qc], QK[0:64, qc], True, True),
                         reads=K0_t + QK_t, writes=[bank_tok[sb]])
                if 0 <= i - 2 < 32:
                    u = i - 2
                    r, n = units[u]
                    sb = SBK[(base + u) % 6]
                    pi = (base + u) % 4
                    lo = 0 if n > 0 else 128
                    P.op("act", (lambda sb=sb, pi=pi, lo=lo: A_.activation(out=Pd[pi][:, lo:256], in_=banks[sb][:, lo:256], func=AF.Exp, scale=0.125)),
                         reads=[bank_tok[sb]], writes=[Pd_t[pi]])
                    P.op("dve", (lambda pi=pi, lo=lo: V.tensor_tensor(out=Pd[pi][:, lo:256], in0=Pd[pi][:, lo:256], in1=DMASK[:, lo:256], op=ALU.mult)),
                         reads=[Pd_t[pi], cmask_t], writes=[Pd_t[pi]])
                if 0 <= i - 3 < 32:
                    u = i - 3
                    r, n = units[u]
                    pi = (base + u) % 4
                    qc = ucols(r, n)
                    ob = 6 + (base + u) % 2
                    if n > 0:
                        P.op("pe", _mm(nc, banks[ob][:, 0:128], VD[:, u - 1, :], Pd[pi][:, 0:128], True, False), reads=[VD_t, Pd_t[pi]], writes=[bank_tok[ob]])
                    P.op("pe", _mm(nc, banks[ob][:, 0:128], VD[:, u, :], Pd[pi][:, 128:256], n == 0, True), reads=[VD_t, Pd_t[pi]], writes=[bank_tok[ob]])
                    if g == 0:
                        P.op("dve", (lambda qc=qc, ob=ob: V.tensor_copy(OACC[:, qc], banks[ob][:, 0:128])), reads=[bank_tok[ob]], writes=[OACC_t])
                    else:
                        P.op("dve", (lambda qc=qc, ob=ob: V.tensor_tensor(out=OACC[:, qc], in0=banks[ob][:, 0:128], in1=OACC[:, qc], op=ALU.add)),
                             reads=[bank_tok[ob], OACC_t], writes=[OACC_t])
        for tt in range(8):
            cols = slice(tt * 512, (tt + 1) * 512)
            finalize_head(ctx, OACC[0:64, cols], OACC_t, OACC[64:128, cols], OACC_t, wdg, wdg_t, h * 64, hT, hT_t, tt,
                          ydil[h * 64:(h + 1) * 64, cols], tmp, tmp_t)


def phase_out(ctx, l, src, dst):
    nc, P, AR = ctx["nc"], ctx["P"], ctx["AR"]
    V, A_, G_ = nc.vector, nc.scalar, nc.gpsimd
    env = ctx["env"]
    banks, bank_tok = ctx["banks"], ctx["bank_tok"]
    TE = 256
    wlo = AR.alloc([8, 1024], BF16)
    wmo = AR.alloc([4, 1024], BF16)
    wdo = AR.alloc([4, 1024], BF16)
    wou = AR.alloc([8, 1024], BF16)
    w_t = Tok()
    for wt_, nm in ((wlo, "w_lru_o"), (wmo, "w_mla_o"), (wdo, "w_dil_o"), (wou, "w_out")):
        P.dma("pool", (lambda wt_=wt_, nm=nm: G_.dma_start(out=wt_, in_=env[nm][l].rearrange("(c p) n -> p c n", p=128))), writes=[w_t])
    YL = [AR.alloc([8, TE], BF16) for _ in range(2)]
    YM = [AR.alloc([4, TE], BF16) for _ in range(2)]
    YD = [AR.alloc([4, TE], BF16) for _ in range(2)]
    GT = [AR.alloc([24, TE], BF16) for _ in range(2)]
    XT = [AR.alloc([2, D], F32) for _ in range(2)]
    OUT = [AR.alloc([2, D], F32) for _ in range(2)]
    M = [AR.alloc([8, TE], BF16) for _ in range(2)]
    t1 = [AR.alloc([TE], F32) for _ in range(2)]
    t2 = [AR.alloc([TE], F32) for _ in range(2)]
    in_t = toks(2)
    out_t = toks(2)
    M_t = toks(2)
    t1_t = toks(2)
    t2_t = toks(2)
    ylru_r = env["ylru"].rearrange("(c p) t -> p c t", p=128)
    ymla_r = env["ymla"].rearrange("(c p) t -> p c t", p=128)
    ydil_r = env["ydil"].rearrange("(c p) t -> p c t", p=128)
    gts_r = env["gts"].rearrange("(c p) t -> p c t", p=128)
    is_final = dst is env["out_d"]
    k = 0
    for e in range(S // TE):
        b = e % 2
        tc_ = slice(e * TE, (e + 1) * TE)
        P.dma("sp", (lambda b=b, tc_=tc_: nc.sync.dma_start(out=YL[b], in_=ylru_r[:, :, tc_])), writes=[in_t[b]])
        P.dma("sp", (lambda b=b, tc_=tc_: nc.sync.dma_start(out=YM[b], in_=ymla_r[:, :, tc_])), writes=[in_t[b]])
        P.dma("sp", (lambda b=b, tc_=tc_: nc.sync.dma_start(out=YD[b], in_=ydil_r[:, :, tc_])), writes=[in_t[b]])
        P.dma("sp", (lambda b=b, tc_=tc_: nc.sync.dma_start(out=GT[b], in_=gts_r[:, :, tc_])), writes=[in_t[b]])
        P.dma("sp", (lambda b=b, e=e: nc.sync.dma_start(out=XT[b], in_=src[e * TE:(e + 1) * TE, :].rearrange("(j p) d -> p j d", p=128))), writes=[in_t[b]])
        for j in range(8):
            js = slice(j * 128, (j + 1) * 128)
            kk = k % 2
            k += 1
            bl, bm, bd = 0 + kk, 2 + kk, 4 + kk
            for c in range(8):
                P.op("pe", _mm(nc, banks[bl][:, 0:TE], wlo[:, c, js], YL[b][:, c, :], c == 0, c == 7), reads=[w_t, in_t[b]], writes=[bank_tok[bl]])
            for c in range(4):
                P.op("pe", _mm(nc, banks[bm][:, 0:TE], wmo[:, c, js], YM[b][:, c, :], c == 0, c == 3), reads=[w_t, in_t[b]], writes=[bank_tok[bm]])
            for c in range(4):
                P.op("pe", _mm(nc, banks[bd][:, 0:TE], wdo[:, c, js], YD[b][:, c, :], c == 0, c == 3), reads=[w_t, in_t[b]], writes=[bank_tok[bd]])
            P.op("dve", (lambda b=b, j=j, kk=kk, bl=bl: V.tensor_tensor(out=t1[kk], in0=banks[bl][:, 0:TE], in1=GT[b][:, j, :], op=ALU.mult)),
                 reads=[bank_tok[bl], in_t[b]], writes=[t1_t[kk]])
            P.op("dve", (lambda b=b, j=j, kk=kk, bm=bm: V.tensor_tensor(out=t2[kk], in0=banks[bm][:, 0:TE], in1=GT[b][:, 8 + j, :], op=ALU.mult)),
                 reads=[bank_tok[bm], in_t[b]], writes=[t2_t[kk]])
            P.op("dve", (lambda kk=kk: V.tensor_tensor(out=t1[kk], in0=t1[kk], in1=t2[kk], op=ALU.add)),
                 reads=[t1_t[kk], t2_t[kk]], writes=[t1_t[kk]])
            P.op("dve", (lambda b=b, j=j, kk=kk, bd=bd: V.tensor_tensor(out=t2[kk], in0=banks[bd][:, 0:TE], in1=GT[b][:, 16 + j, :], op=ALU.mult)),
                 reads=[bank_tok[bd], in_t[b]], writes=[t2_t[kk]])
            P.op("dve", (lambda b=b, j=j, kk=kk: V.tensor_tensor(out=M[b][:, j, :], in0=t1[kk], in1=t2[kk], op=ALU.add)),
                 reads=[t1_t[kk], t2_t[kk]], writes=[M_t[b]])
        for s_ in range(2):
            for hf in range(2):
                bo = 6 + (s_ * 2 + hf) % 2
                for c in range(8):
                    P.op("pe", _mm(nc, banks[bo][:, :], M[b][:, c, s_ * 128:(s_ + 1) * 128], wou[:, c, hf * 512:(hf + 1) * 512], c == 0, c == 7),
                         reads=[w_t, M_t[b]], writes=[bank_tok[bo]])
                P.op("dve", (lambda b=b, s_=s_, hf=hf, bo=bo: V.tensor_tensor(out=OUT[b][:, s_, hf * 512:(hf + 1) * 512], in0=banks[bo][:, :],
                                                                            in1=XT[b][:, s_, hf * 512:(hf + 1) * 512], op=ALU.add)),
                     reads=[bank_tok[bo], in_t[b]], writes=[out_t[b]])
        P.dma("sp", (lambda b=b, e=e: nc.sync.dma_start(out=dst[e * TE:(e + 1) * TE, :].rearrange("(j p) d -> p j d", p=128), in_=OUT[b])),
              reads=[out_t[b]], is_output=is_final)
```

```python
import math
import numpy as np
import ml_dtypes
import concourse.bass as bass
import concourse.mybir as mybir
from concourse.bass_utils import run_bass_kernel_spmd

F32 = mybir.dt.float32
BF16 = mybir.dt.bfloat16
I32 = mybir.dt.int32
AF = mybir.ActivationFunctionType
ALU = mybir.AluOpType

S = 4096
D = 1024
DEPTH = 2
INW = 11168
EPS = 1e-6
C_LX, C_LG, C_CQ, C_CKV, C_KR, C_MG, C_DQ, C_DK, C_DV, C_DG, C_MRG = (
    0, 1024, 2048, 2304, 2432, 2464, 2976, 4512, 6048, 7584, 8096)
DIL = (1, 4, 16)
NDMA_SLOTS = 12

PV_NORMG = 0
PV_CONVW = 8
PV_CONVB = 40
PV_BGX = 48
PV_BGA = 56
PV_LAM = 64
PV_CQG = 72
PV_CKVG = 74
PV_MQG = 75
PV_MKG = 76
PV_DQKG = 77
PV_BMRG = 78
NPV = 102


class Tok:
    __slots__ = ("w", "r", "excl")

    def __init__(self, excl=False):
        self.w = None
        self.r = []
        self.excl = excl


class Prog:
    ENG = ("pe", "act", "dve", "pool", "sp")

    def __init__(self, nc):
        self.nc = nc
        self.h = {"pe": nc.tensor, "act": nc.scalar, "dve": nc.vector, "pool": nc.gpsimd, "sp": nc.sync}
        self.ops = []
        self.seq = {e: 0 for e in self.ENG}
        self.wm = {e: {} for e in self.ENG}
        self.dma_rr = {"sp": 0, "pool": 0, "act": 0}
        self.dma_cnt = {}
        self.out_events = []

    def _deps(self, reads, writes, eng=None):
        deps = []
        for t in reads:
            if t.w is not None:
                deps.append(t.w)
            if t.excl:
                deps.extend(r for r in t.r if r[0] != eng)
        for t in writes:
            if t.w is not None:
                deps.append(t.w)
            deps.extend(t.r)
        return deps

    def _mk_waits(self, eng, deps, skip_same_pe=True):
        best = {}
        for ev in deps:
            key, val = ev
            if key == eng and eng == "pe":
                continue
            if val > best.get(key, 0):
                best[key] = val
        waits = []
        wm = self.wm[eng]
        for key, val in best.items():
            if wm.get(key, 0) >= val:
                continue
            wm[key] = val
            waits.append((key, val))
        return waits

    def op(self, eng, fn, reads=(), writes=()):
        deps = self._deps(reads, writes, eng)
        waits = self._mk_waits(eng, deps)
        self.seq[eng] += 1
        ev = (eng, self.seq[eng])
        self.ops.append(["c", eng, fn, waits, ev, False])
        for t in reads:
            if t.excl:
                t.r = [ev]
            else:
                t.r.append(ev)
        for t in writes:
            t.w = ev
            t.r = []
        return ev

    def dma(self, q, fn, reads=(), writes=(), is_output=False):
        deps = self._deps(reads, writes)
        slot = (q, self.dma_rr[q] % NDMA_SLOTS)
        self.dma_rr[q] += 1
        cnt = self.dma_cnt.get(slot, 0)
        if cnt > 0:
            deps.append((slot, cnt))
        waits = self._mk_waits(q, deps)
        self.dma_cnt[slot] = cnt + 1
        ev = (slot, cnt + 1)
        self.ops.append(["d", q, fn, waits, ev, True])
        for t in reads:
            t.r.append(ev)
        for t in writes:
            t.w = ev
            t.r = []
        if is_output:
            self.out_events.append(ev)
        return ev

    def barrier(self):
        deps = [(e, self.seq[e]) for e in ("pe", "act", "dve", "pool") if self.seq[e] > 0]
        deps += [(slot, cnt) for slot, cnt in self.dma_cnt.items()]
        waits = self._mk_waits("sp", deps)
        self.seq["sp"] += 1
        ev = ("sp", self.seq["sp"])
        self.ops.append(["c", "sp", lambda: self.nc.sync.nop(), waits, ev, False])
        for e in ("pe", "act", "dve", "pool"):
            w = self._mk_waits(e, [ev])
            self.ops.append(["w", e, None, w, None, False])

    def finish(self):
        deps = list(self.out_events)
        waits = self._mk_waits("sp", deps)
        self.ops.append(["w", "sp", None, waits, None, False])

    def emit(self, sems, dma_sems):
        targets = {e: set() for e in self.ENG}
        for o in self.ops:
            for key, val in o[3]:
                if isinstance(key, str):
                    targets[key].add(val)
        valmap = {}
        for e in self.ENG:
            c = 0
            m = {}
            for s in sorted(targets[e]):
                c += 1
                m[s] = c
            valmap[e] = m
        for kind, eng, fn, waits, ev, _ in self.ops:
            hnd = self.h[eng]
            for key, val in waits:
                if isinstance(key, str):
                    hnd.wait_ge(sems[key], valmap[key][val])
                else:
                    hnd.wait_ge(dma_sems[key], 16 * val)
            if kind == "w":
                continue
            ins = fn()
            if kind == "d":
                ins.then_inc(dma_sems[ev[0]], 16)
            else:
                if ev[1] in targets[eng]:
                    ins.then_inc(sems[eng], 1)


class Arena:
    def __init__(self, ap, nbytes):
        self.ap = ap
        self.nbytes = nbytes
        self.off = 0

    def alloc(self, shape, dt):
        esz = 4 if dt in (F32, I32) else 2
        n = 1
        for s_ in shape:
            n *= s_
        nb = (n * esz + 63) // 64 * 64
        assert self.off + nb <= self.nbytes, ("arena overflow", self.off, nb, self.nbytes)
        a = self.ap[:, self.off // 2:(self.off + n * esz) // 2]
        self.off += nb
        if dt != BF16:
            a = a.bitcast(dt)
        if len(shape) == 2:
            a = a.rearrange("p (a b) -> p a b", a=shape[0])
        elif len(shape) == 3:
            a = a.rearrange("p (a b c) -> p a b c", a=shape[0], b=shape[1])
        return a

    def mark(self):
        return self.off

    def release(self, m):
        self.off = m


def toks(n):
    return [Tok() for _ in range(n)]


def build_program(n_layers=DEPTH, stages=None, debug=False, extra_barriers=0, stop=None):
    nc = bass.Bass("TRN2", target_bir_lowering=False)
    P = Prog(nc)
    dram_in = {}

    def din(name, shape, dt=F32):
        dram_in[name] = nc.dram_tensor(name, list(shape), dt, kind="ExternalInput").ap()
        return dram_in[name]

    x_in = din("x", [S, D])
    pos_in = din("pos", [1, S], I32)
    w_in = din("w_in", [DEPTH, D, INW])
    w_gx = din("w_gate_x", [DEPTH, 8, 128, 128])
    w_ga = din("w_gate_a", [DEPTH, 8, 128, 128])
    w_lru_o = din("w_lru_o", [DEPTH, 1024, 1024])
    w_uq = din("w_uq", [DEPTH, 256, 768])
    w_ukv = din("w_ukv", [DEPTH, 128, 1024])
    w_mla_o = din("w_mla_o", [DEPTH, 512, 1024])
    w_dil_o = din("w_dil_o", [DEPTH, 512, 1024])
    w_out = din("w_out", [DEPTH, 1024, 1024])
    pvec_in = din("pvec", [DEPTH, 128, NPV])
    cmat_in = din("cmat", [8, 128, 128])
    cmask_in = din("cmask", [128, 4 * 512 + 256])
    cinv_in = din("cinv", [128, 2])
    out_d = nc.dram_tensor("out", [S, D], F32, kind="ExternalOutput").ap()
    sk = "ExternalOutput" if debug else "Internal"
    xres = nc.dram_tensor("xres", [S, D], F32, kind=sk).ap()
    ylru = nc.dram_tensor("ylru", [1024, S], BF16, kind=sk).ap()
    ymla = nc.dram_tensor("ymla", [512, S], BF16, kind=sk).ap()
    ydil = nc.dram_tensor("ydil", [512, S], BF16, kind=sk).ap()
    gts = nc.dram_tensor("gts", [3072, S], BF16, kind=sk).ap()
    dbg = {}
    if debug:
        dbg["hT"] = nc.dram_tensor("hT_dbg", [128, 8 * S], BF16, kind="ExternalOutput").ap()

    ARENA_BYTES = 204 * 1024
    import contextlib
    with contextlib.ExitStack() as es:
        arena_t = es.enter_context(nc.sbuf_tensor("arena", [128, ARENA_BYTES // 2], BF16))
        banks = [es.enter_context(nc.psum_tensor(f"ps{i}", [128, 512], F32)) for i in range(8)]
        sems = {e: es.enter_context(nc.semaphore(f"s_{e}")) for e in Prog.ENG}
        dma_sems = {}
        for q in ("sp", "pool"):
            for i in range(NDMA_SLOTS):
                dma_sems[(q, i)] = es.enter_context(nc.semaphore(f"d_{q}{i}"))
        es.enter_context(nc.Block())
        es.enter_context(nc.allow_low_precision("bf16 PE transposes / bf16 matmul operands with fp32 accumulation"))
        AR = Arena(arena_t, ARENA_BYTES)
        bank_tok = [Tok(excl=True) for _ in range(8)]
        _emit_all(nc, P, AR, banks, bank_tok, locals(), n_layers, dbg)
        P.finish()
        P.emit(sems, dma_sems)
    return nc


def _emit_all(nc, P, AR, banks, bank_tok, env, n_layers, dbg):
    x_in = env["x_in"]; pos_in = env["pos_in"]; w_in = env["w_in"]
    out_d = env["out_d"]; xres = env["xres"]
    pvec_in = env["pvec_in"]; cmat_in = env["cmat_in"]; cmask_in = env["cmask_in"]; cinv_in = env["cinv_in"]
    V, A_, T_, G_ = nc.vector, nc.scalar, nc.tensor, nc.gpsimd

    cm = AR.alloc([8, 128], BF16)
    cm_t = Tok()
    P.dma("pool", lambda: G_.dma_start(out=cm, in_=cmat_in.rearrange("m p c -> p m c")), writes=[cm_t])
    IDENT, ONES256, ONES128, ONESB64, ONES96, RD_, RM_, EPAD = [cm[:, i, :] for i in range(8)]
    cmask = AR.alloc([4 * 512 + 256], BF16)
    cmask_t = Tok()
    P.dma("pool", lambda: G_.dma_start(out=cmask, in_=cmask_in), writes=[cmask_t])
    cinv = AR.alloc([2], F32)
    cinv_t = Tok()
    P.dma("sp", lambda: nc.sync.dma_start(out=cinv, in_=cinv_in), writes=[cinv_t])
    pv = AR.alloc([DEPTH, NPV], F32)
    pv_t = Tok()
    P.dma("sp", lambda: nc.sync.dma_start(out=pv, in_=pvec_in.rearrange("l p c -> p l c")), writes=[pv_t])
    hbgx = AR.alloc([DEPTH, 8], F32)
    hbga = AR.alloc([DEPTH, 8], F32)
    hc = AR.alloc([DEPTH, 8], F32)
    nhc = AR.alloc([DEPTH, 8], F32)
    hbm = AR.alloc([DEPTH, 24], F32)
    tmpv = AR.alloc([DEPTH, 8], F32)
    der_t = Tok()
    P.op("dve", lambda: V.tensor_scalar_mul(hbgx, pv[:, :, PV_BGX:PV_BGX + 8], -1.0), reads=[pv_t], writes=[der_t])
    P.op("dve", lambda: V.tensor_scalar_mul(hbga, pv[:, :, PV_BGA:PV_BGA + 8], -1.0), reads=[pv_t], writes=[der_t])
    P.op("dve", lambda: V.tensor_scalar_mul(hbm, pv[:, :, PV_BMRG:PV_BMRG + 24], -1.0), reads=[pv_t], writes=[der_t])
    P.op("act", lambda: A_.activation(out=tmpv, in_=pv[:, :, PV_LAM:PV_LAM + 8], func=AF.Exp, scale=-1.0),
         reads=[pv_t], writes=[der_t])
    P.op("act", lambda: A_.activation(out=tmpv, in_=tmpv, func=AF.Ln, bias=1.0, scale=1.0),
         reads=[der_t], writes=[der_t])
    P.op("dve", lambda: V.tensor_scalar_mul(hc, tmpv, -8.0), reads=[der_t], writes=[der_t])

    perm_mark = AR.mark()
    ctx = dict(nc=nc, P=P, AR=AR, banks=banks, bank_tok=bank_tok, env=env, dbg=dbg,
               cm=cm, cm_t=cm_t, cmask=cmask, cmask_t=cmask_t, cinv=cinv, cinv_t=cinv_t,
               pv=pv, pv_t=pv_t, der_t=der_t, hbgx=hbgx, hbga=hbga, hc=hc, nhc=nhc, hbm=hbm)
    for l in range(n_layers):
        src = x_in if l == 0 else xres
        dst = out_d if l == n_layers - 1 else xres
        AR.release(perm_mark)
        P.barrier()
        hT = AR.alloc([8, S], BF16)
        hT_t = toks(8)
        layer_mark = AR.mark()
        phase_norm(ctx, l, src, hT, hT_t)
        if dbg and l == 0:
            P.dma("sp", lambda: nc.sync.dma_start(out=dbg["hT"].rearrange("p (c t) -> p c t", c=8), in_=hT), reads=hT_t)
        stages = env.get("stages") or ("gates", "lru", "mla", "dil", "out")
        if "gates" in stages:
            P.barrier(); AR.release(layer_mark)
            phase_gates(ctx, l, hT, hT_t)
        if "lru" in stages:
            P.barrier(); AR.release(layer_mark)
            phase_lru(ctx, l, hT, hT_t)
        if "mla" in stages:
            P.barrier(); AR.release(layer_mark)
            phase_mla(ctx, l, hT, hT_t)
        if "dil" in stages:
            P.barrier(); AR.release(layer_mark)
            phase_dil(ctx, l, hT, hT_t)
        if "out" in stages:
            P.barrier(); AR.release(perm_mark)
            phase_out(ctx, l, src, dst)
    for _ in range(ctx["env"].get("extra_barriers", 0)):
        P.barrier()
    P.barrier()


def _mm(nc, out, lhsT, rhs, start, stop):
    return lambda: nc.tensor.matmul(out, lhsT, rhs, start=start, stop=stop)


def proj_group(ctx, bank_i, wtile, w_t, c0, m, hT, hT_t, tt, rows=None):
    nc, P = ctx["nc"], ctx["P"]
    bank = ctx["banks"][bank_i]
    bt = ctx["bank_tok"][bank_i]
    for k in range(8):
        P.op("pe", _mm(nc, bank[0:m, :], wtile[:, k, c0:c0 + m], hT[:, k, tt * 512:(tt + 1) * 512], k == 0, k == 7),
             reads=[w_t, hT_t[tt]], writes=[bt])


def phase_norm(ctx, l, src, hT, hT_t):
    nc, P, AR = ctx["nc"], ctx["P"], ctx["AR"]
    V, A_, T_ = nc.vector, nc.scalar, nc.tensor
    banks, bank_tok = ctx["banks"], ctx["bank_tok"]
    pv, pv_t = ctx["pv"], ctx["pv_t"]
    IDENT = ctx["cm"][:, 0, :]
    xt = [AR.alloc([4, D], F32) for _ in range(2)]
    xt_t = toks(2)
    xn = [AR.alloc([D], BF16) for _ in range(2)]
    xn_t = toks(2)
    ss = [AR.alloc([4], F32) for _ in range(2)]
    ss_t = toks(2)
    junk = AR.alloc([D], BF16)
    gB = pv[:, l, PV_NORMG:PV_NORMG + 8].unsqueeze(2).to_broadcast([128, 8, 128])
    for g4 in range(8):
        b = g4 % 2
        P.dma("sp", (lambda b=b, g4=g4: nc.sync.dma_start(
            out=xt[b], in_=src[g4 * 512:(g4 + 1) * 512, :].rearrange("(j p) d -> p j d", p=128))),
            writes=[xt_t[b]])
        P.op("dve", (lambda b=b: V.memset(ss[b], 0.0)), writes=[ss_t[b]])
        for j in range(4):
            P.op("act", (lambda b=b, j=j: A_.activation(out=junk, in_=xt[b][:, j, :], func=AF.Square,
                                                         accum_out=ss[b][:, j:j + 1])),
                 reads=[xt_t[b]], writes=[ss_t[b]])
        P.op("act", (lambda b=b: A_.activation(out=ss[b], in_=ss[b], func=AF.Ln, bias=EPS, scale=1.0 / D)),
             reads=[ss_t[b]], writes=[ss_t[b]])
        P.op("act", (lambda b=b: A_.activation(out=ss[b], in_=ss[b], func=AF.Exp, scale=-0.5)),
             reads=[ss_t[b]], writes=[ss_t[b]])
        for j in range(4):
            kb = (g4 * 4 + j) % 2
            P.op("dve", (lambda b=b, j=j, kb=kb: V.tensor_scalar(xn[kb], xt[b][:, j, :], ss[b][:, j:j + 1], None, ALU.mult)),
                 reads=[xt_t[b], ss_t[b]], writes=[xn_t[kb]])
            bbf = banks[kb][:, :].bitcast(BF16)
            for c in range(8):
                P.op("pe", (lambda kb=kb, c=c, bbf=bbf: T_.transpose(bbf[:, c * 128:(c + 1) * 128],
                                                                     xn[kb][:, c * 128:(c + 1) * 128], IDENT)),
                     reads=[xn_t[kb], ctx["cm_t"]], writes=[bank_tok[kb]])
            t0 = g4 * 512 + j * 128
            P.op("dve", (lambda bbf=bbf, t0=t0: V.tensor_tensor(
                out=hT[:, :, t0:t0 + 128], in0=bbf.rearrange("p (c t) -> p c t", c=8), in1=gB, op=ALU.mult)),
                reads=[bank_tok[kb], pv_t], writes=[hT_t[g4]])


def phase_lru(ctx, l, hT, hT_t):
    nc, P, AR = ctx["nc"], ctx["P"], ctx["AR"]
    V, A_, T_, G_ = nc.vector, nc.scalar, nc.tensor, nc.gpsimd
    env = ctx["env"]
    banks, bank_tok = ctx["banks"], ctx["bank_tok"]
    pv, pv_t, der_t = ctx["pv"], ctx["pv_t"], ctx["der_t"]
    w_in, ylru = env["w_in"], env["ylru"]
    wl = [AR.alloc([8, 256], BF16) for _ in range(2)]
    wl_t = toks(2)
    wgx = AR.alloc([8, 128], BF16)
    wga = AR.alloc([8, 128], BF16)
    wg_t = Tok()
    P.dma("pool", lambda: G_.dma_start(out=wgx, in_=env["w_gx"][l].rearrange("n c d -> c n d")), writes=[wg_t])
    P.dma("pool", lambda: G_.dma_start(out=wga, in_=env["w_ga"][l].rearrange("n c d -> c n d")), writes=[wg_t])
    X = AR.alloc([3 + S], F32)
    X_t = toks(8)
    Xh_t = Tok()
    names = ("G", "XC", "TGX", "TGA", "A", "T", "A2", "H", "SG")
    NSET = 4
    buf = [{nm: AR.alloc([512], F32) for nm in names} for _ in range(NSET)]
    for p_ in range(NSET):
        buf[p_]["XCb"] = AR.alloc([512], BF16)
        buf[p_]["Yb"] = AR.alloc([512], BF16)
    bt = [{nm: Tok() for nm in buf[0]} for _ in range(NSET)]
    P.op("dve", lambda: V.memset(X[:, 0:3], 0.0), writes=[Xh_t])
    w_r = w_in[l].rearrange("(c p) n -> p c n", p=128)
    it = 0
    for n in range(8):
        wb = n % 2
        P.dma("pool", (lambda wb=wb, n=n: G_.dma_start(out=wl[wb][:, :, 0:128], in_=w_r[:, :, C_LX + n * 128:C_LX + (n + 1) * 128])),
              writes=[wl_t[wb]])
        P.dma("pool", (lambda wb=wb, n=n: G_.dma_start(out=wl[wb][:, :, 128:256], in_=w_r[:, :, C_LG + n * 128:C_LG + (n + 1) * 128])),
              writes=[wl_t[wb]])
        cw = lambda k, n=n: pv[:, l, PV_CONVW + n * 4 + k:PV_CONVW + n * 4 + k + 1]
        cb = pv[:, l, PV_CONVB + n:PV_CONVB + n + 1]
        hbx = ctx["hbgx"][:, l, n:n + 1]
        hba = ctx["hbga"][:, l, n:n + 1]
        hcn = ctx["hc"][:, l, n:n + 1]
        nhcn = ctx["nhc"][:, l, n:n + 1]
        def chunk(tt, par, pb, n=n, wb=wb, cw=cw, cb=cb, hbx=hbx, hba=hba, hcn=hcn):
            B_, Bt = buf[par], bt[par]
            c0 = 3 + tt * 512
            bx, bg, bgx, bga = pb, 2 + pb, 4 + pb, 6 + pb
            proj_group(ctx, bx, wl[wb], wl_t[wb], 0, 128, hT, hT_t, tt)
            P.op("act", lambda: A_.copy(X[:, c0:c0 + 512], banks[bx][:, :]), reads=[bank_tok[bx]], writes=[X_t[tt]])
            yield
            proj_group(ctx, bg, wl[wb], wl_t[wb], 128, 128, hT, hT_t, tt)
            P.op("act", lambda: A_.copy(B_["G"], banks[bg][:, :]), reads=[bank_tok[bg]], writes=[Bt["G"]])
            yield
            P.op("act", lambda: A_.activation(out=B_["SG"], in_=B_["G"], func=AF.Exp, scale=-1.0), reads=[Bt["G"]], writes=[Bt["SG"]])
            xprev = Xh_t if tt == 0 else X_t[tt - 1]
            P.op("dve", lambda: V.tensor_scalar(B_["XC"], X[:, c0:c0 + 512], cw(3), cb, ALU.mult, ALU.add),
                 reads=[X_t[tt], pv_t], writes=[Bt["XC"]])
            for k in (2, 1, 0):
                P.op("dve", (lambda k=k: V.scalar_tensor_tensor(B_["XC"], X[:, c0 - 3 + k:c0 - 3 + k + 512], cw(k), B_["XC"], ALU.mult, ALU.add)),
                     reads=[X_t[tt], xprev, Bt["XC"]], writes=[Bt["XC"]])
            yield
            P.op("act", lambda: A_.activation(out=B_["SG"], in_=B_["SG"], func=AF.Ln, bias=1.0), reads=[Bt["SG"]], writes=[Bt["SG"]])
            P.op("act", lambda: A_.copy(B_["XCb"], B_["XC"]), reads=[Bt["XC"]], writes=[Bt["XCb"]])
            yield
            P.op("pe", _mm(nc, banks[bgx][:, :], wgx[:, n, :], B_["XCb"], True, True), reads=[wg_t, Bt["XCb"]], writes=[bank_tok[bgx]])
            P.op("pe", _mm(nc, banks[bga][:, :], wga[:, n, :], B_["XCb"], True, True), reads=[wg_t, Bt["XCb"]], writes=[bank_tok[bga]])
            P.op("act", lambda: A_.activation(out=B_["SG"], in_=B_["SG"], func=AF.Exp, scale=-1.0), reads=[Bt["SG"]], writes=[Bt["SG"]])
            yield
            for nm, bk, hb_ in (("TGX", bgx, hbx), ("TGA", bga, hba)):
                P.op("act", (lambda nm=nm, bk=bk, hb_=hb_: A_.activation(out=B_[nm], in_=banks[bk][:, :], func=AF.Exp, bias=hb_, scale=-1.0)),
                     reads=[bank_tok[bk], der_t], writes=[Bt[nm]])
            P.op("dve", lambda: V.tensor_tensor(out=B_["SG"], in0=B_["SG"], in1=B_["G"], op=ALU.mult), reads=[Bt["SG"], Bt["G"]], writes=[Bt["SG"]])
            yield
            for nm in ("TGX", "TGA"):
                P.op("act", (lambda nm=nm: A_.activation(out=B_[nm], in_=B_[nm], func=AF.Ln, bias=1.0)), reads=[Bt[nm]], writes=[Bt[nm]])
            yield
            for nm in ("TGX", "TGA"):
                P.op("act", (lambda nm=nm: A_.activation(out=B_[nm], in_=B_[nm], func=AF.Exp, scale=-1.0)), reads=[Bt[nm]], writes=[Bt[nm]])
            yield
            P.op("act", lambda: A_.activation(out=B_["A"], in_=B_["TGA"], func=AF.Exp, scale=hcn), reads=[Bt["TGA"], der_t], writes=[Bt["A"]])
            P.op("act", lambda: A_.activation(out=B_["A2"], in_=B_["A"], func=AF.Square), reads=[Bt["A"]], writes=[Bt["A2"]])
            P.op("dve", lambda: V.tensor_tensor(out=B_["TGX"], in0=B_["TGX"], in1=B_["XC"], op=ALU.mult), reads=[Bt["TGX"], Bt["XC"]], writes=[Bt["TGX"]])
            yield
            P.op("dve", lambda: V.tensor_scalar(B_["T"], B_["A2"], -1.0, 1.0, ALU.mult, ALU.add), reads=[Bt["A2"]], writes=[Bt["T"]])
            yield
            P.op("act", lambda: A_.activation(out=B_["T"], in_=B_["T"], func=AF.Ln), reads=[Bt["T"]], writes=[Bt["T"]])
            yield
            P.op("act", lambda: A_.activation(out=B_["T"], in_=B_["T"], func=AF.Exp, scale=0.5), reads=[Bt["T"]], writes=[Bt["T"]])
            yield
            P.op("dve", lambda: V.tensor_tensor(out=B_["T"], in0=B_["T"], in1=B_["TGX"], op=ALU.mult), reads=[Bt["T"], Bt["TGX"]], writes=[Bt["T"]])
            if tt == 0:
                P.op("dve", lambda: V.tensor_tensor_scan(B_["H"], B_["A"], B_["T"], 0.0, ALU.mult, ALU.add), reads=[Bt["A"], Bt["T"]], writes=[Bt["H"]])
            else:
                Bp, Bpt = buf[(par - 1) % NSET], bt[(par - 1) % NSET]
                P.op("dve", lambda: V.tensor_tensor_scan(B_["H"], B_["A"], B_["T"], Bp["H"][:, 511:512], ALU.mult, ALU.add),
                     reads=[Bt["A"], Bt["T"], Bpt["H"]], writes=[Bt["H"]])
            P.op("dve", lambda: V.tensor_tensor(out=B_["Yb"], in0=B_["H"], in1=B_["SG"], op=ALU.mult), reads=[Bt["H"], Bt["SG"]], writes=[Bt["Yb"]])
            P.dma("sp", lambda: nc.sync.dma_start(out=ylru[n * 128:(n + 1) * 128, tt * 512:(tt + 1) * 512], in_=B_["Yb"]), reads=[Bt["Yb"]])

        for tp in range(0, 8, 2):
            gens = []
            for tt in (tp, tp + 1):
                gens.append(chunk(tt, it % NSET, it % 2))
                it += 1
            live = list(gens)
            while live:
                for g_ in list(live):
                    try:
                        next(g_)
                    except StopIteration:
                        live.remove(g_)


def phase_gates(ctx, l, hT, hT_t):
    nc, P, AR = ctx["nc"], ctx["P"], ctx["AR"]
    A_, G_ = nc.scalar, nc.gpsimd
    env = ctx["env"]
    banks, bank_tok = ctx["banks"], ctx["bank_tok"]
    w_r = env["w_in"][l].rearrange("(c p) n -> p c n", p=128)
    gts = env["gts"]
    wt = [AR.alloc([8, 128], BF16) for _ in range(2)]
    wt_t = toks(2)
    st = [AR.alloc([S], BF16) for _ in range(2)]
    st_t = toks(2)
    ebuf = [AR.alloc([512], F32) for _ in range(2)]
    ebuf_t = toks(2)
    for ch in range(24):
        b = ch % 2
        P.dma("pool", (lambda b=b, ch=ch: G_.dma_start(out=wt[b], in_=w_r[:, :, C_MRG + ch * 128:C_MRG + (ch + 1) * 128])),
              writes=[wt_t[b]])
        hb = ctx["hbm"][:, l, ch:ch + 1]
        for tt in range(8):
            bi = (ch * 8 + tt) % 4
            proj_group(ctx, bi, wt[b], wt_t[b], 0, 128, hT, hT_t, tt)
            eb = (ch * 8 + tt) % 2
            P.op("act", (lambda bi=bi, eb=eb, hb=hb: A_.activation(out=ebuf[eb], in_=banks[bi][:, :], func=AF.Exp, bias=hb, scale=-1.0)),
                 reads=[bank_tok[bi], ctx["der_t"]], writes=[ebuf_t[eb]])
            P.op("act", (lambda eb=eb: A_.activation(out=ebuf[eb], in_=ebuf[eb], func=AF.Ln, bias=1.0)), reads=[ebuf_t[eb]], writes=[ebuf_t[eb]])
            P.op("act", (lambda b=b, eb=eb, tt=tt: A_.activation(out=st[b][:, tt * 512:(tt + 1) * 512], in_=ebuf[eb], func=AF.Exp, scale=-1.0)),
                 reads=[ebuf_t[eb]], writes=[st_t[b]])
        P.dma("sp", (lambda b=b, ch=ch: nc.sync.dma_start(out=gts[ch * 128:(ch + 1) * 128, :], in_=st[b])),
              reads=[st_t[b]])
def _host_consts():
    cmat = np.zeros((8, 128, 128), np.float32)
    cmat[0] = np.eye(128, dtype=np.float32)
    cmat[1] = 1.0 / 256
    cmat[2] = 1.0 / 128
    cmat[3, 0:64, 0:64] = 1.0 / 64
    cmat[3, 64:128, 64:128] = 1.0 / 64
    cmat[4, 0:96, 0:96] = 1.0
    for blk in (0, 64):
        for i in range(32):
            cmat[5, blk + i + 32, blk + i] = -1.0
            cmat[5, blk + i, blk + i + 32] = 1.0
    for i in range(16):
        cmat[6, 64 + i + 16, 64 + i] = -1.0
        cmat[6, 64 + i, 64 + i + 16] = 1.0
    for k in range(32):
        cmat[7, k, 64 + k] = 1.0
    cmask = np.zeros((128, 4 * 512 + 256), np.float32)
    kk = np.arange(128)[:, None]
    qq = np.arange(512)[None, :]
    for j in range(4):
        cmask[:, j * 512:(j + 1) * 512] = (qq >= 128 * j + kk)
    q1 = np.arange(128)[None, :]
    cmask[:, 2048:2048 + 128] = (kk >= q1)
    cmask[:, 2048 + 128:2048 + 256] = (kk <= q1)
    cinv = np.zeros((128, 2), np.float32)
    theta = np.float32(10000.0)
    inv64 = theta ** (-np.arange(0, 64, 2, dtype=np.float32) / np.float32(64))
    inv32 = theta ** (-np.arange(0, 32, 2, dtype=np.float32) / np.float32(32))
    for p in range(128):
        cinv[p, 0] = inv64[p % 32]
        if 64 <= p < 96:
            cinv[p, 1] = inv32[(p - 64) % 16]
    return cmat, cmask, cinv


def _host_pvec(inp):
    pv = np.zeros((DEPTH, 128, NPV), np.float32)
    for l in range(DEPTH):
        pv[l, :, PV_NORMG:PV_NORMG + 8] = inp["norm_g"][l].reshape(8, 128).T
        pv[l, :, PV_CONVW:PV_CONVW + 32] = inp["conv_w"][l].reshape(4, 8, 128).transpose(2, 1, 0).reshape(128, 32)
        pv[l, :, PV_CONVB:PV_CONVB + 8] = inp["conv_b"][l].reshape(8, 128).T
        pv[l, :, PV_BGX:PV_BGX + 8] = inp["b_gate_x"][l].T
        pv[l, :, PV_BGA:PV_BGA + 8] = inp["b_gate_a"][l].T
        pv[l, :, PV_LAM:PV_LAM + 8] = inp["lru_lambda"][l].reshape(8, 128).T
        pv[l, :, PV_CQG:PV_CQG + 2] = inp["cq_norm_g"][l].reshape(2, 128).T
        pv[l, :, PV_CKVG] = inp["ckv_norm_g"][l]
        pv[l, 0:96, PV_MQG] = inp["mla_q_norm_g"][l]
        pv[l, 0:96, PV_MKG] = inp["mla_k_norm_g"][l]
        pv[l, 0:64, PV_DQKG] = inp["dil_q_norm_g"][l]
        pv[l, 64:128, PV_DQKG] = inp["dil_k_norm_g"][l]
        pv[l, :, PV_BMRG:PV_BMRG + 24] = inp["b_merge"][l].reshape(24, 128).T
    return pv


def make_in_maps(inp, n_cores=8):
    cmat, cmask, cinv = _host_consts()
    pvec = _host_pvec(inp)
    shared = {
        "w_in": np.ascontiguousarray(inp["w_in"]), "w_gate_x": np.ascontiguousarray(inp["w_gate_x"]),
        "w_gate_a": np.ascontiguousarray(inp["w_gate_a"]), "w_lru_o": np.ascontiguousarray(inp["w_lru_o"]),
        "w_uq": np.ascontiguousarray(inp["w_uq"]), "w_ukv": np.ascontiguousarray(inp["w_ukv"]),
        "w_mla_o": np.ascontiguousarray(inp["w_mla_o"]), "w_dil_o": np.ascontiguousarray(inp["w_dil_o"]),
        "w_out": np.ascontiguousarray(inp["w_out"]), "pvec": pvec, "cmat": cmat, "cmask": cmask, "cinv": cinv,
    }
    maps = []
    for b in range(n_cores):
        m = dict(shared)
        m["x"] = np.ascontiguousarray(inp["x"][b])
        m["pos"] = np.ascontiguousarray(inp["positions"][b].reshape(1, S).astype(np.int32))
        maps.append(m)
    return maps


def kernel(**inputs):
    inp = {k: np.asarray(v) for k, v in inputs.items()}
    nc = build_program()
    maps = make_in_maps(inp, 8)
    res = run_bass_kernel_spmd(nc, maps, core_ids=list(range(8)))
    out = np.stack([np.asarray(res.results[b]["out"]).reshape(S, D) for b in range(8)], axis=0)
    return out.astype(np.float32)


def make_tables(ctx, which, CT, ST, tab_t):
    nc, P, AR = ctx["nc"], ctx["P"], ctx["AR"]
    V, A_ = nc.vector, nc.scalar
    pos_in = ctx["env"]["pos_in"]
    inv = ctx["cinv"][:, which:which + 1]
    m = AR.mark()
    posi = [AR.alloc([1024], I32) for _ in range(2)]
    y0 = [AR.alloc([1024], F32) for _ in range(2)]
    yy = [AR.alloc([1024], F32) for _ in range(2)]
    ki = [AR.alloc([1024], I32) for _ in range(2)]
    kf = [AR.alloc([1024], F32) for _ in range(2)]
    tk = [{n_: Tok() for n_ in ("posi", "y0", "yy", "ki", "kf")} for _ in range(2)]
    TWO_PI = 2.0 * math.pi * (1.0 - 1e-6)
    for c in range(4):
        b = c % 2
        t = tk[b]
        P.dma("sp", (lambda b=b, c=c: nc.sync.dma_start(out=posi[b], in_=pos_in[:, c * 1024:(c + 1) * 1024].broadcast_to([128, 1024]))),
              writes=[t["posi"]])
        P.op("dve", (lambda b=b: V.tensor_copy(y0[b], posi[b])), reads=[t["posi"]], writes=[t["y0"]])
        P.op("dve", (lambda b=b: V.tensor_scalar(y0[b], y0[b], inv, None, ALU.mult)), reads=[t["y0"], ctx["cinv_t"]], writes=[t["y0"]])
        P.op("dve", (lambda b=b: V.tensor_scalar(y0[b], y0[b], 1.0 / (2.0 * math.pi), None, ALU.mult)), reads=[t["y0"]], writes=[t["y0"]])
        for ph, OUT in ((0.0, ST), (0.25, CT)):
            P.op("dve", (lambda b=b, ph=ph: V.tensor_scalar(yy[b], y0[b], ph, None, ALU.add)), reads=[t["y0"]], writes=[t["yy"]])
            P.op("dve", (lambda b=b: V.tensor_copy(ki[b], yy[b])), reads=[t["yy"]], writes=[t["ki"]])
            P.op("dve", (lambda b=b: V.tensor_copy(kf[b], ki[b])), reads=[t["ki"]], writes=[t["kf"]])
            P.op("dve", (lambda b=b: V.tensor_tensor(out=yy[b], in0=yy[b], in1=kf[b], op=ALU.subtract)), reads=[t["yy"], t["kf"]], writes=[t["yy"]])
            P.op("dve", (lambda b=b: V.tensor_scalar(kf[b], yy[b], 0.5, None, ALU.is_gt)), reads=[t["yy"]], writes=[t["kf"]])
            P.op("dve", (lambda b=b: V.tensor_tensor(out=yy[b], in0=yy[b], in1=kf[b], op=ALU.subtract)), reads=[t["yy"], t["kf"]], writes=[t["yy"]])
            P.op("act", (lambda b=b, OUT=OUT, c=c: A_.activation(out=OUT[:, c * 1024:(c + 1) * 1024], in_=yy[b], func=AF.Sin, scale=TWO_PI)),
                 reads=[t["yy"]], writes=[tab_t])
    P.barrier()
    AR.release(m)


def norm_rope_chain(ctx, src_bank, np_, ones_m, eps_v, gcol, CT, ST, rot_m, tt, out_ap, out_tok, tmp, tmp_t, bank_ss, bank_rot):
    nc, P = ctx["nc"], ctx["P"]
    V, A_ = nc.vector, nc.scalar
    banks, bank_tok = ctx["banks"], ctx["bank_tok"]
    IDENT = ctx["cm"][:, 0, :]
    src = banks[src_bank][0:np_, :]
    st_ = bank_tok[src_bank]
    cols = slice(tt * 512, (tt + 1) * 512)
    P.op("act", lambda: A_.activation(out=tmp["sq"][0:np_, :], in_=src, func=AF.Square), reads=[st_], writes=[tmp_t["sq"]])
    P.op("pe", _mm(nc, banks[bank_ss][0:np_, :], ones_m[0:np_, 0:np_], tmp["sq"][0:np_, :], True, True),
         reads=[tmp_t["sq"], ctx["cm_t"]], writes=[bank_tok[bank_ss]])
    P.op("act", lambda: A_.activation(out=tmp["rs"][0:np_, :], in_=banks[bank_ss][0:np_, :], func=AF.Ln, bias=eps_v),
         reads=[bank_tok[bank_ss]], writes=[tmp_t["rs"]])
    P.op("act", lambda: A_.activation(out=tmp["rs"][0:np_, :], in_=tmp["rs"][0:np_, :], func=AF.Exp, scale=-0.5),
         reads=[tmp_t["rs"]], writes=[tmp_t["rs"]])
    P.op("dve", lambda: V.scalar_tensor_tensor(tmp["u1"][0:np_, :], src, gcol[0:np_, :], CT[0:np_, cols], ALU.mult, ALU.mult),
         reads=[st_, ctx["pv_t"], tmp_t["tab"]], writes=[tmp_t["u1"]])
    P.op("dve", lambda: V.scalar_tensor_tensor(tmp["u2"][0:np_, :], src, gcol[0:np_, :], ST[0:np_, cols], ALU.mult, ALU.mult),
         reads=[st_, ctx["pv_t"], tmp_t["tab"]], writes=[tmp_t["u2"]])
    P.op("pe", _mm(nc, banks[bank_rot][0:np_, :], IDENT[0:np_, 0:np_], tmp["u1"][0:np_, :], True, False),
         reads=[tmp_t["u1"], ctx["cm_t"]], writes=[bank_tok[bank_rot]])
    P.op("pe", _mm(nc, banks[bank_rot][0:np_, :], rot_m[0:np_, 0:np_], tmp["u2"][0:np_, :], False, True),
         reads=[tmp_t["u2"], ctx["cm_t"]], writes=[bank_tok[bank_rot]])
    P.op("dve", lambda: V.tensor_tensor(out=out_ap, in0=banks[bank_rot][0:np_, :], in1=tmp["rs"][0:np_, :], op=ALU.mult),
         reads=[bank_tok[bank_rot], tmp_t["rs"]], writes=[out_tok])


def chain_group(ctx, chains, CT, ST):
    nc, P = ctx["nc"], ctx["P"]
    V, A_ = nc.vector, nc.scalar
    banks, bank_tok = ctx["banks"], ctx["bank_tok"]
    IDENT = ctx["cm"][:, 0, :]
    for c in chains:
        n_ = c["np"]
        P.op("act", (lambda c=c, n_=n_: A_.activation(out=c["tmp"]["sq"][0:n_, :], in_=banks[c["src"]][0:n_, :], func=AF.Square)),
             reads=[bank_tok[c["src"]]], writes=[c["tmp_t"]["sq"]])
    for c in chains:
        n_ = c["np"]
        P.op("pe", _mm(nc, banks[c["xb"]][0:n_, :], c["ones"][0:n_, 0:n_], c["tmp"]["sq"][0:n_, :], True, True),
             reads=[c["tmp_t"]["sq"], ctx["cm_t"]], writes=[bank_tok[c["xb"]]])
    for c in chains:
        n_ = c["np"]
        P.op("act", (lambda c=c, n_=n_: A_.activation(out=c["tmp"]["rs"][0:n_, :], in_=banks[c["xb"]][0:n_, :], func=AF.Ln, bias=c["eps"])),
             reads=[bank_tok[c["xb"]]], writes=[c["tmp_t"]["rs"]])
    for c in chains:
        n_ = c["np"]
        P.op("act", (lambda c=c, n_=n_: A_.activation(out=c["tmp"]["rs"][0:n_, :], in_=c["tmp"]["rs"][0:n_, :], func=AF.Exp, scale=-0.5)),
             reads=[c["tmp_t"]["rs"]], writes=[c["tmp_t"]["rs"]])
    for c in chains:
        n_ = c["np"]
        cols = slice(c["tt"] * 512, (c["tt"] + 1) * 512)
        P.op("dve", (lambda c=c, n_=n_, cols=cols: V.scalar_tensor_tensor(c["tmp"]["u1"][0:n_, :], banks[c["src"]][0:n_, :], c["g"][0:n_, :],
                                                                          CT[0:n_, cols], ALU.mult, ALU.mult)),
             reads=[bank_tok[c["src"]], ctx["pv_t"], c["tmp_t"]["tab"]], writes=[c["tmp_t"]["u1"]])
        P.op("dve", (lambda c=c, n_=n_, cols=cols: V.scalar_tensor_tensor(c["tmp"]["u2"][0:n_, :], banks[c["src"]][0:n_, :], c["g"][0:n_, :],
                                                                          ST[0:n_, cols], ALU.mult, ALU.mult)),
             reads=[bank_tok[c["src"]], ctx["pv_t"], c["tmp_t"]["tab"]], writes=[c["tmp_t"]["u2"]])
    for c in chains:
        n_ = c["np"]
        P.op("pe", _mm(nc, banks[c["xb"]][0:n_, :], IDENT[0:n_, 0:n_], c["tmp"]["u1"][0:n_, :], True, False),
             reads=[c["tmp_t"]["u1"], ctx["cm_t"]], writes=[bank_tok[c["xb"]]])
        P.op("pe", _mm(nc, banks[c["xb"]][0:n_, :], c["rot"][0:n_, 0:n_], c["tmp"]["u2"][0:n_, :], False, True),
             reads=[c["tmp_t"]["u2"], ctx["cm_t"]], writes=[bank_tok[c["xb"]]])
    for c in chains:
        n_ = c["np"]
        P.op("dve", (lambda c=c, n_=n_: V.tensor_tensor(out=c["out"], in0=banks[c["xb"]][0:n_, :], in1=c["tmp"]["rs"][0:n_, :], op=ALU.mult)),
             reads=[bank_tok[c["xb"]], c["tmp_t"]["rs"]], writes=[c["out_tok"]])


def finalize_head(ctx, o_ap, o_tok, den_ap, den_tok, gate_w, gate_wt, gcol0, hT, hT_t, tt, ydst, tmp, tmp_t):
    nc, P = ctx["nc"], ctx["P"]
    V, A_ = nc.vector, nc.scalar
    banks, bank_tok = ctx["banks"], ctx["bank_tok"]
    proj_group(ctx, 7, gate_w, gate_wt, gcol0, 64, hT, hT_t, tt)
    g_ps = banks[7][0:64, :]
    P.op("act", lambda: A_.activation(out=tmp["e1"][0:64, :], in_=g_ps, func=AF.Exp, scale=-1.0), reads=[bank_tok[7]], writes=[tmp_t["e1"]])
    P.op("dve", lambda: V.tensor_copy(tmp["den0"][0:64, :], den_ap), reads=[den_tok], writes=[tmp_t["den0"]])
    P.op("dve", lambda: V.scalar_tensor_tensor(tmp["e1"][0:64, :], tmp["e1"][0:64, :], 1.0, tmp["den0"][0:64, :], ALU.add, ALU.mult),
         reads=[tmp_t["e1"], tmp_t["den0"]], writes=[tmp_t["e1"]])
    P.op("act", lambda: A_.activation(out=tmp["e1"][0:64, :], in_=tmp["e1"][0:64, :], func=AF.Ln), reads=[tmp_t["e1"]], writes=[tmp_t["e1"]])
    P.op("act", lambda: A_.activation(out=tmp["e1"][0:64, :], in_=tmp["e1"][0:64, :], func=AF.Exp, scale=-1.0), reads=[tmp_t["e1"]], writes=[tmp_t["e1"]])
    P.op("dve", lambda: V.tensor_tensor(out=tmp["e1"][0:64, :], in0=tmp["e1"][0:64, :], in1=g_ps, op=ALU.mult),
         reads=[tmp_t["e1"], bank_tok[7]], writes=[tmp_t["e1"]])
    P.op("dve", lambda: V.tensor_tensor(out=tmp["yb"][0:64, :], in0=o_ap, in1=tmp["e1"][0:64, :], op=ALU.mult),
         reads=[o_tok, tmp_t["e1"]], writes=[tmp_t["yb"]])
    P.dma("sp", lambda: nc.sync.dma_start(out=ydst, in_=tmp["yb"][0:64, :]), reads=[tmp_t["yb"]])


def _chain_tmp(AR):
    tmp = {"sq": AR.alloc([512], BF16), "rs": AR.alloc([512], F32), "u1": AR.alloc([512], BF16), "u2": AR.alloc([512], BF16),
           "e1": AR.alloc([512], F32), "den0": AR.alloc([512], F32), "yb": AR.alloc([512], BF16), "den": AR.alloc([512], F32)}
    tmp_t = {k: Tok() for k in list(tmp) + ["tab"]}
    return tmp, tmp_t


def _chain_tmp2(AR, tmp_t):
    tmp = {"sq": AR.alloc([512], BF16), "rs": AR.alloc([512], F32), "u1": AR.alloc([512], BF16), "u2": AR.alloc([512], BF16)}
    t2 = {k: Tok() for k in tmp}
    t2["tab"] = tmp_t["tab"]
    return tmp, t2


def phase_mla(ctx, l, hT, hT_t):
    nc, P, AR = ctx["nc"], ctx["P"], ctx["AR"]
    V, A_, G_ = nc.vector, nc.scalar, nc.gpsimd
    env = ctx["env"]
    banks, bank_tok = ctx["banks"], ctx["bank_tok"]
    pv, pv_t, cm, cm_t = ctx["pv"], ctx["pv_t"], ctx["cm"], ctx["cm_t"]
    ONES256, ONES128, ONES96, RM_, EPAD = cm[:, 1, :], cm[:, 2, :], cm[:, 4, :], cm[:, 6, :], cm[:, 7, :]
    cmask, cmask_t = ctx["cmask"], ctx["cmask_t"]
    ymla = env["ymla"]
    w_r = env["w_in"][l].rearrange("(c p) n -> p c n", p=128)
    CT = AR.alloc([S], BF16)
    ST = AR.alloc([S], BF16)
    tmp, tmp_t = _chain_tmp(AR)
    make_tables(ctx, 1, CT, ST, tmp_t["tab"])
    if env.get("stop") == "tables":
        return
    wm = AR.alloc([8, 416], BF16)
    wuq = AR.alloc([2, 768], BF16)
    wukv = AR.alloc([1024], BF16)
    wgm = AR.alloc([8, 512], BF16)
    wkp = AR.alloc([8, 96], BF16)
    w_t = Tok()
    wkp_t = Tok()
    P.dma("pool", lambda: G_.dma_start(out=wm, in_=w_r[:, :, C_CQ:C_CQ + 416]), writes=[w_t])
    P.dma("pool", lambda: G_.dma_start(out=wuq, in_=env["w_uq"][l].rearrange("(c p) n -> p c n", p=128)), writes=[w_t])
    P.dma("pool", lambda: G_.dma_start(out=wukv, in_=env["w_ukv"][l]), writes=[w_t])
    P.dma("pool", lambda: G_.dma_start(out=wgm, in_=w_r[:, :, C_MG:C_MG + 512]), writes=[w_t])
    P.op("dve", lambda: V.memset(wkp, 0.0), writes=[wkp_t])
    P.op("dve", lambda: V.tensor_copy(wkp[:, :, 0:64], wukv.rearrange("p (h c) -> p h c", h=8)[:, :, 0:64]), reads=[w_t], writes=[wkp_t])
    CQN = AR.alloc([2, S], BF16)
    CKVN = AR.alloc([S], BF16)
    KR = AR.alloc([S], BF16)
    lat_t = toks(8)
    m1_mark = AR.mark()
    sq3 = [AR.alloc([3, 512], BF16) for _ in range(2)]
    sq3_t = toks(2)
    rs2 = [AR.alloc([2, 512], F32) for _ in range(2)]
    rs2_t = toks(2)
    for tt in range(8):
        b = tt % 2
        cols = slice(tt * 512, (tt + 1) * 512)
        proj_group(ctx, 0, wm, w_t, 0, 128, hT, hT_t, tt)
        proj_group(ctx, 1, wm, w_t, 128, 128, hT, hT_t, tt)
        proj_group(ctx, 2, wm, w_t, 256, 128, hT, hT_t, tt)
        proj_group(ctx, 3, wm, w_t, 384, 32, hT, hT_t, tt)
        for i in range(3):
            P.op("act", (lambda b=b, i=i: A_.activation(out=sq3[b][:, i, :], in_=banks[i][:, :], func=AF.Square)),
                 reads=[bank_tok[i]], writes=[sq3_t[b]])
        P.op("pe", _mm(nc, banks[4][:, :], ONES256, sq3[b][:, 0, :], True, False), reads=[sq3_t[b], cm_t], writes=[bank_tok[4]])
        P.op("pe", _mm(nc, banks[4][:, :], ONES256, sq3[b][:, 1, :], False, True), reads=[sq3_t[b], cm_t], writes=[bank_tok[4]])
        P.op("pe", _mm(nc, banks[5][:, :], ONES128, sq3[b][:, 2, :], True, True), reads=[sq3_t[b], cm_t], writes=[bank_tok[5]])
        for i, bk in ((0, 4), (1, 5)):
            P.op("act", (lambda b=b, i=i, bk=bk: A_.activation(out=rs2[b][:, i, :], in_=banks[bk][:, :], func=AF.Ln, bias=EPS)),
                 reads=[bank_tok[bk]], writes=[rs2_t[b]])
            P.op("act", (lambda b=b, i=i: A_.activation(out=rs2[b][:, i, :], in_=rs2[b][:, i, :], func=AF.Exp, scale=-0.5)),
                 reads=[rs2_t[b]], writes=[rs2_t[b]])
        for c in range(2):
            P.op("dve", (lambda b=b, c=c, cols=cols: V.scalar_tensor_tensor(CQN[:, c, cols], banks[c][:, :], pv[:, l, PV_CQG + c:PV_CQG + c + 1],
                                                                            rs2[b][:, 0, :], ALU.mult, ALU.mult)),
                 reads=[bank_tok[c], rs2_t[b], pv_t], writes=[lat_t[tt]])
        P.op("dve", (lambda b=b, cols=cols: V.scalar_tensor_tensor(CKVN[:, cols], banks[2][:, :], pv[:, l, PV_CKVG:PV_CKVG + 1],
                                                                   rs2[b][:, 1, :], ALU.mult, ALU.mult)),
             reads=[bank_tok[2], rs2_t[b], pv_t], writes=[lat_t[tt]])
        P.op("act", (lambda cols=cols: A_.copy(KR[0:32, cols], banks[3][0:32, :])), reads=[bank_tok[3]], writes=[lat_t[tt]])
    if env.get("stop") == "latent":
        return
    P.barrier()
    AR.release(m1_mark)
    tmpk, tmpk_t = _chain_tmp2(AR, tmp_t)
    QT = AR.alloc([S], BF16)
    KT = AR.alloc([S], BF16)
    QT_t = toks(8)
    KT_t = toks(8)
    VH = AR.alloc([32, 128], BF16)
    VH_t = Tok()
    P.op("dve", lambda: V.memset(VH[:, :, 64:128], 1.0), writes=[VH_t])
    Pb = [AR.alloc([512], BF16) for _ in range(4)]
    Pb_t = toks(4)
    gq = pv[:, l, PV_MQG:PV_MQG + 1]
    gk = pv[:, l, PV_MKG:PV_MKG + 1]
    SC = math.sqrt(96.0)
    cnt = 0
    csets = [(tmp, tmp_t), (tmpk, tmpk_t)] + [_chain_tmp2(AR, tmp_t) for _ in range(2)]
    for h in range(8):
        for t2 in range(0, 8, 2):
            chains = []
            for jj, tt in enumerate((t2, t2 + 1)):
                cols = slice(tt * 512, (tt + 1) * 512)
                for j in range(4):
                    P.op("pe", _mm(nc, banks[7][:, j * 64:(j + 1) * 64], CKVN[:, tt * 512 + j * 128:tt * 512 + (j + 1) * 128],
                                   wukv[:, h * 128 + 64:h * 128 + 128], True, True),
                         reads=[w_t, lat_t[tt]], writes=[bank_tok[7]])
                P.op("act", (lambda tt=tt: A_.copy(VH[:, tt * 4:(tt + 1) * 4, 0:64], banks[7][:, 0:256].rearrange("p (j c) -> p j c", j=4))),
                     reads=[bank_tok[7]], writes=[VH_t])
            for jj, tt in enumerate((t2, t2 + 1)):
                cols = slice(tt * 512, (tt + 1) * 512)
                bq, bk = 2 * jj, 2 * jj + 1
                for c in range(2):
                    P.op("pe", _mm(nc, banks[bq][0:96, :], wuq[:, c, h * 96:(h + 1) * 96], CQN[:, c, cols], c == 0, c == 1),
                         reads=[w_t, lat_t[tt]], writes=[bank_tok[bq]])
                P.op("pe", _mm(nc, banks[bk][0:96, :], wkp[:, h, :], CKVN[:, cols], True, False), reads=[wkp_t, lat_t[tt]], writes=[bank_tok[bk]])
                P.op("pe", _mm(nc, banks[bk][0:96, :], EPAD[0:32, 0:96], KR[0:32, cols], False, True), reads=[cm_t, lat_t[tt]], writes=[bank_tok[bk]])
                chains.append(dict(src=bq, np=96, ones=ONES96, eps=96 * EPS, g=gq, rot=RM_, tt=tt, out=QT[0:96, cols], out_tok=QT_t[tt],
                                   tmp=csets[2 * jj][0], tmp_t=csets[2 * jj][1], xb=4 + 2 * jj))
                chains.append(dict(src=bk, np=96, ones=ONES96, eps=96 * EPS, g=gk, rot=RM_, tt=tt, out=KT[0:96, cols], out_tok=KT_t[tt],
                                   tmp=csets[2 * jj + 1][0], tmp_t=csets[2 * jj + 1][1], xb=5 + 2 * jj))
            chain_group(ctx, chains, CT, ST)
        if env.get("stop") == "prep":
            return
        for qt in range(8):
            if env.get("stop") == "attn1" and qt == 1:
                return
            qcols = slice(qt * 512, (qt + 1) * 512)
            nkb = 4 * qt + 4
            SBK = (0, 1, 2, 3, 4, 5)
            base = cnt
            cnt += nkb
            for i in range(nkb + 3):
                if i < nkb:
                    kb = i
                    sb = SBK[(base + kb) % 6]
                    P.op("pe", _mm(nc, banks[sb][:, :], KT[0:96, kb * 128:(kb + 1) * 128], QT[0:96, qcols], True, True),
                         reads=[KT_t[kb // 4], QT_t[qt]], writes=[bank_tok[sb]])
                if 0 <= i - 2 < nkb:
                    kb = i - 2
                    sb = SBK[(base + kb) % 6]
                    pi = (base + kb) % 4
                    P.op("act", (lambda sb=sb, pi=pi: A_.activation(out=Pb[pi], in_=banks[sb][:, :], func=AF.Exp, scale=SC)),
                         reads=[bank_tok[sb]], writes=[Pb_t[pi]])
                    j = kb - 4 * qt
                    if j >= 0:
                        P.op("dve", (lambda pi=pi, j=j: V.tensor_tensor(out=Pb[pi], in0=Pb[pi], in1=cmask[:, j * 512:(j + 1) * 512], op=ALU.mult)),
                             reads=[Pb_t[pi], cmask_t], writes=[Pb_t[pi]])
                if 0 <= i - 3 < nkb:
                    kb = i - 3
                    pi = (base + kb) % 4
                    P.op("pe", _mm(nc, banks[6][:, :], VH[:, kb, :], Pb[pi], kb == 0, kb == nkb - 1),
                         reads=[VH_t, Pb_t[pi]], writes=[bank_tok[6]])
            P.op("act", lambda: A_.copy(tmp["den"][64:128, :], banks[6][64:128, :]), reads=[bank_tok[6]], writes=[tmp_t["den"]])
            finalize_head(ctx, banks[6][0:64, :], bank_tok[6], tmp["den"][64:128, :], tmp_t["den"], wgm, w_t, h * 64, hT, hT_t, qt,
                          ymla[h * 64:(h + 1) * 64, qcols], tmp, tmp_t)


def phase_dil(ctx, l, hT, hT_t):
    nc, P, AR = ctx["nc"], ctx["P"], ctx["AR"]
    V, A_, G_ = nc.vector, nc.scalar, nc.gpsimd
    env = ctx["env"]
    banks, bank_tok = ctx["banks"], ctx["bank_tok"]
    pv, pv_t, cm, cm_t = ctx["pv"], ctx["pv_t"], ctx["cm"], ctx["cm_t"]
    ONESB64, RD_ = cm[:, 3, :], cm[:, 5, :]
    cmask, cmask_t = ctx["cmask"], ctx["cmask_t"]
    DMASK = cmask[:, 2048:2048 + 256]
    ydil = env["ydil"]
    w_r = env["w_in"][l].rearrange("(c p) n -> p c n", p=128)
    CT = AR.alloc([S], BF16)
    ST = AR.alloc([S], BF16)
    tmp, tmp_t = _chain_tmp(AR)
    make_tables(ctx, 0, CT, ST, tmp_t["tab"])
    csets = [(tmp, tmp_t)] + [_chain_tmp2(AR, tmp_t) for _ in range(3)]
    wqk = [AR.alloc([8, 128], BF16) for _ in range(2)]
    wv = [AR.alloc([8, 64], BF16) for _ in range(2)]
    wgh_t = toks(2)
    wdg = AR.alloc([8, 512], BF16)
    wdg_t = Tok()
    P.dma("pool", lambda: G_.dma_start(out=wdg, in_=w_r[:, :, C_DG:C_DG + 512]), writes=[wdg_t])
    QK = AR.alloc([S], BF16)
    K0 = AR.alloc([S], BF16)
    QK_t = toks(8)
    K0_t = toks(8)
    VD = AR.alloc([32, 128], BF16)
    VD_t = Tok()
    P.op("dve", lambda: V.memset(VD[:, :, 64:128], 1.0), writes=[VD_t])
    OACC = AR.alloc([S], F32)
    OACC_t = Tok()
    Pd = [AR.alloc([256], BF16) for _ in range(4)]
    Pd_t = toks(4)
    gqk = pv[:, l, PV_DQKG:PV_DQKG + 1]
    it = 0
    cnt = 0
    for h in range(8):
        for g in range(3):
            d = DIL[g]
            nb = S // (128 * d)
            wb = it % 2
            it += 1
            hd = g * 8 + h
            P.dma("pool", (lambda wb=wb, hd=hd: G_.dma_start(out=wqk[wb][:, :, 0:64], in_=w_r[:, :, C_DQ + hd * 64:C_DQ + (hd + 1) * 64])), writes=[wgh_t[wb]])
            P.dma("pool", (lambda wb=wb, hd=hd: G_.dma_start(out=wqk[wb][:, :, 64:128], in_=w_r[:, :, C_DK + hd * 64:C_DK + (hd + 1) * 64])), writes=[wgh_t[wb]])
            P.dma("pool", (lambda wb=wb, hd=hd: G_.dma_start(out=wv[wb], in_=w_r[:, :, C_DV + hd * 64:C_DV + (hd + 1) * 64])), writes=[wgh_t[wb]])
            for t4 in range(0, 8, 4):
                chains = []
                for jj in range(4):
                    tt = t4 + jj
                    cols = slice(tt * 512, (tt + 1) * 512)
                    proj_group(ctx, jj, wqk[wb], wgh_t[wb], 0, 128, hT, hT_t, tt)
                    chains.append(dict(src=jj, np=128, ones=ONESB64, eps=EPS, g=gqk, rot=RD_, tt=tt, out=QK[:, cols], out_tok=QK_t[tt],
                                       tmp=csets[jj][0], tmp_t=csets[jj][1], xb=4 + jj))
                chain_group(ctx, chains, CT, ST)
                for jj in range(4):
                    tt = t4 + jj
                    cols = slice(tt * 512, (tt + 1) * 512)
                    P.op("dve", (lambda cols=cols: V.tensor_copy(K0[0:64, cols], QK[64:128, cols])), reads=[QK_t[tt]], writes=[K0_t[tt]])

            def ucols(r, n, d=d):
                st0 = r + d * 128 * n
                return slice(st0, st0 + d * 127 + 1, d)
            units = [(r, n) for r in range(d) for n in range(nb)]
            for u0 in range(0, 32, 4):
                for jj in range(4):
                    r, n = units[u0 + jj]
                    for k in range(8):
                        P.op("pe", _mm(nc, banks[7][:, jj * 64:(jj + 1) * 64], hT[:, k, ucols(r, n)], wv[wb][:, k, :], k == 0, k == 7),
                             reads=[wgh_t[wb]] + hT_t, writes=[bank_tok[7]])
                P.op("act", (lambda u0=u0: A_.copy(VD[:, u0:u0 + 4, 0:64], banks[7][:, 0:256].rearrange("p (j c) -> p j c", j=4))),
                     reads=[bank_tok[7]], writes=[VD_t])
            SBK = (0, 1, 2, 3, 4, 5)
            base = cnt
            cnt += 32
            for i in range(32 + 3):
                if i < 32:
                    u = i
                    r, n = units[u]
                    sb = SBK[(base + u) % 6]
                    qc = ucols(r, n)
                    if n > 0:
                        P.op("pe", _mm(nc, banks[sb][:, 0:128], K0[0:64, ucols(r, n - 1)], QK[0:64, qc], True, True),
                             reads=K0_t + QK_t, writes=[bank_tok[sb]])
                    P.op("pe", _mm(nc, banks[sb][:, 128:256], K0[0:64, qc], QK[0:64, qc], True, True),
                         reads=K0_t + QK_t, writes=[bank_tok[sb]])
                if 0 <= i - 2 < 32:
                    u = i - 2
                    r, n = units[u]
                    sb = SBK[(base + u) % 6]
                    pi = (base + u) % 4
                    lo = 0 if n > 0 else 128
                    P.op("act", (lambda sb=sb, pi=pi, lo=lo: A_.activation(out=Pd[pi][:, lo:256], in_=banks[sb][:, lo:256], func=AF.Exp, scale=0.125)),
                         reads=[bank_tok[sb]], writes=[Pd_t[pi]])
                    P.op("dve", (lambda pi=pi, lo=lo: V.tensor_tensor(out=Pd[pi][:, lo:256], in0=Pd[pi][:, lo:256], in1=DMASK[:, lo:256], op=ALU.mult)),
                         reads=[Pd_t[pi], cmask_t], writes=[Pd_t[pi]])
                if 0 <= i - 3 < 32:
                    u = i - 3
                    r, n = units[u]
                    pi = (base + u) % 4
                    qc = ucols(r, n)
                    ob = 6 + (base + u) % 2
                    if n > 0:
                        P.op("pe", _mm(nc, banks[ob][:, 0:128], VD[:, u - 1, :], Pd[pi][:, 0:128], True, False), reads=[VD_t, Pd_t[pi]], writes=[bank_tok[ob]])
                    P.op("pe", _mm(nc, banks[ob][:, 0:128], VD[:, u, :], Pd[pi][:, 128:256], n == 0, True), reads=[VD_t, Pd_t[pi]], writes=[bank_tok[ob]])
                    if g == 0:
                        P.op("dve", (lambda qc=qc, ob=ob: V.tensor_copy(OACC[:, qc], banks[ob][:, 0:128])), reads=[bank_tok[ob]], writes=[OACC_t])
                    else:
                        P.op("dve", (lambda qc=qc, ob=ob: V.tensor_tensor(out=OACC[:, qc], in0=banks[ob][:, 0:128], in1=OACC[:, qc], op=ALU.add)),
                             reads=[bank_tok[ob], OACC_t], writes=[OACC_t])
        for tt in range(8):
            cols = slice(tt * 512, (tt + 1) * 512)
            finalize_head(ctx, OACC[0:64, cols], OACC_t, OACC[64:128, cols], OACC_t, wdg, wdg_t, h * 64, hT, hT_t, tt,
                          ydil[h * 64:(h + 1) * 64, cols], tmp, tmp_t)


def phase_out(ctx, l, src, dst):
    nc, P, AR = ctx["nc"], ctx["P"], ctx["AR"]
    V, A_, G_ = nc.vector, nc.scalar, nc.gpsimd
    env = ctx["env"]
    banks, bank_tok = ctx["banks"], ctx["bank_tok"]
    TE = 256
    wlo = AR.alloc([8, 1024], BF16)
    wmo = AR.alloc([4, 1024], BF16)
    wdo = AR.alloc([4, 1024], BF16)
    wou = AR.alloc([8, 1024], BF16)
    w_t = Tok()
    for wt_, nm in ((wlo, "w_lru_o"), (wmo, "w_mla_o"), (wdo, "w_dil_o"), (wou, "w_out")):
        P.dma("pool", (lambda wt_=wt_, nm=nm: G_.dma_start(out=wt_, in_=env[nm][l].rearrange("(c p) n -> p c n", p=128))), writes=[w_t])
    YL = [AR.alloc([8, TE], BF16) for _ in range(2)]
    YM = [AR.alloc([4, TE], BF16) for _ in range(2)]
    YD = [AR.alloc([4, TE], BF16) for _ in range(2)]
    GT = [AR.alloc([24, TE], BF16) for _ in range(2)]
    XT = [AR.alloc([2, D], F32) for _ in range(2)]
    OUT = [AR.alloc([2, D], F32) for _ in range(2)]
    M = [AR.alloc([8, TE], BF16) for _ in range(2)]
    t1 = [AR.alloc([TE], F32) for _ in range(2)]
    t2 = [AR.alloc([TE], F32) for _ in range(2)]
    in_t = toks(2)
    out_t = toks(2)
    M_t = toks(2)
    t1_t = toks(2)
    t2_t = toks(2)
    ylru_r = env["ylru"].rearrange("(c p) t -> p c t", p=128)
    ymla_r = env["ymla"].rearrange("(c p) t -> p c t", p=128)
    ydil_r = env["ydil"].rearrange("(c p) t -> p c t", p=128)
    gts_r = env["gts"].rearrange("(c p) t -> p c t", p=128)
    is_final = dst is env["out_d"]
    k = 0
    pending = []
    for e in range(S // TE):
        b = e % 2
        tc_ = slice(e * TE, (e + 1) * TE)
        P.dma("sp", (lambda b=b, tc_=tc_: nc.sync.dma_start(out=YL[b], in_=ylru_r[:, :, tc_])), writes=[in_t[b]])
        P.dma("sp", (lambda b=b, tc_=tc_: nc.sync.dma_start(out=YM[b], in_=ymla_r[:, :, tc_])), writes=[in_t[b]])
        P.dma("sp", (lambda b=b, tc_=tc_: nc.sync.dma_start(out=YD[b], in_=ydil_r[:, :, tc_])), writes=[in_t[b]])
        P.dma("sp", (lambda b=b, tc_=tc_: nc.sync.dma_start(out=GT[b], in_=gts_r[:, :, tc_])), writes=[in_t[b]])
        P.dma("sp", (lambda b=b, e=e: nc.sync.dma_start(out=XT[b], in_=src[e * TE:(e + 1) * TE, :].rearrange("(j p) d -> p j d", p=128))), writes=[in_t[b]])
        for j in range(8):
            if j == 4 and pending:
                pending.pop(0)()
            js = slice(j * 128, (j + 1) * 128)
            kk = k % 2
            k += 1
            bl, bm, bd = 0 + kk, 2 + kk, 4 + kk
            for c in range(8):
                P.op("pe", _mm(nc, banks[bl][:, 0:TE], wlo[:, c, js], YL[b][:, c, :], c == 0, c == 7), reads=[w_t, in_t[b]], writes=[bank_tok[bl]])
            for c in range(4):
                P.op("pe", _mm(nc, banks[bm][:, 0:TE], wmo[:, c, js], YM[b][:, c, :], c == 0, c == 3), reads=[w_t, in_t[b]], writes=[bank_tok[bm]])
            for c in range(4):
                P.op("pe", _mm(nc, banks[bd][:, 0:TE], wdo[:, c, js], YD[b][:, c, :], c == 0, c == 3), reads=[w_t, in_t[b]], writes=[bank_tok[bd]])
            P.op("dve", (lambda b=b, j=j, kk=kk, bl=bl: V.tensor_tensor(out=t1[kk], in0=banks[bl][:, 0:TE], in1=GT[b][:, j, :], op=ALU.mult)),
                 reads=[bank_tok[bl], in_t[b]], writes=[t1_t[kk]])
            P.op("dve", (lambda b=b, j=j, kk=kk, bm=bm: V.tensor_tensor(out=t2[kk], in0=banks[bm][:, 0:TE], in1=GT[b][:, 8 + j, :], op=ALU.mult)),
                 reads=[bank_tok[bm], in_t[b]], writes=[t2_t[kk]])
            P.op("dve", (lambda kk=kk: V.tensor_tensor(out=t1[kk], in0=t1[kk], in1=t2[kk], op=ALU.add)),
                 reads=[t1_t[kk], t2_t[kk]], writes=[t1_t[kk]])
            P.op("dve", (lambda b=b, j=j, kk=kk, bd=bd: V.tensor_tensor(out=t2[kk], in0=banks[bd][:, 0:TE], in1=GT[b][:, 16 + j, :], op=ALU.mult)),
                 reads=[bank_tok[bd], in_t[b]], writes=[t2_t[kk]])
            P.op("dve", (lambda b=b, j=j, kk=kk: V.tensor_tensor(out=M[b][:, j, :], in0=t1[kk], in1=t2[kk], op=ALU.add)),
                 reads=[t1_t[kk], t2_t[kk]], writes=[M_t[b]])
        def wout(b=b, e=e):
            for s_ in range(2):
                for hf in range(2):
                    bo = 6 + (s_ * 2 + hf) % 2
                    for c in range(8):
                        P.op("pe", _mm(nc, banks[bo][:, :], M[b][:, c, s_ * 128:(s_ + 1) * 128], wou[:, c, hf * 512:(hf + 1) * 512], c == 0, c == 7),
                             reads=[w_t, M_t[b]], writes=[bank_tok[bo]])
                    P.op("dve", (lambda b=b, s_=s_, hf=hf, bo=bo: V.tensor_tensor(out=OUT[b][:, s_, hf * 512:(hf + 1) * 512], in0=banks[bo][:, :],
                                                                                in1=XT[b][:, s_, hf * 512:(hf + 1) * 512], op=ALU.add)),
                         reads=[bank_tok[bo], in_t[b]], writes=[out_t[b]])
            P.dma("sp", (lambda b=b, e=e: nc.sync.dma_start(out=dst[e * TE:(e + 1) * TE, :].rearrange("(j p) d -> p j d", p=128), in_=OUT[b])),
                  reads=[out_t[b]], is_output=is_final)
        pending.append(wout)
    while pending:
        pending.pop(0)()
```

```python
import math
import numpy as np
import ml_dtypes
import concourse.bass as bass
import concourse.mybir as mybir
from concourse.bass_utils import run_bass_kernel_spmd

F32 = mybir.dt.float32
BF16 = mybir.dt.bfloat16
I32 = mybir.dt.int32
AF = mybir.ActivationFunctionType
ALU = mybir.AluOpType

S = 4096
D = 1024
DEPTH = 2
INW = 11168
EPS = 1e-6
C_LX, C_LG, C_CQ, C_CKV, C_KR, C_MG, C_DQ, C_DK, C_DV, C_DG, C_MRG = (
    0, 1024, 2048, 2304, 2432, 2464, 2976, 4512, 6048, 7584, 8096)
DIL = (1, 4, 16)
NDMA_SLOTS = 12

PV_NORMG = 0
PV_CONVW = 8
PV_CONVB = 40
PV_BGX = 48
PV_BGA = 56
PV_LAM = 64
PV_CQG = 72
PV_CKVG = 74
PV_MQG = 75
PV_MKG = 76
PV_DQKG = 77
PV_BMRG = 78
NPV = 102


class Tok:
    __slots__ = ("w", "r", "excl")

    def __init__(self, excl=False):
        self.w = None
        self.r = []
        self.excl = excl


class Prog:
    ENG = ("pe", "act", "dve", "pool", "sp")

    def __init__(self, nc):
        self.nc = nc
        self.h = {"pe": nc.tensor, "act": nc.scalar, "dve": nc.vector, "pool": nc.gpsimd, "sp": nc.sync}
        self.ops = []
        self.seq = {e: 0 for e in self.ENG}
        self.wm = {e: {} for e in self.ENG}
        self.dma_rr = {"sp": 0, "pool": 0, "act": 0}
        self.dma_cnt = {}
        self.out_events = []

    def _deps(self, reads, writes, eng=None):
        deps = []
        for t in reads:
            if t.w is not None:
                deps.append(t.w)
            if t.excl:
                deps.extend(r for r in t.r if r[0] != eng)
        for t in writes:
            if t.w is not None:
                deps.append(t.w)
            deps.extend(t.r)
        return deps

    def _mk_waits(self, eng, deps, skip_same_pe=True):
        best = {}
        for ev in deps:
            key, val = ev
            if key == eng and eng == "pe":
                continue
            if val > best.get(key, 0):
                best[key] = val
        waits = []
        wm = self.wm[eng]
        for key, val in best.items():
            if wm.get(key, 0) >= val:
                continue
            wm[key] = val
            waits.append((key, val))
        return waits

    def op(self, eng, fn, reads=(), writes=()):
        deps = self._deps(reads, writes, eng)
        waits = self._mk_waits(eng, deps)
        self.seq[eng] += 1
        ev = (eng, self.seq[eng])
        self.ops.append(["c", eng, fn, waits, ev, False])
        for t in reads:
            if t.excl:
                t.r = [ev]
            else:
                t.r.append(ev)
        for t in writes:
            t.w = ev
            t.r = []
        return ev

    def dma(self, q, fn, reads=(), writes=(), is_output=False):
        deps = self._deps(reads, writes)
        slot = (q, self.dma_rr[q] % NDMA_SLOTS)
        self.dma_rr[q] += 1
        cnt = self.dma_cnt.get(slot, 0)
        if cnt > 0:
            deps.append((slot, cnt))
        waits = self._mk_waits(q, deps)
        self.dma_cnt[slot] = cnt + 1
        ev = (slot, cnt + 1)
        self.ops.append(["d", q, fn, waits, ev, True])
        for t in reads:
            t.r.append(ev)
        for t in writes:
            t.w = ev
            t.r = []
        if is_output:
            self.out_events.append(ev)
        return ev

    def barrier(self):
        deps = [(e, self.seq[e]) for e in ("pe", "act", "dve", "pool") if self.seq[e] > 0]
        deps += [(slot, cnt) for slot, cnt in self.dma_cnt.items()]
        waits = self._mk_waits("sp", deps)
        self.seq["sp"] += 1
        ev = ("sp", self.seq["sp"])
        self.ops.append(["c", "sp", lambda: self.nc.sync.nop(), waits, ev, False])
        for e in ("pe", "act", "dve", "pool"):
            w = self._mk_waits(e, [ev])
            self.ops.append(["w", e, None, w, None, False])

    def finish(self):
        deps = list(self.out_events)
        waits = self._mk_waits("sp", deps)
        self.ops.append(["w", "sp", None, waits, None, False])

    def emit(self, sems, dma_sems):
        targets = {e: set() for e in self.ENG}
        for o in self.ops:
            for key, val in o[3]:
                if isinstance(key, str):
                    targets[key].add(val)
        valmap = {}
        for e in self.ENG:
            c = 0
            m = {}
            for s in sorted(targets[e]):
                c += 1
                m[s] = c
            valmap[e] = m
        for kind, eng, fn, waits, ev, _ in self.ops:
            hnd = self.h[eng]
            for key, val in waits:
                if isinstance(key, str):
                    hnd.wait_ge(sems[key], valmap[key][val])
                else:
                    hnd.wait_ge(dma_sems[key], 16 * val)
            if kind == "w":
                continue
            ins = fn()
            if kind == "d":
                ins.then_inc(dma_sems[ev[0]], 16)
            else:
                if ev[1] in targets[eng]:
                    ins.then_inc(sems[eng], 1)


class Arena:
    def __init__(self, ap, nbytes):
        self.ap = ap
        self.nbytes = nbytes
        self.off = 0

    def alloc(self, shape, dt):
        esz = 4 if dt in (F32, I32) else 2
        n = 1
        for s_ in shape:
            n *= s_
        nb = (n * esz + 63) // 64 * 64
        assert self.off + nb <= self.nbytes, ("arena overflow", self.off, nb, self.nbytes)
        a = self.ap[:, self.off // 2:(self.off + n * esz) // 2]
        self.off += nb
        if dt != BF16:
            a = a.bitcast(dt)
        if len(shape) == 2:
            a = a.rearrange("p (a b) -> p a b", a=shape[0])
        elif len(shape) == 3:
            a = a.rearrange("p (a b c) -> p a b c", a=shape[0], b=shape[1])
        return a

    def mark(self):
        return self.off

    def release(self, m):
        self.off = m


def toks(n):
    return [Tok() for _ in range(n)]


def build_program(n_layers=DEPTH, stages=None, debug=False, extra_barriers=0, stop=None):
    nc = bass.Bass("TRN2", target_bir_lowering=False)
    P = Prog(nc)
    dram_in = {}

    def din(name, shape, dt=F32):
        dram_in[name] = nc.dram_tensor(name, list(shape), dt, kind="ExternalInput").ap()
        return dram_in[name]

    x_in = din("x", [S, D])
    pos_in = din("pos", [1, S], I32)
    w_in = din("w_in", [DEPTH, D, INW])
    w_gx = din("w_gate_x", [DEPTH, 8, 128, 128])
    w_ga = din("w_gate_a", [DEPTH, 8, 128, 128])
    w_lru_o = din("w_lru_o", [DEPTH, 1024, 1024])
    w_uq = din("w_uq", [DEPTH, 256, 768])
    w_ukv = din("w_ukv", [DEPTH, 128, 1024])
    w_mla_o = din("w_mla_o", [DEPTH, 512, 1024])
    w_dil_o = din("w_dil_o", [DEPTH, 512, 1024])
    w_out = din("w_out", [DEPTH, 1024, 1024])
    pvec_in = din("pvec", [DEPTH, 128, NPV])
    cmat_in = din("cmat", [8, 128, 128])
    cmask_in = din("cmask", [128, 4 * 512 + 256])
    cinv_in = din("cinv", [128, 2])
    out_d = nc.dram_tensor("out", [S, D], F32, kind="ExternalOutput").ap()
    sk = "ExternalOutput" if debug else "Internal"
    xres = nc.dram_tensor("xres", [S, D], F32, kind=sk).ap()
    ylru = nc.dram_tensor("ylru", [1024, S], BF16, kind=sk).ap()
    ymla = nc.dram_tensor("ymla", [512, S], BF16, kind=sk).ap()
    ydil = nc.dram_tensor("ydil", [512, S], BF16, kind=sk).ap()
    gts = nc.dram_tensor("gts", [3072, S], BF16, kind=sk).ap()
    dbg = {}
    if debug:
        dbg["hT"] = nc.dram_tensor("hT_dbg", [128, 8 * S], BF16, kind="ExternalOutput").ap()

    ARENA_BYTES = 204 * 1024
    import contextlib
    with contextlib.ExitStack() as es:
        arena_t = es.enter_context(nc.sbuf_tensor("arena", [128, ARENA_BYTES // 2], BF16))
        banks = [es.enter_context(nc.psum_tensor(f"ps{i}", [128, 512], F32)) for i in range(8)]
        sems = {e: es.enter_context(nc.semaphore(f"s_{e}")) for e in Prog.ENG}
        dma_sems = {}
        for q in ("sp", "pool"):
            for i in range(NDMA_SLOTS):
                dma_sems[(q, i)] = es.enter_context(nc.semaphore(f"d_{q}{i}"))
        es.enter_context(nc.Block())
        es.enter_context(nc.allow_low_precision("bf16 PE transposes / bf16 matmul operands with fp32 accumulation"))
        AR = Arena(arena_t, ARENA_BYTES)
        bank_tok = [Tok(excl=True) for _ in range(8)]
        _emit_all(nc, P, AR, banks, bank_tok, locals(), n_layers, dbg)
        P.finish()
        P.emit(sems, dma_sems)
    return nc


def _emit_all(nc, P, AR, banks, bank_tok, env, n_layers, dbg):
    x_in = env["x_in"]; pos_in = env["pos_in"]; w_in = env["w_in"]
    out_d = env["out_d"]; xres = env["xres"]
    pvec_in = env["pvec_in"]; cmat_in = env["cmat_in"]; cmask_in = env["cmask_in"]; cinv_in = env["cinv_in"]
    V, A_, T_, G_ = nc.vector, nc.scalar, nc.tensor, nc.gpsimd

    cm = AR.alloc([8, 128], BF16)
    cm_t = Tok()
    P.dma("pool", lambda: G_.dma_start(out=cm, in_=cmat_in.rearrange("m p c -> p m c")), writes=[cm_t])
    IDENT, ONES256, ONES128, ONESB64, ONES96, RD_, RM_, EPAD = [cm[:, i, :] for i in range(8)]
    cmask = AR.alloc([4 * 512 + 256], BF16)
    cmask_t = Tok()
    P.dma("pool", lambda: G_.dma_start(out=cmask, in_=cmask_in), writes=[cmask_t])
    cinv = AR.alloc([2], F32)
    cinv_t = Tok()
    P.dma("sp", lambda: nc.sync.dma_start(out=cinv, in_=cinv_in), writes=[cinv_t])
    pv = AR.alloc([DEPTH, NPV], F32)
    pv_t = Tok()
    P.dma("sp", lambda: nc.sync.dma_start(out=pv, in_=pvec_in.rearrange("l p c -> p l c")), writes=[pv_t])
    hbgx = AR.alloc([DEPTH, 8], F32)
    hbga = AR.alloc([DEPTH, 8], F32)
    hc = AR.alloc([DEPTH, 8], F32)
    nhc = AR.alloc([DEPTH, 8], F32)
    hbm = AR.alloc([DEPTH, 24], F32)
    tmpv = AR.alloc([DEPTH, 8], F32)
    der_t = Tok()
    P.op("dve", lambda: V.tensor_scalar_mul(hbgx, pv[:, :, PV_BGX:PV_BGX + 8], -1.0), reads=[pv_t], writes=[der_t])
    P.op("dve", lambda: V.tensor_scalar_mul(hbga, pv[:, :, PV_BGA:PV_BGA + 8], -1.0), reads=[pv_t], writes=[der_t])
    P.op("dve", lambda: V.tensor_scalar_mul(hbm, pv[:, :, PV_BMRG:PV_BMRG + 24], -1.0), reads=[pv_t], writes=[der_t])
    P.op("act", lambda: A_.activation(out=tmpv, in_=pv[:, :, PV_LAM:PV_LAM + 8], func=AF.Exp, scale=-1.0),
         reads=[pv_t], writes=[der_t])
    P.op("act", lambda: A_.activation(out=tmpv, in_=tmpv, func=AF.Ln, bias=1.0, scale=1.0),
         reads=[der_t], writes=[der_t])
    P.op("dve", lambda: V.tensor_scalar_mul(hc, tmpv, -8.0), reads=[der_t], writes=[der_t])

    perm_mark = AR.mark()
    ctx = dict(nc=nc, P=P, AR=AR, banks=banks, bank_tok=bank_tok, env=env, dbg=dbg,
               cm=cm, cm_t=cm_t, cmask=cmask, cmask_t=cmask_t, cinv=cinv, cinv_t=cinv_t,
               pv=pv, pv_t=pv_t, der_t=der_t, hbgx=hbgx, hbga=hbga, hc=hc, nhc=nhc, hbm=hbm)
    for l in range(n_layers):
        src = x_in if l == 0 else xres
        dst = out_d if l == n_layers - 1 else xres
        AR.release(perm_mark)
        P.barrier()
        hT = AR.alloc([8, S], BF16)
        hT_t = toks(8)
        layer_mark = AR.mark()
        phase_norm(ctx, l, src, hT, hT_t)
        if dbg and l == 0:
            P.dma("sp", lambda: nc.sync.dma_start(out=dbg["hT"].rearrange("p (c t) -> p c t", c=8), in_=hT), reads=hT_t)
        stages = env.get("stages") or ("gates", "lru", "mla", "dil", "out")
        if "gates" in stages:
            P.barrier(); AR.release(layer_mark)
            phase_gates(ctx, l, hT, hT_t)
        if "lru" in stages:
            P.barrier(); AR.release(layer_mark)
            phase_lru(ctx, l, hT, hT_t)
        if "mla" in stages:
            P.barrier(); AR.release(layer_mark)
            phase_mla(ctx, l, hT, hT_t)
        if "dil" in stages:
            P.barrier(); AR.release(layer_mark)
            phase_dil(ctx, l, hT, hT_t)
        if "out" in stages:
            P.barrier(); AR.release(perm_mark)
            phase_out(ctx, l, src, dst)
    for _ in range(ctx["env"].get("extra_barriers", 0)):
        P.barrier()
    P.barrier()


def _mm(nc, out, lhsT, rhs, start, stop):
    return lambda: nc.tensor.matmul(out, lhsT, rhs, start=start, stop=stop)


def proj_group(ctx, bank_i, wtile, w_t, c0, m, hT, hT_t, tt, rows=None):
    nc, P = ctx["nc"], ctx["P"]
    bank = ctx["banks"][bank_i]
    bt = ctx["bank_tok"][bank_i]
    for k in range(8):
        P.op("pe", _mm(nc, bank[0:m, :], wtile[:, k, c0:c0 + m], hT[:, k, tt * 512:(tt + 1) * 512], k == 0, k == 7),
             reads=[w_t, hT_t[tt]], writes=[bt])


def phase_norm(ctx, l, src, hT, hT_t):
    nc, P, AR = ctx["nc"], ctx["P"], ctx["AR"]
    V, A_, T_ = nc.vector, nc.scalar, nc.tensor
    banks, bank_tok = ctx["banks"], ctx["bank_tok"]
    pv, pv_t = ctx["pv"], ctx["pv_t"]
    IDENT = ctx["cm"][:, 0, :]
    xt = [AR.alloc([4, D], F32) for _ in range(2)]
    xt_t = toks(2)
    xn = [AR.alloc([D], BF16) for _ in range(2)]
    xn_t = toks(2)
    ss = [AR.alloc([4], F32) for _ in range(2)]
    ss_t = toks(2)
    junk = AR.alloc([D], BF16)
    gB = pv[:, l, PV_NORMG:PV_NORMG + 8].unsqueeze(2).to_broadcast([128, 8, 128])
    for g4 in range(8):
        b = g4 % 2
        P.dma("sp", (lambda b=b, g4=g4: nc.sync.dma_start(
            out=xt[b], in_=src[g4 * 512:(g4 + 1) * 512, :].rearrange("(j p) d -> p j d", p=128))),
            writes=[xt_t[b]])
        P.op("dve", (lambda b=b: V.memset(ss[b], 0.0)), writes=[ss_t[b]])
        for j in range(4):
            P.op("act", (lambda b=b, j=j: A_.activation(out=junk, in_=xt[b][:, j, :], func=AF.Square,
                                                         accum_out=ss[b][:, j:j + 1])),
                 reads=[xt_t[b]], writes=[ss_t[b]])
        P.op("act", (lambda b=b: A_.activation(out=ss[b], in_=ss[b], func=AF.Ln, bias=EPS, scale=1.0 / D)),
             reads=[ss_t[b]], writes=[ss_t[b]])
        P.op("act", (lambda b=b: A_.activation(out=ss[b], in_=ss[b], func=AF.Exp, scale=-0.5)),
             reads=[ss_t[b]], writes=[ss_t[b]])
        for j in range(4):
            kb = (g4 * 4 + j) % 2
            P.op("dve", (lambda b=b, j=j, kb=kb: V.tensor_scalar(xn[kb], xt[b][:, j, :], ss[b][:, j:j + 1], None, ALU.mult)),
                 reads=[xt_t[b], ss_t[b]], writes=[xn_t[kb]])
            bbf = banks[kb][:, :].bitcast(BF16)
            for c in range(8):
                P.op("pe", (lambda kb=kb, c=c, bbf=bbf: T_.transpose(bbf[:, c * 128:(c + 1) * 128],
                                                                     xn[kb][:, c * 128:(c + 1) * 128], IDENT)),
                     reads=[xn_t[kb], ctx["cm_t"]], writes=[bank_tok[kb]])
            t0 = g4 * 512 + j * 128
            P.op("dve", (lambda bbf=bbf, t0=t0: V.tensor_tensor(
                out=hT[:, :, t0:t0 + 128], in0=bbf.rearrange("p (c t) -> p c t", c=8), in1=gB, op=ALU.mult)),
                reads=[bank_tok[kb], pv_t], writes=[hT_t[g4]])


def phase_lru(ctx, l, hT, hT_t):
    nc, P, AR = ctx["nc"], ctx["P"], ctx["AR"]
    V, A_, T_, G_ = nc.vector, nc.scalar, nc.tensor, nc.gpsimd
    env = ctx["env"]
    banks, bank_tok = ctx["banks"], ctx["bank_tok"]
    pv, pv_t, der_t = ctx["pv"], ctx["pv_t"], ctx["der_t"]
    w_in, ylru = env["w_in"], env["ylru"]
    wl = [AR.alloc([8, 256], BF16) for _ in range(2)]
    wl_t = toks(2)
    wgx = AR.alloc([8, 128], BF16)
    wga = AR.alloc([8, 128], BF16)
    wg_t = Tok()
    P.dma("pool", lambda: G_.dma_start(out=wgx, in_=env["w_gx"][l].rearrange("n c d -> c n d")), writes=[wg_t])
    P.dma("pool", lambda: G_.dma_start(out=wga, in_=env["w_ga"][l].rearrange("n c d -> c n d")), writes=[wg_t])
    X = AR.alloc([3 + S], F32)
    X_t = toks(8)
    Xh_t = Tok()
    names = ("G", "XC", "TGX", "TGA", "A", "T", "A2", "H", "SG")
    NSET = 4
    buf = [{nm: AR.alloc([512], F32) for nm in names} for _ in range(NSET)]
    for p_ in range(NSET):
        buf[p_]["XCb"] = AR.alloc([512], BF16)
        buf[p_]["Yb"] = AR.alloc([512], BF16)
    bt = [{nm: Tok() for nm in buf[0]} for _ in range(NSET)]
    P.op("dve", lambda: V.memset(X[:, 0:3], 0.0), writes=[Xh_t])
    w_r = w_in[l].rearrange("(c p) n -> p c n", p=128)
    it = 0
    for n in range(8):
        wb = n % 2
        P.dma("pool", (lambda wb=wb, n=n: G_.dma_start(out=wl[wb][:, :, 0:128], in_=w_r[:, :, C_LX + n * 128:C_LX + (n + 1) * 128])),
              writes=[wl_t[wb]])
        P.dma("pool", (lambda wb=wb, n=n: G_.dma_start(out=wl[wb][:, :, 128:256], in_=w_r[:, :, C_LG + n * 128:C_LG + (n + 1) * 128])),
              writes=[wl_t[wb]])
        cw = lambda k, n=n: pv[:, l, PV_CONVW + n * 4 + k:PV_CONVW + n * 4 + k + 1]
        cb = pv[:, l, PV_CONVB + n:PV_CONVB + n + 1]
        hbx = ctx["hbgx"][:, l, n:n + 1]
        hba = ctx["hbga"][:, l, n:n + 1]
        hcn = ctx["hc"][:, l, n:n + 1]
        nhcn = ctx["nhc"][:, l, n:n + 1]
        def chunk(tt, par, pb, n=n, wb=wb, cw=cw, cb=cb, hbx=hbx, hba=hba, hcn=hcn):
            B_, Bt = buf[par], bt[par]
            c0 = 3 + tt * 512
            bx, bg, bgx, bga = pb, 2 + pb, 4 + pb, 6 + pb
            proj_group(ctx, bx, wl[wb], wl_t[wb], 0, 128, hT, hT_t, tt)
            P.op("act", lambda: A_.copy(X[:, c0:c0 + 512], banks[bx][:, :]), reads=[bank_tok[bx]], writes=[X_t[tt]])
            yield
            proj_group(ctx, bg, wl[wb], wl_t[wb], 128, 128, hT, hT_t, tt)
            P.op("act", lambda: A_.copy(B_["G"], banks[bg][:, :]), reads=[bank_tok[bg]], writes=[Bt["G"]])
            yield
            P.op("act", lambda: A_.activation(out=B_["SG"], in_=B_["G"], func=AF.Exp, scale=-1.0), reads=[Bt["G"]], writes=[Bt["SG"]])
            xprev = Xh_t if tt == 0 else X_t[tt - 1]
            P.op("dve", lambda: V.tensor_scalar(B_["XC"], X[:, c0:c0 + 512], cw(3), cb, ALU.mult, ALU.add),
                 reads=[X_t[tt], pv_t], writes=[Bt["XC"]])
            for k in (2, 1, 0):
                P.op("dve", (lambda k=k: V.scalar_tensor_tensor(B_["XC"], X[:, c0 - 3 + k:c0 - 3 + k + 512], cw(k), B_["XC"], ALU.mult, ALU.add)),
                     reads=[X_t[tt], xprev, Bt["XC"]], writes=[Bt["XC"]])
            yield
            P.op("act", lambda: A_.activation(out=B_["SG"], in_=B_["SG"], func=AF.Ln, bias=1.0), reads=[Bt["SG"]], writes=[Bt["SG"]])
            P.op("act", lambda: A_.copy(B_["XCb"], B_["XC"]), reads=[Bt["XC"]], writes=[Bt["XCb"]])
            yield
            P.op("pe", _mm(nc, banks[bgx][:, :], wgx[:, n, :], B_["XCb"], True, True), reads=[wg_t, Bt["XCb"]], writes=[bank_tok[bgx]])
            P.op("pe", _mm(nc, banks[bga][:, :], wga[:, n, :], B_["XCb"], True, True), reads=[wg_t, Bt["XCb"]], writes=[bank_tok[bga]])
            P.op("act", lambda: A_.activation(out=B_["SG"], in_=B_["SG"], func=AF.Exp, scale=-1.0), reads=[Bt["SG"]], writes=[Bt["SG"]])
            yield
            for nm, bk, hb_ in (("TGX", bgx, hbx), ("TGA", bga, hba)):
                P.op("act", (lambda nm=nm, bk=bk, hb_=hb_: A_.activation(out=B_[nm], in_=banks[bk][:, :], func=AF.Exp, bias=hb_, scale=-1.0)),
                     reads=[bank_tok[bk], der_t], writes=[Bt[nm]])
            P.op("dve", lambda: V.tensor_tensor(out=B_["SG"], in0=B_["SG"], in1=B_["G"], op=ALU.mult), reads=[Bt["SG"], Bt["G"]], writes=[Bt["SG"]])
            yield
            for nm in ("TGX", "TGA"):
                P.op("act", (lambda nm=nm: A_.activation(out=B_[nm], in_=B_[nm], func=AF.Ln, bias=1.0)), reads=[Bt[nm]], writes=[Bt[nm]])
            yield
            for nm in ("TGX", "TGA"):
                P.op("act", (lambda nm=nm: A_.activation(out=B_[nm], in_=B_[nm], func=AF.Exp, scale=-1.0)), reads=[Bt[nm]], writes=[Bt[nm]])
            yield
            P.op("act", lambda: A_.activation(out=B_["A"], in_=B_["TGA"], func=AF.Exp, scale=hcn), reads=[Bt["TGA"], der_t], writes=[Bt["A"]])
            P.op("act", lambda: A_.activation(out=B_["A2"], in_=B_["A"], func=AF.Square), reads=[Bt["A"]], writes=[Bt["A2"]])
            P.op("dve", lambda: V.tensor_tensor(out=B_["TGX"], in0=B_["TGX"], in1=B_["XC"], op=ALU.mult), reads=[Bt["TGX"], Bt["XC"]], writes=[Bt["TGX"]])
            yield
            P.op("dve", lambda: V.tensor_scalar(B_["T"], B_["A2"], -1.0, 1.0, ALU.mult, ALU.add), reads=[Bt["A2"]], writes=[Bt["T"]])
            yield
            P.op("act", lambda: A_.activation(out=B_["T"], in_=B_["T"], func=AF.Ln), reads=[Bt["T"]], writes=[Bt["T"]])
            yield
            P.op("act", lambda: A_.activation(out=B_["T"], in_=B_["T"], func=AF.Exp, scale=0.5), reads=[Bt["T"]], writes=[Bt["T"]])
            yield
            P.op("dve", lambda: V.tensor_tensor(out=B_["T"], in0=B_["T"], in1=B_["TGX"], op=ALU.mult), reads=[Bt["T"], Bt["TGX"]], writes=[Bt["T"]])
            if tt == 0:
                P.op("dve", lambda: V.tensor_tensor_scan(B_["H"], B_["A"], B_["T"], 0.0, ALU.mult, ALU.add), reads=[Bt["A"], Bt["T"]], writes=[Bt["H"]])
            else:
                Bp, Bpt = buf[(par - 1) % NSET], bt[(par - 1) % NSET]
                P.op("dve", lambda: V.tensor_tensor_scan(B_["H"], B_["A"], B_["T"], Bp["H"][:, 511:512], ALU.mult, ALU.add),
                     reads=[Bt["A"], Bt["T"], Bpt["H"]], writes=[Bt["H"]])
            P.op("dve", lambda: V.tensor_tensor(out=B_["Yb"], in0=B_["H"], in1=B_["SG"], op=ALU.mult), reads=[Bt["H"], Bt["SG"]], writes=[Bt["Yb"]])
            P.dma("sp", lambda: nc.sync.dma_start(out=ylru[n * 128:(n + 1) * 128, tt * 512:(tt + 1) * 512], in_=B_["Yb"]), reads=[Bt["Yb"]])

        for tp in range(0, 8, 2):
            gens = []
            for tt in (tp, tp + 1):
                gens.append(chunk(tt, it % NSET, it % 2))
                it += 1
            live = list(gens)
            while live:
                for g_ in list(live):
                    try:
                        next(g_)
                    except StopIteration:
                        live.remove(g_)


def phase_gates(ctx, l, hT, hT_t):
    nc, P, AR = ctx["nc"], ctx["P"], ctx["AR"]
    A_, G_ = nc.scalar, nc.gpsimd
    env = ctx["env"]
    banks, bank_tok = ctx["banks"], ctx["bank_tok"]
    w_r = env["w_in"][l].rearrange("(c p) n -> p c n", p=128)
    gts = env["gts"]
    wt = [AR.alloc([8, 128], BF16) for _ in range(2)]
    wt_t = toks(2)
    st = [AR.alloc([S], BF16) for _ in range(2)]
    st_t = toks(2)
    ebuf = [AR.alloc([512], F32) for _ in range(2)]
    ebuf_t = toks(2)
    for ch in range(24):
        b = ch % 2
        P.dma("pool", (lambda b=b, ch=ch: G_.dma_start(out=wt[b], in_=w_r[:, :, C_MRG + ch * 128:C_MRG + (ch + 1) * 128])),
              writes=[wt_t[b]])
        hb = ctx["hbm"][:, l, ch:ch + 1]
        for tt in range(8):
            bi = (ch * 8 + tt) % 4
            proj_group(ctx, bi, wt[b], wt_t[b], 0, 128, hT, hT_t, tt)
            eb = (ch * 8 + tt) % 2
            P.op("act", (lambda bi=bi, eb=eb, hb=hb: A_.activation(out=ebuf[eb], in_=banks[bi][:, :], func=AF.Exp, bias=hb, scale=-1.0)),
                 reads=[bank_tok[bi], ctx["der_t"]], writes=[ebuf_t[eb]])
            P.op("act", (lambda eb=eb: A_.activation(out=ebuf[eb], in_=ebuf[eb], func=AF.Ln, bias=1.0)), reads=[ebuf_t[eb]], writes=[ebuf_t[eb]])
            P.op("act", (lambda b=b, eb=eb, tt=tt: A_.activation(out=st[b][:, tt * 512:(tt + 1) * 512], in_=ebuf[eb], func=AF.Exp, scale=-1.0)),
                 reads=[ebuf_t[eb]], writes=[st_t[b]])
        P.dma("sp", (lambda b=b, ch=ch: nc.sync.dma_start(out=gts[ch * 128:(ch + 1) * 128, :], in_=st[b])),
              reads=[st_t[b]])
def _host_consts():
    cmat = np.zeros((8, 128, 128), np.float32)
    cmat[0] = np.eye(128, dtype=np.float32)
    cmat[1] = 1.0 / 256
    cmat[2] = 1.0 / 128
    cmat[3, 0:64, 0:64] = 1.0 / 64
    cmat[3, 64:128, 64:128] = 1.0 / 64
    cmat[4, 0:96, 0:96] = 1.0
    for blk in (0, 64):
        for i in range(32):
            cmat[5, blk + i + 32, blk + i] = -1.0
            cmat[5, blk + i, blk + i + 32] = 1.0
    for i in range(16):
        cmat[6, 64 + i + 16, 64 + i] = -1.0
        cmat[6, 64 + i, 64 + i + 16] = 1.0
    for k in range(32):
        cmat[7, k, 64 + k] = 1.0
    cmask = np.zeros((128, 4 * 512 + 256), np.float32)
    kk = np.arange(128)[:, None]
    qq = np.arange(512)[None, :]
    for j in range(4):
        cmask[:, j * 512:(j + 1) * 512] = (qq >= 128 * j + kk)
    q1 = np.arange(128)[None, :]
    cmask[:, 2048:2048 + 128] = (kk >= q1)
    cmask[:, 2048 + 128:2048 + 256] = (kk <= q1)
    cinv = np.zeros((128, 2), np.float32)
    theta = np.float32(10000.0)
    inv64 = theta ** (-np.arange(0, 64, 2, dtype=np.float32) / np.float32(64))
    inv32 = theta ** (-np.arange(0, 32, 2, dtype=np.float32) / np.float32(32))
    for p in range(128):
        cinv[p, 0] = inv64[p % 32]
        if 64 <= p < 96:
            cinv[p, 1] = inv32[(p - 64) % 16]
    return cmat, cmask, cinv


def _host_pvec(inp):
    pv = np.zeros((DEPTH, 128, NPV), np.float32)
    for l in range(DEPTH):
        pv[l, :, PV_NORMG:PV_NORMG + 8] = inp["norm_g"][l].reshape(8, 128).T
        pv[l, :, PV_CONVW:PV_CONVW + 32] = inp["conv_w"][l].reshape(4, 8, 128).transpose(2, 1, 0).reshape(128, 32)
        pv[l, :, PV_CONVB:PV_CONVB + 8] = inp["conv_b"][l].reshape(8, 128).T
        pv[l, :, PV_BGX:PV_BGX + 8] = inp["b_gate_x"][l].T
        pv[l, :, PV_BGA:PV_BGA + 8] = inp["b_gate_a"][l].T
        pv[l, :, PV_LAM:PV_LAM + 8] = inp["lru_lambda"][l].reshape(8, 128).T
        pv[l, :, PV_CQG:PV_CQG + 2] = inp["cq_norm_g"][l].reshape(2, 128).T
        pv[l, :, PV_CKVG] = inp["ckv_norm_g"][l]
        pv[l, 0:96, PV_MQG] = inp["mla_q_norm_g"][l]
        pv[l, 0:96, PV_MKG] = inp["mla_k_norm_g"][l]
        pv[l, 0:64, PV_DQKG] = inp["dil_q_norm_g"][l]
        pv[l, 64:128, PV_DQKG] = inp["dil_k_norm_g"][l]
        pv[l, :, PV_BMRG:PV_BMRG + 24] = inp["b_merge"][l].reshape(24, 128).T
    return pv


def make_in_maps(inp, n_cores=8):
    cmat, cmask, cinv = _host_consts()
    pvec = _host_pvec(inp)
    shared = {
        "w_in": np.ascontiguousarray(inp["w_in"]), "w_gate_x": np.ascontiguousarray(inp["w_gate_x"]),
        "w_gate_a": np.ascontiguousarray(inp["w_gate_a"]), "w_lru_o": np.ascontiguousarray(inp["w_lru_o"]),
        "w_uq": np.ascontiguousarray(inp["w_uq"]), "w_ukv": np.ascontiguousarray(inp["w_ukv"]),
        "w_mla_o": np.ascontiguousarray(inp["w_mla_o"]), "w_dil_o": np.ascontiguousarray(inp["w_dil_o"]),
        "w_out": np.ascontiguousarray(inp["w_out"]), "pvec": pvec, "cmat": cmat, "cmask": cmask, "cinv": cinv,
    }
    maps = []
    for b in range(n_cores):
        m = dict(shared)
        m["x"] = np.ascontiguousarray(inp["x"][b])
        m["pos"] = np.ascontiguousarray(inp["positions"][b].reshape(1, S).astype(np.int32))
        maps.append(m)
    return maps


def kernel(**inputs):
    inp = {k: np.asarray(v) for k, v in inputs.items()}
    nc = build_program()
    maps = make_in_maps(inp, 8)
    res = run_bass_kernel_spmd(nc, maps, core_ids=list(range(8)))
    out = np.stack([np.asarray(res.results[b]["out"]).reshape(S, D) for b in range(8)], axis=0)
    return out.astype(np.float32)


def make_tables(ctx, which, CT, ST, tab_t):
    nc, P, AR = ctx["nc"], ctx["P"], ctx["AR"]
    V, A_ = nc.vector, nc.scalar
    pos_in = ctx["env"]["pos_in"]
    inv = ctx["cinv"][:, which:which + 1]
    m = AR.mark()
    posi = [AR.alloc([1024], I32) for _ in range(2)]
    y0 = [AR.alloc([1024], F32) for _ in range(2)]
    yy = [AR.alloc([1024], F32) for _ in range(2)]
    ki = [AR.alloc([1024], I32) for _ in range(2)]
    kf = [AR.alloc([1024], F32) for _ in range(2)]
    tk = [{n_: Tok() for n_ in ("posi", "y0", "yy", "ki", "kf")} for _ in range(2)]
    TWO_PI = 2.0 * math.pi * (1.0 - 1e-6)
    for c in range(4):
        b = c % 2
        t = tk[b]
        P.dma("sp", (lambda b=b, c=c: nc.sync.dma_start(out=posi[b], in_=pos_in[:, c * 1024:(c + 1) * 1024].broadcast_to([128, 1024]))),
              writes=[t["posi"]])
        P.op("dve", (lambda b=b: V.tensor_copy(y0[b], posi[b])), reads=[t["posi"]], writes=[t["y0"]])
        P.op("dve", (lambda b=b: V.tensor_scalar(y0[b], y0[b], inv, None, ALU.mult)), reads=[t["y0"], ctx["cinv_t"]], writes=[t["y0"]])
        P.op("dve", (lambda b=b: V.tensor_scalar(y0[b], y0[b], 1.0 / (2.0 * math.pi), None, ALU.mult)), reads=[t["y0"]], writes=[t["y0"]])
        for ph, OUT in ((0.0, ST), (0.25, CT)):
            P.op("dve", (lambda b=b, ph=ph: V.tensor_scalar(yy[b], y0[b], ph, None, ALU.add)), reads=[t["y0"]], writes=[t["yy"]])
            P.op("dve", (lambda b=b: V.tensor_copy(ki[b], yy[b])), reads=[t["yy"]], writes=[t["ki"]])
            P.op("dve", (lambda b=b: V.tensor_copy(kf[b], ki[b])), reads=[t["ki"]], writes=[t["kf"]])
            P.op("dve", (lambda b=b: V.tensor_tensor(out=yy[b], in0=yy[b], in1=kf[b], op=ALU.subtract)), reads=[t["yy"], t["kf"]], writes=[t["yy"]])
            P.op("dve", (lambda b=b: V.tensor_scalar(kf[b], yy[b], 0.5, None, ALU.is_gt)), reads=[t["yy"]], writes=[t["kf"]])
            P.op("dve", (lambda b=b: V.tensor_tensor(out=yy[b], in0=yy[b], in1=kf[b], op=ALU.subtract)), reads=[t["yy"], t["kf"]], writes=[t["yy"]])
            P.op("act", (lambda b=b, OUT=OUT, c=c: A_.activation(out=OUT[:, c * 1024:(c + 1) * 1024], in_=yy[b], func=AF.Sin, scale=TWO_PI)),
                 reads=[t["yy"]], writes=[tab_t])
    P.barrier()
    AR.release(m)


def norm_rope_chain(ctx, src_bank, np_, ones_m, eps_v, gcol, CT, ST, rot_m, tt, out_ap, out_tok, tmp, tmp_t, bank_ss, bank_rot):
    nc, P = ctx["nc"], ctx["P"]
    V, A_ = nc.vector, nc.scalar
    banks, bank_tok = ctx["banks"], ctx["bank_tok"]
    IDENT = ctx["cm"][:, 0, :]
    src = banks[src_bank][0:np_, :]
    st_ = bank_tok[src_bank]
    cols = slice(tt * 512, (tt + 1) * 512)
    P.op("act", lambda: A_.activation(out=tmp["sq"][0:np_, :], in_=src, func=AF.Square), reads=[st_], writes=[tmp_t["sq"]])
    P.op("pe", _mm(nc, banks[bank_ss][0:np_, :], ones_m[0:np_, 0:np_], tmp["sq"][0:np_, :], True, True),
         reads=[tmp_t["sq"], ctx["cm_t"]], writes=[bank_tok[bank_ss]])
    P.op("act", lambda: A_.activation(out=tmp["rs"][0:np_, :], in_=banks[bank_ss][0:np_, :], func=AF.Ln, bias=eps_v),
         reads=[bank_tok[bank_ss]], writes=[tmp_t["rs"]])
    P.op("act", lambda: A_.activation(out=tmp["rs"][0:np_, :], in_=tmp["rs"][0:np_, :], func=AF.Exp, scale=-0.5),
         reads=[tmp_t["rs"]], writes=[tmp_t["rs"]])
    P.op("dve", lambda: V.scalar_tensor_tensor(tmp["u1"][0:np_, :], src, gcol[0:np_, :], CT[0:np_, cols], ALU.mult, ALU.mult),
         reads=[st_, ctx["pv_t"], tmp_t["tab"]], writes=[tmp_t["u1"]])
    P.op("dve", lambda: V.scalar_tensor_tensor(tmp["u2"][0:np_, :], src, gcol[0:np_, :], ST[0:np_, cols], ALU.mult, ALU.mult),
         reads=[st_, ctx["pv_t"], tmp_t["tab"]], writes=[tmp_t["u2"]])
    P.op("pe", _mm(nc, banks[bank_rot][0:np_, :], IDENT[0:np_, 0:np_], tmp["u1"][0:np_, :], True, False),
         reads=[tmp_t["u1"], ctx["cm_t"]], writes=[bank_tok[bank_rot]])
    P.op("pe", _mm(nc, banks[bank_rot][0:np_, :], rot_m[0:np_, 0:np_], tmp["u2"][0:np_, :], False, True),
         reads=[tmp_t["u2"], ctx["cm_t"]], writes=[bank_tok[bank_rot]])
    P.op("dve", lambda: V.tensor_tensor(out=out_ap, in0=banks[bank_rot][0:np_, :], in1=tmp["rs"][0:np_, :], op=ALU.mult),
         reads=[bank_tok[bank_rot], tmp_t["rs"]], writes=[out_tok])


def chain_group(ctx, chains, CT, ST):
    nc, P = ctx["nc"], ctx["P"]
    V, A_ = nc.vector, nc.scalar
    banks, bank_tok = ctx["banks"], ctx["bank_tok"]
    IDENT = ctx["cm"][:, 0, :]
    for c in chains:
        n_ = c["np"]
        P.op("act", (lambda c=c, n_=n_: A_.activation(out=c["tmp"]["sq"][0:n_, :], in_=banks[c["src"]][0:n_, :], func=AF.Square)),
             reads=[bank_tok[c["src"]]], writes=[c["tmp_t"]["sq"]])
    for c in chains:
        n_ = c["np"]
        P.op("pe", _mm(nc, banks[c["xb"]][0:n_, :], c["ones"][0:n_, 0:n_], c["tmp"]["sq"][0:n_, :], True, True),
             reads=[c["tmp_t"]["sq"], ctx["cm_t"]], writes=[bank_tok[c["xb"]]])
    for c in chains:
        n_ = c["np"]
        P.op("act", (lambda c=c, n_=n_: A_.activation(out=c["tmp"]["rs"][0:n_, :], in_=banks[c["xb"]][0:n_, :], func=AF.Ln, bias=c["eps"])),
             reads=[bank_tok[c["xb"]]], writes=[c["tmp_t"]["rs"]])
    for c in chains:
        n_ = c["np"]
        P.op("act", (lambda c=c, n_=n_: A_.activation(out=c["tmp"]["rs"][0:n_, :], in_=c["tmp"]["rs"][0:n_, :], func=AF.Exp, scale=-0.5)),
             reads=[c["tmp_t"]["rs"]], writes=[c["tmp_t"]["rs"]])
    for c in chains:
        n_ = c["np"]
        cols = slice(c["tt"] * 512, (c["tt"] + 1) * 512)
        P.op("dve", (lambda c=c, n_=n_, cols=cols: V.scalar_tensor_tensor(c["tmp"]["u1"][0:n_, :], banks[c["src"]][0:n_, :], c["g"][0:n_, :],
                                                                          CT[0:n_, cols], ALU.mult, ALU.mult)),
             reads=[bank_tok[c["src"]], ctx["pv_t"], c["tmp_t"]["tab"]], writes=[c["tmp_t"]["u1"]])
        P.op("dve", (lambda c=c, n_=n_, cols=cols: V.scalar_tensor_tensor(c["tmp"]["u2"][0:n_, :], banks[c["src"]][0:n_, :], c["g"][0:n_, :],
                                                                          ST[0:n_, cols], ALU.mult, ALU.mult)),
             reads=[bank_tok[c["src"]], ctx["pv_t"], c["tmp_t"]["tab"]], writes=[c["tmp_t"]["u2"]])
    for c in chains:
        n_ = c["np"]
        P.op("pe", _mm(nc, banks[c["xb"]][0:n_, :], IDENT[0:n_, 0:n_], c["tmp"]["u1"][0:n_, :], True, False),
             reads=[c["tmp_t"]["u1"], ctx["cm_t"]], writes=[bank_tok[c["xb"]]])
        P.op("pe", _mm(nc, banks[c["xb"]][0:n_, :], c["rot"][0:n_, 0:n_], c["tmp"]["u2"][0:n_, :], False, True),
             reads=[c["tmp_t"]["u2"], ctx["cm_t"]], writes=[bank_tok[c["xb"]]])
    for c in chains:
        n_ = c["np"]
        P.op("dve", (lambda c=c, n_=n_: V.tensor_tensor(out=c["out"], in0=banks[c["xb"]][0:n_, :], in1=c["tmp"]["rs"][0:n_, :], op=ALU.mult)),
             reads=[bank_tok[c["xb"]], c["tmp_t"]["rs"]], writes=[c["out_tok"]])


def finalize_head(ctx, o_ap, o_tok, den_ap, den_tok, gate_w, gate_wt, gcol0, hT, hT_t, tt, ydst, tmp, tmp_t, gbank=7):
    nc, P = ctx["nc"], ctx["P"]
    V, A_ = nc.vector, nc.scalar
    banks, bank_tok = ctx["banks"], ctx["bank_tok"]
    proj_group(ctx, gbank, gate_w, gate_wt, gcol0, 64, hT, hT_t, tt)
    g_ps = banks[gbank][0:64, :]
    P.op("dve", lambda: V.tensor_copy(tmp["den0"][0:64, :], den_ap), reads=[den_tok], writes=[tmp_t["den0"]])
    yield
    P.op("act", lambda: A_.activation(out=tmp["e1"][0:64, :], in_=g_ps, func=AF.Exp, scale=-1.0), reads=[bank_tok[gbank]], writes=[tmp_t["e1"]])
    yield
    P.op("dve", lambda: V.scalar_tensor_tensor(tmp["e1"][0:64, :], tmp["e1"][0:64, :], 1.0, tmp["den0"][0:64, :], ALU.add, ALU.mult),
         reads=[tmp_t["e1"], tmp_t["den0"]], writes=[tmp_t["e1"]])
    yield
    P.op("act", lambda: A_.activation(out=tmp["e1"][0:64, :], in_=tmp["e1"][0:64, :], func=AF.Ln), reads=[tmp_t["e1"]], writes=[tmp_t["e1"]])
    yield
    P.op("act", lambda: A_.activation(out=tmp["e1"][0:64, :], in_=tmp["e1"][0:64, :], func=AF.Exp, scale=-1.0), reads=[tmp_t["e1"]], writes=[tmp_t["e1"]])
    yield
    P.op("dve", lambda: V.tensor_tensor(out=tmp["e1"][0:64, :], in0=tmp["e1"][0:64, :], in1=g_ps, op=ALU.mult),
         reads=[tmp_t["e1"], bank_tok[gbank]], writes=[tmp_t["e1"]])
    yield
    P.op("dve", lambda: V.tensor_tensor(out=tmp["yb"][0:64, :], in0=o_ap, in1=tmp["e1"][0:64, :], op=ALU.mult),
         reads=[o_tok, tmp_t["e1"]], writes=[tmp_t["yb"]])
    P.dma("sp", lambda: nc.sync.dma_start(out=ydst, in_=tmp["yb"][0:64, :]), reads=[tmp_t["yb"]])


def _fin_tmp(AR):
    tmp = {"e1": AR.alloc([512], F32), "den0": AR.alloc([512], F32), "yb": AR.alloc([512], BF16), "den": AR.alloc([512], F32)}
    return tmp, {k: Tok() for k in tmp}


def lockstep(gens):
    live = list(gens)
    while live:
        for g_ in list(live):
            try:
                next(g_)
            except StopIteration:
                live.remove(g_)


def _chain_tmp(AR):
    tmp = {"sq": AR.alloc([512], BF16), "rs": AR.alloc([512], F32), "u1": AR.alloc([512], BF16), "u2": AR.alloc([512], BF16),
           "e1": AR.alloc([512], F32), "den0": AR.alloc([512], F32), "yb": AR.alloc([512], BF16), "den": AR.alloc([512], F32)}
    tmp_t = {k: Tok() for k in list(tmp) + ["tab"]}
    return tmp, tmp_t


def _chain_tmp2(AR, tmp_t):
    tmp = {"sq": AR.alloc([512], BF16), "rs": AR.alloc([512], F32), "u1": AR.alloc([512], BF16), "u2": AR.alloc([512], BF16)}
    t2 = {k: Tok() for k in tmp}
    t2["tab"] = tmp_t["tab"]
    return tmp, t2


def phase_mla(ctx, l, hT, hT_t):
    nc, P, AR = ctx["nc"], ctx["P"], ctx["AR"]
    V, A_, G_ = nc.vector, nc.scalar, nc.gpsimd
    env = ctx["env"]
    banks, bank_tok = ctx["banks"], ctx["bank_tok"]
    pv, pv_t, cm, cm_t = ctx["pv"], ctx["pv_t"], ctx["cm"], ctx["cm_t"]
    ONES256, ONES128, ONES96, RM_, EPAD = cm[:, 1, :], cm[:, 2, :], cm[:, 4, :], cm[:, 6, :], cm[:, 7, :]
    cmask, cmask_t = ctx["cmask"], ctx["cmask_t"]
    ymla = env["ymla"]
    w_r = env["w_in"][l].rearrange("(c p) n -> p c n", p=128)
    CT = AR.alloc([S], BF16)
    ST = AR.alloc([S], BF16)
    tmp, tmp_t = _chain_tmp(AR)
    make_tables(ctx, 1, CT, ST, tmp_t["tab"])
    if env.get("stop") == "tables":
        return
    wm = AR.alloc([8, 416], BF16)
    wuq = AR.alloc([2, 768], BF16)
    wukv = AR.alloc([1024], BF16)
    wgm = AR.alloc([8, 512], BF16)
    wkp = AR.alloc([8, 96], BF16)
    w_t = Tok()
    wkp_t = Tok()
    P.dma("pool", lambda: G_.dma_start(out=wm, in_=w_r[:, :, C_CQ:C_CQ + 416]), writes=[w_t])
    P.dma("pool", lambda: G_.dma_start(out=wuq, in_=env["w_uq"][l].rearrange("(c p) n -> p c n", p=128)), writes=[w_t])
    P.dma("pool", lambda: G_.dma_start(out=wukv, in_=env["w_ukv"][l]), writes=[w_t])
    P.dma("pool", lambda: G_.dma_start(out=wgm, in_=w_r[:, :, C_MG:C_MG + 512]), writes=[w_t])
    P.op("dve", lambda: V.memset(wkp, 0.0), writes=[wkp_t])
    P.op("dve", lambda: V.tensor_copy(wkp[:, :, 0:64], wukv.rearrange("p (h c) -> p h c", h=8)[:, :, 0:64]), reads=[w_t], writes=[wkp_t])
    CQN = AR.alloc([2, S], BF16)
    CKVN = AR.alloc([S], BF16)
    KR = AR.alloc([S], BF16)
    lat_t = toks(8)
    m1_mark = AR.mark()
    sq3 = [AR.alloc([3, 512], BF16) for _ in range(2)]
    sq3_t = toks(2)
    rs2 = [AR.alloc([2, 512], F32) for _ in range(2)]
    rs2_t = toks(2)
    for tt in range(8):
        b = tt % 2
        cols = slice(tt * 512, (tt + 1) * 512)
        proj_group(ctx, 0, wm, w_t, 0, 128, hT, hT_t, tt)
        proj_group(ctx, 1, wm, w_t, 128, 128, hT, hT_t, tt)
        proj_group(ctx, 2, wm, w_t, 256, 128, hT, hT_t, tt)
        proj_group(ctx, 3, wm, w_t, 384, 32, hT, hT_t, tt)
        for i in range(3):
            P.op("act", (lambda b=b, i=i: A_.activation(out=sq3[b][:, i, :], in_=banks[i][:, :], func=AF.Square)),
                 reads=[bank_tok[i]], writes=[sq3_t[b]])
        P.op("pe", _mm(nc, banks[4][:, :], ONES256, sq3[b][:, 0, :], True, False), reads=[sq3_t[b], cm_t], writes=[bank_tok[4]])
        P.op("pe", _mm(nc, banks[4][:, :], ONES256, sq3[b][:, 1, :], False, True), reads=[sq3_t[b], cm_t], writes=[bank_tok[4]])
        P.op("pe", _mm(nc, banks[5][:, :], ONES128, sq3[b][:, 2, :], True, True), reads=[sq3_t[b], cm_t], writes=[bank_tok[5]])
        for i, bk in ((0, 4), (1, 5)):
            P.op("act", (lambda b=b, i=i, bk=bk: A_.activation(out=rs2[b][:, i, :], in_=banks[bk][:, :], func=AF.Ln, bias=EPS)),
                 reads=[bank_tok[bk]], writes=[rs2_t[b]])
            P.op("act", (lambda b=b, i=i: A_.activation(out=rs2[b][:, i, :], in_=rs2[b][:, i, :], func=AF.Exp, scale=-0.5)),
                 reads=[rs2_t[b]], writes=[rs2_t[b]])
        for c in range(2):
            P.op("dve", (lambda b=b, c=c, cols=cols: V.scalar_tensor_tensor(CQN[:, c, cols], banks[c][:, :], pv[:, l, PV_CQG + c:PV_CQG + c + 1],
                                                                            rs2[b][:, 0, :], ALU.mult, ALU.mult)),
                 reads=[bank_tok[c], rs2_t[b], pv_t], writes=[lat_t[tt]])
        P.op("dve", (lambda b=b, cols=cols: V.scalar_tensor_tensor(CKVN[:, cols], banks[2][:, :], pv[:, l, PV_CKVG:PV_CKVG + 1],
                                                                   rs2[b][:, 1, :], ALU.mult, ALU.mult)),
             reads=[bank_tok[2], rs2_t[b], pv_t], writes=[lat_t[tt]])
        P.op("act", (lambda cols=cols: A_.copy(KR[0:32, cols], banks[3][0:32, :])), reads=[bank_tok[3]], writes=[lat_t[tt]])
    if env.get("stop") == "latent":
        return
    P.barrier()
    AR.release(m1_mark)
    tmpk, tmpk_t = _chain_tmp2(AR, tmp_t)
    QT = AR.alloc([S], BF16)
    KT = AR.alloc([S], BF16)
    QT_t = toks(8)
    KT_t = toks(8)
    VH = AR.alloc([32, 128], BF16)
    VH_t = Tok()
    P.op("dve", lambda: V.memset(VH[:, :, 64:128], 1.0), writes=[VH_t])
    Pb = [AR.alloc([512], BF16) for _ in range(4)]
    Pb_t = toks(4)
    gq = pv[:, l, PV_MQG:PV_MQG + 1]
    gk = pv[:, l, PV_MKG:PV_MKG + 1]
    SC = math.sqrt(96.0)
    cnt = 0
    csets = [(tmp, tmp_t), (tmpk, tmpk_t)] + [_chain_tmp2(AR, tmp_t) for _ in range(2)]
    fsets = [(tmp, tmp_t), _fin_tmp(AR)]
    fin_gen = None
    for h in range(8):
        for t2 in range(0, 8, 2):
            chains = []
            for jj, tt in enumerate((t2, t2 + 1)):
                cols = slice(tt * 512, (tt + 1) * 512)
                for j in range(4):
                    P.op("pe", _mm(nc, banks[7][:, j * 64:(j + 1) * 64], CKVN[:, tt * 512 + j * 128:tt * 512 + (j + 1) * 128],
                                   wukv[:, h * 128 + 64:h * 128 + 128], True, True),
                         reads=[w_t, lat_t[tt]], writes=[bank_tok[7]])
                P.op("act", (lambda tt=tt: A_.copy(VH[:, tt * 4:(tt + 1) * 4, 0:64], banks[7][:, 0:256].rearrange("p (j c) -> p j c", j=4))),
                     reads=[bank_tok[7]], writes=[VH_t])
            for jj, tt in enumerate((t2, t2 + 1)):
                cols = slice(tt * 512, (tt + 1) * 512)
                bq, bk = 2 * jj, 2 * jj + 1
                for c in range(2):
                    P.op("pe", _mm(nc, banks[bq][0:96, :], wuq[:, c, h * 96:(h + 1) * 96], CQN[:, c, cols], c == 0, c == 1),
                         reads=[w_t, lat_t[tt]], writes=[bank_tok[bq]])
                P.op("pe", _mm(nc, banks[bk][0:96, :], wkp[:, h, :], CKVN[:, cols], True, False), reads=[wkp_t, lat_t[tt]], writes=[bank_tok[bk]])
                P.op("pe", _mm(nc, banks[bk][0:96, :], EPAD[0:32, 0:96], KR[0:32, cols], False, True), reads=[cm_t, lat_t[tt]], writes=[bank_tok[bk]])
                chains.append(dict(src=bq, np=96, ones=ONES96, eps=96 * EPS, g=gq, rot=RM_, tt=tt, out=QT[0:96, cols], out_tok=QT_t[tt],
                                   tmp=csets[2 * jj][0], tmp_t=csets[2 * jj][1], xb=4 + 2 * jj))
                chains.append(dict(src=bk, np=96, ones=ONES96, eps=96 * EPS, g=gk, rot=RM_, tt=tt, out=KT[0:96, cols], out_tok=KT_t[tt],
                                   tmp=csets[2 * jj + 1][0], tmp_t=csets[2 * jj + 1][1], xb=5 + 2 * jj))
            chain_group(ctx, chains, CT, ST)
        if env.get("stop") == "prep":
            return
        for qt in range(8):
            if env.get("stop") == "attn1" and qt == 1:
                return
            qcols = slice(qt * 512, (qt + 1) * 512)
            nkb = 4 * qt + 4
            SBK = (0, 1, 2, 3, 4)
            NSB = 5
            ob = 6 + qt % 2
            fs = fsets[qt % 2]
            base = cnt
            cnt += nkb
            for i in range(nkb + 3):
                if fin_gen is not None:
                    try:
                        next(fin_gen)
                    except StopIteration:
                        fin_gen = None
                if i < nkb:
                    kb = i
                    sb = SBK[(base + kb) % NSB]
                    P.op("pe", _mm(nc, banks[sb][:, :], KT[0:96, kb * 128:(kb + 1) * 128], QT[0:96, qcols], True, True),
                         reads=[KT_t[kb // 4], QT_t[qt]], writes=[bank_tok[sb]])
                if 0 <= i - 2 < nkb:
                    kb = i - 2
                    sb = SBK[(base + kb) % NSB]
                    pi = (base + kb) % 4
                    P.op("act", (lambda sb=sb, pi=pi: A_.activation(out=Pb[pi], in_=banks[sb][:, :], func=AF.Exp, scale=SC)),
                         reads=[bank_tok[sb]], writes=[Pb_t[pi]])
                    j = kb - 4 * qt
                    if j >= 0:
                        P.op("dve", (lambda pi=pi, j=j: V.tensor_tensor(out=Pb[pi], in0=Pb[pi], in1=cmask[:, j * 512:(j + 1) * 512], op=ALU.mult)),
                             reads=[Pb_t[pi], cmask_t], writes=[Pb_t[pi]])
                if 0 <= i - 3 < nkb:
                    kb = i - 3
                    pi = (base + kb) % 4
                    P.op("pe", _mm(nc, banks[ob][:, :], VH[:, kb, :], Pb[pi], kb == 0, kb == nkb - 1),
                         reads=[VH_t, Pb_t[pi]], writes=[bank_tok[ob]])
            while fin_gen is not None:
                try:
                    next(fin_gen)
                except StopIteration:
                    fin_gen = None
            P.op("act", (lambda fs=fs, ob=ob: A_.copy(fs[0]["den"][64:128, :], banks[ob][64:128, :])), reads=[bank_tok[ob]], writes=[fs[1]["den"]])
            fin_gen = finalize_head(ctx, banks[ob][0:64, :], bank_tok[ob], fs[0]["den"][64:128, :], fs[1]["den"], wgm, w_t, h * 64, hT, hT_t, qt,
                                    ymla[h * 64:(h + 1) * 64, qcols], fs[0], fs[1], gbank=5)
        while fin_gen is not None:
            try:
                next(fin_gen)
            except StopIteration:
                fin_gen = None


def phase_dil(ctx, l, hT, hT_t):
    nc, P, AR = ctx["nc"], ctx["P"], ctx["AR"]
    V, A_, G_ = nc.vector, nc.scalar, nc.gpsimd
    env = ctx["env"]
    banks, bank_tok = ctx["banks"], ctx["bank_tok"]
    pv, pv_t, cm, cm_t = ctx["pv"], ctx["pv_t"], ctx["cm"], ctx["cm_t"]
    ONESB64, RD_ = cm[:, 3, :], cm[:, 5, :]
    cmask, cmask_t = ctx["cmask"], ctx["cmask_t"]
    DMASK = cmask[:, 2048:2048 + 256]
    ydil = env["ydil"]
    w_r = env["w_in"][l].rearrange("(c p) n -> p c n", p=128)
    CT = AR.alloc([S], BF16)
    ST = AR.alloc([S], BF16)
    tmp, tmp_t = _chain_tmp(AR)
    make_tables(ctx, 0, CT, ST, tmp_t["tab"])
    csets = [(tmp, tmp_t)] + [_chain_tmp2(AR, tmp_t) for _ in range(3)]
    fsets = [(tmp, tmp_t)] + [_fin_tmp(AR) for _ in range(3)]
    wqk = [AR.alloc([8, 128], BF16) for _ in range(2)]
    wv = [AR.alloc([8, 64], BF16) for _ in range(2)]
    wgh_t = toks(2)
    wdg = AR.alloc([8, 512], BF16)
    wdg_t = Tok()
    P.dma("pool", lambda: G_.dma_start(out=wdg, in_=w_r[:, :, C_DG:C_DG + 512]), writes=[wdg_t])
    QK = AR.alloc([S], BF16)
    K0 = AR.alloc([S], BF16)
    QK_t = toks(8)
    K0_t = toks(8)
    VD = AR.alloc([32, 128], BF16)
    VD_t = Tok()
    P.op("dve", lambda: V.memset(VD[:, :, 64:128], 1.0), writes=[VD_t])
    OACC = AR.alloc([S], F32)
    OACC_t = Tok()
    Pd = [AR.alloc([256], BF16) for _ in range(4)]
    Pd_t = toks(4)
    gqk = pv[:, l, PV_DQKG:PV_DQKG + 1]
    it = 0
    cnt = 0
    for h in range(8):
        for g in range(3):
            d = DIL[g]
            nb = S // (128 * d)
            wb = it % 2
            it += 1
            hd = g * 8 + h
            P.dma("pool", (lambda wb=wb, hd=hd: G_.dma_start(out=wqk[wb][:, :, 0:64], in_=w_r[:, :, C_DQ + hd * 64:C_DQ + (hd + 1) * 64])), writes=[wgh_t[wb]])
            P.dma("pool", (lambda wb=wb, hd=hd: G_.dma_start(out=wqk[wb][:, :, 64:128], in_=w_r[:, :, C_DK + hd * 64:C_DK + (hd + 1) * 64])), writes=[wgh_t[wb]])
            P.dma("pool", (lambda wb=wb, hd=hd: G_.dma_start(out=wv[wb], in_=w_r[:, :, C_DV + hd * 64:C_DV + (hd + 1) * 64])), writes=[wgh_t[wb]])
            for t4 in range(0, 8, 4):
                chains = []
                for jj in range(4):
                    tt = t4 + jj
                    cols = slice(tt * 512, (tt + 1) * 512)
                    proj_group(ctx, jj, wqk[wb], wgh_t[wb], 0, 128, hT, hT_t, tt)
                    chains.append(dict(src=jj, np=128, ones=ONESB64, eps=EPS, g=gqk, rot=RD_, tt=tt, out=QK[:, cols], out_tok=QK_t[tt],
                                       tmp=csets[jj][0], tmp_t=csets[jj][1], xb=4 + jj))
                chain_group(ctx, chains, CT, ST)
                for jj in range(4):
                    tt = t4 + jj
                    cols = slice(tt * 512, (tt + 1) * 512)
                    P.op("dve", (lambda cols=cols: V.tensor_copy(K0[0:64, cols], QK[64:128, cols])), reads=[QK_t[tt]], writes=[K0_t[tt]])

            def ucols(r, n, d=d):
                st0 = r + d * 128 * n
                return slice(st0, st0 + d * 127 + 1, d)
            units = [(r, n) for r in range(d) for n in range(nb)]
            for u0 in range(0, 32, 4):
                for jj in range(4):
                    r, n = units[u0 + jj]
                    for k in range(8):
                        P.op("pe", _mm(nc, banks[7][:, jj * 64:(jj + 1) * 64], hT[:, k, ucols(r, n)], wv[wb][:, k, :], k == 0, k == 7),
                             reads=[wgh_t[wb]] + hT_t, writes=[bank_tok[7]])
                P.op("act", (lambda u0=u0: A_.copy(VD[:, u0:u0 + 4, 0:64], banks[7][:, 0:256].rearrange("p (j c) -> p j c", j=4))),
                     reads=[bank_tok[7]], writes=[VD_t])
            SBK = (0, 1, 2, 3, 4, 5)
            base = cnt
            cnt += 32
            for i in range(32 + 3):
                if i < 32:
                    u = i
                    r, n = units[u]
                    sb = SBK[(base + u) % 6]
                    qc = ucols(r, n)
                    if n > 0:
                        P.op("pe", _mm(nc, banks[sb][:, 0:128], K0[0:64, ucols(r, n - 1)], QK[0:64, qc], True, True),
                             reads=K0_t + QK_t, writes=[bank_tok[sb]])
                    P.op("pe", _mm(nc, banks[sb][:, 128:256], K0[0:64, qc], QK[0:64, qc], True, True),
                         reads=K0_t + QK_t, writes=[bank_tok[sb]])
                if 0 <= i - 2 < 32:
                    u = i - 2
                    r, n = units[u]
                    sb = SBK[(base + u) % 6]
                    pi = (base + u) % 4
                    lo = 0 if n > 0 else 128
                    P.op("act", (lambda sb=sb, pi=pi, lo=lo: A_.activation(out=Pd[pi][:, lo:256], in_=banks[sb][:, lo:256], func=AF.Exp, scale=0.125)),
                         reads=[bank_tok[sb]], writes=[Pd_t[pi]])
                    P.op("dve", (lambda pi=pi, lo=lo: V.tensor_tensor(out=Pd[pi][:, lo:256], in0=Pd[pi][:, lo:256], in1=DMASK[:, lo:256], op=ALU.mult)),
                         reads=[Pd_t[pi], cmask_t], writes=[Pd_t[pi]])
                if 0 <= i - 3 < 32:
                    u = i - 3
                    r, n = units[u]
                    pi = (base + u) % 4
                    qc = ucols(r, n)
                    ob = 6 + (base + u) % 2
                    if n > 0:
                        P.op("pe", _mm(nc, banks[ob][:, 0:128], VD[:, u - 1, :], Pd[pi][:, 0:128], True, False), reads=[VD_t, Pd_t[pi]], writes=[bank_tok[ob]])
                    P.op("pe", _mm(nc, banks[ob][:, 0:128], VD[:, u, :], Pd[pi][:, 128:256], n == 0, True), reads=[VD_t, Pd_t[pi]], writes=[bank_tok[ob]])
                    if g == 0:
                        P.op("dve", (lambda qc=qc, ob=ob: V.tensor_copy(OACC[:, qc], banks[ob][:, 0:128])), reads=[bank_tok[ob]], writes=[OACC_t])
                    else:
                        P.op("dve", (lambda qc=qc, ob=ob: V.tensor_tensor(out=OACC[:, qc], in0=banks[ob][:, 0:128], in1=OACC[:, qc], op=ALU.add)),
                             reads=[bank_tok[ob], OACC_t], writes=[OACC_t])
        for t4 in range(0, 8, 4):
            gens = []
            for jj in range(4):
                tt = t4 + jj
                cols = slice(tt * 512, (tt + 1) * 512)
                gens.append(finalize_head(ctx, OACC[0:64, cols], OACC_t, OACC[64:128, cols], OACC_t, wdg, wdg_t, h * 64, hT, hT_t, tt,
                                          ydil[h * 64:(h + 1) * 64, cols], fsets[jj][0], fsets[jj][1], gbank=4 + jj))
            lockstep(gens)


def phase_out(ctx, l, src, dst):
    nc, P, AR = ctx["nc"], ctx["P"], ctx["AR"]
    V, A_, G_ = nc.vector, nc.scalar, nc.gpsimd
    env = ctx["env"]
    banks, bank_tok = ctx["banks"], ctx["bank_tok"]
    TE = 256
    wlo = AR.alloc([8, 1024], BF16)
    wmo = AR.alloc([4, 1024], BF16)
    wdo = AR.alloc([4, 1024], BF16)
    wou = AR.alloc([8, 1024], BF16)
    w_t = Tok()
    for wt_, nm in ((wlo, "w_lru_o"), (wmo, "w_mla_o"), (wdo, "w_dil_o"), (wou, "w_out")):
        P.dma("pool", (lambda wt_=wt_, nm=nm: G_.dma_start(out=wt_, in_=env[nm][l].rearrange("(c p) n -> p c n", p=128))), writes=[w_t])
    YL = [AR.alloc([8, TE], BF16) for _ in range(2)]
    YM = [AR.alloc([4, TE], BF16) for _ in range(2)]
    YD = [AR.alloc([4, TE], BF16) for _ in range(2)]
    GT = [AR.alloc([24, TE], BF16) for _ in range(2)]
    XT = [AR.alloc([2, D], F32) for _ in range(2)]
    OUT = [AR.alloc([2, D], F32) for _ in range(2)]
    M = [AR.alloc([8, TE], BF16) for _ in range(2)]
    t1 = [AR.alloc([TE], F32) for _ in range(2)]
    t2 = [AR.alloc([TE], F32) for _ in range(2)]
    in_t = toks(2)
    out_t = toks(2)
    M_t = toks(2)
    t1_t = toks(2)
    t2_t = toks(2)
    ylru_r = env["ylru"].rearrange("(c p) t -> p c t", p=128)
    ymla_r = env["ymla"].rearrange("(c p) t -> p c t", p=128)
    ydil_r = env["ydil"].rearrange("(c p) t -> p c t", p=128)
    gts_r = env["gts"].rearrange("(c p) t -> p c t", p=128)
    is_final = dst is env["out_d"]
    k = 0
    pending = []
    for e in range(S // TE):
        b = e % 2
        tc_ = slice(e * TE, (e + 1) * TE)
        P.dma("sp", (lambda b=b, tc_=tc_: nc.sync.dma_start(out=YL[b], in_=ylru_r[:, :, tc_])), writes=[in_t[b]])
        P.dma("sp", (lambda b=b, tc_=tc_: nc.sync.dma_start(out=YM[b], in_=ymla_r[:, :, tc_])), writes=[in_t[b]])
        P.dma("sp", (lambda b=b, tc_=tc_: nc.sync.dma_start(out=YD[b], in_=ydil_r[:, :, tc_])), writes=[in_t[b]])
        P.dma("sp", (lambda b=b, tc_=tc_: nc.sync.dma_start(out=GT[b], in_=gts_r[:, :, tc_])), writes=[in_t[b]])
        P.dma("sp", (lambda b=b, e=e: nc.sync.dma_start(out=XT[b], in_=src[e * TE:(e + 1) * TE, :].rearrange("(j p) d -> p j d", p=128))), writes=[in_t[b]])
        for j in range(8):
            if j == 4 and pending:
                pending.pop(0)()
            js = slice(j * 128, (j + 1) * 128)
            kk = k % 2
            k += 1
            bl, bm, bd = 0 + kk, 2 + kk, 4 + kk
            for c in range(8):
                P.op("pe", _mm(nc, banks[bl][:, 0:TE], wlo[:, c, js], YL[b][:, c, :], c == 0, c == 7), reads=[w_t, in_t[b]], writes=[bank_tok[bl]])
            for c in range(4):
                P.op("pe", _mm(nc, banks[bm][:, 0:TE], wmo[:, c, js], YM[b][:, c, :], c == 0, c == 3), reads=[w_t, in_t[b]], writes=[bank_tok[bm]])
            for c in range(4):
                P.op("pe", _mm(nc, banks[bd][:, 0:TE], wdo[:, c, js], YD[b][:, c, :], c == 0, c == 3), reads=[w_t, in_t[b]], writes=[bank_tok[bd]])
            P.op("dve", (lambda b=b, j=j, kk=kk, bl=bl: V.tensor_tensor(out=t1[kk], in0=banks[bl][:, 0:TE], in1=GT[b][:, j, :], op=ALU.mult)),
                 reads=[bank_tok[bl], in_t[b]], writes=[t1_t[kk]])
            P.op("dve", (lambda b=b, j=j, kk=kk, bm=bm: V.tensor_tensor(out=t2[kk], in0=banks[bm][:, 0:TE], in1=GT[b][:, 8 + j, :], op=ALU.mult)),
                 reads=[bank_tok[bm], in_t[b]], writes=[t2_t[kk]])
            P.op("dve", (lambda kk=kk: V.tensor_tensor(out=t1[kk], in0=t1[kk], in1=t2[kk], op=ALU.add)),
                 reads=[t1_t[kk], t2_t[kk]], writes=[t1_t[kk]])
            P.op("dve", (lambda b=b, j=j, kk=kk, bd=bd: V.tensor_tensor(out=t2[kk], in0=banks[bd][:, 0:TE], in1=GT[b][:, 16 + j, :], op=ALU.mult)),
                 reads=[bank_tok[bd], in_t[b]], writes=[t2_t[kk]])
            P.op("dve", (lambda b=b, j=j, kk=kk: V.tensor_tensor(out=M[b][:, j, :], in0=t1[kk], in1=t2[kk], op=ALU.add)),
                 reads=[t1_t[kk], t2_t[kk]], writes=[M_t[b]])
        def wout(b=b, e=e):
            for s_ in range(2):
                for hf in range(2):
                    bo = 6 + (s_ * 2 + hf) % 2
                    for c in range(8):
                        P.op("pe", _mm(nc, banks[bo][:, :], M[b][:, c, s_ * 128:(s_ + 1) * 128], wou[:, c, hf * 512:(hf + 1) * 512], c == 0, c == 7),
                             reads=[w_t, M_t[b]], writes=[bank_tok[bo]])
                    P.op("dve", (lambda b=b, s_=s_, hf=hf, bo=bo: V.tensor_tensor(out=OUT[b][:, s_, hf * 512:(hf + 1) * 512], in0=banks[bo][:, :],
                                                                                in1=XT[b][:, s_, hf * 512:(hf + 1) * 512], op=ALU.add)),
                         reads=[bank_tok[bo], in_t[b]], writes=[out_t[b]])
            P.dma("sp", (lambda b=b, e=e: nc.sync.dma_start(out=dst[e * TE:(e + 1) * TE, :].rearrange("(j p) d -> p j d", p=128), in_=OUT[b])),
                  reads=[out_t[b]], is_output=is_final)
        pending.append(wout)
    while pending:
        pending.pop(0)()
```

```python
import math
import numpy as np
import ml_dtypes
import concourse.bass as bass
import concourse.mybir as mybir
from concourse.bass_utils import run_bass_kernel_spmd

F32 = mybir.dt.float32
BF16 = mybir.dt.bfloat16
I32 = mybir.dt.int32
AF = mybir.ActivationFunctionType
ALU = mybir.AluOpType

S = 4096
D = 1024
DEPTH = 2
INW = 11168
EPS = 1e-6
C_LX, C_LG, C_CQ, C_CKV, C_KR, C_MG, C_DQ, C_DK, C_DV, C_DG, C_MRG = (
    0, 1024, 2048, 2304, 2432, 2464, 2976, 4512, 6048, 7584, 8096)
DIL = (1, 4, 16)
NDMA_SLOTS = 12

PV_NORMG = 0
PV_CONVW = 8
PV_CONVB = 40
PV_BGX = 48
PV_BGA = 56
PV_LAM = 64
PV_CQG = 72
PV_CKVG = 74
PV_MQG = 75
PV_MKG = 76
PV_DQKG = 77
PV_BMRG = 78
NPV = 102


class Tok:
    __slots__ = ("w", "r", "excl")

    def __init__(self, excl=False):
        self.w = None
        self.r = []
        self.excl = excl


class Prog:
    ENG = ("pe", "act", "dve", "pool", "sp")

    def __init__(self, nc):
        self.nc = nc
        self.h = {"pe": nc.tensor, "act": nc.scalar, "dve": nc.vector, "pool": nc.gpsimd, "sp": nc.sync}
        self.ops = []
        self.seq = {e: 0 for e in self.ENG}
        self.wm = {e: {} for e in self.ENG}
        self.dma_rr = {"sp": 0, "pool": 0, "act": 0}
        self.dma_cnt = {}
        self.out_events = []

    def _deps(self, reads, writes, eng=None):
        deps = []
        for t in reads:
            if t.w is not None:
                deps.append(t.w)
            if t.excl:
                deps.extend(r for r in t.r if r[0] != eng)
        for t in writes:
            if t.w is not None:
                deps.append(t.w)
            deps.extend(t.r)
        return deps

    def _mk_waits(self, eng, deps, skip_same_pe=True):
        best = {}
        for ev in deps:
            key, val = ev
            if key == eng and eng == "pe":
                continue
            if val > best.get(key, 0):
                best[key] = val
        waits = []
        wm = self.wm[eng]
        for key, val in best.items():
            if wm.get(key, 0) >= val:
                continue
            wm[key] = val
            waits.append((key, val))
        return waits

    def op(self, eng, fn, reads=(), writes=()):
        deps = self._deps(reads, writes, eng)
        waits = self._mk_waits(eng, deps)
        self.seq[eng] += 1
        ev = (eng, self.seq[eng])
        self.ops.append(["c", eng, fn, waits, ev, False])
        for t in reads:
            if t.excl:
                t.r = [ev]
            else:
                t.r.append(ev)
        for t in writes:
            t.w = ev
            t.r = []
        return ev

    def dma(self, q, fn, reads=(), writes=(), is_output=False):
        deps = self._deps(reads, writes)
        slot = (q, self.dma_rr[q] % NDMA_SLOTS)
        self.dma_rr[q] += 1
        cnt = self.dma_cnt.get(slot, 0)
        if cnt > 0:
            deps.append((slot, cnt))
        waits = self._mk_waits(q, deps)
        self.dma_cnt[slot] = cnt + 1
        ev = (slot, cnt + 1)
        self.ops.append(["d", q, fn, waits, ev, True])
        for t in reads:
            t.r.append(ev)
        for t in writes:
            t.w = ev
            t.r = []
        if is_output:
            self.out_events.append(ev)
        return ev

    def barrier(self):
        deps = [(e, self.seq[e]) for e in ("pe", "act", "dve", "pool") if self.seq[e] > 0]
        deps += [(slot, cnt) for slot, cnt in self.dma_cnt.items()]
        waits = self._mk_waits("sp", deps)
        self.seq["sp"] += 1
        ev = ("sp", self.seq["sp"])
        self.ops.append(["c", "sp", lambda: self.nc.sync.nop(), waits, ev, False])
        for e in ("pe", "act", "dve", "pool"):
            w = self._mk_waits(e, [ev])
            self.ops.append(["w", e, None, w, None, False])

    def finish(self):
        deps = list(self.out_events)
        waits = self._mk_waits("sp", deps)
        self.ops.append(["w", "sp", None, waits, None, False])

    def emit(self, sems, dma_sems):
        targets = {e: set() for e in self.ENG}
        for o in self.ops:
            for key, val in o[3]:
                if isinstance(key, str):
                    targets[key].add(val)
        valmap = {}
        for e in self.ENG:
            c = 0
            m = {}
            for s in sorted(targets[e]):
                c += 1
                m[s] = c
            valmap[e] = m
        for kind, eng, fn, waits, ev, _ in self.ops:
            hnd = self.h[eng]
            for key, val in waits:
                if isinstance(key, str):
                    hnd.wait_ge(sems[key], valmap[key][val])
                else:
                    hnd.wait_ge(dma_sems[key], 16 * val)
            if kind == "w":
                continue
            ins = fn()
            if kind == "d":
                ins.then_inc(dma_sems[ev[0]], 16)
            else:
                if ev[1] in targets[eng]:
                    ins.then_inc(sems[eng], 1)


class Arena:
    def __init__(self, ap, nbytes):
        self.ap = ap
        self.nbytes = nbytes
        self.off = 0

    def alloc(self, shape, dt):
        esz = 4 if dt in (F32, I32) else 2
        n = 1
        for s_ in shape:
            n *= s_
        nb = (n * esz + 63) // 64 * 64
        assert self.off + nb <= self.nbytes, ("arena overflow", self.off, nb, self.nbytes)
        a = self.ap[:, self.off // 2:(self.off + n * esz) // 2]
        self.off += nb
        if dt != BF16:
            a = a.bitcast(dt)
        if len(shape) == 2:
            a = a.rearrange("p (a b) -> p a b", a=shape[0])
        elif len(shape) == 3:
            a = a.rearrange("p (a b c) -> p a b c", a=shape[0], b=shape[1])
        return a

    def mark(self):
        return self.off

    def release(self, m):
        self.off = m


def toks(n):
    return [Tok() for _ in range(n)]


def build_program(n_layers=DEPTH, stages=None, debug=False, extra_barriers=0, stop=None):
    nc = bass.Bass("TRN2", target_bir_lowering=False)
    P = Prog(nc)
    dram_in = {}

    def din(name, shape, dt=F32):
        dram_in[name] = nc.dram_tensor(name, list(shape), dt, kind="ExternalInput").ap()
        return dram_in[name]

    x_in = din("x", [S, D])
    pos_in = din("pos", [1, S], I32)
    w_in = din("w_in", [DEPTH, D, INW])
    w_gx = din("w_gate_x", [DEPTH, 8, 128, 128])
    w_ga = din("w_gate_a", [DEPTH, 8, 128, 128])
    w_lru_o = din("w_lru_o", [DEPTH, 1024, 1024])
    w_uq = din("w_uq", [DEPTH, 256, 768])
    w_ukv = din("w_ukv", [DEPTH, 128, 1024])
    w_mla_o = din("w_mla_o", [DEPTH, 512, 1024])
    w_dil_o = din("w_dil_o", [DEPTH, 512, 1024])
    w_out = din("w_out", [DEPTH, 1024, 1024])
    pvec_in = din("pvec", [DEPTH, 128, NPV])
    cmat_in = din("cmat", [8, 128, 128])
    cmask_in = din("cmask", [128, 4 * 512 + 256])
    cinv_in = din("cinv", [128, 2])
    out_d = nc.dram_tensor("out", [S, D], F32, kind="ExternalOutput").ap()
    sk = "ExternalOutput" if debug else "Internal"
    xres = nc.dram_tensor("xres", [S, D], F32, kind=sk).ap()
    ylru = nc.dram_tensor("ylru", [1024, S], BF16, kind=sk).ap()
    ymla = nc.dram_tensor("ymla", [512, S], BF16, kind=sk).ap()
    ydil = nc.dram_tensor("ydil", [512, S], BF16, kind=sk).ap()
    gts = nc.dram_tensor("gts", [3072, S], BF16, kind=sk).ap()
    dbg = {}
    if debug:
        dbg["hT"] = nc.dram_tensor("hT_dbg", [128, 8 * S], BF16, kind="ExternalOutput").ap()

    ARENA_BYTES = 204 * 1024
    import contextlib
    with contextlib.ExitStack() as es:
        arena_t = es.enter_context(nc.sbuf_tensor("arena", [128, ARENA_BYTES // 2], BF16))
        banks = [es.enter_context(nc.psum_tensor(f"ps{i}", [128, 512], F32)) for i in range(8)]
        sems = {e: es.enter_context(nc.semaphore(f"s_{e}")) for e in Prog.ENG}
        dma_sems = {}
        for q in ("sp", "pool"):
            for i in range(NDMA_SLOTS):
                dma_sems[(q, i)] = es.enter_context(nc.semaphore(f"d_{q}{i}"))
        es.enter_context(nc.Block())
        es.enter_context(nc.allow_low_precision("bf16 PE transposes / bf16 matmul operands with fp32 accumulation"))
        AR = Arena(arena_t, ARENA_BYTES)
        bank_tok = [Tok(excl=True) for _ in range(8)]
        _emit_all(nc, P, AR, banks, bank_tok, locals(), n_layers, dbg)
        P.finish()
        P.emit(sems, dma_sems)
    return nc


def _emit_all(nc, P, AR, banks, bank_tok, env, n_layers, dbg):
    x_in = env["x_in"]; pos_in = env["pos_in"]; w_in = env["w_in"]
    out_d = env["out_d"]; xres = env["xres"]
    pvec_in = env["pvec_in"]; cmat_in = env["cmat_in"]; cmask_in = env["cmask_in"]; cinv_in = env["cinv_in"]
    V, A_, T_, G_ = nc.vector, nc.scalar, nc.tensor, nc.gpsimd

    cm = AR.alloc([8, 128], BF16)
    cm_t = Tok()
    P.dma("pool", lambda: G_.dma_start(out=cm, in_=cmat_in.rearrange("m p c -> p m c")), writes=[cm_t])
    IDENT, ONES256, ONES128, ONESB64, ONES96, RD_, RM_, EPAD = [cm[:, i, :] for i in range(8)]
    cmask = AR.alloc([4 * 512 + 256], BF16)
    cmask_t = Tok()
    P.dma("pool", lambda: G_.dma_start(out=cmask, in_=cmask_in), writes=[cmask_t])
    cinv = AR.alloc([2], F32)
    cinv_t = Tok()
    P.dma("sp", lambda: nc.sync.dma_start(out=cinv, in_=cinv_in), writes=[cinv_t])
    pv = AR.alloc([DEPTH, NPV], F32)
    pv_t = Tok()
    P.dma("sp", lambda: nc.sync.dma_start(out=pv, in_=pvec_in.rearrange("l p c -> p l c")), writes=[pv_t])
    hbgx = AR.alloc([DEPTH, 8], F32)
    hbga = AR.alloc([DEPTH, 8], F32)
    hc = AR.alloc([DEPTH, 8], F32)
    nhc = AR.alloc([DEPTH, 8], F32)
    hbm = AR.alloc([DEPTH, 24], F32)
    tmpv = AR.alloc([DEPTH, 8], F32)
    der_t = Tok()
    P.op("dve", lambda: V.tensor_scalar_mul(hbgx, pv[:, :, PV_BGX:PV_BGX + 8], -1.0), reads=[pv_t], writes=[der_t])
    P.op("dve", lambda: V.tensor_scalar_mul(hbga, pv[:, :, PV_BGA:PV_BGA + 8], -1.0), reads=[pv_t], writes=[der_t])
    P.op("dve", lambda: V.tensor_scalar_mul(hbm, pv[:, :, PV_BMRG:PV_BMRG + 24], -1.0), reads=[pv_t], writes=[der_t])
    P.op("act", lambda: A_.activation(out=tmpv, in_=pv[:, :, PV_LAM:PV_LAM + 8], func=AF.Exp, scale=-1.0),
         reads=[pv_t], writes=[der_t])
    P.op("act", lambda: A_.activation(out=tmpv, in_=tmpv, func=AF.Ln, bias=1.0, scale=1.0),
         reads=[der_t], writes=[der_t])
    P.op("dve", lambda: V.tensor_scalar_mul(hc, tmpv, -8.0), reads=[der_t], writes=[der_t])

    perm_mark = AR.mark()
    ctx = dict(nc=nc, P=P, AR=AR, banks=banks, bank_tok=bank_tok, env=env, dbg=dbg,
               cm=cm, cm_t=cm_t, cmask=cmask, cmask_t=cmask_t, cinv=cinv, cinv_t=cinv_t,
               pv=pv, pv_t=pv_t, der_t=der_t, hbgx=hbgx, hbga=hbga, hc=hc, nhc=nhc, hbm=hbm)
    for l in range(n_layers):
        src = x_in if l == 0 else xres
        dst = out_d if l == n_layers - 1 else xres
        AR.release(perm_mark)
        P.barrier()
        hT = AR.alloc([8, S], BF16)
        hT_t = toks(8)
        layer_mark = AR.mark()
        phase_norm(ctx, l, src, hT, hT_t)
        if dbg and l == 0:
            P.dma("sp", lambda: nc.sync.dma_start(out=dbg["hT"].rearrange("p (c t) -> p c t", c=8), in_=hT), reads=hT_t)
        stages = env.get("stages") or ("gates", "lru", "mla", "dil", "out")
        if "gates" in stages:
            P.barrier(); AR.release(layer_mark)
            phase_gates(ctx, l, hT, hT_t)
        if "lru" in stages:
            P.barrier(); AR.release(layer_mark)
            phase_lru(ctx, l, hT, hT_t)
        if "mla" in stages:
            P.barrier(); AR.release(layer_mark)
            phase_mla(ctx, l, hT, hT_t)
        if "dil" in stages:
            P.barrier(); AR.release(layer_mark)
            phase_dil(ctx, l, hT, hT_t)
        if "out" in stages:
            P.barrier(); AR.release(perm_mark)
            phase_out(ctx, l, src, dst)
    for _ in range(ctx["env"].get("extra_barriers", 0)):
        P.barrier()
    P.barrier()


def _mm(nc, out, lhsT, rhs, start, stop):
    return lambda: nc.tensor.matmul(out, lhsT, rhs, start=start, stop=stop)


def proj_group(ctx, bank_i, wtile, w_t, c0, m, hT, hT_t, tt, rows=None):
    nc, P = ctx["nc"], ctx["P"]
    bank = ctx["banks"][bank_i]
    bt = ctx["bank_tok"][bank_i]
    for k in range(8):
        P.op("pe", _mm(nc, bank[0:m, :], wtile[:, k, c0:c0 + m], hT[:, k, tt * 512:(tt + 1) * 512], k == 0, k == 7),
             reads=[w_t, hT_t[tt]], writes=[bt])


def phase_norm(ctx, l, src, hT, hT_t):
    nc, P, AR = ctx["nc"], ctx["P"], ctx["AR"]
    V, A_, T_ = nc.vector, nc.scalar, nc.tensor
    banks, bank_tok = ctx["banks"], ctx["bank_tok"]
    pv, pv_t = ctx["pv"], ctx["pv_t"]
    IDENT = ctx["cm"][:, 0, :]
    xt = [AR.alloc([4, D], F32) for _ in range(2)]
    xt_t = toks(2)
    xn = [AR.alloc([D], BF16) for _ in range(2)]
    xn_t = toks(2)
    ss = [AR.alloc([4], F32) for _ in range(2)]
    ss_t = toks(2)
    junk = AR.alloc([D], BF16)
    gB = pv[:, l, PV_NORMG:PV_NORMG + 8].unsqueeze(2).to_broadcast([128, 8, 128])
    for g4 in range(8):
        b = g4 % 2
        P.dma("sp", (lambda b=b, g4=g4: nc.sync.dma_start(
            out=xt[b], in_=src[g4 * 512:(g4 + 1) * 512, :].rearrange("(j p) d -> p j d", p=128))),
            writes=[xt_t[b]])
        P.op("dve", (lambda b=b: V.memset(ss[b], 0.0)), writes=[ss_t[b]])
        for j in range(4):
            P.op("act", (lambda b=b, j=j: A_.activation(out=junk, in_=xt[b][:, j, :], func=AF.Square,
                                                         accum_out=ss[b][:, j:j + 1])),
                 reads=[xt_t[b]], writes=[ss_t[b]])
        P.op("act", (lambda b=b: A_.activation(out=ss[b], in_=ss[b], func=AF.Ln, bias=EPS, scale=1.0 / D)),
             reads=[ss_t[b]], writes=[ss_t[b]])
        P.op("act", (lambda b=b: A_.activation(out=ss[b], in_=ss[b], func=AF.Exp, scale=-0.5)),
             reads=[ss_t[b]], writes=[ss_t[b]])
        for j in range(4):
            kb = (g4 * 4 + j) % 2
            P.op("dve", (lambda b=b, j=j, kb=kb: V.tensor_scalar(xn[kb], xt[b][:, j, :], ss[b][:, j:j + 1], None, ALU.mult)),
                 reads=[xt_t[b], ss_t[b]], writes=[xn_t[kb]])
            bbf = banks[kb][:, :].bitcast(BF16)
            for c in range(8):
                P.op("pe", (lambda kb=kb, c=c, bbf=bbf: T_.transpose(bbf[:, c * 128:(c + 1) * 128],
                                                                     xn[kb][:, c * 128:(c + 1) * 128], IDENT)),
                     reads=[xn_t[kb], ctx["cm_t"]], writes=[bank_tok[kb]])
            t0 = g4 * 512 + j * 128
            P.op("dve", (lambda bbf=bbf, t0=t0: V.tensor_tensor(
                out=hT[:, :, t0:t0 + 128], in0=bbf.rearrange("p (c t) -> p c t", c=8), in1=gB, op=ALU.mult)),
                reads=[bank_tok[kb], pv_t], writes=[hT_t[g4]])


def phase_lru(ctx, l, hT, hT_t):
    nc, P, AR = ctx["nc"], ctx["P"], ctx["AR"]
    V, A_, T_, G_ = nc.vector, nc.scalar, nc.tensor, nc.gpsimd
    env = ctx["env"]
    banks, bank_tok = ctx["banks"], ctx["bank_tok"]
    pv, pv_t, der_t = ctx["pv"], ctx["pv_t"], ctx["der_t"]
    w_in, ylru = env["w_in"], env["ylru"]
    wl = [AR.alloc([8, 256], BF16) for _ in range(2)]
    wl_t = toks(2)
    wgx = AR.alloc([8, 128], BF16)
    wga = AR.alloc([8, 128], BF16)
    wg_t = Tok()
    P.dma("pool", lambda: G_.dma_start(out=wgx, in_=env["w_gx"][l].rearrange("n c d -> c n d")), writes=[wg_t])
    P.dma("pool", lambda: G_.dma_start(out=wga, in_=env["w_ga"][l].rearrange("n c d -> c n d")), writes=[wg_t])
    X = AR.alloc([3 + S], F32)
    X_t = toks(8)
    Xh_t = Tok()
    names = ("G", "XC", "TGX", "TGA", "A", "T", "A2", "H", "SG")
    NSET = 4
    buf = [{nm: AR.alloc([512], F32) for nm in names} for _ in range(NSET)]
    for p_ in range(NSET):
        buf[p_]["XCb"] = AR.alloc([512], BF16)
        buf[p_]["Yb"] = AR.alloc([512], BF16)
    bt = [{nm: Tok() for nm in buf[0]} for _ in range(NSET)]
    P.op("dve", lambda: V.memset(X[:, 0:3], 0.0), writes=[Xh_t])
    w_r = w_in[l].rearrange("(c p) n -> p c n", p=128)
    it = 0
    for n in range(8):
        wb = n % 2
        P.dma("pool", (lambda wb=wb, n=n: G_.dma_start(out=wl[wb][:, :, 0:128], in_=w_r[:, :, C_LX + n * 128:C_LX + (n + 1) * 128])),
              writes=[wl_t[wb]])
        P.dma("pool", (lambda wb=wb, n=n: G_.dma_start(out=wl[wb][:, :, 128:256], in_=w_r[:, :, C_LG + n * 128:C_LG + (n + 1) * 128])),
              writes=[wl_t[wb]])
        cw = lambda k, n=n: pv[:, l, PV_CONVW + n * 4 + k:PV_CONVW + n * 4 + k + 1]
        cb = pv[:, l, PV_CONVB + n:PV_CONVB + n + 1]
        hbx = ctx["hbgx"][:, l, n:n + 1]
        hba = ctx["hbga"][:, l, n:n + 1]
        hcn = ctx["hc"][:, l, n:n + 1]
        nhcn = ctx["nhc"][:, l, n:n + 1]
        def chunk(tt, par, pb, n=n, wb=wb, cw=cw, cb=cb, hbx=hbx, hba=hba, hcn=hcn):
            B_, Bt = buf[par], bt[par]
            c0 = 3 + tt * 512
            bx, bg, bgx, bga = pb, 2 + pb, 4 + pb, 6 + pb
            proj_group(ctx, bx, wl[wb], wl_t[wb], 0, 128, hT, hT_t, tt)
            P.op("act", lambda: A_.copy(X[:, c0:c0 + 512], banks[bx][:, :]), reads=[bank_tok[bx]], writes=[X_t[tt]])
            yield
            proj_group(ctx, bg, wl[wb], wl_t[wb], 128, 128, hT, hT_t, tt)
            P.op("act", lambda: A_.copy(B_["G"], banks[bg][:, :]), reads=[bank_tok[bg]], writes=[Bt["G"]])
            yield
            P.op("act", lambda: A_.activation(out=B_["SG"], in_=B_["G"], func=AF.Exp, scale=-1.0), reads=[Bt["G"]], writes=[Bt["SG"]])
            xprev = Xh_t if tt == 0 else X_t[tt - 1]
            P.op("dve", lambda: V.tensor_scalar(B_["XC"], X[:, c0:c0 + 512], cw(3), cb, ALU.mult, ALU.add),
                 reads=[X_t[tt], pv_t], writes=[Bt["XC"]])
            for k in (2, 1, 0):
                P.op("dve", (lambda k=k: V.scalar_tensor_tensor(B_["XC"], X[:, c0 - 3 + k:c0 - 3 + k + 512], cw(k), B_["XC"], ALU.mult, ALU.add)),
                     reads=[X_t[tt], xprev, Bt["XC"]], writes=[Bt["XC"]])
            yield
            P.op("act", lambda: A_.activation(out=B_["SG"], in_=B_["SG"], func=AF.Ln, bias=1.0), reads=[Bt["SG"]], writes=[Bt["SG"]])
            P.op("act", lambda: A_.copy(B_["XCb"], B_["XC"]), reads=[Bt["XC"]], writes=[Bt["XCb"]])
            yield
            P.op("pe", _mm(nc, banks[bgx][:, :], wgx[:, n, :], B_["XCb"], True, True), reads=[wg_t, Bt["XCb"]], writes=[bank_tok[bgx]])
            P.op("pe", _mm(nc, banks[bga][:, :], wga[:, n, :], B_["XCb"], True, True), reads=[wg_t, Bt["XCb"]], writes=[bank_tok[bga]])
            P.op("act", lambda: A_.activation(out=B_["SG"], in_=B_["SG"], func=AF.Exp, scale=-1.0), reads=[Bt["SG"]], writes=[Bt["SG"]])
            yield
            for nm, bk, hb_ in (("TGX", bgx, hbx), ("TGA", bga, hba)):
                P.op("act", (lambda nm=nm, bk=bk, hb_=hb_: A_.activation(out=B_[nm], in_=banks[bk][:, :], func=AF.Exp, bias=hb_, scale=-1.0)),
                     reads=[bank_tok[bk], der_t], writes=[Bt[nm]])
            P.op("dve", lambda: V.tensor_tensor(out=B_["SG"], in0=B_["SG"], in1=B_["G"], op=ALU.mult), reads=[Bt["SG"], Bt["G"]], writes=[Bt["SG"]])
            yield
            for nm in ("TGX", "TGA"):
                P.op("act", (lambda nm=nm: A_.activation(out=B_[nm], in_=B_[nm], func=AF.Ln, bias=1.0)), reads=[Bt[nm]], writes=[Bt[nm]])
            yield
            for nm in ("TGX", "TGA"):
                P.op("act", (lambda nm=nm: A_.activation(out=B_[nm], in_=B_[nm], func=AF.Exp, scale=-1.0)), reads=[Bt[nm]], writes=[Bt[nm]])
            yield
            P.op("act", lambda: A_.activation(out=B_["A"], in_=B_["TGA"], func=AF.Exp, scale=hcn), reads=[Bt["TGA"], der_t], writes=[Bt["A"]])
            P.op("act", lambda: A_.activation(out=B_["A2"], in_=B_["A"], func=AF.Square), reads=[Bt["A"]], writes=[Bt["A2"]])
            P.op("dve", lambda: V.tensor_tensor(out=B_["TGX"], in0=B_["TGX"], in1=B_["XC"], op=ALU.mult), reads=[Bt["TGX"], Bt["XC"]], writes=[Bt["TGX"]])
            yield
            P.op("dve", lambda: V.tensor_scalar(B_["T"], B_["A2"], -1.0, 1.0, ALU.mult, ALU.add), reads=[Bt["A2"]], writes=[Bt["T"]])
            yield
            P.op("act", lambda: A_.activation(out=B_["T"], in_=B_["T"], func=AF.Ln), reads=[Bt["T"]], writes=[Bt["T"]])
            yield
            P.op("act", lambda: A_.activation(out=B_["T"], in_=B_["T"], func=AF.Exp, scale=0.5), reads=[Bt["T"]], writes=[Bt["T"]])
            yield
            P.op("dve", lambda: V.tensor_tensor(out=B_["T"], in0=B_["T"], in1=B_["TGX"], op=ALU.mult), reads=[Bt["T"], Bt["TGX"]], writes=[Bt["T"]])
            if tt == 0:
                P.op("dve", lambda: V.tensor_tensor_scan(B_["H"], B_["A"], B_["T"], 0.0, ALU.mult, ALU.add), reads=[Bt["A"], Bt["T"]], writes=[Bt["H"]])
            else:
                Bp, Bpt = buf[(par - 1) % NSET], bt[(par - 1) % NSET]
                P.op("dve", lambda: V.tensor_tensor_scan(B_["H"], B_["A"], B_["T"], Bp["H"][:, 511:512], ALU.mult, ALU.add),
                     reads=[Bt["A"], Bt["T"], Bpt["H"]], writes=[Bt["H"]])
            P.op("dve", lambda: V.tensor_tensor(out=B_["Yb"], in0=B_["H"], in1=B_["SG"], op=ALU.mult), reads=[Bt["H"], Bt["SG"]], writes=[Bt["Yb"]])
            P.dma("sp", lambda: nc.sync.dma_start(out=ylru[n * 128:(n + 1) * 128, tt * 512:(tt + 1) * 512], in_=B_["Yb"]), reads=[Bt["Yb"]])

        for tp in range(0, 8, 2):
            gens = []
            for tt in (tp, tp + 1):
                gens.append(chunk(tt, it % NSET, it % 2))
                it += 1
            live = list(gens)
            while live:
                for g_ in list(live):
                    try:
                        next(g_)
                    except StopIteration:
                        live.remove(g_)


def phase_gates(ctx, l, hT, hT_t):
    nc, P, AR = ctx["nc"], ctx["P"], ctx["AR"]
    A_, G_ = nc.scalar, nc.gpsimd
    env = ctx["env"]
    banks, bank_tok = ctx["banks"], ctx["bank_tok"]
    w_r = env["w_in"][l].rearrange("(c p) n -> p c n", p=128)
    gts = env["gts"]
    wt = [AR.alloc([8, 128], BF16) for _ in range(2)]
    wt_t = toks(2)
    st = [AR.alloc([S], BF16) for _ in range(2)]
    st_t = toks(2)
    ebuf = [AR.alloc([512], F32) for _ in range(2)]
    ebuf_t = toks(2)
    for ch in range(24):
        b = ch % 2
        P.dma("pool", (lambda b=b, ch=ch: G_.dma_start(out=wt[b], in_=w_r[:, :, C_MRG + ch * 128:C_MRG + (ch + 1) * 128])),
              writes=[wt_t[b]])
        hb = ctx["hbm"][:, l, ch:ch + 1]
        for tt in range(8):
            bi = (ch * 8 + tt) % 4
            proj_group(ctx, bi, wt[b], wt_t[b], 0, 128, hT, hT_t, tt)
            eb = (ch * 8 + tt) % 2
            P.op("act", (lambda bi=bi, eb=eb, hb=hb: A_.activation(out=ebuf[eb], in_=banks[bi][:, :], func=AF.Exp, bias=hb, scale=-1.0)),
                 reads=[bank_tok[bi], ctx["der_t"]], writes=[ebuf_t[eb]])
            P.op("act", (lambda eb=eb: A_.activation(out=ebuf[eb], in_=ebuf[eb], func=AF.Ln, bias=1.0)), reads=[ebuf_t[eb]], writes=[ebuf_t[eb]])
            P.op("act", (lambda b=b, eb=eb, tt=tt: A_.activation(out=st[b][:, tt * 512:(tt + 1) * 512], in_=ebuf[eb], func=AF.Exp, scale=-1.0)),
                 reads=[ebuf_t[eb]], writes=[st_t[b]])
        P.dma("sp", (lambda b=b, ch=ch: nc.sync.dma_start(out=gts[ch * 128:(ch + 1) * 128, :], in_=st[b])),
              reads=[st_t[b]])
def _host_consts():
    cmat = np.zeros((8, 128, 128), np.float32)
    cmat[0] = np.eye(128, dtype=np.float32)
    cmat[1] = 1.0 / 256
    cmat[2] = 1.0 / 128
    cmat[3, 0:64, 0:64] = 1.0 / 64
    cmat[3, 64:128, 64:128] = 1.0 / 64
    cmat[4, 0:96, 0:96] = 1.0
    for blk in (0, 64):
        for i in range(32):
            cmat[5, blk + i + 32, blk + i] = -1.0
            cmat[5, blk + i, blk + i + 32] = 1.0
    for i in range(16):
        cmat[6, 64 + i + 16, 64 + i] = -1.0
        cmat[6, 64 + i, 64 + i + 16] = 1.0
    for k in range(32):
        cmat[7, k, 64 + k] = 1.0
    cmask = np.zeros((128, 4 * 512 + 256), np.float32)
    kk = np.arange(128)[:, None]
    qq = np.arange(512)[None, :]
    for j in range(4):
        cmask[:, j * 512:(j + 1) * 512] = (qq >= 128 * j + kk)
    q1 = np.arange(128)[None, :]
    cmask[:, 2048:2048 + 128] = (kk >= q1)
    cmask[:, 2048 + 128:2048 + 256] = (kk <= q1)
    cinv = np.zeros((128, 2), np.float32)
    theta = np.float32(10000.0)
    inv64 = theta ** (-np.arange(0, 64, 2, dtype=np.float32) / np.float32(64))
    inv32 = theta ** (-np.arange(0, 32, 2, dtype=np.float32) / np.float32(32))
    for p in range(128):
        cinv[p, 0] = inv64[p % 32]
        if 64 <= p < 96:
            cinv[p, 1] = inv32[(p - 64) % 16]
    return cmat, cmask, cinv


def _host_pvec(inp):
    pv = np.zeros((DEPTH, 128, NPV), np.float32)
    for l in range(DEPTH):
        pv[l, :, PV_NORMG:PV_NORMG + 8] = inp["norm_g"][l].reshape(8, 128).T
        pv[l, :, PV_CONVW:PV_CONVW + 32] = inp["conv_w"][l].reshape(4, 8, 128).transpose(2, 1, 0).reshape(128, 32)
        pv[l, :, PV_CONVB:PV_CONVB + 8] = inp["conv_b"][l].reshape(8, 128).T
        pv[l, :, PV_BGX:PV_BGX + 8] = inp["b_gate_x"][l].T
        pv[l, :, PV_BGA:PV_BGA + 8] = inp["b_gate_a"][l].T
        pv[l, :, PV_LAM:PV_LAM + 8] = inp["lru_lambda"][l].reshape(8, 128).T
        pv[l, :, PV_CQG:PV_CQG + 2] = inp["cq_norm_g"][l].reshape(2, 128).T
        pv[l, :, PV_CKVG] = inp["ckv_norm_g"][l]
        pv[l, 0:96, PV_MQG] = inp["mla_q_norm_g"][l]
        pv[l, 0:96, PV_MKG] = inp["mla_k_norm_g"][l]
        pv[l, 0:64, PV_DQKG] = inp["dil_q_norm_g"][l]
        pv[l, 64:128, PV_DQKG] = inp["dil_k_norm_g"][l]
        pv[l, :, PV_BMRG:PV_BMRG + 24] = inp["b_merge"][l].reshape(24, 128).T
    return pv


def make_in_maps(inp, n_cores=8):
    cmat, cmask, cinv = _host_consts()
    pvec = _host_pvec(inp)
    shared = {
        "w_in": np.ascontiguousarray(inp["w_in"]), "w_gate_x": np.ascontiguousarray(inp["w_gate_x"]),
        "w_gate_a": np.ascontiguousarray(inp["w_gate_a"]), "w_lru_o": np.ascontiguousarray(inp["w_lru_o"]),
        "w_uq": np.ascontiguousarray(inp["w_uq"]), "w_ukv": np.ascontiguousarray(inp["w_ukv"]),
        "w_mla_o": np.ascontiguousarray(inp["w_mla_o"]), "w_dil_o": np.ascontiguousarray(inp["w_dil_o"]),
        "w_out": np.ascontiguousarray(inp["w_out"]), "pvec": pvec, "cmat": cmat, "cmask": cmask, "cinv": cinv,
    }
    maps = []
    for b in range(n_cores):
        m = dict(shared)
        m["x"] = np.ascontiguousarray(inp["x"][b])
        m["pos"] = np.ascontiguousarray(inp["positions"][b].reshape(1, S).astype(np.int32))
        maps.append(m)
    return maps


def kernel(**inputs):
    inp = {k: np.asarray(v) for k, v in inputs.items()}
    nc = build_program()
    maps = make_in_maps(inp, 8)
    res = run_bass_kernel_spmd(nc, maps, core_ids=list(range(8)))
    out = np.stack([np.asarray(res.results[b]["out"]).reshape(S, D) for b in range(8)], axis=0)
    return out.astype(np.float32)


def make_tables(ctx, which, CT, ST, tab_t):
    nc, P, AR = ctx["nc"], ctx["P"], ctx["AR"]
    V, A_ = nc.vector, nc.scalar
    pos_in = ctx["env"]["pos_in"]
    inv = ctx["cinv"][:, which:which + 1]
    m = AR.mark()
    posi = [AR.alloc([1024], I32) for _ in range(2)]
    y0 = [AR.alloc([1024], F32) for _ in range(2)]
    yy = [AR.alloc([1024], F32) for _ in range(2)]
    ki = [AR.alloc([1024], I32) for _ in range(2)]
    kf = [AR.alloc([1024], F32) for _ in range(2)]
    tk = [{n_: Tok() for n_ in ("posi", "y0", "yy", "ki", "kf")} for _ in range(2)]
    TWO_PI = 2.0 * math.pi * (1.0 - 1e-6)
    for c in range(4):
        b = c % 2
        t = tk[b]
        P.dma("sp", (lambda b=b, c=c: nc.sync.dma_start(out=posi[b], in_=pos_in[:, c * 1024:(c + 1) * 1024].broadcast_to([128, 1024]))),
              writes=[t["posi"]])
        P.op("dve", (lambda b=b: V.tensor_copy(y0[b], posi[b])), reads=[t["posi"]], writes=[t["y0"]])
        P.op("dve", (lambda b=b: V.tensor_scalar(y0[b], y0[b], inv, None, ALU.mult)), reads=[t["y0"], ctx["cinv_t"]], writes=[t["y0"]])
        P.op("dve", (lambda b=b: V.tensor_scalar(y0[b], y0[b], 1.0 / (2.0 * math.pi), None, ALU.mult)), reads=[t["y0"]], writes=[t["y0"]])
        for ph, OUT in ((0.0, ST), (0.25, CT)):
            P.op("dve", (lambda b=b, ph=ph: V.tensor_scalar(yy[b], y0[b], ph, None, ALU.add)), reads=[t["y0"]], writes=[t["yy"]])
            P.op("dve", (lambda b=b: V.tensor_copy(ki[b], yy[b])), reads=[t["yy"]], writes=[t["ki"]])
            P.op("dve", (lambda b=b: V.tensor_copy(kf[b], ki[b])), reads=[t["ki"]], writes=[t["kf"]])
            P.op("dve", (lambda b=b: V.tensor_tensor(out=yy[b], in0=yy[b], in1=kf[b], op=ALU.subtract)), reads=[t["yy"], t["kf"]], writes=[t["yy"]])
            P.op("dve", (lambda b=b: V.tensor_scalar(kf[b], yy[b], 0.5, None, ALU.is_gt)), reads=[t["yy"]], writes=[t["kf"]])
            P.op("dve", (lambda b=b: V.tensor_tensor(out=yy[b], in0=yy[b], in1=kf[b], op=ALU.subtract)), reads=[t["yy"], t["kf"]], writes=[t["yy"]])
            P.op("act", (lambda b=b, OUT=OUT, c=c: A_.activation(out=OUT[:, c * 1024:(c + 1) * 1024], in_=yy[b], func=AF.Sin, scale=TWO_PI)),
                 reads=[t["yy"]], writes=[tab_t])
    P.barrier()
    AR.release(m)


def norm_rope_chain(ctx, src_bank, np_, ones_m, eps_v, gcol, CT, ST, rot_m, tt, out_ap, out_tok, tmp, tmp_t, bank_ss, bank_rot):
    nc, P = ctx["nc"], ctx["P"]
    V, A_ = nc.vector, nc.scalar
    banks, bank_tok = ctx["banks"], ctx["bank_tok"]
    IDENT = ctx["cm"][:, 0, :]
    src = banks[src_bank][0:np_, :]
    st_ = bank_tok[src_bank]
    cols = slice(tt * 512, (tt + 1) * 512)
    P.op("act", lambda: A_.activation(out=tmp["sq"][0:np_, :], in_=src, func=AF.Square), reads=[st_], writes=[tmp_t["sq"]])
    P.op("pe", _mm(nc, banks[bank_ss][0:np_, :], ones_m[0:np_, 0:np_], tmp["sq"][0:np_, :], True, True),
         reads=[tmp_t["sq"], ctx["cm_t"]], writes=[bank_tok[bank_ss]])
    P.op("act", lambda: A_.activation(out=tmp["rs"][0:np_, :], in_=banks[bank_ss][0:np_, :], func=AF.Ln, bias=eps_v),
         reads=[bank_tok[bank_ss]], writes=[tmp_t["rs"]])
    P.op("act", lambda: A_.activation(out=tmp["rs"][0:np_, :], in_=tmp["rs"][0:np_, :], func=AF.Exp, scale=-0.5),
         reads=[tmp_t["rs"]], writes=[tmp_t["rs"]])
    P.op("dve", lambda: V.scalar_tensor_tensor(tmp["u1"][0:np_, :], src, gcol[0:np_, :], CT[0:np_, cols], ALU.mult, ALU.mult),
         reads=[st_, ctx["pv_t"], tmp_t["tab"]], writes=[tmp_t["u1"]])
    P.op("dve", lambda: V.scalar_tensor_tensor(tmp["u2"][0:np_, :], src, gcol[0:np_, :], ST[0:np_, cols], ALU.mult, ALU.mult),
         reads=[st_, ctx["pv_t"], tmp_t["tab"]], writes=[tmp_t["u2"]])
    P.op("pe", _mm(nc, banks[bank_rot][0:np_, :], IDENT[0:np_, 0:np_], tmp["u1"][0:np_, :], True, False),
         reads=[tmp_t["u1"], ctx["cm_t"]], writes=[bank_tok[bank_rot]])
    P.op("pe", _mm(nc, banks[bank_rot][0:np_, :], rot_m[0:np_, 0:np_], tmp["u2"][0:np_, :], False, True),
         reads=[tmp_t["u2"], ctx["cm_t"]], writes=[bank_tok[bank_rot]])
    P.op("dve", lambda: V.tensor_tensor(out=out_ap, in0=banks[bank_rot][0:np_, :], in1=tmp["rs"][0:np_, :], op=ALU.mult),
         reads=[bank_tok[bank_rot], tmp_t["rs"]], writes=[out_tok])


def chain_group(ctx, chains, CT, ST):
    nc, P = ctx["nc"], ctx["P"]
    V, A_ = nc.vector, nc.scalar
    banks, bank_tok = ctx["banks"], ctx["bank_tok"]
    IDENT = ctx["cm"][:, 0, :]
    for c in chains:
        n_ = c["np"]
        P.op("act", (lambda c=c, n_=n_: A_.activation(out=c["tmp"]["sq"][0:n_, :], in_=banks[c["src"]][0:n_, :], func=AF.Square)),
             reads=[bank_tok[c["src"]]], writes=[c["tmp_t"]["sq"]])
    for c in chains:
        n_ = c["np"]
        P.op("pe", _mm(nc, banks[c["xb"]][0:n_, :], c["ones"][0:n_, 0:n_], c["tmp"]["sq"][0:n_, :], True, True),
             reads=[c["tmp_t"]["sq"], ctx["cm_t"]], writes=[bank_tok[c["xb"]]])
    for c in chains:
        n_ = c["np"]
        P.op("act", (lambda c=c, n_=n_: A_.activation(out=c["tmp"]["rs"][0:n_, :], in_=banks[c["xb"]][0:n_, :], func=AF.Ln, bias=c["eps"])),
             reads=[bank_tok[c["xb"]]], writes=[c["tmp_t"]["rs"]])
    for c in chains:
        n_ = c["np"]
        P.op("act", (lambda c=c, n_=n_: A_.activation(out=c["tmp"]["rs"][0:n_, :], in_=c["tmp"]["rs"][0:n_, :], func=AF.Exp, scale=-0.5)),
             reads=[c["tmp_t"]["rs"]], writes=[c["tmp_t"]["rs"]])
    for c in chains:
        n_ = c["np"]
        cols = slice(c["tt"] * 512, (c["tt"] + 1) * 512)
        P.op("dve", (lambda c=c, n_=n_, cols=cols: V.scalar_tensor_tensor(c["tmp"]["u1"][0:n_, :], banks[c["src"]][0:n_, :], c["g"][0:n_, :],
                                                                          CT[0:n_, cols], ALU.mult, ALU.mult)),
             reads=[bank_tok[c["src"]], ctx["pv_t"], c["tmp_t"]["tab"]], writes=[c["tmp_t"]["u1"]])
        P.op("dve", (lambda c=c, n_=n_, cols=cols: V.scalar_tensor_tensor(c["tmp"]["u2"][0:n_, :], banks[c["src"]][0:n_, :], c["g"][0:n_, :],
                                                                          ST[0:n_, cols], ALU.mult, ALU.mult)),
             reads=[bank_tok[c["src"]], ctx["pv_t"], c["tmp_t"]["tab"]], writes=[c["tmp_t"]["u2"]])
    for c in chains:
        n_ = c["np"]
        P.op("pe", _mm(nc, banks[c["xb"]][0:n_, :], IDENT[0:n_, 0:n_], c["tmp"]["u1"][0:n_, :], True, False),
             reads=[c["tmp_t"]["u1"], ctx["cm_t"]], writes=[bank_tok[c["xb"]]])
        P.op("pe", _mm(nc, banks[c["xb"]][0:n_, :], c["rot"][0:n_, 0:n_], c["tmp"]["u2"][0:n_, :], False, True),
             reads=[c["tmp_t"]["u2"], ctx["cm_t"]], writes=[bank_tok[c["xb"]]])
    for c in chains:
        n_ = c["np"]
        P.op("dve", (lambda c=c, n_=n_: V.tensor_tensor(out=c["out"], in0=banks[c["xb"]][0:n_, :], in1=c["tmp"]["rs"][0:n_, :], op=ALU.mult)),
             reads=[bank_tok[c["xb"]], c["tmp_t"]["rs"]], writes=[c["out_tok"]])


def chain_pipeline(ctx, chains, CT, ST):
    nc, P = ctx["nc"], ctx["P"]
    V, A_ = nc.vector, nc.scalar
    banks, bank_tok = ctx["banks"], ctx["bank_tok"]
    IDENT = ctx["cm"][:, 0, :]

    def stage(c, s_):
        n_ = c["np"]
        t_, tt_ = c["tmp"], c["tmp_t"]
        cols = slice(c["tt"] * 512, (c["tt"] + 1) * 512)
        src = banks[c["src"]][0:n_, :]
        xb = banks[c["xb"]][0:n_, :]
        if s_ == 0:
            c["proj"]()
        elif s_ == 1:
            P.op("act", lambda: A_.activation(out=t_["sq"][0:n_, :], in_=src, func=AF.Square), reads=[bank_tok[c["src"]]], writes=[tt_["sq"]])
            P.op("dve", lambda: V.scalar_tensor_tensor(t_["u1"][0:n_, :], src, c["g"][0:n_, :], CT[0:n_, cols], ALU.mult, ALU.mult),
                 reads=[bank_tok[c["src"]], ctx["pv_t"], tt_["tab"]], writes=[tt_["u1"]])
        elif s_ == 2:
            P.op("pe", _mm(nc, xb, c["ones"][0:n_, 0:n_], t_["sq"][0:n_, :], True, True), reads=[tt_["sq"], ctx["cm_t"]], writes=[bank_tok[c["xb"]]])
            P.op("dve", lambda: V.scalar_tensor_tensor(t_["u2"][0:n_, :], src, c["g"][0:n_, :], ST[0:n_, cols], ALU.mult, ALU.mult),
                 reads=[bank_tok[c["src"]], ctx["pv_t"], tt_["tab"]], writes=[tt_["u2"]])
        elif s_ == 3:
            P.op("act", lambda: A_.activation(out=t_["rs"][0:n_, :], in_=xb, func=AF.Ln, bias=c["eps"]), reads=[bank_tok[c["xb"]]], writes=[tt_["rs"]])
        elif s_ == 4:
            P.op("act", lambda: A_.activation(out=t_["rs"][0:n_, :], in_=t_["rs"][0:n_, :], func=AF.Exp, scale=-0.5), reads=[tt_["rs"]], writes=[tt_["rs"]])
            P.op("pe", _mm(nc, xb, IDENT[0:n_, 0:n_], t_["u1"][0:n_, :], True, False), reads=[tt_["u1"], ctx["cm_t"]], writes=[bank_tok[c["xb"]]])
            P.op("pe", _mm(nc, xb, c["rot"][0:n_, 0:n_], t_["u2"][0:n_, :], False, True), reads=[tt_["u2"], ctx["cm_t"]], writes=[bank_tok[c["xb"]]])
        elif s_ == 5:
            P.op("dve", lambda: V.tensor_tensor(out=c["out"], in0=xb, in1=t_["rs"][0:n_, :], op=ALU.mult),
                 reads=[bank_tok[c["xb"]], tt_["rs"]], writes=[c["out_tok"]])
        elif s_ == 6:
            if c.get("post"):
                c["post"]()

    NST = 7
    for t in range(len(chains) + NST - 1):
        for ci, c in enumerate(chains):
            s_ = t - ci
            if 0 <= s_ < NST:
                stage(c, s_)


def finalize_head(ctx, o_ap, o_tok, den_ap, den_tok, gate_w, gate_wt, gcol0, hT, hT_t, tt, ydst, tmp, tmp_t, gbank=7):
    nc, P = ctx["nc"], ctx["P"]
    V, A_ = nc.vector, nc.scalar
    banks, bank_tok = ctx["banks"], ctx["bank_tok"]
    proj_group(ctx, gbank, gate_w, gate_wt, gcol0, 64, hT, hT_t, tt)
    g_ps = banks[gbank][0:64, :]
    P.op("dve", lambda: V.tensor_copy(tmp["den0"][0:64, :], den_ap), reads=[den_tok], writes=[tmp_t["den0"]])
    yield
    P.op("act", lambda: A_.activation(out=tmp["e1"][0:64, :], in_=g_ps, func=AF.Exp, scale=-1.0), reads=[bank_tok[gbank]], writes=[tmp_t["e1"]])
    yield
    P.op("dve", lambda: V.scalar_tensor_tensor(tmp["e1"][0:64, :], tmp["e1"][0:64, :], 1.0, tmp["den0"][0:64, :], ALU.add, ALU.mult),
         reads=[tmp_t["e1"], tmp_t["den0"]], writes=[tmp_t["e1"]])
    yield
    P.op("act", lambda: A_.activation(out=tmp["e1"][0:64, :], in_=tmp["e1"][0:64, :], func=AF.Ln), reads=[tmp_t["e1"]], writes=[tmp_t["e1"]])
    yield
    P.op("act", lambda: A_.activation(out=tmp["e1"][0:64, :], in_=tmp["e1"][0:64, :], func=AF.Exp, scale=-1.0), reads=[tmp_t["e1"]], writes=[tmp_t["e1"]])
    yield
    P.op("dve", lambda: V.tensor_tensor(out=tmp["e1"][0:64, :], in0=tmp["e1"][0:64, :], in1=g_ps, op=ALU.mult),
         reads=[tmp_t["e1"], bank_tok[gbank]], writes=[tmp_t["e1"]])
    yield
    P.op("dve", lambda: V.tensor_tensor(out=tmp["yb"][0:64, :], in0=o_ap, in1=tmp["e1"][0:64, :], op=ALU.mult),
         reads=[o_tok, tmp_t["e1"]], writes=[tmp_t["yb"]])
    P.dma("sp", lambda: nc.sync.dma_start(out=ydst, in_=tmp["yb"][0:64, :]), reads=[tmp_t["yb"]])


def _fin_tmp(AR):
    tmp = {"e1": AR.alloc([512], F32), "den0": AR.alloc([512], F32), "yb": AR.alloc([512], BF16), "den": AR.alloc([512], F32)}
    return tmp, {k: Tok() for k in tmp}


def lockstep(gens):
    live = list(gens)
    while live:
        for g_ in list(live):
            try:
                next(g_)
            except StopIteration:
                live.remove(g_)


def _chain_tmp(AR):
    tmp = {"sq": AR.alloc([512], BF16), "rs": AR.alloc([512], F32), "u1": AR.alloc([512], BF16), "u2": AR.alloc([512], BF16),
           "e1": AR.alloc([512], F32), "den0": AR.alloc([512], F32), "yb": AR.alloc([512], BF16), "den": AR.alloc([512], F32)}
    tmp_t = {k: Tok() for k in list(tmp) + ["tab"]}
    return tmp, tmp_t


def _chain_tmp2(AR, tmp_t):
    tmp = {"sq": AR.alloc([512], BF16), "rs": AR.alloc([512], F32), "u1": AR.alloc([512], BF16), "u2": AR.alloc([512], BF16)}
    t2 = {k: Tok() for k in tmp}
    t2["tab"] = tmp_t["tab"]
    return tmp, t2


def phase_mla(ctx, l, hT, hT_t):
    nc, P, AR = ctx["nc"], ctx["P"], ctx["AR"]
    V, A_, G_ = nc.vector, nc.scalar, nc.gpsimd
    env = ctx["env"]
    banks, bank_tok = ctx["banks"], ctx["bank_tok"]
    pv, pv_t, cm, cm_t = ctx["pv"], ctx["pv_t"], ctx["cm"], ctx["cm_t"]
    ONES256, ONES128, ONES96, RM_, EPAD = cm[:, 1, :], cm[:, 2, :], cm[:, 4, :], cm[:, 6, :], cm[:, 7, :]
    cmask, cmask_t = ctx["cmask"], ctx["cmask_t"]
    ymla = env["ymla"]
    w_r = env["w_in"][l].rearrange("(c p) n -> p c n", p=128)
    CT = AR.alloc([S], BF16)
    ST = AR.alloc([S], BF16)
    tmp, tmp_t = _chain_tmp(AR)
    make_tables(ctx, 1, CT, ST, tmp_t["tab"])
    if env.get("stop") == "tables":
        return
    wm = AR.alloc([8, 416], BF16)
    wuq = AR.alloc([2, 768], BF16)
    wukv = AR.alloc([1024], BF16)
    wgm = AR.alloc([8, 512], BF16)
    wkp = AR.alloc([8, 96], BF16)
    w_t = Tok()
    wkp_t = Tok()
    P.dma("pool", lambda: G_.dma_start(out=wm, in_=w_r[:, :, C_CQ:C_CQ + 416]), writes=[w_t])
    P.dma("pool", lambda: G_.dma_start(out=wuq, in_=env["w_uq"][l].rearrange("(c p) n -> p c n", p=128)), writes=[w_t])
    P.dma("pool", lambda: G_.dma_start(out=wukv, in_=env["w_ukv"][l]), writes=[w_t])
    P.dma("pool", lambda: G_.dma_start(out=wgm, in_=w_r[:, :, C_MG:C_MG + 512]), writes=[w_t])
    P.op("dve", lambda: V.memset(wkp, 0.0), writes=[wkp_t])
    P.op("dve", lambda: V.tensor_copy(wkp[:, :, 0:64], wukv.rearrange("p (h c) -> p h c", h=8)[:, :, 0:64]), reads=[w_t], writes=[wkp_t])
    CQN = AR.alloc([2, S], BF16)
    CKVN = AR.alloc([S], BF16)
    KR = AR.alloc([S], BF16)
    lat_t = toks(8)
    m1_mark = AR.mark()
    sq3 = [AR.alloc([3, 512], BF16) for _ in range(2)]
    sq3_t = toks(2)
    rs2 = [AR.alloc([2, 512], F32) for _ in range(2)]
    rs2_t = toks(2)
    for tt in range(8):
        b = tt % 2
        cols = slice(tt * 512, (tt + 1) * 512)
        proj_group(ctx, 0, wm, w_t, 0, 128, hT, hT_t, tt)
        proj_group(ctx, 1, wm, w_t, 128, 128, hT, hT_t, tt)
        proj_group(ctx, 2, wm, w_t, 256, 128, hT, hT_t, tt)
        proj_group(ctx, 3, wm, w_t, 384, 32, hT, hT_t, tt)
        for i in range(3):
            P.op("act", (lambda b=b, i=i: A_.activation(out=sq3[b][:, i, :], in_=banks[i][:, :], func=AF.Square)),
                 reads=[bank_tok[i]], writes=[sq3_t[b]])
        P.op("pe", _mm(nc, banks[4][:, :], ONES256, sq3[b][:, 0, :], True, False), reads=[sq3_t[b], cm_t], writes=[bank_tok[4]])
        P.op("pe", _mm(nc, banks[4][:, :], ONES256, sq3[b][:, 1, :], False, True), reads=[sq3_t[b], cm_t], writes=[bank_tok[4]])
        P.op("pe", _mm(nc, banks[5][:, :], ONES128, sq3[b][:, 2, :], True, True), reads=[sq3_t[b], cm_t], writes=[bank_tok[5]])
        for i, bk in ((0, 4), (1, 5)):
            P.op("act", (lambda b=b, i=i, bk=bk: A_.activation(out=rs2[b][:, i, :], in_=banks[bk][:, :], func=AF.Ln, bias=EPS)),
                 reads=[bank_tok[bk]], writes=[rs2_t[b]])
            P.op("act", (lambda b=b, i=i: A_.activation(out=rs2[b][:, i, :], in_=rs2[b][:, i, :], func=AF.Exp, scale=-0.5)),
                 reads=[rs2_t[b]], writes=[rs2_t[b]])
        for c in range(2):
            P.op("dve", (lambda b=b, c=c, cols=cols: V.scalar_tensor_tensor(CQN[:, c, cols], banks[c][:, :], pv[:, l, PV_CQG + c:PV_CQG + c + 1],
                                                                            rs2[b][:, 0, :], ALU.mult, ALU.mult)),
                 reads=[bank_tok[c], rs2_t[b], pv_t], writes=[lat_t[tt]])
        P.op("dve", (lambda b=b, cols=cols: V.scalar_tensor_tensor(CKVN[:, cols], banks[2][:, :], pv[:, l, PV_CKVG:PV_CKVG + 1],
                                                                   rs2[b][:, 1, :], ALU.mult, ALU.mult)),
             reads=[bank_tok[2], rs2_t[b], pv_t], writes=[lat_t[tt]])
        P.op("act", (lambda cols=cols: A_.copy(KR[0:32, cols], banks[3][0:32, :])), reads=[bank_tok[3]], writes=[lat_t[tt]])
    if env.get("stop") == "latent":
        return
    P.barrier()
    AR.release(m1_mark)
    tmpk, tmpk_t = _chain_tmp2(AR, tmp_t)
    QT = AR.alloc([S], BF16)
    KT = AR.alloc([S], BF16)
    QT_t = toks(8)
    KT_t = toks(8)
    VH = AR.alloc([32, 128], BF16)
    VH_t = Tok()
    P.op("dve", lambda: V.memset(VH[:, :, 64:128], 1.0), writes=[VH_t])
    Pb = [AR.alloc([512], BF16) for _ in range(4)]
    Pb_t = toks(4)
    gq = pv[:, l, PV_MQG:PV_MQG + 1]
    gk = pv[:, l, PV_MKG:PV_MKG + 1]
    SC = math.sqrt(96.0)
    cnt = 0
    csets = [(tmp, tmp_t), (tmpk, tmpk_t)] + [_chain_tmp2(AR, tmp_t) for _ in range(2)]
    fsets = [(tmp, tmp_t), _fin_tmp(AR)]
    fin_gen = None
    for h in range(8):
        for tt in range(8):
            for j in range(4):
                P.op("pe", _mm(nc, banks[7][:, j * 64:(j + 1) * 64], CKVN[:, tt * 512 + j * 128:tt * 512 + (j + 1) * 128],
                               wukv[:, h * 128 + 64:h * 128 + 128], True, True),
                     reads=[w_t, lat_t[tt]], writes=[bank_tok[7]])
            P.op("act", (lambda tt=tt: A_.copy(VH[:, tt * 4:(tt + 1) * 4, 0:64], banks[7][:, 0:256].rearrange("p (j c) -> p j c", j=4))),
                 reads=[bank_tok[7]], writes=[VH_t])
        chains = []
        for ci in range(16):
            tt, kind = ci // 2, ci % 2
            st_ = ci % 4
            cols = slice(tt * 512, (tt + 1) * 512)
            if kind == 0:
                def proj(st_=st_, cols=cols, tt=tt, h=h):
                    for c in range(2):
                        P.op("pe", _mm(nc, banks[st_][0:96, :], wuq[:, c, h * 96:(h + 1) * 96], CQN[:, c, cols], c == 0, c == 1),
                             reads=[w_t, lat_t[tt]], writes=[bank_tok[st_]])
                chains.append(dict(src=st_, np=96, ones=ONES96, eps=96 * EPS, g=gq, rot=RM_, tt=tt, out=QT[0:96, cols], out_tok=QT_t[tt],
                                   tmp=csets[st_][0], tmp_t=csets[st_][1], xb=4 + st_, proj=proj))
            else:
                def proj(st_=st_, cols=cols, tt=tt, h=h):
                    P.op("pe", _mm(nc, banks[st_][0:96, :], wkp[:, h, :], CKVN[:, cols], True, False), reads=[wkp_t, lat_t[tt]], writes=[bank_tok[st_]])
                    P.op("pe", _mm(nc, banks[st_][0:96, :], EPAD[0:32, 0:96], KR[0:32, cols], False, True), reads=[cm_t, lat_t[tt]], writes=[bank_tok[st_]])
                chains.append(dict(src=st_, np=96, ones=ONES96, eps=96 * EPS, g=gk, rot=RM_, tt=tt, out=KT[0:96, cols], out_tok=KT_t[tt],
                                   tmp=csets[st_][0], tmp_t=csets[st_][1], xb=4 + st_, proj=proj))
        chain_pipeline(ctx, chains, CT, ST)
        if env.get("stop") == "prep":
            return
        for qt in range(8):
            if env.get("stop") == "attn1" and qt == 1:
                return
            qcols = slice(qt * 512, (qt + 1) * 512)
            nkb = 4 * qt + 4
            SBK = (0, 1, 2, 3, 4)
            NSB = 5
            ob = 6 + qt % 2
            fs = fsets[qt % 2]
            base = cnt
            cnt += nkb
            for i in range(nkb + 3):
                if fin_gen is not None:
                    try:
                        next(fin_gen)
                    except StopIteration:
                        fin_gen = None
                if i < nkb:
                    kb = i
                    sb = SBK[(base + kb) % NSB]
                    P.op("pe", _mm(nc, banks[sb][:, :], KT[0:96, kb * 128:(kb + 1) * 128], QT[0:96, qcols], True, True),
                         reads=[KT_t[kb // 4], QT_t[qt]], writes=[bank_tok[sb]])
                if 0 <= i - 2 < nkb:
                    kb = i - 2
                    sb = SBK[(base + kb) % NSB]
                    pi = (base + kb) % 4
                    P.op("act", (lambda sb=sb, pi=pi: A_.activation(out=Pb[pi], in_=banks[sb][:, :], func=AF.Exp, scale=SC)),
                         reads=[bank_tok[sb]], writes=[Pb_t[pi]])
                    j = kb - 4 * qt
                    if j >= 0:
                        P.op("dve", (lambda pi=pi, j=j: V.tensor_tensor(out=Pb[pi], in0=Pb[pi], in1=cmask[:, j * 512:(j + 1) * 512], op=ALU.mult)),
                             reads=[Pb_t[pi], cmask_t], writes=[Pb_t[pi]])
                if 0 <= i - 3 < nkb:
                    kb = i - 3
                    pi = (base + kb) % 4
                    P.op("pe", _mm(nc, banks[ob][:, :], VH[:, kb, :], Pb[pi], kb == 0, kb == nkb - 1),
                         reads=[VH_t, Pb_t[pi]], writes=[bank_tok[ob]])
            while fin_gen is not None:
                try:
                    next(fin_gen)
                except StopIteration:
                    fin_gen = None
            P.op("act", (lambda fs=fs, ob=ob: A_.copy(fs[0]["den"][64:128, :], banks[ob][64:128, :])), reads=[bank_tok[ob]], writes=[fs[1]["den"]])
            fin_gen = finalize_head(ctx, banks[ob][0:64, :], bank_tok[ob], fs[0]["den"][64:128, :], fs[1]["den"], wgm, w_t, h * 64, hT, hT_t, qt,
                                    ymla[h * 64:(h + 1) * 64, qcols], fs[0], fs[1], gbank=5)
        while fin_gen is not None:
            try:
                next(fin_gen)
            except StopIteration:
                fin_gen = None


def phase_dil(ctx, l, hT, hT_t):
    nc, P, AR = ctx["nc"], ctx["P"], ctx["AR"]
    V, A_, G_ = nc.vector, nc.scalar, nc.gpsimd
    env = ctx["env"]
    banks, bank_tok = ctx["banks"], ctx["bank_tok"]
    pv, pv_t, cm, cm_t = ctx["pv"], ctx["pv_t"], ctx["cm"], ctx["cm_t"]
    ONESB64, RD_ = cm[:, 3, :], cm[:, 5, :]
    cmask, cmask_t = ctx["cmask"], ctx["cmask_t"]
    DMASK = cmask[:, 2048:2048 + 256]
    ydil = env["ydil"]
    w_r = env["w_in"][l].rearrange("(c p) n -> p c n", p=128)
    CT = AR.alloc([S], BF16)
    ST = AR.alloc([S], BF16)
    tmp, tmp_t = _chain_tmp(AR)
    make_tables(ctx, 0, CT, ST, tmp_t["tab"])
    csets = [(tmp, tmp_t)] + [_chain_tmp2(AR, tmp_t) for _ in range(3)]
    fsets = [(tmp, tmp_t)] + [_fin_tmp(AR) for _ in range(3)]
    wqk = [AR.alloc([8, 128], BF16) for _ in range(2)]
    wv = [AR.alloc([8, 64], BF16) for _ in range(2)]
    wgh_t = toks(2)
    wdg = AR.alloc([8, 512], BF16)
    wdg_t = Tok()
    P.dma("pool", lambda: G_.dma_start(out=wdg, in_=w_r[:, :, C_DG:C_DG + 512]), writes=[wdg_t])
    QK = AR.alloc([S], BF16)
    K0 = AR.alloc([S], BF16)
    QK_t = toks(8)
    K0_t = toks(8)
    VD = AR.alloc([32, 128], BF16)
    VD_t = Tok()
    P.op("dve", lambda: V.memset(VD[:, :, 64:128], 1.0), writes=[VD_t])
    OACC = AR.alloc([S], F32)
    OACC_t = Tok()
    Pd = [AR.alloc([256], BF16) for _ in range(4)]
    Pd_t = toks(4)
    gqk = pv[:, l, PV_DQKG:PV_DQKG + 1]
    it = 0
    cnt = 0
    for h in range(8):
        for g in range(3):
            d = DIL[g]
            nb = S // (128 * d)
            wb = it % 2
            it += 1
            hd = g * 8 + h
            P.dma("pool", (lambda wb=wb, hd=hd: G_.dma_start(out=wqk[wb][:, :, 0:64], in_=w_r[:, :, C_DQ + hd * 64:C_DQ + (hd + 1) * 64])), writes=[wgh_t[wb]])
            P.dma("pool", (lambda wb=wb, hd=hd: G_.dma_start(out=wqk[wb][:, :, 64:128], in_=w_r[:, :, C_DK + hd * 64:C_DK + (hd + 1) * 64])), writes=[wgh_t[wb]])
            P.dma("pool", (lambda wb=wb, hd=hd: G_.dma_start(out=wv[wb], in_=w_r[:, :, C_DV + hd * 64:C_DV + (hd + 1) * 64])), writes=[wgh_t[wb]])
            chains = []
            for tt in range(8):
                st_ = tt % 4
                cols = slice(tt * 512, (tt + 1) * 512)

                def proj(st_=st_, tt=tt, wb=wb):
                    proj_group(ctx, st_, wqk[wb], wgh_t[wb], 0, 128, hT, hT_t, tt)

                def post(cols=cols, tt=tt):
                    P.op("dve", lambda: V.tensor_copy(K0[0:64, cols], QK[64:128, cols]), reads=[QK_t[tt]], writes=[K0_t[tt]])
                chains.append(dict(src=st_, np=128, ones=ONESB64, eps=EPS, g=gqk, rot=RD_, tt=tt, out=QK[:, cols], out_tok=QK_t[tt],
                                   tmp=csets[st_][0], tmp_t=csets[st_][1], xb=4 + st_, proj=proj, post=post))
            chain_pipeline(ctx, chains, CT, ST)

            def ucols(r, n, d=d):
                st0 = r + d * 128 * n
                return slice(st0, st0 + d * 127 + 1, d)
            units = [(r, n) for r in range(d) for n in range(nb)]
            for u0 in range(0, 32, 4):
                for jj in range(4):
                    r, n = units[u0 + jj]
                    for k in range(8):
                        P.op("pe", _mm(nc, banks[7][:, jj * 64:(jj + 1) * 64], hT[:, k, ucols(r, n)], wv[wb][:, k, :], k == 0, k == 7),
                             reads=[wgh_t[wb]] + hT_t, writes=[bank_tok[7]])
                P.op("act", (lambda u0=u0: A_.copy(VD[:, u0:u0 + 4, 0:64], banks[7][:, 0:256].rearrange("p (j c) -> p j c", j=4))),
                     reads=[bank_tok[7]], writes=[VD_t])
            SBK = (0, 1, 2, 3, 4, 5)
            base = cnt
            cnt += 32
            for i in range(32 + 3):
                if i < 32:
                    u = i
                    r, n = units[u]
                    sb = SBK[(base + u) % 6]
                    qc = ucols(r, n)
                    if n > 0:
                        P.op("pe", _mm(nc, banks[sb][:, 0:128], K0[0:64, ucols(r, n - 1)], QK[0:64, qc], True, True),
                             reads=K0_t + QK_t, writes=[bank_tok[sb]])
                    P.op("pe", _mm(nc, banks[sb][:, 128:256], K0[0:64, qc], QK[0:64, qc], True, True),
                         reads=K0_t + QK_t, writes=[bank_tok[sb]])
                if 0 <= i - 2 < 32:
                    u = i - 2
                    r, n = units[u]
                    sb = SBK[(base + u) % 6]
                    pi = (base + u) % 4
                    lo = 0 if n > 0 else 128
                    P.op("act", (lambda sb=sb, pi=pi, lo=lo: A_.activation(out=Pd[pi][:, lo:256], in_=banks[sb][:, lo:256], func=AF.Exp, scale=0.125)),
                         reads=[bank_tok[sb]], writes=[Pd_t[pi]])
                    P.op("dve", (lambda pi=pi, lo=lo: V.tensor_tensor(out=Pd[pi][:, lo:256], in0=Pd[pi][:, lo:256], in1=DMASK[:, lo:256], op=ALU.mult)),
                         reads=[Pd_t[pi], cmask_t], writes=[Pd_t[pi]])
                if 0 <= i - 3 < 32:
                    u = i - 3
                    r, n = units[u]
                    pi = (base + u) % 4
                    qc = ucols(r, n)
                    ob = 6 + (base + u) % 2
                    if n > 0:
                        P.op("pe", _mm(nc, banks[ob][:, 0:128], VD[:, u - 1, :], Pd[pi][:, 0:128], True, False), reads=[VD_t, Pd_t[pi]], writes=[bank_tok[ob]])
                    P.op("pe", _mm(nc, banks[ob][:, 0:128], VD[:, u, :], Pd[pi][:, 128:256], n == 0, True), reads=[VD_t, Pd_t[pi]], writes=[bank_tok[ob]])
                    if g == 0:
                        P.op("dve", (lambda qc=qc, ob=ob: V.tensor_copy(OACC[:, qc], banks[ob][:, 0:128])), reads=[bank_tok[ob]], writes=[OACC_t])
                    else:
                        P.op("dve", (lambda qc=qc, ob=ob: V.tensor_tensor(out=OACC[:, qc], in0=banks[ob][:, 0:128], in1=OACC[:, qc], op=ALU.add)),
                             reads=[bank_tok[ob], OACC_t], writes=[OACC_t])
        for t4 in range(0, 8, 4):
            gens = []
            for jj in range(4):
                tt = t4 + jj
                cols = slice(tt * 512, (tt + 1) * 512)
                gens.append(finalize_head(ctx, OACC[0:64, cols], OACC_t, OACC[64:128, cols], OACC_t, wdg, wdg_t, h * 64, hT, hT_t, tt,
                                          ydil[h * 64:(h + 1) * 64, cols], fsets[jj][0], fsets[jj][1], gbank=4 + jj))
            lockstep(gens)


def phase_out(ctx, l, src, dst):
    nc, P, AR = ctx["nc"], ctx["P"], ctx["AR"]
    V, A_, G_ = nc.vector, nc.scalar, nc.gpsimd
    env = ctx["env"]
    banks, bank_tok = ctx["banks"], ctx["bank_tok"]
    TE = 256
    wlo = AR.alloc([8, 1024], BF16)
    wmo = AR.alloc([4, 1024], BF16)
    wdo = AR.alloc([4, 1024], BF16)
    wou = AR.alloc([8, 1024], BF16)
    w_t = Tok()
    for wt_, nm in ((wlo, "w_lru_o"), (wmo, "w_mla_o"), (wdo, "w_dil_o"), (wou, "w_out")):
        P.dma("pool", (lambda wt_=wt_, nm=nm: G_.dma_start(out=wt_, in_=env[nm][l].rearrange("(c p) n -> p c n", p=128))), writes=[w_t])
    YL = [AR.alloc([8, TE], BF16) for _ in range(2)]
    YM = [AR.alloc([4, TE], BF16) for _ in range(2)]
    YD = [AR.alloc([4, TE], BF16) for _ in range(2)]
    GT = [AR.alloc([24, TE], BF16) for _ in range(2)]
    XT = [AR.alloc([2, D], F32) for _ in range(2)]
    OUT = [AR.alloc([2, D], F32) for _ in range(2)]
    M = [AR.alloc([8, TE], BF16) for _ in range(2)]
    t1 = [AR.alloc([TE], F32) for _ in range(2)]
    t2 = [AR.alloc([TE], F32) for _ in range(2)]
    in_t = toks(2)
    out_t = toks(2)
    M_t = toks(2)
    t1_t = toks(2)
    t2_t = toks(2)
    ylru_r = env["ylru"].rearrange("(c p) t -> p c t", p=128)
    ymla_r = env["ymla"].rearrange("(c p) t -> p c t", p=128)
    ydil_r = env["ydil"].rearrange("(c p) t -> p c t", p=128)
    gts_r = env["gts"].rearrange("(c p) t -> p c t", p=128)
    is_final = dst is env["out_d"]
    k = 0
    pending = []
    for e in range(S // TE):
        b = e % 2
        tc_ = slice(e * TE, (e + 1) * TE)
        P.dma("sp", (lambda b=b, tc_=tc_: nc.sync.dma_start(out=YL[b], in_=ylru_r[:, :, tc_])), writes=[in_t[b]])
        P.dma("sp", (lambda b=b, tc_=tc_: nc.sync.dma_start(out=YM[b], in_=ymla_r[:, :, tc_])), writes=[in_t[b]])
        P.dma("sp", (lambda b=b, tc_=tc_: nc.sync.dma_start(out=YD[b], in_=ydil_r[:, :, tc_])), writes=[in_t[b]])
        P.dma("sp", (lambda b=b, tc_=tc_: nc.sync.dma_start(out=GT[b], in_=gts_r[:, :, tc_])), writes=[in_t[b]])
        P.dma("sp", (lambda b=b, e=e: nc.sync.dma_start(out=XT[b], in_=src[e * TE:(e + 1) * TE, :].rearrange("(j p) d -> p j d", p=128))), writes=[in_t[b]])
        for j in range(8):
            if j == 4 and pending:
                pending.pop(0)()
            js = slice(j * 128, (j + 1) * 128)
            kk = k % 2
            k += 1
            bl, bm, bd = 0 + kk, 2 + kk, 4 + kk
            for c in range(8):
                P.op("pe", _mm(nc, banks[bl][:, 0:TE], wlo[:, c, js], YL[b][:, c, :], c == 0, c == 7), reads=[w_t, in_t[b]], writes=[bank_tok[bl]])
            for c in range(4):
                P.op("pe", _mm(nc, banks[bm][:, 0:TE], wmo[:, c, js], YM[b][:, c, :], c == 0, c == 3), reads=[w_t, in_t[b]], writes=[bank_tok[bm]])
            for c in range(4):
                P.op("pe", _mm(nc, banks[bd][:, 0:TE], wdo[:, c, js], YD[b][:, c, :], c == 0, c == 3), reads=[w_t, in_t[b]], writes=[bank_tok[bd]])
            P.op("dve", (lambda b=b, j=j, kk=kk, bl=bl: V.tensor_tensor(out=t1[kk], in0=banks[bl][:, 0:TE], in1=GT[b][:, j, :], op=ALU.mult)),
                 reads=[bank_tok[bl], in_t[b]], writes=[t1_t[kk]])
            P.op("dve", (lambda b=b, j=j, kk=kk, bm=bm: V.tensor_tensor(out=t2[kk], in0=banks[bm][:, 0:TE], in1=GT[b][:, 8 + j, :], op=ALU.mult)),
                 reads=[bank_tok[bm], in_t[b]], writes=[t2_t[kk]])
            P.op("dve", (lambda kk=kk: V.tensor_tensor(out=t1[kk], in0=t1[kk], in1=t2[kk], op=ALU.add)),
                 reads=[t1_t[kk], t2_t[kk]], writes=[t1_t[kk]])
            P.op("dve", (lambda b=b, j=j, kk=kk, bd=bd: V.tensor_tensor(out=t2[kk], in0=banks[bd][:, 0:TE], in1=GT[b][:, 16 + j, :], op=ALU.mult)),
                 reads=[bank_tok[bd], in_t[b]], writes=[t2_t[kk]])
            P.op("dve", (lambda b=b, j=j, kk=kk: V.tensor_tensor(out=M[b][:, j, :], in0=t1[kk], in1=t2[kk], op=ALU.add)),
                 reads=[t1_t[kk], t2_t[kk]], writes=[M_t[b]])
        def wout(b=b, e=e):
            for s_ in range(2):
                for hf in range(2):
                    bo = 6 + (s_ * 2 + hf) % 2
                    for c in range(8):
                        P.op("pe", _mm(nc, banks[bo][:, :], M[b][:, c, s_ * 128:(s_ + 1) * 128], wou[:, c, hf * 512:(hf + 1) * 512], c == 0, c == 7),
                             reads=[w_t, M_t[b]], writes=[bank_tok[bo]])
                    P.op("dve", (lambda b=b, s_=s_, hf=hf, bo=bo: V.tensor_tensor(out=OUT[b][:, s_, hf * 512:(hf + 1) * 512], in0=banks[bo][:, :],
                                                                                in1=XT[b][:, s_, hf * 512:(hf + 1) * 512], op=ALU.add)),
                         reads=[bank_tok[bo], in_t[b]], writes=[out_t[b]])
            P.dma("sp", (lambda b=b, e=e: nc.sync.dma_start(out=dst[e * TE:(e + 1) * TE, :].rearrange("(j p) d -> p j d", p=128), in_=OUT[b])),
                  reads=[out_t[b]], is_output=is_final)
        pending.append(wout)
    while pending:
        pending.pop(0)()
```

```python
import math
import numpy as np
import ml_dtypes
import concourse.bass as bass
import concourse.mybir as mybir
from concourse.bass_utils import run_bass_kernel_spmd

F32 = mybir.dt.float32
BF16 = mybir.dt.bfloat16
I32 = mybir.dt.int32
AF = mybir.ActivationFunctionType
ALU = mybir.AluOpType

S = 4096
D = 1024
DEPTH = 2
INW = 11168
EPS = 1e-6
C_LX, C_LG, C_CQ, C_CKV, C_KR, C_MG, C_DQ, C_DK, C_DV, C_DG, C_MRG = (
    0, 1024, 2048, 2304, 2432, 2464, 2976, 4512, 6048, 7584, 8096)
DIL = (1, 4, 16)
NDMA_SLOTS = 12

PV_NORMG = 0
PV_CONVW = 8
PV_CONVB = 40
PV_BGX = 48
PV_BGA = 56
PV_LAM = 64
PV_CQG = 72
PV_CKVG = 74
PV_MQG = 75
PV_MKG = 76
PV_DQKG = 77
PV_BMRG = 78
NPV = 102


class Tok:
    __slots__ = ("w", "r", "excl")

    def __init__(self, excl=False):
        self.w = None
        self.r = []
        self.excl = excl


class Prog:
    ENG = ("pe", "act", "dve", "pool", "sp")

    def __init__(self, nc):
        self.nc = nc
        self.h = {"pe": nc.tensor, "act": nc.scalar, "dve": nc.vector, "pool": nc.gpsimd, "sp": nc.sync}
        self.ops = []
        self.seq = {e: 0 for e in self.ENG}
        self.wm = {e: {} for e in self.ENG}
        self.dma_rr = {"sp": 0, "pool": 0, "act": 0}
        self.dma_cnt = {}
        self.out_events = []

    def _deps(self, reads, writes, eng=None):
        deps = []
        for t in reads:
            if t.w is not None:
                deps.append(t.w)
            if t.excl:
                deps.extend(r for r in t.r if r[0] != eng)
        for t in writes:
            if t.w is not None:
                deps.append(t.w)
            deps.extend(t.r)
        return deps

    def _mk_waits(self, eng, deps, skip_same_pe=True):
        best = {}
        for ev in deps:
            key, val = ev
            if key == eng and eng == "pe":
                continue
            if val > best.get(key, 0):
                best[key] = val
        waits = []
        wm = self.wm[eng]
        for key, val in best.items():
            if wm.get(key, 0) >= val:
                continue
            wm[key] = val
            waits.append((key, val))
        return waits

    def op(self, eng, fn, reads=(), writes=()):
        deps = self._deps(reads, writes, eng)
        waits = self._mk_waits(eng, deps)
        self.seq[eng] += 1
        ev = (eng, self.seq[eng])
        self.ops.append(["c", eng, fn, waits, ev, False])
        for t in reads:
            if t.excl:
                t.r = [ev]
            else:
                t.r.append(ev)
        for t in writes:
            t.w = ev
            t.r = []
        return ev

    def dma(self, q, fn, reads=(), writes=(), is_output=False):
        deps = self._deps(reads, writes)
        slot = (q, self.dma_rr[q] % NDMA_SLOTS)
        self.dma_rr[q] += 1
        cnt = self.dma_cnt.get(slot, 0)
        if cnt > 0:
            deps.append((slot, cnt))
        waits = self._mk_waits(q, deps)
        self.dma_cnt[slot] = cnt + 1
        ev = (slot, cnt + 1)
        self.ops.append(["d", q, fn, waits, ev, True])
        for t in reads:
            t.r.append(ev)
        for t in writes:
            t.w = ev
            t.r = []
        if is_output:
            self.out_events.append(ev)
        return ev

    def barrier(self):
        deps = [(e, self.seq[e]) for e in ("pe", "act", "dve", "pool") if self.seq[e] > 0]
        deps += [(slot, cnt) for slot, cnt in self.dma_cnt.items()]
        waits = self._mk_waits("sp", deps)
        self.seq["sp"] += 1
        ev = ("sp", self.seq["sp"])
        self.ops.append(["c", "sp", lambda: self.nc.sync.nop(), waits, ev, False])
        for e in ("pe", "act", "dve", "pool"):
            w = self._mk_waits(e, [ev])
            self.ops.append(["w", e, None, w, None, False])

    def finish(self):
        deps = list(self.out_events)
        waits = self._mk_waits("sp", deps)
        self.ops.append(["w", "sp", None, waits, None, False])

    def emit(self, sems, dma_sems):
        targets = {e: set() for e in self.ENG}
        for o in self.ops:
            for key, val in o[3]:
                if isinstance(key, str):
                    targets[key].add(val)
        valmap = {}
        for e in self.ENG:
            c = 0
            m = {}
            for s in sorted(targets[e]):
                c += 1
                m[s] = c
            valmap[e] = m
        for kind, eng, fn, waits, ev, _ in self.ops:
            hnd = self.h[eng]
            for key, val in waits:
                if isinstance(key, str):
                    hnd.wait_ge(sems[key], valmap[key][val])
                else:
                    hnd.wait_ge(dma_sems[key], 16 * val)
            if kind == "w":
                continue
            ins = fn()
            if kind == "d":
                ins.then_inc(dma_sems[ev[0]], 16)
            else:
                if ev[1] in targets[eng]:
                    ins.then_inc(sems[eng], 1)


class Arena:
    def __init__(self, ap, nbytes):
        self.ap = ap
        self.nbytes = nbytes
        self.off = 0

    def alloc(self, shape, dt):
        esz = 4 if dt in (F32, I32) else 2
        n = 1
        for s_ in shape:
            n *= s_
        nb = (n * esz + 63) // 64 * 64
        assert self.off + nb <= self.nbytes, ("arena overflow", self.off, nb, self.nbytes)
        a = self.ap[:, self.off // 2:(self.off + n * esz) // 2]
        self.off += nb
        if dt != BF16:
            a = a.bitcast(dt)
        if len(shape) == 2:
            a = a.rearrange("p (a b) -> p a b", a=shape[0])
        elif len(shape) == 3:
            a = a.rearrange("p (a b c) -> p a b c", a=shape[0], b=shape[1])
        return a

    def mark(self):
        return self.off

    def release(self, m):
        self.off = m


def toks(n):
    return [Tok() for _ in range(n)]


def build_program(n_layers=DEPTH, stages=None, debug=False, extra_barriers=0, stop=None):
    nc = bass.Bass("TRN2", target_bir_lowering=False)
    P = Prog(nc)
    dram_in = {}

    def din(name, shape, dt=F32):
        dram_in[name] = nc.dram_tensor(name, list(shape), dt, kind="ExternalInput").ap()
        return dram_in[name]

    x_in = din("x", [S, D])
    pos_in = din("pos", [1, S], I32)
    w_in = din("w_in", [DEPTH, D, INW])
    w_gx = din("w_gate_x", [DEPTH, 8, 128, 128])
    w_ga = din("w_gate_a", [DEPTH, 8, 128, 128])
    w_lru_o = din("w_lru_o", [DEPTH, 1024, 1024])
    w_uq = din("w_uq", [DEPTH, 256, 768])
    w_ukv = din("w_ukv", [DEPTH, 128, 1024])
    w_mla_o = din("w_mla_o", [DEPTH, 512, 1024])
    w_dil_o = din("w_dil_o", [DEPTH, 512, 1024])
    w_out = din("w_out", [DEPTH, 1024, 1024])
    pvec_in = din("pvec", [DEPTH, 128, NPV])
    cmat_in = din("cmat", [8, 128, 128])
    cmask_in = din("cmask", [128, 4 * 512 + 256])
    cinv_in = din("cinv", [128, 2])
    out_d = nc.dram_tensor("out", [S, D], F32, kind="ExternalOutput").ap()
    sk = "ExternalOutput" if debug else "Internal"
    xres = nc.dram_tensor("xres", [S, D], F32, kind=sk).ap()
    ylru = nc.dram_tensor("ylru", [1024, S], BF16, kind=sk).ap()
    ymla = nc.dram_tensor("ymla", [512, S], BF16, kind=sk).ap()
    ydil = nc.dram_tensor("ydil", [512, S], BF16, kind=sk).ap()
    gts = nc.dram_tensor("gts", [3072, S], BF16, kind=sk).ap()
    tabs = nc.dram_tensor("tabs", [4, 128, S], BF16, kind="Internal").ap()
    dbg = {}
    if debug:
        dbg["hT"] = nc.dram_tensor("hT_dbg", [128, 8 * S], BF16, kind="ExternalOutput").ap()

    ARENA_BYTES = 204 * 1024
    import contextlib
    with contextlib.ExitStack() as es:
        arena_t = es.enter_context(nc.sbuf_tensor("arena", [128, ARENA_BYTES // 2], BF16))
        banks = [es.enter_context(nc.psum_tensor(f"ps{i}", [128, 512], F32)) for i in range(8)]
        sems = {e: es.enter_context(nc.semaphore(f"s_{e}")) for e in Prog.ENG}
        dma_sems = {}
        for q in ("sp", "pool"):
            for i in range(NDMA_SLOTS):
                dma_sems[(q, i)] = es.enter_context(nc.semaphore(f"d_{q}{i}"))
        es.enter_context(nc.Block())
        es.enter_context(nc.allow_low_precision("bf16 PE transposes / bf16 matmul operands with fp32 accumulation"))
        AR = Arena(arena_t, ARENA_BYTES)
        bank_tok = [Tok(excl=True) for _ in range(8)]
        _emit_all(nc, P, AR, banks, bank_tok, locals(), n_layers, dbg)
        P.finish()
        P.emit(sems, dma_sems)
    return nc


def _emit_all(nc, P, AR, banks, bank_tok, env, n_layers, dbg):
    x_in = env["x_in"]; pos_in = env["pos_in"]; w_in = env["w_in"]
    out_d = env["out_d"]; xres = env["xres"]
    pvec_in = env["pvec_in"]; cmat_in = env["cmat_in"]; cmask_in = env["cmask_in"]; cinv_in = env["cinv_in"]
    V, A_, T_, G_ = nc.vector, nc.scalar, nc.tensor, nc.gpsimd

    cm = AR.alloc([8, 128], BF16)
    cm_t = Tok()
    P.dma("pool", lambda: G_.dma_start(out=cm, in_=cmat_in.rearrange("m p c -> p m c")), writes=[cm_t])
    IDENT, ONES256, ONES128, ONESB64, ONES96, RD_, RM_, EPAD = [cm[:, i, :] for i in range(8)]
    cmask = AR.alloc([4 * 512 + 256], BF16)
    cmask_t = Tok()
    P.dma("pool", lambda: G_.dma_start(out=cmask, in_=cmask_in), writes=[cmask_t])
    cinv = AR.alloc([2], F32)
    cinv_t = Tok()
    P.dma("sp", lambda: nc.sync.dma_start(out=cinv, in_=cinv_in), writes=[cinv_t])
    pv = AR.alloc([DEPTH, NPV], F32)
    pv_t = Tok()
    P.dma("sp", lambda: nc.sync.dma_start(out=pv, in_=pvec_in.rearrange("l p c -> p l c")), writes=[pv_t])
    hbgx = AR.alloc([DEPTH, 8], F32)
    hbga = AR.alloc([DEPTH, 8], F32)
    hc = AR.alloc([DEPTH, 8], F32)
    nhc = AR.alloc([DEPTH, 8], F32)
    hbm = AR.alloc([DEPTH, 24], F32)
    tmpv = AR.alloc([DEPTH, 8], F32)
    der_t = Tok()
    P.op("dve", lambda: V.tensor_scalar_mul(hbgx, pv[:, :, PV_BGX:PV_BGX + 8], -1.0), reads=[pv_t], writes=[der_t])
    P.op("dve", lambda: V.tensor_scalar_mul(hbga, pv[:, :, PV_BGA:PV_BGA + 8], -1.0), reads=[pv_t], writes=[der_t])
    P.op("dve", lambda: V.tensor_scalar_mul(hbm, pv[:, :, PV_BMRG:PV_BMRG + 24], -1.0), reads=[pv_t], writes=[der_t])
    P.op("act", lambda: A_.activation(out=tmpv, in_=pv[:, :, PV_LAM:PV_LAM + 8], func=AF.Exp, scale=-1.0),
         reads=[pv_t], writes=[der_t])
    P.op("act", lambda: A_.activation(out=tmpv, in_=tmpv, func=AF.Ln, bias=1.0, scale=1.0),
         reads=[der_t], writes=[der_t])
    P.op("dve", lambda: V.tensor_scalar_mul(hc, tmpv, -8.0), reads=[der_t], writes=[der_t])

    perm_mark = AR.mark()
    ctx = dict(nc=nc, P=P, AR=AR, banks=banks, bank_tok=bank_tok, env=env, dbg=dbg,
               cm=cm, cm_t=cm_t, cmask=cmask, cmask_t=cmask_t, cinv=cinv, cinv_t=cinv_t,
               pv=pv, pv_t=pv_t, der_t=der_t, hbgx=hbgx, hbga=hbga, hc=hc, nhc=nhc, hbm=hbm)
    tabs = env["tabs"]
    tab_tok = Tok()
    CT0 = AR.alloc([S], BF16)
    ST0 = AR.alloc([S], BF16)
    for which in (0, 1):
        make_tables(ctx, which, CT0, ST0, tab_tok)
        P.dma("sp", (lambda which=which: nc.sync.dma_start(out=tabs[2 * which], in_=CT0)), reads=[tab_tok])
        P.dma("sp", (lambda which=which: nc.sync.dma_start(out=tabs[2 * which + 1], in_=ST0)), reads=[tab_tok])
        P.barrier()
    for l in range(n_layers):
        src = x_in if l == 0 else xres
        dst = out_d if l == n_layers - 1 else xres
        AR.release(perm_mark)
        P.barrier()
        hT = AR.alloc([8, S], BF16)
        hT_t = toks(8)
        layer_mark = AR.mark()
        phase_norm(ctx, l, src, hT, hT_t)
        if dbg and l == 0:
            P.dma("sp", lambda: nc.sync.dma_start(out=dbg["hT"].rearrange("p (c t) -> p c t", c=8), in_=hT), reads=hT_t)
        stages = env.get("stages") or ("gates", "lru", "mla", "dil", "out")
        if "gates" in stages:
            P.barrier(); AR.release(layer_mark)
            phase_gates(ctx, l, hT, hT_t)
        if "lru" in stages:
            P.barrier(); AR.release(layer_mark)
            phase_lru(ctx, l, hT, hT_t)
        if "mla" in stages:
            P.barrier(); AR.release(layer_mark)
            phase_mla(ctx, l, hT, hT_t)
        if "dil" in stages:
            P.barrier(); AR.release(layer_mark)
            phase_dil(ctx, l, hT, hT_t)
        if "out" in stages:
            P.barrier(); AR.release(perm_mark)
            phase_out(ctx, l, src, dst)
    for _ in range(ctx["env"].get("extra_barriers", 0)):
        P.barrier()
    P.barrier()


def _mm(nc, out, lhsT, rhs, start, stop):
    return lambda: nc.tensor.matmul(out, lhsT, rhs, start=start, stop=stop)


def proj_group(ctx, bank_i, wtile, w_t, c0, m, hT, hT_t, tt, rows=None):
    nc, P = ctx["nc"], ctx["P"]
    bank = ctx["banks"][bank_i]
    bt = ctx["bank_tok"][bank_i]
    for k in range(8):
        P.op("pe", _mm(nc, bank[0:m, :], wtile[:, k, c0:c0 + m], hT[:, k, tt * 512:(tt + 1) * 512], k == 0, k == 7),
             reads=[w_t, hT_t[tt]], writes=[bt])


def phase_norm(ctx, l, src, hT, hT_t):
    nc, P, AR = ctx["nc"], ctx["P"], ctx["AR"]
    V, A_, T_ = nc.vector, nc.scalar, nc.tensor
    banks, bank_tok = ctx["banks"], ctx["bank_tok"]
    pv, pv_t = ctx["pv"], ctx["pv_t"]
    IDENT = ctx["cm"][:, 0, :]
    xt = [AR.alloc([4, D], F32) for _ in range(2)]
    xt_t = toks(2)
    xn = [AR.alloc([D], BF16) for _ in range(2)]
    xn_t = toks(2)
    ss = [AR.alloc([4], F32) for _ in range(2)]
    ss_t = toks(2)
    junk = AR.alloc([D], BF16)
    gB = pv[:, l, PV_NORMG:PV_NORMG + 8].unsqueeze(2).to_broadcast([128, 8, 128])
    for g4 in range(8):
        b = g4 % 2
        P.dma("sp", (lambda b=b, g4=g4: nc.sync.dma_start(
            out=xt[b], in_=src[g4 * 512:(g4 + 1) * 512, :].rearrange("(j p) d -> p j d", p=128))),
            writes=[xt_t[b]])
        P.op("dve", (lambda b=b: V.memset(ss[b], 0.0)), writes=[ss_t[b]])
        for j in range(4):
            P.op("act", (lambda b=b, j=j: A_.activation(out=junk, in_=xt[b][:, j, :], func=AF.Square,
                                                         accum_out=ss[b][:, j:j + 1])),
                 reads=[xt_t[b]], writes=[ss_t[b]])
        P.op("act", (lambda b=b: A_.activation(out=ss[b], in_=ss[b], func=AF.Ln, bias=EPS, scale=1.0 / D)),
             reads=[ss_t[b]], writes=[ss_t[b]])
        P.op("act", (lambda b=b: A_.activation(out=ss[b], in_=ss[b], func=AF.Exp, scale=-0.5)),
             reads=[ss_t[b]], writes=[ss_t[b]])
        for j in range(4):
            kb = (g4 * 4 + j) % 2
            P.op("dve", (lambda b=b, j=j, kb=kb: V.tensor_scalar(xn[kb], xt[b][:, j, :], ss[b][:, j:j + 1], None, ALU.mult)),
                 reads=[xt_t[b], ss_t[b]], writes=[xn_t[kb]])
            bbf = banks[kb][:, :].bitcast(BF16)
            for c in range(8):
                P.op("pe", (lambda kb=kb, c=c, bbf=bbf: T_.transpose(bbf[:, c * 128:(c + 1) * 128],
                                                                     xn[kb][:, c * 128:(c + 1) * 128], IDENT)),
                     reads=[xn_t[kb], ctx["cm_t"]], writes=[bank_tok[kb]])
            t0 = g4 * 512 + j * 128
            P.op("dve", (lambda bbf=bbf, t0=t0: V.tensor_tensor(
                out=hT[:, :, t0:t0 + 128], in0=bbf.rearrange("p (c t) -> p c t", c=8), in1=gB, op=ALU.mult)),
                reads=[bank_tok[kb], pv_t], writes=[hT_t[g4]])


def phase_lru(ctx, l, hT, hT_t):
    nc, P, AR = ctx["nc"], ctx["P"], ctx["AR"]
    V, A_, T_, G_ = nc.vector, nc.scalar, nc.tensor, nc.gpsimd
    env = ctx["env"]
    banks, bank_tok = ctx["banks"], ctx["bank_tok"]
    pv, pv_t, der_t = ctx["pv"], ctx["pv_t"], ctx["der_t"]
    w_in, ylru = env["w_in"], env["ylru"]
    wl = [AR.alloc([8, 256], BF16) for _ in range(2)]
    wl_t = toks(2)
    wgx = AR.alloc([8, 128], BF16)
    wga = AR.alloc([8, 128], BF16)
    wg_t = Tok()
    P.dma("pool", lambda: G_.dma_start(out=wgx, in_=env["w_gx"][l].rearrange("n c d -> c n d")), writes=[wg_t])
    P.dma("pool", lambda: G_.dma_start(out=wga, in_=env["w_ga"][l].rearrange("n c d -> c n d")), writes=[wg_t])
    X = AR.alloc([3 + S], F32)
    X_t = toks(8)
    Xh_t = Tok()
    names = ("G", "XC", "TGX", "TGA", "A", "T", "A2", "H", "SG")
    NSET = 4
    buf = [{nm: AR.alloc([512], F32) for nm in names} for _ in range(NSET)]
    for p_ in range(NSET):
        buf[p_]["XCb"] = AR.alloc([512], BF16)
        buf[p_]["Yb"] = AR.alloc([512], BF16)
    bt = [{nm: Tok() for nm in buf[0]} for _ in range(NSET)]
    P.op("dve", lambda: V.memset(X[:, 0:3], 0.0), writes=[Xh_t])
    w_r = w_in[l].rearrange("(c p) n -> p c n", p=128)
    it = 0
    for n in range(8):
        wb = n % 2
        P.dma("pool", (lambda wb=wb, n=n: G_.dma_start(out=wl[wb][:, :, 0:128], in_=w_r[:, :, C_LX + n * 128:C_LX + (n + 1) * 128])),
              writes=[wl_t[wb]])
        P.dma("pool", (lambda wb=wb, n=n: G_.dma_start(out=wl[wb][:, :, 128:256], in_=w_r[:, :, C_LG + n * 128:C_LG + (n + 1) * 128])),
              writes=[wl_t[wb]])
        cw = lambda k, n=n: pv[:, l, PV_CONVW + n * 4 + k:PV_CONVW + n * 4 + k + 1]
        cb = pv[:, l, PV_CONVB + n:PV_CONVB + n + 1]
        hbx = ctx["hbgx"][:, l, n:n + 1]
        hba = ctx["hbga"][:, l, n:n + 1]
        hcn = ctx["hc"][:, l, n:n + 1]
        nhcn = ctx["nhc"][:, l, n:n + 1]
        def chunk(tt, par, pb, n=n, wb=wb, cw=cw, cb=cb, hbx=hbx, hba=hba, hcn=hcn):
            B_, Bt = buf[par], bt[par]
            c0 = 3 + tt * 512
            bx, bg, bgx, bga = pb, 2 + pb, 4 + pb, 6 + pb
            proj_group(ctx, bx, wl[wb], wl_t[wb], 0, 128, hT, hT_t, tt)
            P.op("act", lambda: A_.copy(X[:, c0:c0 + 512], banks[bx][:, :]), reads=[bank_tok[bx]], writes=[X_t[tt]])
            yield
            proj_group(ctx, bg, wl[wb], wl_t[wb], 128, 128, hT, hT_t, tt)
            P.op("act", lambda: A_.copy(B_["G"], banks[bg][:, :]), reads=[bank_tok[bg]], writes=[Bt["G"]])
            yield
            P.op("act", lambda: A_.activation(out=B_["SG"], in_=B_["G"], func=AF.Exp, scale=-1.0), reads=[Bt["G"]], writes=[Bt["SG"]])
            xprev = Xh_t if tt == 0 else X_t[tt - 1]
            P.op("dve", lambda: V.tensor_scalar(B_["XC"], X[:, c0:c0 + 512], cw(3), cb, ALU.mult, ALU.add),
                 reads=[X_t[tt], pv_t], writes=[Bt["XC"]])
            for k in (2, 1, 0):
                P.op("dve", (lambda k=k: V.scalar_tensor_tensor(B_["XC"], X[:, c0 - 3 + k:c0 - 3 + k + 512], cw(k), B_["XC"], ALU.mult, ALU.add)),
                     reads=[X_t[tt], xprev, Bt["XC"]], writes=[Bt["XC"]])
            yield
            P.op("act", lambda: A_.activation(out=B_["SG"], in_=B_["SG"], func=AF.Ln, bias=1.0), reads=[Bt["SG"]], writes=[Bt["SG"]])
            P.op("act", lambda: A_.copy(B_["XCb"], B_["XC"]), reads=[Bt["XC"]], writes=[Bt["XCb"]])
            yield
            P.op("pe", _mm(nc, banks[bgx][:, :], wgx[:, n, :], B_["XCb"], True, True), reads=[wg_t, Bt["XCb"]], writes=[bank_tok[bgx]])
            P.op("pe", _mm(nc, banks[bga][:, :], wga[:, n, :], B_["XCb"], True, True), reads=[wg_t, Bt["XCb"]], writes=[bank_tok[bga]])
            P.op("act", lambda: A_.activation(out=B_["SG"], in_=B_["SG"], func=AF.Exp, scale=-1.0), reads=[Bt["SG"]], writes=[Bt["SG"]])
            yield
            for nm, bk, hb_ in (("TGX", bgx, hbx), ("TGA", bga, hba)):
                P.op("act", (lambda nm=nm, bk=bk, hb_=hb_: A_.activation(out=B_[nm], in_=banks[bk][:, :], func=AF.Exp, bias=hb_, scale=-1.0)),
                     reads=[bank_tok[bk], der_t], writes=[Bt[nm]])
            P.op("dve", lambda: V.tensor_tensor(out=B_["SG"], in0=B_["SG"], in1=B_["G"], op=ALU.mult), reads=[Bt["SG"], Bt["G"]], writes=[Bt["SG"]])
            yield
            for nm in ("TGX", "TGA"):
                P.op("act", (lambda nm=nm: A_.activation(out=B_[nm], in_=B_[nm], func=AF.Ln, bias=1.0)), reads=[Bt[nm]], writes=[Bt[nm]])
            yield
            for nm in ("TGX", "TGA"):
                P.op("act", (lambda nm=nm: A_.activation(out=B_[nm], in_=B_[nm], func=AF.Exp, scale=-1.0)), reads=[Bt[nm]], writes=[Bt[nm]])
            yield
            P.op("act", lambda: A_.activation(out=B_["A"], in_=B_["TGA"], func=AF.Exp, scale=hcn), reads=[Bt["TGA"], der_t], writes=[Bt["A"]])
            P.op("act", lambda: A_.activation(out=B_["A2"], in_=B_["A"], func=AF.Square), reads=[Bt["A"]], writes=[Bt["A2"]])
            P.op("dve", lambda: V.tensor_tensor(out=B_["TGX"], in0=B_["TGX"], in1=B_["XC"], op=ALU.mult), reads=[Bt["TGX"], Bt["XC"]], writes=[Bt["TGX"]])
            yield
            P.op("dve", lambda: V.tensor_scalar(B_["T"], B_["A2"], -1.0, 1.0, ALU.mult, ALU.add), reads=[Bt["A2"]], writes=[Bt["T"]])
            yield
            P.op("act", lambda: A_.activation(out=B_["T"], in_=B_["T"], func=AF.Ln), reads=[Bt["T"]], writes=[Bt["T"]])
            yield
            P.op("act", lambda: A_.activation(out=B_["T"], in_=B_["T"], func=AF.Exp, scale=0.5), reads=[Bt["T"]], writes=[Bt["T"]])
            yield
            P.op("dve", lambda: V.tensor_tensor(out=B_["T"], in0=B_["T"], in1=B_["TGX"], op=ALU.mult), reads=[Bt["T"], Bt["TGX"]], writes=[Bt["T"]])
            if tt == 0:
                P.op("dve", lambda: V.tensor_tensor_scan(B_["H"], B_["A"], B_["T"], 0.0, ALU.mult, ALU.add), reads=[Bt["A"], Bt["T"]], writes=[Bt["H"]])
            else:
                Bp, Bpt = buf[(par - 1) % NSET], bt[(par - 1) % NSET]
                P.op("dve", lambda: V.tensor_tensor_scan(B_["H"], B_["A"], B_["T"], Bp["H"][:, 511:512], ALU.mult, ALU.add),
                     reads=[Bt["A"], Bt["T"], Bpt["H"]], writes=[Bt["H"]])
            P.op("dve", lambda: V.tensor_tensor(out=B_["Yb"], in0=B_["H"], in1=B_["SG"], op=ALU.mult), reads=[Bt["H"], Bt["SG"]], writes=[Bt["Yb"]])
            P.dma("sp", lambda: nc.sync.dma_start(out=ylru[n * 128:(n + 1) * 128, tt * 512:(tt + 1) * 512], in_=B_["Yb"]), reads=[Bt["Yb"]])

        for tp in range(0, 8, 2):
            gens = []
            for tt in (tp, tp + 1):
                gens.append(chunk(tt, it % NSET, it % 2))
                it += 1
            live = list(gens)
            while live:
                for g_ in list(live):
                    try:
                        next(g_)
                    except StopIteration:
                        live.remove(g_)


def phase_gates(ctx, l, hT, hT_t):
    nc, P, AR = ctx["nc"], ctx["P"], ctx["AR"]
    A_, G_ = nc.scalar, nc.gpsimd
    env = ctx["env"]
    banks, bank_tok = ctx["banks"], ctx["bank_tok"]
    w_r = env["w_in"][l].rearrange("(c p) n -> p c n", p=128)
    gts = env["gts"]
    wt = [AR.alloc([8, 128], BF16) for _ in range(2)]
    wt_t = toks(2)
    st = [AR.alloc([S], BF16) for _ in range(2)]
    st_t = toks(2)
    ebuf = [AR.alloc([512], F32) for _ in range(2)]
    ebuf_t = toks(2)
    for ch in range(24):
        b = ch % 2
        P.dma("pool", (lambda b=b, ch=ch: G_.dma_start(out=wt[b], in_=w_r[:, :, C_MRG + ch * 128:C_MRG + (ch + 1) * 128])),
              writes=[wt_t[b]])
        hb = ctx["hbm"][:, l, ch:ch + 1]
        for tt in range(8):
            bi = (ch * 8 + tt) % 4
            proj_group(ctx, bi, wt[b], wt_t[b], 0, 128, hT, hT_t, tt)
            eb = (ch * 8 + tt) % 2
            P.op("act", (lambda bi=bi, eb=eb, hb=hb: A_.activation(out=ebuf[eb], in_=banks[bi][:, :], func=AF.Exp, bias=hb, scale=-1.0)),
                 reads=[bank_tok[bi], ctx["der_t"]], writes=[ebuf_t[eb]])
            P.op("act", (lambda eb=eb: A_.activation(out=ebuf[eb], in_=ebuf[eb], func=AF.Ln, bias=1.0)), reads=[ebuf_t[eb]], writes=[ebuf_t[eb]])
            P.op("act", (lambda b=b, eb=eb, tt=tt: A_.activation(out=st[b][:, tt * 512:(tt + 1) * 512], in_=ebuf[eb], func=AF.Exp, scale=-1.0)),
                 reads=[ebuf_t[eb]], writes=[st_t[b]])
        P.dma("sp", (lambda b=b, ch=ch: nc.sync.dma_start(out=gts[ch * 128:(ch + 1) * 128, :], in_=st[b])),
              reads=[st_t[b]])
def _host_consts():
    cmat = np.zeros((8, 128, 128), np.float32)
    cmat[0] = np.eye(128, dtype=np.float32)
    cmat[1] = 1.0 / 256
    cmat[2] = 1.0 / 128
    cmat[3, 0:64, 0:64] = 1.0 / 64
    cmat[3, 64:128, 64:128] = 1.0 / 64
    cmat[4, 0:96, 0:96] = 1.0
    for blk in (0, 64):
        for i in range(32):
            cmat[5, blk + i + 32, blk + i] = -1.0
            cmat[5, blk + i, blk + i + 32] = 1.0
    for i in range(16):
        cmat[6, 64 + i + 16, 64 + i] = -1.0
        cmat[6, 64 + i, 64 + i + 16] = 1.0
    for k in range(32):
        cmat[7, k, 64 + k] = 1.0
    cmask = np.zeros((128, 4 * 512 + 256), np.float32)
    kk = np.arange(128)[:, None]
    qq = np.arange(512)[None, :]
    for j in range(4):
        cmask[:, j * 512:(j + 1) * 512] = (qq >= 128 * j + kk)
    q1 = np.arange(128)[None, :]
    cmask[:, 2048:2048 + 128] = (kk >= q1)
    cmask[:, 2048 + 128:2048 + 256] = (kk <= q1)
    cinv = np.zeros((128, 2), np.float32)
    theta = np.float32(10000.0)
    inv64 = theta ** (-np.arange(0, 64, 2, dtype=np.float32) / np.float32(64))
    inv32 = theta ** (-np.arange(0, 32, 2, dtype=np.float32) / np.float32(32))
    for p in range(128):
        cinv[p, 0] = inv64[p % 32]
        if 64 <= p < 96:
            cinv[p, 1] = inv32[(p - 64) % 16]
    return cmat, cmask, cinv


def _host_pvec(inp):
    pv = np.zeros((DEPTH, 128, NPV), np.float32)
    for l in range(DEPTH):
        pv[l, :, PV_NORMG:PV_NORMG + 8] = inp["norm_g"][l].reshape(8, 128).T
        pv[l, :, PV_CONVW:PV_CONVW + 32] = inp["conv_w"][l].reshape(4, 8, 128).transpose(2, 1, 0).reshape(128, 32)
        pv[l, :, PV_CONVB:PV_CONVB + 8] = inp["conv_b"][l].reshape(8, 128).T
        pv[l, :, PV_BGX:PV_BGX + 8] = inp["b_gate_x"][l].T
        pv[l, :, PV_BGA:PV_BGA + 8] = inp["b_gate_a"][l].T
        pv[l, :, PV_LAM:PV_LAM + 8] = inp["lru_lambda"][l].reshape(8, 128).T
        pv[l, :, PV_CQG:PV_CQG + 2] = inp["cq_norm_g"][l].reshape(2, 128).T
        pv[l, :, PV_CKVG] = inp["ckv_norm_g"][l]
        pv[l, 0:96, PV_MQG] = inp["mla_q_norm_g"][l]
        pv[l, 0:96, PV_MKG] = inp["mla_k_norm_g"][l]
        pv[l, 0:64, PV_DQKG] = inp["dil_q_norm_g"][l]
        pv[l, 64:128, PV_DQKG] = inp["dil_k_norm_g"][l]
        pv[l, :, PV_BMRG:PV_BMRG + 24] = inp["b_merge"][l].reshape(24, 128).T
    return pv


def make_in_maps(inp, n_cores=8):
    cmat, cmask, cinv = _host_consts()
    pvec = _host_pvec(inp)
    shared = {
        "w_in": np.ascontiguousarray(inp["w_in"]), "w_gate_x": np.ascontiguousarray(inp["w_gate_x"]),
        "w_gate_a": np.ascontiguousarray(inp["w_gate_a"]), "w_lru_o": np.ascontiguousarray(inp["w_lru_o"]),
        "w_uq": np.ascontiguousarray(inp["w_uq"]), "w_ukv": np.ascontiguousarray(inp["w_ukv"]),
        "w_mla_o": np.ascontiguousarray(inp["w_mla_o"]), "w_dil_o": np.ascontiguousarray(inp["w_dil_o"]),
        "w_out": np.ascontiguousarray(inp["w_out"]), "pvec": pvec, "cmat": cmat, "cmask": cmask, "cinv": cinv,
    }
    maps = []
    for b in range(n_cores):
        m = dict(shared)
        m["x"] = np.ascontiguousarray(inp["x"][b])
        m["pos"] = np.ascontiguousarray(inp["positions"][b].reshape(1, S).astype(np.int32))
        maps.append(m)
    return maps


def kernel(**inputs):
    inp = {k: np.asarray(v) for k, v in inputs.items()}
    nc = build_program()
    maps = make_in_maps(inp, 8)
    res = run_bass_kernel_spmd(nc, maps, core_ids=list(range(8)))
    out = np.stack([np.asarray(res.results[b]["out"]).reshape(S, D) for b in range(8)], axis=0)
    return out.astype(np.float32)


def make_tables(ctx, which, CT, ST, tab_t):
    nc, P, AR = ctx["nc"], ctx["P"], ctx["AR"]
    V, A_ = nc.vector, nc.scalar
    pos_in = ctx["env"]["pos_in"]
    inv = ctx["cinv"][:, which:which + 1]
    m = AR.mark()
    posi = [AR.alloc([1024], I32) for _ in range(2)]
    y0 = [AR.alloc([1024], F32) for _ in range(2)]
    yy = [AR.alloc([1024], F32) for _ in range(2)]
    ki = [AR.alloc([1024], I32) for _ in range(2)]
    kf = [AR.alloc([1024], F32) for _ in range(2)]
    tk = [{n_: Tok() for n_ in ("posi", "y0", "yy", "ki", "kf")} for _ in range(2)]
    TWO_PI = 2.0 * math.pi * (1.0 - 1e-6)
    for c in range(4):
        b = c % 2
        t = tk[b]
        P.dma("sp", (lambda b=b, c=c: nc.sync.dma_start(out=posi[b], in_=pos_in[:, c * 1024:(c + 1) * 1024].broadcast_to([128, 1024]))),
              writes=[t["posi"]])
        P.op("dve", (lambda b=b: V.tensor_copy(y0[b], posi[b])), reads=[t["posi"]], writes=[t["y0"]])
        P.op("dve", (lambda b=b: V.tensor_scalar(y0[b], y0[b], inv, None, ALU.mult)), reads=[t["y0"], ctx["cinv_t"]], writes=[t["y0"]])
        P.op("dve", (lambda b=b: V.tensor_scalar(y0[b], y0[b], 1.0 / (2.0 * math.pi), None, ALU.mult)), reads=[t["y0"]], writes=[t["y0"]])
        for ph, OUT in ((0.0, ST), (0.25, CT)):
            P.op("dve", (lambda b=b, ph=ph: V.tensor_scalar(yy[b], y0[b], ph, None, ALU.add)), reads=[t["y0"]], writes=[t["yy"]])
            P.op("dve", (lambda b=b: V.tensor_copy(ki[b], yy[b])), reads=[t["yy"]], writes=[t["ki"]])
            P.op("dve", (lambda b=b: V.tensor_copy(kf[b], ki[b])), reads=[t["ki"]], writes=[t["kf"]])
            P.op("dve", (lambda b=b: V.tensor_tensor(out=yy[b], in0=yy[b], in1=kf[b], op=ALU.subtract)), reads=[t["yy"], t["kf"]], writes=[t["yy"]])
            P.op("dve", (lambda b=b: V.tensor_scalar(kf[b], yy[b], 0.5, None, ALU.is_gt)), reads=[t["yy"]], writes=[t["kf"]])
            P.op("dve", (lambda b=b: V.tensor_tensor(out=yy[b], in0=yy[b], in1=kf[b], op=ALU.subtract)), reads=[t["yy"], t["kf"]], writes=[t["yy"]])
            P.op("act", (lambda b=b, OUT=OUT, c=c: A_.activation(out=OUT[:, c * 1024:(c + 1) * 1024], in_=yy[b], func=AF.Sin, scale=TWO_PI)),
                 reads=[t["yy"]], writes=[tab_t])
    P.barrier()
    AR.release(m)


def norm_rope_chain(ctx, src_bank, np_, ones_m, eps_v, gcol, CT, ST, rot_m, tt, out_ap, out_tok, tmp, tmp_t, bank_ss, bank_rot):
    nc, P = ctx["nc"], ctx["P"]
    V, A_ = nc.vector, nc.scalar
    banks, bank_tok = ctx["banks"], ctx["bank_tok"]
    IDENT = ctx["cm"][:, 0, :]
    src = banks[src_bank][0:np_, :]
    st_ = bank_tok[src_bank]
    cols = slice(tt * 512, (tt + 1) * 512)
    P.op("act", lambda: A_.activation(out=tmp["sq"][0:np_, :], in_=src, func=AF.Square), reads=[st_], writes=[tmp_t["sq"]])
    P.op("pe", _mm(nc, banks[bank_ss][0:np_, :], ones_m[0:np_, 0:np_], tmp["sq"][0:np_, :], True, True),
         reads=[tmp_t["sq"], ctx["cm_t"]], writes=[bank_tok[bank_ss]])
    P.op("act", lambda: A_.activation(out=tmp["rs"][0:np_, :], in_=banks[bank_ss][0:np_, :], func=AF.Ln, bias=eps_v),
         reads=[bank_tok[bank_ss]], writes=[tmp_t["rs"]])
    P.op("act", lambda: A_.activation(out=tmp["rs"][0:np_, :], in_=tmp["rs"][0:np_, :], func=AF.Exp, scale=-0.5),
         reads=[tmp_t["rs"]], writes=[tmp_t["rs"]])
    P.op("dve", lambda: V.scalar_tensor_tensor(tmp["u1"][0:np_, :], src, gcol[0:np_, :], CT[0:np_, cols], ALU.mult, ALU.mult),
         reads=[st_, ctx["pv_t"], tmp_t["tab"]], writes=[tmp_t["u1"]])
    P.op("dve", lambda: V.scalar_tensor_tensor(tmp["u2"][0:np_, :], src, gcol[0:np_, :], ST[0:np_, cols], ALU.mult, ALU.mult),
         reads=[st_, ctx["pv_t"], tmp_t["tab"]], writes=[tmp_t["u2"]])
    P.op("pe", _mm(nc, banks[bank_rot][0:np_, :], IDENT[0:np_, 0:np_], tmp["u1"][0:np_, :], True, False),
         reads=[tmp_t["u1"], ctx["cm_t"]], writes=[bank_tok[bank_rot]])
    P.op("pe", _mm(nc, banks[bank_rot][0:np_, :], rot_m[0:np_, 0:np_], tmp["u2"][0:np_, :], False, True),
         reads=[tmp_t["u2"], ctx["cm_t"]], writes=[bank_tok[bank_rot]])
    P.op("dve", lambda: V.tensor_tensor(out=out_ap, in0=banks[bank_rot][0:np_, :], in1=tmp["rs"][0:np_, :], op=ALU.mult),
         reads=[bank_tok[bank_rot], tmp_t["rs"]], writes=[out_tok])


def chain_group(ctx, chains, CT, ST):
    nc, P = ctx["nc"], ctx["P"]
    V, A_ = nc.vector, nc.scalar
    banks, bank_tok = ctx["banks"], ctx["bank_tok"]
    IDENT = ctx["cm"][:, 0, :]
    for c in chains:
        n_ = c["np"]
        P.op("act", (lambda c=c, n_=n_: A_.activation(out=c["tmp"]["sq"][0:n_, :], in_=banks[c["src"]][0:n_, :], func=AF.Square)),
             reads=[bank_tok[c["src"]]], writes=[c["tmp_t"]["sq"]])
    for c in chains:
        n_ = c["np"]
        P.op("pe", _mm(nc, banks[c["xb"]][0:n_, :], c["ones"][0:n_, 0:n_], c["tmp"]["sq"][0:n_, :], True, True),
             reads=[c["tmp_t"]["sq"], ctx["cm_t"]], writes=[bank_tok[c["xb"]]])
    for c in chains:
        n_ = c["np"]
        P.op("act", (lambda c=c, n_=n_: A_.activation(out=c["tmp"]["rs"][0:n_, :], in_=banks[c["xb"]][0:n_, :], func=AF.Ln, bias=c["eps"])),
             reads=[bank_tok[c["xb"]]], writes=[c["tmp_t"]["rs"]])
    for c in chains:
        n_ = c["np"]
        P.op("act", (lambda c=c, n_=n_: A_.activation(out=c["tmp"]["rs"][0:n_, :], in_=c["tmp"]["rs"][0:n_, :], func=AF.Exp, scale=-0.5)),
             reads=[c["tmp_t"]["rs"]], writes=[c["tmp_t"]["rs"]])
    for c in chains:
        n_ = c["np"]
        cols = slice(c["tt"] * 512, (c["tt"] + 1) * 512)
        P.op("dve", (lambda c=c, n_=n_, cols=cols: V.scalar_tensor_tensor(c["tmp"]["u1"][0:n_, :], banks[c["src"]][0:n_, :], c["g"][0:n_, :],
                                                                          CT[0:n_, cols], ALU.mult, ALU.mult)),
             reads=[bank_tok[c["src"]], ctx["pv_t"], c["tmp_t"]["tab"]], writes=[c["tmp_t"]["u1"]])
        P.op("dve", (lambda c=c, n_=n_, cols=cols: V.scalar_tensor_tensor(c["tmp"]["u2"][0:n_, :], banks[c["src"]][0:n_, :], c["g"][0:n_, :],
                                                                          ST[0:n_, cols], ALU.mult, ALU.mult)),
             reads=[bank_tok[c["src"]], ctx["pv_t"], c["tmp_t"]["tab"]], writes=[c["tmp_t"]["u2"]])
    for c in chains:
        n_ = c["np"]
        P.op("pe", _mm(nc, banks[c["xb"]][0:n_, :], IDENT[0:n_, 0:n_], c["tmp"]["u1"][0:n_, :], True, False),
             reads=[c["tmp_t"]["u1"], ctx["cm_t"]], writes=[bank_tok[c["xb"]]])
        P.op("pe", _mm(nc, banks[c["xb"]][0:n_, :], c["rot"][0:n_, 0:n_], c["tmp"]["u2"][0:n_, :], False, True),
             reads=[c["tmp_t"]["u2"], ctx["cm_t"]], writes=[bank_tok[c["xb"]]])
    for c in chains:
        n_ = c["np"]
        P.op("dve", (lambda c=c, n_=n_: V.tensor_tensor(out=c["out"], in0=banks[c["xb"]][0:n_, :], in1=c["tmp"]["rs"][0:n_, :], op=ALU.mult)),
             reads=[bank_tok[c["xb"]], c["tmp_t"]["rs"]], writes=[c["out_tok"]])


def chain_pipeline(ctx, chains, CT, ST):
    nc, P = ctx["nc"], ctx["P"]
    V, A_ = nc.vector, nc.scalar
    banks, bank_tok = ctx["banks"], ctx["bank_tok"]
    IDENT = ctx["cm"][:, 0, :]

    def stage(c, s_):
        n_ = c["np"]
        t_, tt_ = c["tmp"], c["tmp_t"]
        cols = slice(c["tt"] * 512, (c["tt"] + 1) * 512)
        src = banks[c["src"]][0:n_, :]
        xb = banks[c["xb"]][0:n_, :]
        if s_ == 0:
            c["proj"]()
        elif s_ == 1:
            P.op("act", lambda: A_.activation(out=t_["sq"][0:n_, :], in_=src, func=AF.Square), reads=[bank_tok[c["src"]]], writes=[tt_["sq"]])
            P.op("dve", lambda: V.scalar_tensor_tensor(t_["u1"][0:n_, :], src, c["g"][0:n_, :], CT[0:n_, cols], ALU.mult, ALU.mult),
                 reads=[bank_tok[c["src"]], ctx["pv_t"], tt_["tab"]], writes=[tt_["u1"]])
        elif s_ == 2:
            P.op("pe", _mm(nc, xb, c["ones"][0:n_, 0:n_], t_["sq"][0:n_, :], True, True), reads=[tt_["sq"], ctx["cm_t"]], writes=[bank_tok[c["xb"]]])
            P.op("dve", lambda: V.scalar_tensor_tensor(t_["u2"][0:n_, :], src, c["g"][0:n_, :], ST[0:n_, cols], ALU.mult, ALU.mult),
                 reads=[bank_tok[c["src"]], ctx["pv_t"], tt_["tab"]], writes=[tt_["u2"]])
        elif s_ == 3:
            P.op("act", lambda: A_.activation(out=t_["rs"][0:n_, :], in_=xb, func=AF.Ln, bias=c["eps"]), reads=[bank_tok[c["xb"]]], writes=[tt_["rs"]])
        elif s_ == 4:
            P.op("act", lambda: A_.activation(out=t_["rs"][0:n_, :], in_=t_["rs"][0:n_, :], func=AF.Exp, scale=-0.5), reads=[tt_["rs"]], writes=[tt_["rs"]])
            P.op("pe", _mm(nc, xb, IDENT[0:n_, 0:n_], t_["u1"][0:n_, :], True, False), reads=[tt_["u1"], ctx["cm_t"]], writes=[bank_tok[c["xb"]]])
            P.op("pe", _mm(nc, xb, c["rot"][0:n_, 0:n_], t_["u2"][0:n_, :], False, True), reads=[tt_["u2"], ctx["cm_t"]], writes=[bank_tok[c["xb"]]])
        elif s_ == 5:
            P.op("dve", lambda: V.tensor_tensor(out=c["out"], in0=xb, in1=t_["rs"][0:n_, :], op=ALU.mult),
                 reads=[bank_tok[c["xb"]], tt_["rs"]], writes=[c["out_tok"]])
        elif s_ == 6:
            if c.get("post"):
                c["post"]()

    NST = 7
    for t in range(len(chains) + NST - 1):
        for ci, c in enumerate(chains):
            s_ = t - ci
            if 0 <= s_ < NST:
                stage(c, s_)


def finalize_head(ctx, o_ap, o_tok, den_ap, den_tok, gate_w, gate_wt, gcol0, hT, hT_t, tt, ydst, tmp, tmp_t, gbank=7):
    nc, P = ctx["nc"], ctx["P"]
    V, A_ = nc.vector, nc.scalar
    banks, bank_tok = ctx["banks"], ctx["bank_tok"]
    proj_group(ctx, gbank, gate_w, gate_wt, gcol0, 64, hT, hT_t, tt)
    g_ps = banks[gbank][0:64, :]
    P.op("dve", lambda: V.tensor_copy(tmp["den0"][0:64, :], den_ap), reads=[den_tok], writes=[tmp_t["den0"]])
    yield
    P.op("act", lambda: A_.activation(out=tmp["e1"][0:64, :], in_=g_ps, func=AF.Exp, scale=-1.0), reads=[bank_tok[gbank]], writes=[tmp_t["e1"]])
    yield
    P.op("dve", lambda: V.scalar_tensor_tensor(tmp["e1"][0:64, :], tmp["e1"][0:64, :], 1.0, tmp["den0"][0:64, :], ALU.add, ALU.mult),
         reads=[tmp_t["e1"], tmp_t["den0"]], writes=[tmp_t["e1"]])
    yield
    P.op("act", lambda: A_.activation(out=tmp["e1"][0:64, :], in_=tmp["e1"][0:64, :], func=AF.Ln), reads=[tmp_t["e1"]], writes=[tmp_t["e1"]])
    yield
    P.op("act", lambda: A_.activation(out=tmp["e1"][0:64, :], in_=tmp["e1"][0:64, :], func=AF.Exp, scale=-1.0), reads=[tmp_t["e1"]], writes=[tmp_t["e1"]])
    yield
    P.op("dve", lambda: V.tensor_tensor(out=tmp["e1"][0:64, :], in0=tmp["e1"][0:64, :], in1=g_ps, op=ALU.mult),
         reads=[tmp_t["e1"], bank_tok[gbank]], writes=[tmp_t["e1"]])
    yield
    P.op("dve", lambda: V.tensor_tensor(out=tmp["yb"][0:64, :], in0=o_ap, in1=tmp["e1"][0:64, :], op=ALU.mult),
         reads=[o_tok, tmp_t["e1"]], writes=[tmp_t["yb"]])
    P.dma("sp", lambda: nc.sync.dma_start(out=ydst, in_=tmp["yb"][0:64, :]), reads=[tmp_t["yb"]])


def _fin_tmp(AR):
    tmp = {"e1": AR.alloc([512], F32), "den0": AR.alloc([512], F32), "yb": AR.alloc([512], BF16), "den": AR.alloc([512], F32)}
    return tmp, {k: Tok() for k in tmp}


def lockstep(gens):
    live = list(gens)
    while live:
        for g_ in list(live):
            try:
                next(g_)
            except StopIteration:
                live.remove(g_)


def _chain_tmp(AR):
    tmp = {"sq": AR.alloc([512], BF16), "rs": AR.alloc([512], F32), "u1": AR.alloc([512], BF16), "u2": AR.alloc([512], BF16),
           "e1": AR.alloc([512], F32), "den0": AR.alloc([512], F32), "yb": AR.alloc([512], BF16), "den": AR.alloc([512], F32)}
    tmp_t = {k: Tok() for k in list(tmp) + ["tab"]}
    return tmp, tmp_t


def _chain_tmp2(AR, tmp_t):
    tmp = {"sq": AR.alloc([512], BF16), "rs": AR.alloc([512], F32), "u1": AR.alloc([512], BF16), "u2": AR.alloc([512], BF16)}
    t2 = {k: Tok() for k in tmp}
    t2["tab"] = tmp_t["tab"]
    return tmp, t2


def phase_mla(ctx, l, hT, hT_t):
    nc, P, AR = ctx["nc"], ctx["P"], ctx["AR"]
    V, A_, G_ = nc.vector, nc.scalar, nc.gpsimd
    env = ctx["env"]
    banks, bank_tok = ctx["banks"], ctx["bank_tok"]
    pv, pv_t, cm, cm_t = ctx["pv"], ctx["pv_t"], ctx["cm"], ctx["cm_t"]
    ONES256, ONES128, ONES96, RM_, EPAD = cm[:, 1, :], cm[:, 2, :], cm[:, 4, :], cm[:, 6, :], cm[:, 7, :]
    cmask, cmask_t = ctx["cmask"], ctx["cmask_t"]
    ymla = env["ymla"]
    w_r = env["w_in"][l].rearrange("(c p) n -> p c n", p=128)
    CT = AR.alloc([S], BF16)
    ST = AR.alloc([S], BF16)
    tmp, tmp_t = _chain_tmp(AR)
    P.dma("sp", lambda: nc.sync.dma_start(out=CT, in_=env["tabs"][2]), writes=[tmp_t["tab"]])
    P.dma("sp", lambda: nc.sync.dma_start(out=ST, in_=env["tabs"][3]), writes=[tmp_t["tab"]])
    if env.get("stop") == "tables":
        return
    wm = AR.alloc([8, 416], BF16)
    wuq = AR.alloc([2, 768], BF16)
    wukv = AR.alloc([1024], BF16)
    wgm = AR.alloc([8, 512], BF16)
    wkp = AR.alloc([8, 96], BF16)
    w_t = Tok()
    wkp_t = Tok()
    P.dma("pool", lambda: G_.dma_start(out=wm, in_=w_r[:, :, C_CQ:C_CQ + 416]), writes=[w_t])
    P.dma("pool", lambda: G_.dma_start(out=wuq, in_=env["w_uq"][l].rearrange("(c p) n -> p c n", p=128)), writes=[w_t])
    P.dma("pool", lambda: G_.dma_start(out=wukv, in_=env["w_ukv"][l]), writes=[w_t])
    P.dma("pool", lambda: G_.dma_start(out=wgm, in_=w_r[:, :, C_MG:C_MG + 512]), writes=[w_t])
    P.op("dve", lambda: V.memset(wkp, 0.0), writes=[wkp_t])
    P.op("dve", lambda: V.tensor_copy(wkp[:, :, 0:64], wukv.rearrange("p (h c) -> p h c", h=8)[:, :, 0:64]), reads=[w_t], writes=[wkp_t])
    CQN = AR.alloc([2, S], BF16)
    CKVN = AR.alloc([S], BF16)
    KR = AR.alloc([S], BF16)
    lat_t = toks(8)
    m1_mark = AR.mark()
    sq3 = [AR.alloc([3, 512], BF16) for _ in range(2)]
    sq3_t = toks(2)
    rs2 = [AR.alloc([2, 512], F32) for _ in range(2)]
    rs2_t = toks(2)
    for tt in range(8):
        b = tt % 2
        cols = slice(tt * 512, (tt + 1) * 512)
        proj_group(ctx, 0, wm, w_t, 0, 128, hT, hT_t, tt)
        proj_group(ctx, 1, wm, w_t, 128, 128, hT, hT_t, tt)
        proj_group(ctx, 2, wm, w_t, 256, 128, hT, hT_t, tt)
        proj_group(ctx, 3, wm, w_t, 384, 32, hT, hT_t, tt)
        for i in range(3):
            P.op("act", (lambda b=b, i=i: A_.activation(out=sq3[b][:, i, :], in_=banks[i][:, :], func=AF.Square)),
                 reads=[bank_tok[i]], writes=[sq3_t[b]])
        P.op("pe", _mm(nc, banks[4][:, :], ONES256, sq3[b][:, 0, :], True, False), reads=[sq3_t[b], cm_t], writes=[bank_tok[4]])
        P.op("pe", _mm(nc, banks[4][:, :], ONES256, sq3[b][:, 1, :], False, True), reads=[sq3_t[b], cm_t], writes=[bank_tok[4]])
        P.op("pe", _mm(nc, banks[5][:, :], ONES128, sq3[b][:, 2, :], True, True), reads=[sq3_t[b], cm_t], writes=[bank_tok[5]])
        for i, bk in ((0, 4), (1, 5)):
            P.op("act", (lambda b=b, i=i, bk=bk: A_.activation(out=rs2[b][:, i, :], in_=banks[bk][:, :], func=AF.Ln, bias=EPS)),
                 reads=[bank_tok[bk]], writes=[rs2_t[b]])
            P.op("act", (lambda b=b, i=i: A_.activation(out=rs2[b][:, i, :], in_=rs2[b][:, i, :], func=AF.Exp, scale=-0.5)),
                 reads=[rs2_t[b]], writes=[rs2_t[b]])
        for c in range(2):
            P.op("dve", (lambda b=b, c=c, cols=cols: V.scalar_tensor_tensor(CQN[:, c, cols], banks[c][:, :], pv[:, l, PV_CQG + c:PV_CQG + c + 1],
                                                                            rs2[b][:, 0, :], ALU.mult, ALU.mult)),
                 reads=[bank_tok[c], rs2_t[b], pv_t], writes=[lat_t[tt]])
        P.op("dve", (lambda b=b, cols=cols: V.scalar_tensor_tensor(CKVN[:, cols], banks[2][:, :], pv[:, l, PV_CKVG:PV_CKVG + 1],
                                                                   rs2[b][:, 1, :], ALU.mult, ALU.mult)),
             reads=[bank_tok[2], rs2_t[b], pv_t], writes=[lat_t[tt]])
        P.op("act", (lambda cols=cols: A_.copy(KR[0:32, cols], banks[3][0:32, :])), reads=[bank_tok[3]], writes=[lat_t[tt]])
    if env.get("stop") == "latent":
        return
    P.barrier()
    AR.release(m1_mark)
    tmpk, tmpk_t = _chain_tmp2(AR, tmp_t)
    QT = AR.alloc([S], BF16)
    KT = AR.alloc([S], BF16)
    QT_t = toks(8)
    KT_t = toks(8)
    VH = AR.alloc([32, 128], BF16)
    VH_t = Tok()
    P.op("dve", lambda: V.memset(VH[:, :, 64:128], 1.0), writes=[VH_t])
    Pb = [AR.alloc([512], BF16) for _ in range(4)]
    Pb_t = toks(4)
    gq = pv[:, l, PV_MQG:PV_MQG + 1]
    gk = pv[:, l, PV_MKG:PV_MKG + 1]
    SC = math.sqrt(96.0)
    cnt = 0
    csets = [(tmp, tmp_t), (tmpk, tmpk_t)] + [_chain_tmp2(AR, tmp_t) for _ in range(2)]
    fsets = [(tmp, tmp_t), _fin_tmp(AR)]
    fin_gen = None
    for h in range(8):
        for tt in range(8):
            for j in range(4):
                P.op("pe", _mm(nc, banks[7][:, j * 64:(j + 1) * 64], CKVN[:, tt * 512 + j * 128:tt * 512 + (j + 1) * 128],
                               wukv[:, h * 128 + 64:h * 128 + 128], True, True),
                     reads=[w_t, lat_t[tt]], writes=[bank_tok[7]])
            P.op("act", (lambda tt=tt: A_.copy(VH[:, tt * 4:(tt + 1) * 4, 0:64], banks[7][:, 0:256].rearrange("p (j c) -> p j c", j=4))),
                 reads=[bank_tok[7]], writes=[VH_t])
        chains = []
        for ci in range(16):
            tt, kind = ci // 2, ci % 2
            st_ = ci % 4
            cols = slice(tt * 512, (tt + 1) * 512)
            if kind == 0:
                def proj(st_=st_, cols=cols, tt=tt, h=h):
                    for c in range(2):
                        P.op("pe", _mm(nc, banks[st_][0:96, :], wuq[:, c, h * 96:(h + 1) * 96], CQN[:, c, cols], c == 0, c == 1),
                             reads=[w_t, lat_t[tt]], writes=[bank_tok[st_]])
                chains.append(dict(src=st_, np=96, ones=ONES96, eps=96 * EPS, g=gq, rot=RM_, tt=tt, out=QT[0:96, cols], out_tok=QT_t[tt],
                                   tmp=csets[st_][0], tmp_t=csets[st_][1], xb=4 + st_, proj=proj))
            else:
                def proj(st_=st_, cols=cols, tt=tt, h=h):
                    P.op("pe", _mm(nc, banks[st_][0:96, :], wkp[:, h, :], CKVN[:, cols], True, False), reads=[wkp_t, lat_t[tt]], writes=[bank_tok[st_]])
                    P.op("pe", _mm(nc, banks[st_][0:96, :], EPAD[0:32, 0:96], KR[0:32, cols], False, True), reads=[cm_t, lat_t[tt]], writes=[bank_tok[st_]])
                chains.append(dict(src=st_, np=96, ones=ONES96, eps=96 * EPS, g=gk, rot=RM_, tt=tt, out=KT[0:96, cols], out_tok=KT_t[tt],
                                   tmp=csets[st_][0], tmp_t=csets[st_][1], xb=4 + st_, proj=proj))
        chain_pipeline(ctx, chains, CT, ST)
        if env.get("stop") == "prep":
            return
        for qt in range(8):
            if env.get("stop") == "attn1" and qt == 1:
                return
            qcols = slice(qt * 512, (qt + 1) * 512)
            nkb = 4 * qt + 4
            SBK = (0, 1, 2, 3, 4)
            NSB = 5
            ob = 6 + qt % 2
            fs = fsets[qt % 2]
            base = cnt
            cnt += nkb
            for i in range(nkb + 3):
                if fin_gen is not None:
                    try:
                        next(fin_gen)
                    except StopIteration:
                        fin_gen = None
                if i < nkb:
                    kb = i
                    sb = SBK[(base + kb) % NSB]
                    c0 = max(0, kb - 4 * qt) * 128
                    P.op("pe", _mm(nc, banks[sb][:, c0:512], KT[0:96, kb * 128:(kb + 1) * 128], QT[0:96, qt * 512 + c0:(qt + 1) * 512], True, True),
                         reads=[KT_t[kb // 4], QT_t[qt]], writes=[bank_tok[sb]])
                if 0 <= i - 2 < nkb:
                    kb = i - 2
                    sb = SBK[(base + kb) % NSB]
                    pi = (base + kb) % 4
                    c0 = max(0, kb - 4 * qt) * 128
                    P.op("act", (lambda sb=sb, pi=pi, c0=c0: A_.activation(out=Pb[pi][:, c0:512], in_=banks[sb][:, c0:512], func=AF.Exp, scale=SC)),
                         reads=[bank_tok[sb]], writes=[Pb_t[pi]])
                    j = kb - 4 * qt
                    if j >= 0:
                        P.op("dve", (lambda pi=pi, c0=c0: V.tensor_tensor(out=Pb[pi][:, c0:c0 + 128], in0=Pb[pi][:, c0:c0 + 128], in1=cmask[:, 0:128], op=ALU.mult)),
                             reads=[Pb_t[pi], cmask_t], writes=[Pb_t[pi]])
                if 0 <= i - 3 < nkb:
                    kb = i - 3
                    pi = (base + kb) % 4
                    c0 = max(0, kb - 4 * qt) * 128
                    P.op("pe", _mm(nc, banks[ob][:, c0:512], VH[:, kb, :], Pb[pi][:, c0:512], kb == 0, kb == nkb - 1),
                         reads=[VH_t, Pb_t[pi]], writes=[bank_tok[ob]])
            while fin_gen is not None:
                try:
                    next(fin_gen)
                except StopIteration:
                    fin_gen = None
            P.op("act", (lambda fs=fs, ob=ob: A_.copy(fs[0]["den"][64:128, :], banks[ob][64:128, :])), reads=[bank_tok[ob]], writes=[fs[1]["den"]])
            fin_gen = finalize_head(ctx, banks[ob][0:64, :], bank_tok[ob], fs[0]["den"][64:128, :], fs[1]["den"], wgm, w_t, h * 64, hT, hT_t, qt,
                                    ymla[h * 64:(h + 1) * 64, qcols], fs[0], fs[1], gbank=5)
        while fin_gen is not None:
            try:
                next(fin_gen)
            except StopIteration:
                fin_gen = None


def phase_dil(ctx, l, hT, hT_t):
    nc, P, AR = ctx["nc"], ctx["P"], ctx["AR"]
    V, A_, G_ = nc.vector, nc.scalar, nc.gpsimd
    env = ctx["env"]
    banks, bank_tok = ctx["banks"], ctx["bank_tok"]
    pv, pv_t, cm, cm_t = ctx["pv"], ctx["pv_t"], ctx["cm"], ctx["cm_t"]
    ONESB64, RD_ = cm[:, 3, :], cm[:, 5, :]
    cmask, cmask_t = ctx["cmask"], ctx["cmask_t"]
    DMASK = cmask[:, 2048:2048 + 256]
    ydil = env["ydil"]
    w_r = env["w_in"][l].rearrange("(c p) n -> p c n", p=128)
    CT = AR.alloc([S], BF16)
    ST = AR.alloc([S], BF16)
    tmp, tmp_t = _chain_tmp(AR)
    P.dma("sp", lambda: nc.sync.dma_start(out=CT, in_=env["tabs"][0]), writes=[tmp_t["tab"]])
    P.dma("sp", lambda: nc.sync.dma_start(out=ST, in_=env["tabs"][1]), writes=[tmp_t["tab"]])
    csets = [(tmp, tmp_t)] + [_chain_tmp2(AR, tmp_t) for _ in range(3)]
    fsets = [(tmp, tmp_t)] + [_fin_tmp(AR) for _ in range(3)]
    wqk = [AR.alloc([8, 128], BF16) for _ in range(2)]
    wv = [AR.alloc([8, 64], BF16) for _ in range(2)]
    wgh_t = toks(2)
    wdg = AR.alloc([8, 512], BF16)
    wdg_t = Tok()
    P.dma("pool", lambda: G_.dma_start(out=wdg, in_=w_r[:, :, C_DG:C_DG + 512]), writes=[wdg_t])
    QK = AR.alloc([S], BF16)
    K0 = AR.alloc([S], BF16)
    QK_t = toks(8)
    K0_t = toks(8)
    VD = AR.alloc([32, 128], BF16)
    VD_t = Tok()
    P.op("dve", lambda: V.memset(VD[:, :, 64:128], 1.0), writes=[VD_t])
    OACC = AR.alloc([S], F32)
    OACC_t = Tok()
    Pd = [AR.alloc([256], BF16) for _ in range(4)]
    Pd_t = toks(4)
    gqk = pv[:, l, PV_DQKG:PV_DQKG + 1]
    it = 0
    cnt = 0
    for h in range(8):
        for g in range(3):
            d = DIL[g]
            nb = S // (128 * d)
            wb = it % 2
            it += 1
            hd = g * 8 + h
            P.dma("pool", (lambda wb=wb, hd=hd: G_.dma_start(out=wqk[wb][:, :, 0:64], in_=w_r[:, :, C_DQ + hd * 64:C_DQ + (hd + 1) * 64])), writes=[wgh_t[wb]])
            P.dma("pool", (lambda wb=wb, hd=hd: G_.dma_start(out=wqk[wb][:, :, 64:128], in_=w_r[:, :, C_DK + hd * 64:C_DK + (hd + 1) * 64])), writes=[wgh_t[wb]])
            P.dma("pool", (lambda wb=wb, hd=hd: G_.dma_start(out=wv[wb], in_=w_r[:, :, C_DV + hd * 64:C_DV + (hd + 1) * 64])), writes=[wgh_t[wb]])
            chains = []
            for tt in range(8):
                st_ = tt % 4
                cols = slice(tt * 512, (tt + 1) * 512)

                def proj(st_=st_, tt=tt, wb=wb):
                    proj_group(ctx, st_, wqk[wb], wgh_t[wb], 0, 128, hT, hT_t, tt)

                def post(cols=cols, tt=tt):
                    P.op("dve", lambda: V.tensor_copy(K0[0:64, cols], QK[64:128, cols]), reads=[QK_t[tt]], writes=[K0_t[tt]])
                chains.append(dict(src=st_, np=128, ones=ONESB64, eps=EPS, g=gqk, rot=RD_, tt=tt, out=QK[:, cols], out_tok=QK_t[tt],
                                   tmp=csets[st_][0], tmp_t=csets[st_][1], xb=4 + st_, proj=proj, post=post))
            chain_pipeline(ctx, chains, CT, ST)

            def ucols(r, n, d=d):
                st0 = r + d * 128 * n
                return slice(st0, st0 + d * 127 + 1, d)
            units = [(r, n) for r in range(d) for n in range(nb)]
            for u0 in range(0, 32, 4):
                for jj in range(4):
                    r, n = units[u0 + jj]
                    for k in range(8):
                        P.op("pe", _mm(nc, banks[7][:, jj * 64:(jj + 1) * 64], hT[:, k, ucols(r, n)], wv[wb][:, k, :], k == 0, k == 7),
                             reads=[wgh_t[wb]] + hT_t, writes=[bank_tok[7]])
                P.op("act", (lambda u0=u0: A_.copy(VD[:, u0:u0 + 4, 0:64], banks[7][:, 0:256].rearrange("p (j c) -> p j c", j=4))),
                     reads=[bank_tok[7]], writes=[VD_t])
            SBK = (0, 1, 2, 3, 4, 5)
            base = cnt
            cnt += 32
            for i in range(32 + 3):
                if i < 32:
                    u = i
                    r, n = units[u]
                    sb = SBK[(base + u) % 6]
                    qc = ucols(r, n)
                    if n > 0:
                        P.op("pe", _mm(nc, banks[sb][:, 0:128], K0[0:64, ucols(r, n - 1)], QK[0:64, qc], True, True),
                             reads=K0_t + QK_t, writes=[bank_tok[sb]])
                    P.op("pe", _mm(nc, banks[sb][:, 128:256], K0[0:64, qc], QK[0:64, qc], True, True),
                         reads=K0_t + QK_t, writes=[bank_tok[sb]])
                if 0 <= i - 2 < 32:
                    u = i - 2
                    r, n = units[u]
                    sb = SBK[(base + u) % 6]
                    pi = (base + u) % 4
                    lo = 0 if n > 0 else 128
                    P.op("act", (lambda sb=sb, pi=pi, lo=lo: A_.activation(out=Pd[pi][:, lo:256], in_=banks[sb][:, lo:256], func=AF.Exp, scale=0.125)),
                         reads=[bank_tok[sb]], writes=[Pd_t[pi]])
                    P.op("dve", (lambda pi=pi, lo=lo: V.tensor_tensor(out=Pd[pi][:, lo:256], in0=Pd[pi][:, lo:256], in1=DMASK[:, lo:256], op=ALU.mult)),
                         reads=[Pd_t[pi], cmask_t], writes=[Pd_t[pi]])
                if 0 <= i - 3 < 32:
                    u = i - 3
                    r, n = units[u]
                    pi = (base + u) % 4
                    qc = ucols(r, n)
                    ob = 6 + (base + u) % 2
                    if n > 0:
                        P.op("pe", _mm(nc, banks[ob][:, 0:128], VD[:, u - 1, :], Pd[pi][:, 0:128], True, False), reads=[VD_t, Pd_t[pi]], writes=[bank_tok[ob]])
                    P.op("pe", _mm(nc, banks[ob][:, 0:128], VD[:, u, :], Pd[pi][:, 128:256], n == 0, True), reads=[VD_t, Pd_t[pi]], writes=[bank_tok[ob]])
                    if g == 0:
                        P.op("dve", (lambda qc=qc, ob=ob: V.tensor_copy(OACC[:, qc], banks[ob][:, 0:128])), reads=[bank_tok[ob]], writes=[OACC_t])
                    else:
                        P.op("dve", (lambda qc=qc, ob=ob: V.tensor_tensor(out=OACC[:, qc], in0=banks[ob][:, 0:128], in1=OACC[:, qc], op=ALU.add)),
                             reads=[bank_tok[ob], OACC_t], writes=[OACC_t])
        for t4 in range(0, 8, 4):
            gens = []
            for jj in range(4):
                tt = t4 + jj
                cols = slice(tt * 512, (tt + 1) * 512)
                gens.append(finalize_head(ctx, OACC[0:64, cols], OACC_t, OACC[64:128, cols], OACC_t, wdg, wdg_t, h * 64, hT, hT_t, tt,
                                          ydil[h * 64:(h + 1) * 64, cols], fsets[jj][0], fsets[jj][1], gbank=4 + jj))
            lockstep(gens)


def phase_out(ctx, l, src, dst):
    nc, P, AR = ctx["nc"], ctx["P"], ctx["AR"]
    V, A_, G_ = nc.vector, nc.scalar, nc.gpsimd
    env = ctx["env"]
    banks, bank_tok = ctx["banks"], ctx["bank_tok"]
    TE = 256
    wlo = AR.alloc([8, 1024], BF16)
    wmo = AR.alloc([4, 1024], BF16)
    wdo = AR.alloc([4, 1024], BF16)
    wou = AR.alloc([8, 1024], BF16)
    w_t = Tok()
    for wt_, nm in ((wlo, "w_lru_o"), (wmo, "w_mla_o"), (wdo, "w_dil_o"), (wou, "w_out")):
        P.dma("pool", (lambda wt_=wt_, nm=nm: G_.dma_start(out=wt_, in_=env[nm][l].rearrange("(c p) n -> p c n", p=128))), writes=[w_t])
    YL = [AR.alloc([8, TE], BF16) for _ in range(2)]
    YM = [AR.alloc([4, TE], BF16) for _ in range(2)]
    YD = [AR.alloc([4, TE], BF16) for _ in range(2)]
    GT = [AR.alloc([24, TE], BF16) for _ in range(2)]
    XT = [AR.alloc([2, D], F32) for _ in range(2)]
    OUT = [AR.alloc([2, D], F32) for _ in range(2)]
    M = [AR.alloc([8, TE], BF16) for _ in range(2)]
    t1 = [AR.alloc([TE], F32) for _ in range(2)]
    t2 = [AR.alloc([TE], F32) for _ in range(2)]
    in_t = toks(2)
    out_t = toks(2)
    M_t = toks(2)
    t1_t = toks(2)
    t2_t = toks(2)
    ylru_r = env["ylru"].rearrange("(c p) t -> p c t", p=128)
    ymla_r = env["ymla"].rearrange("(c p) t -> p c t", p=128)
    ydil_r = env["ydil"].rearrange("(c p) t -> p c t", p=128)
    gts_r = env["gts"].rearrange("(c p) t -> p c t", p=128)
    is_final = dst is env["out_d"]
    k = 0
    pending = []
    for e in range(S // TE):
        b = e % 2
        tc_ = slice(e * TE, (e + 1) * TE)
        P.dma("sp", (lambda b=b, tc_=tc_: nc.sync.dma_start(out=YL[b], in_=ylru_r[:, :, tc_])), writes=[in_t[b]])
        P.dma("sp", (lambda b=b, tc_=tc_: nc.sync.dma_start(out=YM[b], in_=ymla_r[:, :, tc_])), writes=[in_t[b]])
        P.dma("sp", (lambda b=b, tc_=tc_: nc.sync.dma_start(out=YD[b], in_=ydil_r[:, :, tc_])), writes=[in_t[b]])
        P.dma("sp", (lambda b=b, tc_=tc_: nc.sync.dma_start(out=GT[b], in_=gts_r[:, :, tc_])), writes=[in_t[b]])
        P.dma("sp", (lambda b=b, e=e: nc.sync.dma_start(out=XT[b], in_=src[e * TE:(e + 1) * TE, :].rearrange("(j p) d -> p j d", p=128))), writes=[in_t[b]])
        for j in range(8):
            if j == 4 and pending:
                pending.pop(0)()
            js = slice(j * 128, (j + 1) * 128)
            kk = k % 2
            k += 1
            bl, bm, bd = 0 + kk, 2 + kk, 4 + kk
            for c in range(8):
                P.op("pe", _mm(nc, banks[bl][:, 0:TE], wlo[:, c, js], YL[b][:, c, :], c == 0, c == 7), reads=[w_t, in_t[b]], writes=[bank_tok[bl]])
            for c in range(4):
                P.op("pe", _mm(nc, banks[bm][:, 0:TE], wmo[:, c, js], YM[b][:, c, :], c == 0, c == 3), reads=[w_t, in_t[b]], writes=[bank_tok[bm]])
            for c in range(4):
                P.op("pe", _mm(nc, banks[bd][:, 0:TE], wdo[:, c, js], YD[b][:, c, :], c == 0, c == 3), reads=[w_t, in_t[b]], writes=[bank_tok[bd]])
            P.op("dve", (lambda b=b, j=j, kk=kk, bl=bl: V.tensor_tensor(out=t1[kk], in0=banks[bl][:, 0:TE], in1=GT[b][:, j, :], op=ALU.mult)),
                 reads=[bank_tok[bl], in_t[b]], writes=[t1_t[kk]])
            P.op("dve", (lambda b=b, j=j, kk=kk, bm=bm: V.tensor_tensor(out=t2[kk], in0=banks[bm][:, 0:TE], in1=GT[b][:, 8 + j, :], op=ALU.mult)),
                 reads=[bank_tok[bm], in_t[b]], writes=[t2_t[kk]])
            P.op("dve", (lambda kk=kk: V.tensor_tensor(out=t1[kk], in0=t1[kk], in1=t2[kk], op=ALU.add)),
                 reads=[t1_t[kk], t2_t[kk]], writes=[t1_t[kk]])
            P.op("dve", (lambda b=b, j=j, kk=kk, bd=bd: V.tensor_tensor(out=t2[kk], in0=banks[bd][:, 0:TE], in1=GT[b][:, 16 + j, :], op=ALU.mult)),
                 reads=[bank_tok[bd], in_t[b]], writes=[t2_t[kk]])
            P.op("dve", (lambda b=b, j=j, kk=kk: V.tensor_tensor(out=M[b][:, j, :], in0=t1[kk], in1=t2[kk], op=ALU.add)),
                 reads=[t1_t[kk], t2_t[kk]], writes=[M_t[b]])
        def wout(b=b, e=e):
            for s_ in range(2):
                for hf in range(2):
                    bo = 6 + (s_ * 2 + hf) % 2
                    for c in range(8):
                        P.op("pe", _mm(nc, banks[bo][:, :], M[b][:, c, s_ * 128:(s_ + 1) * 128], wou[:, c, hf * 512:(hf + 1) * 512], c == 0, c == 7),
                             reads=[w_t, M_t[b]], writes=[bank_tok[bo]])
                    P.op("dve", (lambda b=b, s_=s_, hf=hf, bo=bo: V.tensor_tensor(out=OUT[b][:, s_, hf * 512:(hf + 1) * 512], in0=banks[bo][:, :],
                                                                                in1=XT[b][:, s_, hf * 512:(hf + 1) * 512], op=ALU.add)),
                         reads=[bank_tok[bo], in_t[b]], writes=[out_t[b]])
            P.dma("sp", (lambda b=b, e=e: nc.sync.dma_start(out=dst[e * TE:(e + 1) * TE, :].rearrange("(j p) d -> p j d", p=128), in_=OUT[b])),
                  reads=[out_t[b]], is_output=is_final)
        pending.append(wout)
    while pending:
        pending.pop(0)()
```
